# Optimizing a Trainium2 kernel written in Bass

```python
import math
import jax, jax.numpy as jnp
from jax import lax
import numpy as np

D_MODEL = 1024
BATCH = 8
SEQ = 4096
DEPTH = 4

N_POOL_LAYERS = DEPTH // 2
N_ATTN_LAYERS = DEPTH - N_POOL_LAYERS
POOL_WINDOWS = (2, 4, 8, 16)
N_POOL_GROUPS = len(POOL_WINDOWS)
POOL_GROUP_DIM = D_MODEL // N_POOL_GROUPS
BRANCHES = ((128, 1), (512, 4), (2048, 16))
N_BRANCHES = len(BRANCHES)
HEAD_DIM = 64
N_HEADS = D_MODEL // HEAD_DIM
D_ATTN = N_HEADS * HEAD_DIM
ATTN_BLOCK = 128
D_FF = 2816
CONV_WIDTH = 3
EPS = 1e-6
ADA_SCALE = 0.5

kernel_name = "yoco_pool_dilated_alibi_hybrid"


def _rmsnorm(x, g):
    x32 = x.astype(jnp.float32)
    y = x32 * lax.rsqrt(jnp.mean(x32 * x32, axis=-1, keepdims=True) + EPS)
    return (y * g.astype(jnp.float32)).astype(x.dtype)


def _modulate(h, shift, scale):
    return h * (1 + scale[:, None, :]) + shift[:, None, :]


def _alibi_slopes(n):
    def pow2(m):
        start = 2.0 ** (-(2.0 ** -(math.log2(m) - 3)))
        return [start ** (i + 1) for i in range(m)]
    if math.log2(n).is_integer():
        s = pow2(n)
    else:
        c = 2 ** math.floor(math.log2(n))
        s = pow2(c) + pow2(2 * c)[0::2][: n - c]
    s = np.asarray(s, dtype=np.float32)
    return -np.sort(-s)


def _pool_mixer(h, w_in, w_grp, scale, w_out):
    b, s, _ = h.shape
    u = (h @ w_in).reshape(b, s, N_POOL_GROUPS, POOL_GROUP_DIM)
    u32 = u.astype(jnp.float32)
    csum = jnp.cumsum(u32, axis=1)
    t = jnp.arange(s)
    outs = []
    for g, w in enumerate(POOL_WINDOWS):
        cs = csum[:, :, g]
        lag = jnp.pad(cs, ((0, 0), (w, 0), (0, 0)))[:, :s]
        count = jnp.minimum(t + 1, w).astype(jnp.float32)[None, :, None]
        pooled = (cs - lag) / count - u32[:, :, g]
        outs.append(jnp.einsum('bsc,cd->bsd', pooled.astype(h.dtype), w_grp[g]))
    y = jnp.concatenate(outs, axis=-1) * scale
    return y @ w_out


def _causal_dwconv(a, w, bias):
    s = a.shape[1]
    ap = jnp.pad(a, ((0, 0), (CONV_WIDTH - 1, 0), (0, 0)))
    y = bias
    for k in range(CONV_WIDTH):
        y = y + ap[:, k:k + s] * w[k]
    return y


def _conv_ffn(h, w_up, conv_w, conv_b, w_down):
    a, v = jnp.split(h @ w_up, 2, axis=-1)
    a = _causal_dwconv(a, conv_w, conv_b)
    return (jax.nn.silu(a) * v) @ w_down


def _dilated_branch(q, k, v, window, dilation, slopes):
    b, s, h, dh = q.shape
    n_steps = window // dilation
    blk = max(ATTN_BLOCK, n_steps)
    sub_len = s // dilation
    nb = -(-sub_len // blk)
    sub_pad = nb * blk

    def to_sub(t):
        t = t.reshape(b, sub_len, dilation, h, dh).transpose(0, 2, 1, 3, 4).reshape(b * dilation, sub_len, h, dh)
        return jnp.pad(t, ((0, 0), (0, sub_pad - sub_len), (0, 0), (0, 0)))

    def with_prev(t):
        tb = t.reshape(-1, nb, blk, h, dh)
        prev = jnp.pad(tb, ((0, 0), (1, 0), (0, 0), (0, 0), (0, 0)))[:, :nb]
        return jnp.concatenate([prev, tb], axis=2)

    qb = to_sub(q).reshape(-1, nb, blk, h, dh)
    kb = with_prev(to_sub(k))
    vb = with_prev(to_sub(v))

    scores = jnp.einsum('bnqhd,bnkhd->bnhqk', qb, kb).astype(jnp.float32) * (dh ** -0.5)
    qi = jnp.arange(blk)[:, None] + blk
    ki = jnp.arange(2 * blk)[None, :]
    delta = qi - ki
    key_idx = jnp.arange(nb)[:, None] * blk + jnp.arange(2 * blk)[None, :] - blk
    valid = ((delta >= 0) & (delta <= n_steps))[None] & (key_idx >= 0)[:, None, :]
    bias = -slopes[:, None, None] * (delta * dilation).astype(jnp.float32)[None]
    scores = jnp.where(valid[None, :, None], scores + bias[None, None], -jnp.inf)
    lse = jax.nn.logsumexp(scores, axis=-1)
    p = jnp.exp(scores - lse[..., None])
    out = jnp.einsum('bnhqk,bnkhd->bnqhd', p.astype(v.dtype), vb)

    out = out.reshape(b, dilation, sub_pad, h, dh)[:, :, :sub_len]
    out = out.transpose(0, 2, 1, 3, 4).reshape(b, s, h, dh)
    lse = lse.transpose(0, 1, 3, 2).reshape(b, dilation, sub_pad, h)[:, :, :sub_len]
    lse = lse.transpose(0, 2, 1, 3).reshape(b, s, h)
    return out, lse


def _dilated_attention(h, kv, w_q, w_o, slopes):
    b, s, _ = h.shape
    q = (h @ w_q).reshape(b, s, N_BRANCHES, N_HEADS, HEAD_DIM)
    outs, lses = [], []
    for g, (window, dil) in enumerate(BRANCHES):
        o, l = _dilated_branch(q[:, :, g], kv[:, :, 0, g], kv[:, :, 1, g], window, dil, slopes[g])
        outs.append(o)
        lses.append(l)
    wts = jax.nn.softmax(jnp.stack(lses, axis=0), axis=0)
    o = jnp.sum(wts[..., None] * jnp.stack(outs, axis=0).astype(jnp.float32), axis=0)
    return o.reshape(b, s, D_ATTN).astype(h.dtype) @ w_o


def setup_inputs(seed: int = 0) -> dict:
    key = jax.random.key(seed)
    ks = jax.random.split(key, 24)

    def nrm(k, shape, scale):
        return jax.random.normal(k, shape, jnp.float32) * scale

    D, F, G = D_MODEL, D_FF, N_BRANCHES
    return {
        "x": nrm(ks[0], (BATCH, SEQ, D), 1.0),
        "c": nrm(ks[1], (BATCH, D), 1.0),
        "ada_w": nrm(ks[2], (DEPTH, D, 6 * D), ADA_SCALE * D ** -0.5),
        "ada_b": nrm(ks[3], (DEPTH, 6 * D), 0.02),
        "norm1_g": 1.0 + nrm(ks[4], (DEPTH, D), 0.05),
        "norm2_g": 1.0 + nrm(ks[5], (DEPTH, D), 0.05),
        "pool_w_in": nrm(ks[6], (N_POOL_LAYERS, D, D), D ** -0.5),
        "pool_w_grp": nrm(ks[7], (N_POOL_LAYERS, N_POOL_GROUPS, POOL_GROUP_DIM, POOL_GROUP_DIM), POOL_GROUP_DIM ** -0.5),
        "pool_scale": 1.0 + nrm(ks[8], (N_POOL_LAYERS, D), 0.1),
        "pool_w_out": nrm(ks[9], (N_POOL_LAYERS, D, D), D ** -0.5),
        "kv_norm_g": 1.0 + nrm(ks[10], (D,), 0.05),
        "kv_ada_w": nrm(ks[11], (D, 2 * D), ADA_SCALE * D ** -0.5),
        "kv_ada_b": nrm(ks[12], (2 * D,), 0.02),
        "w_kv": nrm(ks[13], (D, 2 * G * D_ATTN), D ** -0.5),
        "attn_w_q": nrm(ks[14], (N_ATTN_LAYERS, D, G * D_ATTN), D ** -0.5),
        "attn_w_o": nrm(ks[15], (N_ATTN_LAYERS, D_ATTN, D), D_ATTN ** -0.5),
        "ffn_w_up": nrm(ks[16], (DEPTH, D, 2 * F), D ** -0.5),
        "ffn_conv_w": nrm(ks[17], (DEPTH, CONV_WIDTH, F), CONV_WIDTH ** -0.5),
        "ffn_conv_b": nrm(ks[18], (DEPTH, F), 0.02),
        "ffn_w_down": nrm(ks[19], (DEPTH, F, D), F ** -0.5),
        "final_g": 1.0 + nrm(ks[20], (D,), 0.05),
    }


def reference(x, c, ada_w, ada_b, norm1_g, norm2_g, pool_w_in, pool_w_grp, pool_scale, pool_w_out,
              kv_norm_g, kv_ada_w, kv_ada_b, w_kv, attn_w_q, attn_w_o,
              ffn_w_up, ffn_conv_w, ffn_conv_b, ffn_w_down, final_g):
    b, s, _ = x.shape
    cond = jax.nn.silu(c)
    slopes = jnp.asarray(_alibi_slopes(N_BRANCHES * N_HEADS)).reshape(N_BRANCHES, N_HEADS)
    kv = None
    for layer in range(DEPTH):
        mod = cond @ ada_w[layer] + ada_b[layer]
        sh1, sc1, g1, sh2, sc2, g2 = jnp.split(mod, 6, axis=-1)
        h = _modulate(_rmsnorm(x, norm1_g[layer]), sh1, sc1)
        if layer < N_POOL_LAYERS:
            y = _pool_mixer(h, pool_w_in[layer], pool_w_grp[layer], pool_scale[layer], pool_w_out[layer])
        else:
            if layer == N_POOL_LAYERS:
                kv_shift, kv_scale = jnp.split(cond @ kv_ada_w + kv_ada_b, 2, axis=-1)
                hkv = _modulate(_rmsnorm(x, kv_norm_g), kv_shift, kv_scale)
                kv = (hkv @ w_kv).reshape(b, s, 2, N_BRANCHES, N_HEADS, HEAD_DIM)
            j = layer - N_POOL_LAYERS
            y = _dilated_attention(h, kv, attn_w_q[j], attn_w_o[j], slopes)
        x = x + g1[:, None, :] * y
        h = _modulate(_rmsnorm(x, norm2_g[layer]), sh2, sc2)
        x = x + g2[:, None, :] * _conv_ffn(h, ffn_w_up[layer], ffn_conv_w[layer], ffn_conv_b[layer], ffn_w_down[layer])
    return _rmsnorm(x, final_g)
```

```python
import math
from contextlib import ExitStack

import numpy as np
import concourse.bass as bass
import concourse.mybir as mybir
from concourse.bass_utils import run_bass_kernel_spmd

F32 = mybir.dt.float32
BF16 = mybir.dt.bfloat16
AF = mybir.ActivationFunctionType
ALU = mybir.AluOpType

D = 1024
S = 4096
FF = 2816
NFC = 22
T = 512
NT = S // T
KC = 8
EPS = 1e-6
POOLW = (2, 4, 8, 16)
DIL = (1, 4, 16)
NH = 16
HD = 64
NSLOT = 5
SLOTW = 4096
BIG = 30000.0


def _alibi_slopes(n):
    def pow2(m):
        start = 2.0 ** (-(2.0 ** -(math.log2(m) - 3)))
        return [start ** (i + 1) for i in range(m)]
    if math.log2(n).is_integer():
        s = pow2(n)
    else:
        c = 2 ** math.floor(math.log2(n))
        s = pow2(c) + pow2(2 * c)[0::2][: n - c]
    s = np.asarray(s, dtype=np.float32)
    return -np.sort(-s)


SLOPES = _alibi_slopes(3 * NH).reshape(3, NH)


STAGES = ["mix0", "ffn0", "mix1", "ffn1", "kv", "mix2", "ffn2", "mix3", "ffn3"]


def stage_pieces(st):
    P = []
    if st.startswith("mix"):
        l = int(st[3])
        if l < 2:
            P += [(f"win{l}_0", 4096), (f"win{l}_1", 4096), (f"wgrp{l}", 2048), (f"wout{l}_0", 4096), (f"wout{l}_1", 4096)]
        else:
            P += [(f"wq{l}_{hp}", 3072) for hp in range(8)]
            P += [(f"wo{l}_0", 4096), (f"wo{l}_1", 4096)]
    elif st == "kv":
        P += [(f"wk_{i}", 4096) for i in range(6)]
        P += [(f"wv_{i}", 4096) for i in range(6)]
    else:
        l = int(st[3])
        P += [(f"wup{l}_{i}", 4096) for i in range(11)]
        for r in range(2):
            P += [(f"wdn{l}_{r}_0", 4096), (f"wdn{l}_{r}_1", 4096), (f"wdn{l}_{r}_2", 3072)]
    return P


def piece_list(nstages=len(STAGES)):
    P = []
    for st in STAGES[:nstages]:
        P += stage_pieces(st)
    return P


def ada_piece_list():
    P = []
    for l in range(4):
        P += [(f"ada{l}_{i}", 4096) for i in range(12)]
    P += [(f"kvada_{i}", 4096) for i in range(4)]
    return P


def _offsets(pl):
    off = {}
    o = 0
    for n, w in pl:
        off[n] = (o, w)
        o += w
    return off, o


def vec_layout():
    L = {}
    o = 0

    def add(name, n):
        nonlocal o
        L[name] = (o, n)
        o += n
    for l in range(4):
        add(f"ada_b{l}", 48)
        add(f"n1g{l}", 8)
        add(f"n2g{l}", 8)
        add(f"cw{l}", 66)
        add(f"cb{l}", 22)
    for l in range(2):
        add(f"psc{l}", 8)
    add("kvg", 8)
    add("kvb", 16)
    add("fg", 8)
    add("c", 8)
    add("invc", 64)
    add("dist", 256)
    add("dist2", 96)
    return L, o


def pmaj(v):
    return np.ascontiguousarray(v.reshape(-1, 128).T)


def kmajor(W, c0, ncols):
    K = W.shape[0]
    a = W[:, c0:c0 + ncols].reshape(K // 128, 128, ncols).transpose(1, 0, 2)
    return a.reshape(128, -1)


def host_prepare(inp):
    pl = piece_list()
    off, tot = _offsets(pl)
    W = np.empty((128, tot), np.float32)

    def put(name, arr):
        o, w = off[name]
        assert arr.shape == (128, w), (name, arr.shape, w)
        W[:, o:o + w] = arr
    for l in range(4):
        if l < 2:
            for m in range(2):
                put(f"win{l}_{m}", kmajor(inp["pool_w_in"][l], m * 512, 512))
                put(f"wout{l}_{m}", kmajor(inp["pool_w_out"][l], m * 512, 512))
            g = inp["pool_w_grp"][l]
            put(f"wgrp{l}", g.reshape(4, 2, 128, 256).transpose(2, 0, 1, 3).reshape(128, 2048))
        else:
            j = l - 2
            wq = inp["attn_w_q"][j]
            for hp in range(8):
                a = np.stack([kmajor(wq, g * 1024 + hp * 128, 128).reshape(128, 8, 128) for g in range(3)], axis=1)
                put(f"wq{l}_{hp}", a.reshape(128, 3072))
            for m in range(2):
                put(f"wo{l}_{m}", kmajor(inp["attn_w_o"][j], m * 512, 512))
        if l == 2:
            for i in range(6):
                put(f"wk_{i}", kmajor(inp["w_kv"], i * 512, 512))
                put(f"wv_{i}", kmajor(inp["w_kv"], 3072 + i * 512, 512))
        wu = inp["ffn_w_up"][l]
        for i in range(11):
            parts = []
            for jj in range(2):
                fc = 2 * i + jj
                for av in range(2):
                    parts.append(kmajor(wu, av * FF + fc * 128, 128))
            put(f"wup{l}_{i}", np.concatenate(parts, axis=1))
        wd = inp["ffn_w_down"][l]
        for r in range(2):
            a = wd[:, r * 512:(r + 1) * 512].reshape(NFC, 128, 512).transpose(1, 0, 2)
            put(f"wdn{l}_{r}_0", a[:, 0:8].reshape(128, 4096))
            put(f"wdn{l}_{r}_1", a[:, 8:16].reshape(128, 4096))
            put(f"wdn{l}_{r}_2", a[:, 16:22].reshape(128, 3072))
    apl = ada_piece_list()
    aoff, atot = _offsets(apl)
    A = np.empty((128, atot), np.float32)
    for l in range(4):
        for i in range(12):
            o, w = aoff[f"ada{l}_{i}"]
            A[:, o:o + w] = kmajor(inp["ada_w"][l], i * 512, 512)
    for i in range(4):
        o, w = aoff[f"kvada_{i}"]
        A[:, o:o + w] = kmajor(inp["kv_ada_w"], i * 512, 512)
    VL, nv = vec_layout()
    shared = np.zeros((128, nv), np.float32)

    def vput(name, arr):
        o, n = VL[name]
        assert arr.shape == (128, n), (name, arr.shape)
        shared[:, o:o + n] = arr
    for l in range(4):
        vput(f"ada_b{l}", pmaj(inp["ada_b"][l]))
        vput(f"n1g{l}", pmaj(inp["norm1_g"][l]))
        vput(f"n2g{l}", pmaj(inp["norm2_g"][l]))
        cw = inp["ffn_conv_w"][l]
        vput(f"cw{l}", np.concatenate([pmaj(cw[k]) for k in range(3)], axis=1))
        vput(f"cb{l}", pmaj(inp["ffn_conv_b"][l]))
    for l in range(2):
        vput(f"psc{l}", pmaj(inp["pool_scale"][l]))
    vput("kvg", pmaj(inp["kv_norm_g"]))
    vput("kvb", pmaj(inp["kv_ada_b"]))
    vput("fg", pmaj(inp["final_g"]))
    invc = np.zeros((128, 4, 16), np.float32)
    for g, w in enumerate(POOLW):
        invc[:, g, :] = 1.0 / np.minimum(np.arange(16) + 1, w)
    vput("invc", invc.reshape(128, 64))
    k = np.arange(128)[:, None]
    q = np.arange(128)[None, :]
    dprev = (q - k + 128).astype(np.float32)
    dprev[dprev > 128] = BIG
    dcur = (q - k).astype(np.float32)
    dcur[dcur < 0] = BIG
    vput("dist", np.concatenate([dprev, dcur], axis=1))
    d2 = []
    for na in (32, 64, 96):
        dd = (q[:, 0:32] - k + na).astype(np.float32)
        dd[dd > 128] = BIG
        dd[k[:, 0] >= na, :] = BIG
        d2.append(dd)
    vput("dist2", np.concatenate(d2, axis=1))
    per_core = []
    for b in range(8):
        v = shared.copy()
        o, n = VL["c"]
        v[:, o:o + n] = pmaj(inp["c"][b])
        per_core.append({"xT": np.ascontiguousarray(inp["x"][b].T), "vecs": v})
    return W, A, per_core


class Chan:
    __slots__ = ("sem", "val")


class Buf:
    __slots__ = ("name", "w", "r")

    def __init__(self, name, seed=None):
        self.name = name
        self.w = {}
        self.r = dict(seed) if seed else {}

    def tokens(self):
        d = dict(self.w)
        for ch, v in self.r.items():
            if d.get(ch, 0) < v:
                d[ch] = v
        return d


def _merge(d, ch, v):
    if d.get(ch, 0) < v:
        d[ch] = v


class Ctx:
    def __init__(self, nc, es):
        self.nc = nc
        self.es = es
        self.nsem = 0

    def new_chan(self):
        sem = self.es.enter_context(self.nc.semaphore(f"sm{self.nsem}"))
        self.nsem += 1
        c = Chan()
        c.sem = sem
        c.val = 0
        return c


class Eng:
    EPOCH = 16000

    def __init__(self, ctx, eng, name, is_pe=False, n_dma=0):
        self.ctx = ctx
        self.e = eng
        self.name = name
        self.is_pe = is_pe
        self.chan = ctx.new_chan()
        self.waited = {}
        self.dma_pool = [ctx.new_chan() for _ in range(n_dma)]
        self.dma_i = 0
        self.n = 0

    def wait_tok(self, ch, v):
        if ch is self.chan and self.is_pe:
            return
        if self.waited.get(ch, 0) >= v:
            return
        self.e.wait_ge(ch.sem, v)
        self.waited[ch] = v

    def sync(self, reads, writes):
        for b in reads:
            for ch, v in b.w.items():
                self.wait_tok(ch, v)
        for b in writes:
            for ch, v in b.w.items():
                self.wait_tok(ch, v)
            for ch, v in b.r.items():
                self.wait_tok(ch, v)

    def op(self, fn, reads, writes, *a, **k):
        self.sync(reads, writes)
        ins = fn(*a, **k)
        if self.chan.val >= self.EPOCH:
            self.chan = self.ctx.new_chan()
        ch = self.chan
        ch.val += 1
        ins.then_inc(ch.sem, 1)
        for b in reads:
            _merge(b.r, ch, ch.val)
        for b in writes:
            _merge(b.w, ch, ch.val)
        self.n += 1
        return ins

    def dma(self, out, in_, reads, writes, **k):
        ch = self.dma_pool[self.dma_i % len(self.dma_pool)]
        self.dma_i += 1
        if ch.val:
            self.wait_tok(ch, ch.val)
        self.sync(reads, writes)
        ins = self.e.dma_start(out=out, in_=in_, **k)
        ch.val += 16
        ins.then_inc(ch.sem, 16)
        for b in reads:
            _merge(b.r, ch, ch.val)
        for b in writes:
            _merge(b.w, ch, ch.val)
        self.n += 1
        return ins

    def wait_all(self, bufs):
        for b in bufs:
            for ch, v in b.tokens().items():
                self.wait_tok(ch, v)


class Prog:
    def __init__(self, n_tiles=NT, nstages=len(STAGES), dbg=None):
        self.n_tiles = n_tiles
        self.nstages = nstages
        self.full = nstages == len(STAGES)
        self.dbg = dbg
        nc = bass.Bass("TRN2", target_bir_lowering=False)
        self.nc = nc
        self.es = ExitStack()
        self.pl = piece_list()
        self.poff, self.ptot = _offsets(self.pl)
        self.apl = ada_piece_list()
        self.aoff, self.atot = _offsets(self.apl)
        self.VL, self.nv = vec_layout()
        self.seed = {}

    def sb(self, es, name, shape, dt):
        self._nalloc = getattr(self, "_nalloc", 0) + 1
        return es.enter_context(self.nc.sbuf_tensor(f"{name}_{self._nalloc}", list(shape), dt))

    def newbuf(self, name):
        b = Buf(name, self.seed)
        self.stage_bufs.append(b)
        return b

    def stage_begin(self):
        self.stage_bufs = []
        return ExitStack()

    def stage_end(self, es):
        seed = dict(self.seed)
        for b in self.stage_bufs:
            for ch, v in b.tokens().items():
                _merge(seed, ch, v)
        self.seed = seed
        es.close()

    def w_init(self):
        self.wslots = [self.sb(self.es, f"wslot{i}", [128, SLOTW], BF16) for i in range(NSLOT)]
        self.wbufs = [Buf(f"wslot{i}") for i in range(NSLOT)]
        self.wsched = []
        self.w_issued = 0
        self.w_used = 0

    def w_issue_upto(self, idx):
        while self.w_issued <= idx and self.w_issued < len(self.wsched):
            i = self.w_issued
            src, w, name = self.wsched[i]
            slot = i % NSLOT
            self.POOL.dma(self.wslots[slot][:, 0:w], src, [], [self.wbufs[slot]], max_dma_last_dim=2048)
            self.w_issued += 1

    def w_next(self, name):
        i = self.w_used
        src, w, nm = self.wsched[i]
        assert nm == name, (nm, name)
        self.w_issue_upto(i + NSLOT - 1)
        self.w_used += 1
        slot = i % NSLOT
        return self.wslots[slot], self.wbufs[slot]

    def build(self):
        nc = self.nc
        es = self.es
        ctx = Ctx(nc, es)
        self.ctx = ctx
        nt = self.n_tiles
        self.xT = nc.dram_tensor("xT", [D, S], F32, kind="ExternalInput").ap()
        self.wts = nc.dram_tensor("wts", [128, self.ptot], F32, kind="ExternalInput").ap()
        self.adaw = nc.dram_tensor("adaw", [128, self.atot], F32, kind="ExternalInput").ap()
        self.vecs_d = nc.dram_tensor("vecs", [128, self.nv], F32, kind="ExternalInput").ap()
        self.outT = nc.dram_tensor("outT", [D, S], F32, kind="ExternalOutput").ap()
        self.kTd = nc.dram_tensor("kTd", [3, 8, 128, S], BF16, kind="Internal").ap()
        self.vd = nc.dram_tensor("vd", [3, S, D], BF16, kind="Internal").ap()
        self.kv_bufs = [Buf(f"kv{s}") for s in range(NT)]

        self.PE = Eng(ctx, nc.tensor, "pe", is_pe=True)
        self.ACT = Eng(ctx, nc.scalar, "act")
        self.DVE = Eng(ctx, nc.vector, "dve")
        self.POOL = Eng(ctx, nc.gpsimd, "pool", n_dma=NSLOT + 1)
        self.SP = Eng(ctx, nc.sync, "sp", n_dma=40)
        PE, ACT, DVE, POOL, SP = self.PE, self.ACT, self.DVE, self.POOL, self.SP

        self.vecs = self.sb(es, "vecs_sb", [128, self.nv], F32)
        self.vecs_b = Buf("vecs")
        self.modT = self.sb(es, "modT", [128, 4 * 48 + 16], F32)
        self.der = self.sb(es, "der", [128, 4 * 16 + 8], F32)
        self.mod_b = Buf("mod")
        self.ones = self.sb(es, "ones", [128, 128], BF16)
        self.ones_b = Buf("ones")
        self.condb = self.sb(es, "condb", [128, 8], BF16)
        self.cond_b = Buf("cond")
        self.xTs = [self.sb(es, f"xTs{i}", [128, KC, T], F32) for i in range(2)]
        self.x_bufs = [Buf(f"x{i}") for i in range(2)]
        self.hT = self.sb(es, "hT", [128, KC, T], BF16)
        self.h_b = Buf("hT")
        self.sq = self.sb(es, "sq", [128, KC, T], BF16)
        self.sq_b = Buf("sq")
        self.std = self.sb(es, "std", [128, T], F32)
        self.rstd = self.sb(es, "rstd", [128, T], F32)
        self.std_b = Buf("std")
        self.rstd_b = Buf("rstd")
        self.ntmp = [self.sb(es, f"ntmp{i}", [128, T], F32) for i in range(2)]
        self.ntmp_b = [Buf(f"ntmp{i}") for i in range(2)]
        self.uhalo = [self.sb(es, f"uhalo{l}", [128, KC, 16], F32) for l in range(2)]
        self.uhalo_b = [Buf(f"uhalo{l}") for l in range(2)]
        self.ahalo = [self.sb(es, f"ahalo{l}", [128, NFC, 2], F32) for l in range(4)]
        self.ahalo_b = [Buf(f"ahalo{l}") for l in range(4)]
        self.masks = self.sb(es, "masks", [128, 48, 256], BF16)
        self.masks_b = Buf("masks")
        self.masks2 = self.sb(es, "masks2", [128, 3, NH, 32], BF16)
        self.ps = [es.enter_context(nc.psum_tensor(f"ps{i}", [128, 512], F32)) for i in range(8)]
        self.ps_b = [Buf(f"ps{i}") for i in range(8)]
        self.w_init()

        for n, w in self.apl:
            o, _ = self.aoff[n]
            self.wsched.append((self.adaw[:, o:o + w], w, n))
        for s in range(nt):
            for n, w in piece_list(self.nstages):
                o, _ = self.poff[n]
                self.wsched.append((self.wts[:, o:o + w], w, n))

        self.prologue()
        for s in range(nt):
            self.tile(s)
        for b in self.out_wait:
            SP.wait_all([b])
        return nc

    def vcol(self, name, c0=0, n=1):
        o, _ = self.VL[name]
        return self.vecs[:, o + c0:o + c0 + n]

    def prologue(self):
        nc = self.nc
        PE, ACT, DVE, POOL, SP = self.PE, self.ACT, self.DVE, self.POOL, self.SP
        SP.dma(self.vecs[:, :], self.vecs_d[:, :], [], [self.vecs_b])
        DVE.op(nc.vector.memset, [], [self.ones_b], self.ones[:, :], 1.0)
        for l in range(2):
            DVE.op(nc.vector.memset, [], [self.uhalo_b[l]], self.uhalo[l][:, :, :], 0.0)
        for l in range(4):
            DVE.op(nc.vector.memset, [], [self.ahalo_b[l]], self.ahalo[l][:, :, :], 0.0)
        ACT.op(nc.scalar.activation, [self.vecs_b], [self.cond_b], out=self.condb[:, :], in_=self.vcol("c", 0, 8), func=AF.Silu)
        pb = 0
        for l in range(5):
            ncol = 48 if l < 4 else 16
            npieces = 12 if l < 4 else 4
            pt, pbuf = self.ps[pb], self.ps_b[pb]
            first = True
            for i in range(npieces):
                slot, sbuf_ = self.w_next(f"ada{l}_{i}" if l < 4 else f"kvada_{i}")
                for mm in range(4):
                    col = i * 4 + mm
                    for kc in range(KC):
                        PE.op(nc.tensor.matmul, [sbuf_, self.cond_b], [pbuf], pt[:, col:col + 1],
                              lhsT=slot[:, kc * 512 + mm * 128: kc * 512 + mm * 128 + 128], rhs=self.condb[:, kc:kc + 1],
                              start=first, stop=(kc == KC - 1))
                        first = False
            bname = f"ada_b{l}" if l < 4 else "kvb"
            DVE.op(nc.vector.tensor_tensor, [pbuf, self.vecs_b], [self.mod_b], out=self.modT[:, l * 48:l * 48 + ncol],
                   in0=pt[:, 0:ncol], in1=self.vcol(bname, 0, ncol), op=ALU.add)
            pb = (pb + 1) % 8
        for l in range(4):
            for j, (gn, sc0) in enumerate(((f"n1g{l}", 8), (f"n2g{l}", 32))):
                DVE.op(nc.vector.scalar_tensor_tensor, [self.mod_b, self.vecs_b], [self.mod_b],
                       out=self.der[:, l * 16 + j * 8: l * 16 + j * 8 + 8], in0=self.modT[:, l * 48 + sc0: l * 48 + sc0 + 8],
                       scalar=1.0, in1=self.vcol(gn, 0, 8), op0=ALU.add, op1=ALU.mult)
        DVE.op(nc.vector.scalar_tensor_tensor, [self.mod_b, self.vecs_b], [self.mod_b],
               out=self.der[:, 64:72], in0=self.modT[:, 192 + 8:192 + 16], scalar=1.0, in1=self.vcol("kvg", 0, 8),
               op0=ALU.add, op1=ALU.mult)
        if self.nstages > 5:
            for g in range(3):
                for h in range(NH):
                    ACT.op(nc.scalar.activation, [self.vecs_b], [self.masks_b], out=self.masks[:, g * NH + h, :],
                           in_=self.vcol("dist", 0, 256), func=AF.Exp, scale=-float(SLOPES[g, h]) * DIL[g])
            for i in range(3):
                for h in range(NH):
                    ACT.op(nc.scalar.activation, [self.vecs_b], [self.masks_b], out=self.masks2[:, i, h, :],
                           in_=self.vcol("dist2", i * 32, 32), func=AF.Exp, scale=-float(SLOPES[2, h]) * DIL[2])

    def A(self, l, which, c):
        return self.der[:, l * 16 + which * 8 + c: l * 16 + which * 8 + c + 1]

    def M(self, l, j, c):
        return self.modT[:, l * 48 + j * 8 + c: l * 48 + j * 8 + c + 1]

    def norm(self, x, xb, acol, bcol, out=None, out_b=None, final=False):
        nc = self.nc
        PE, ACT, DVE = self.PE, self.ACT, self.DVE
        ACT.op(nc.scalar.activation, [xb], [self.sq_b], out=self.sq[:, :, :], in_=x[:, :, :], func=AF.Square)
        pi = self.psrr()
        pt, pbuf = self.ps[pi], self.ps_b[pi]
        for c in range(KC):
            PE.op(nc.tensor.matmul, [self.sq_b, self.ones_b], [pbuf], pt[:, :], lhsT=self.ones[:, :], rhs=self.sq[:, c, :],
                  start=(c == 0), stop=(c == KC - 1))
        ACT.op(nc.scalar.activation, [pbuf, self.eps_b], [self.std_b], out=self.std[:, :], in_=pt[:, :], func=AF.Sqrt,
               scale=1.0 / D, bias=self.epsc[:, 0:1])
        DVE.op(nc.vector.reciprocal, [self.std_b], [self.rstd_b], out=self.rstd[:, :], in_=self.std[:, :])
        for c in range(KC):
            tb = self.ntmp_b[c % 2]
            tt = self.ntmp[c % 2]
            DVE.op(nc.vector.tensor_tensor, [xb, self.rstd_b], [tb], out=tt[:, :], in0=x[:, c, :], in1=self.rstd[:, :], op=ALU.mult)
            if final:
                ACT.op(nc.scalar.activation, [tb, self.vecs_b], [out_b], out=out[:, c, :], in_=tt[:, :], func=AF.Copy,
                       scale=acol(c))
            else:
                ACT.op(nc.scalar.activation, [tb, self.mod_b], [self.h_b], out=self.hT[:, c, :], in_=tt[:, :], func=AF.Identity,
                       scale=acol(c), bias=bcol(c))

    def psrr(self):
        i = self._psi
        self._psi = (self._psi + 1) % 8
        return i

    def tile(self, s):
        nc = self.nc
        PE, ACT, DVE, POOL, SP = self.PE, self.ACT, self.DVE, self.POOL, self.SP
        if s == 0:
            self._psi = 0
            self.out_wait = []
            self.epsc = self.sb(self.es, "epsc", [128, 1], F32)
            self.eps_b = Buf("eps")
            DVE.op(nc.vector.memset, [], [self.eps_b], self.epsc[:, :], EPS)
        x = self.xTs[s % 2]
        xb = self.x_bufs[s % 2]
        SP.dma(x[:, :, :], self.xT.rearrange("(c p) t -> p c t", p=128)[:, :, s * T:(s + 1) * T], [], [xb])
        for st in STAGES[:self.nstages]:
            if st == "kv":
                self.kv_stage(s, x, xb)
            elif st.startswith("mix"):
                l = int(st[3])
                if l < 2:
                    self.pool_mixer(l, s, x, xb)
                else:
                    self.attn_mixer(l, s, x, xb)
            else:
                self.ffn(int(st[3]), s, x, xb)
        es = self.stage_begin()
        o = self.sb(es, "otile", [128, KC, T], F32)
        ob = self.newbuf("otile")
        if self.full:
            self.norm(x, xb, lambda c: self.vcol("fg", c, 1), None, out=o, out_b=ob, final=True)
        else:
            ACT.op(nc.scalar.activation, [xb], [ob], out=o[:, :, :], in_=x[:, :, :], func=AF.Copy)
        SP.dma(self.outT.rearrange("(c p) t -> p c t", p=128)[:, :, s * T:(s + 1) * T], o[:, :, :], [ob], [])
        self.out_wait.append(ob)
        self.stage_end(es)

    def pool_mixer(self, l, s, x, xb):
        nc = self.nc
        PE, ACT, DVE = self.PE, self.ACT, self.DVE
        es = self.stage_begin()
        U = self.sb(es, "poolU", [128, KC, 16 + T], F32)
        Ub = self.newbuf("U")
        A_ = self.sb(es, "poolA", [128, KC, 16 + T], F32)
        Ab = self.newbuf("A")
        B_ = self.sb(es, "poolB", [128, KC, 16 + T], F32)
        Bb = self.newbuf("B")
        Pb = self.sb(es, "poolP", [128, KC, T], BF16)
        Pbb = self.newbuf("P")
        Zb = self.sb(es, "poolZ", [128, KC, T], BF16)
        Zbb = self.newbuf("Z")
        self.norm(x, xb, lambda c: self.A(l, 0, c), lambda c: self.M(l, 0, c))
        DVE.op(nc.vector.tensor_copy, [self.uhalo_b[l]], [Ub], out=U[:, :, 0:16], in_=self.uhalo[l][:, :, :])
        for half in range(2):
            slot, sbuf_ = self.w_next(f"win{l}_{half}")
            for mm in range(4):
                m = half * 4 + mm
                pi = self.psrr()
                pt, pbuf = self.ps[pi], self.ps_b[pi]
                for kc in range(KC):
                    PE.op(nc.tensor.matmul, [sbuf_, self.h_b], [pbuf], pt[:, :], lhsT=slot[:, kc * 512 + mm * 128: kc * 512 + mm * 128 + 128],
                          rhs=self.hT[:, kc, :], start=(kc == 0), stop=(kc == KC - 1))
                ACT.op(nc.scalar.activation, [pbuf], [Ub], out=U[:, m, 16:16 + T], in_=pt[:, :], func=AF.Copy)
        DVE.op(nc.vector.tensor_copy, [Ub], [self.uhalo_b[l]], out=self.uhalo[l][:, :, :], in_=U[:, :, T:T + 16])
        W_ = 16 + T
        DVE.op(nc.vector.tensor_tensor, [Ub], [Ab], out=A_[:, :, 1:W_], in0=U[:, :, 1:W_], in1=U[:, :, 0:W_ - 1], op=ALU.add)
        DVE.op(nc.vector.tensor_tensor, [Ab], [Bb], out=B_[:, 2:8, 3:W_], in0=A_[:, 2:8, 3:W_], in1=A_[:, 2:8, 1:W_ - 2], op=ALU.add)
        DVE.op(nc.vector.tensor_tensor, [Bb], [Ab], out=A_[:, 4:8, 7:W_], in0=B_[:, 4:8, 7:W_], in1=B_[:, 4:8, 3:W_ - 4], op=ALU.add)
        DVE.op(nc.vector.tensor_tensor, [Ab], [Bb], out=B_[:, 6:8, 15:W_], in0=A_[:, 6:8, 15:W_], in1=A_[:, 6:8, 7:W_ - 8], op=ALU.add)
        srcs = [(A_, Ab), (B_, Bb), (A_, Ab), (B_, Bb)]
        for g in range(4):
            St, Sb_ = srcs[g]
            DVE.op(nc.vector.scalar_tensor_tensor, [Sb_, Ub], [Pbb], out=Pb[:, 2 * g:2 * g + 2, :], in0=St[:, 2 * g:2 * g + 2, 16:16 + T],
                   scalar=1.0 / POOLW[g], in1=U[:, 2 * g:2 * g + 2, 16:16 + T], op0=ALU.mult, op1=ALU.subtract)
            if s == 0:
                o, _ = self.VL["invc"]
                for cc in range(2):
                    c = 2 * g + cc
                    tb, tt = self.ntmp_b[cc], self.ntmp[cc]
                    DVE.op(nc.vector.tensor_tensor, [Sb_, self.vecs_b], [tb], out=tt[:, 0:16], in0=St[:, c, 16:32],
                           in1=self.vecs[:, o + g * 16:o + g * 16 + 16], op=ALU.mult)
                    DVE.op(nc.vector.tensor_tensor, [tb, Ub], [Pbb], out=Pb[:, c, 0:16], in0=tt[:, 0:16], in1=U[:, c, 16:32], op=ALU.subtract)
        slot, sbuf_ = self.w_next(f"wgrp{l}")
        for g in range(4):
            for mo in range(2):
                c = 2 * g + mo
                pi = self.psrr()
                pt, pbuf = self.ps[pi], self.ps_b[pi]
                for ki in range(2):
                    base = (g * 2 + ki) * 256 + mo * 128
                    PE.op(nc.tensor.matmul, [sbuf_, Pbb], [pbuf], pt[:, :], lhsT=slot[:, base:base + 128], rhs=Pb[:, 2 * g + ki, :],
                          start=(ki == 0), stop=(ki == 1))
                ACT.op(nc.scalar.activation, [pbuf, self.vecs_b], [Zbb], out=Zb[:, c, :], in_=pt[:, :], func=AF.Copy,
                       scale=self.vcol(f"psc{l}", c, 1))
        for half in range(2):
            slot, sbuf_ = self.w_next(f"wout{l}_{half}")
            for mm in range(4):
                m = half * 4 + mm
                pi = self.psrr()
                pt, pbuf = self.ps[pi], self.ps_b[pi]
                for kc in range(KC):
                    PE.op(nc.tensor.matmul, [sbuf_, Zbb], [pbuf], pt[:, :], lhsT=slot[:, kc * 512 + mm * 128: kc * 512 + mm * 128 + 128],
                          rhs=Zb[:, kc, :], start=(kc == 0), stop=(kc == KC - 1))
                DVE.op(nc.vector.scalar_tensor_tensor, [pbuf, self.mod_b, xb], [xb], out=x[:, m, :], in0=pt[:, :], scalar=self.M(l, 2, m),
                       in1=x[:, m, :], op0=ALU.mult, op1=ALU.add)
        self.stage_end(es)

    def ffn(self, l, s, x, xb):
        nc = self.nc
        PE, ACT, DVE = self.PE, self.ACT, self.DVE
        es = self.stage_begin()
        gT = self.sb(es, "gT", [128, NFC, T], BF16)
        gb = self.newbuf("gT")
        abuf = [self.sb(es, f"abuf{i}", [128, 2 + T], F32) for i in range(2)]
        ab = [self.newbuf(f"abuf{i}") for i in range(2)]
        c1 = [self.sb(es, f"c1_{i}", [128, T], F32) for i in range(2)]
        c1b = [self.newbuf(f"c1_{i}") for i in range(2)]
        c2 = [self.sb(es, f"c2_{i}", [128, T], F32) for i in range(2)]
        c2b = [self.newbuf(f"c2_{i}") for i in range(2)]
        self.norm(x, xb, lambda c: self.A(l, 1, c), lambda c: self.M(l, 3, c))
        cwo, _ = self.VL[f"cw{l}"]
        cbo, _ = self.VL[f"cb{l}"]
        for i in range(11):
            slot, sbuf_ = self.w_next(f"wup{l}_{i}")
            for jj in range(2):
                fc = 2 * i + jj
                par = fc % 2
                pa, pab = self.ps[par * 2], self.ps_b[par * 2]
                pv, pvb = self.ps[par * 2 + 1], self.ps_b[par * 2 + 1]
                for av, (pt, pbuf) in enumerate(((pa, pab), (pv, pvb))):
                    base = (jj * 2 + av) * 1024
                    for kc in range(KC):
                        PE.op(nc.tensor.matmul, [sbuf_, self.h_b], [pbuf], pt[:, :], lhsT=slot[:, base + kc * 128: base + kc * 128 + 128],
                              rhs=self.hT[:, kc, :], start=(kc == 0), stop=(kc == KC - 1))
                A_, Ab = abuf[par], ab[par]
                ACT.op(nc.scalar.activation, [self.ahalo_b[l]], [Ab], out=A_[:, 0:2], in_=self.ahalo[l][:, fc, :], func=AF.Copy)
                ACT.op(nc.scalar.activation, [pab], [Ab], out=A_[:, 2:2 + T], in_=pa[:, :], func=AF.Copy)
                ACT.op(nc.scalar.activation, [Ab], [self.ahalo_b[l]], out=self.ahalo[l][:, fc, :], in_=A_[:, T:T + 2], func=AF.Copy)
                ACT.op(nc.scalar.activation, [pab, self.vecs_b], [c1b[par]], out=c1[par][:, :], in_=pa[:, :], func=AF.Identity,
                       scale=self.vecs[:, cwo + 2 * NFC + fc: cwo + 2 * NFC + fc + 1], bias=self.vecs[:, cbo + fc:cbo + fc + 1])
                DVE.op(nc.vector.scalar_tensor_tensor, [Ab, self.vecs_b, c1b[par]], [c2b[par]], out=c2[par][:, :], in0=A_[:, 1:1 + T],
                       scalar=self.vecs[:, cwo + NFC + fc: cwo + NFC + fc + 1], in1=c1[par][:, :], op0=ALU.mult, op1=ALU.add)
                DVE.op(nc.vector.scalar_tensor_tensor, [Ab, self.vecs_b, c2b[par]], [c1b[par]], out=c1[par][:, :], in0=A_[:, 0:T],
                       scalar=self.vecs[:, cwo + fc: cwo + fc + 1], in1=c2[par][:, :], op0=ALU.mult, op1=ALU.add)
                ACT.op(nc.scalar.activation, [c1b[par]], [c2b[par]], out=c2[par][:, :], in_=c1[par][:, :], func=AF.Silu)
                DVE.op(nc.vector.tensor_tensor, [c2b[par], pvb], [gb], out=gT[:, fc, :], in0=c2[par][:, :], in1=pv[:, :], op=ALU.mult)
        for r in range(2):
            for j in range(3):
                slot, sbuf_ = self.w_next(f"wdn{l}_{r}_{j}")
                nf = 8 if j < 2 else 6
                for fl in range(nf):
                    fc = 8 * j + fl
                    for mm in range(4):
                        PE.op(nc.tensor.matmul, [sbuf_, gb], [self.ps_b[4 + mm]], self.ps[4 + mm][:, :],
                              lhsT=slot[:, fl * 512 + mm * 128: fl * 512 + mm * 128 + 128], rhs=gT[:, fc, :],
                              start=(fc == 0), stop=(fc == NFC - 1))
            for mm in range(4):
                m = r * 4 + mm
                DVE.op(nc.vector.scalar_tensor_tensor, [self.ps_b[4 + mm], self.mod_b, xb], [xb], out=x[:, m, :], in0=self.ps[4 + mm][:, :],
                       scalar=self.M(l, 5, m), in1=x[:, m, :], op0=ALU.mult, op1=ALU.add)
        self._psi = 0
        self.stage_end(es)

    def kv_stage(self, s, x, xb):
        nc = self.nc
        PE, ACT, DVE, SP = self.PE, self.ACT, self.DVE, self.SP
        es = self.stage_begin()
        kst = [self.sb(es, f"kst{i}", [128, T], BF16) for i in range(3)]
        kstb = [self.newbuf(f"kst{i}") for i in range(3)]
        vst = [self.sb(es, f"vst{i}", [128, 1024], BF16) for i in range(2)]
        vstb = [self.newbuf(f"vst{i}") for i in range(2)]
        kvb = self.kv_bufs[s]
        self.norm(x, xb, lambda c: self.der[:, 64 + c:65 + c], lambda c: self.modT[:, 192 + c:193 + c])
        h16 = self.sb(es, "h16", [128, KC, T], BF16)
        h16b = self.newbuf("h16")
        for kc in range(KC):
            DVE.op(nc.vector.tensor_copy, [self.h_b], [h16b], out=h16[:, kc, :].rearrange("p (r j) -> p r j", r=16),
                   in_=self.hT[:, kc, :].rearrange("p (j r) -> p r j", r=16))
        n = 0
        for i in range(6):
            slot, sbuf_ = self.w_next(f"wk_{i}")
            g = i // 2
            d = DIL[g]
            for mm in range(4):
                hp = (i % 2) * 4 + mm
                pi = self.psrr()
                pt, pbuf = self.ps[pi], self.ps_b[pi]
                for kc in range(KC):
                    PE.op(nc.tensor.matmul, [sbuf_, self.h_b], [pbuf], pt[:, :], lhsT=slot[:, kc * 512 + mm * 128: kc * 512 + mm * 128 + 128],
                          rhs=self.hT[:, kc, :], start=(kc == 0), stop=(kc == KC - 1))
                kt, ktb = kst[n % 3], kstb[n % 3]
                n += 1
                nj = T // d
                if d == 1:
                    ACT.op(nc.scalar.activation, [pbuf], [ktb], out=kt[:, :], in_=pt[:, :], func=AF.Copy)
                    SP.dma(self.kTd[g, hp, :, s * T:(s + 1) * T], kt[:, :], [ktb], [kvb])
                else:
                    ACT.op(nc.scalar.activation, [pbuf], [ktb], out=kt[:, :].rearrange("p (r j) -> p r j", r=d),
                           in_=pt[:, :].rearrange("p (j r) -> p r j", r=d), func=AF.Copy)
                    dst = self.kTd[g, hp, :, :].rearrange("p (r j) -> p r j", r=d)[:, :, s * nj:(s + 1) * nj]
                    SP.dma(dst, kt[:, :].rearrange("p (r j) -> p r j", r=d), [ktb], [kvb])
        n = 0
        for i in range(6):
            slot, sbuf_ = self.w_next(f"wv_{i}")
            g = i // 2
            d = DIL[g]
            half = i % 2
            for ch in range(4):
                if d == 1:
                    cols = lambda kc: self.hT[:, kc, ch * 128:(ch + 1) * 128]
                elif d == 4:
                    cols = lambda kc: self.hT[:, kc, :].rearrange("p (j r) -> p r j", r=4)[:, ch, :]
                else:
                    cols = lambda kc: h16[:, kc, ch * 128:(ch + 1) * 128]
                pi = self.psrr()
                pt, pbuf = self.ps[pi], self.ps_b[pi]
                for kc in range(KC):
                    PE.op(nc.tensor.matmul, [sbuf_, self.h_b, h16b], [pbuf], pt[:, :], lhsT=cols(kc), rhs=slot[:, kc * 512:(kc + 1) * 512],
                          start=(kc == 0), stop=(kc == KC - 1))
                vt, vtb = vst[n % 2], vstb[n % 2]
                n += 1
                DVE.op(nc.vector.tensor_copy, [pbuf], [vtb], out=vt[:, 0:512], in_=pt[:, :])
                if d == 1:
                    r0 = s * T + ch * 128
                    SP.dma(self.vd[g, r0:r0 + 128, half * 512:(half + 1) * 512], vt[:, 0:512], [vtb], [kvb])
                elif d == 4:
                    r0 = ch * 1024 + s * 128
                    SP.dma(self.vd[g, r0:r0 + 128, half * 512:(half + 1) * 512], vt[:, 0:512], [vtb], [kvb])
                else:
                    for rl in range(4):
                        r0 = (ch * 4 + rl) * 256 + s * 32
                        SP.dma(self.vd[g, r0:r0 + 32, half * 512:(half + 1) * 512], vt[rl * 32:(rl + 1) * 32, 0:512], [vtb], [kvb])
        self.stage_end(es)

    def attn_mixer(self, l, s, x, xb):
        nc = self.nc
        PE, ACT, DVE, SP = self.PE, self.ACT, self.DVE, self.SP
        es = self.stage_begin()
        oT = self.sb(es, "oT", [128, KC, T], BF16)
        oTb = self.newbuf("oT")
        qT = [self.sb(es, f"qT{i}", [128, 3, T], BF16) for i in range(2)]
        qTb = [self.newbuf(f"qT{i}") for i in range(2)]
        KW = 640 + 1024 + 2560
        kt = [self.sb(es, f"ktile{i}", [128, KW], BF16) for i in range(2)]
        ktb = [self.newbuf(f"ktile{i}") for i in range(2)]
        NVC = 5 + 8 + 32
        vt = [self.sb(es, f"vtile{i}", [128, NVC, 128], BF16) for i in range(2)]
        vtb = [self.newbuf(f"vtile{i}") for i in range(2)]
        ET = [self.sb(es, f"E{i}", [128, T], BF16) for i in range(3)]
        ETb = [self.newbuf(f"E{i}") for i in range(3)]
        PT = [self.sb(es, f"P{i}", [128, T], BF16) for i in range(3)]
        PTb = [self.newbuf(f"P{i}") for i in range(3)]
        rD = self.sb(es, "rD", [128, T], F32)
        rDb = self.newbuf("rD")
        self.norm(x, xb, lambda c: self.A(l, 0, c), lambda c: self.M(l, 0, c))
        kv_reads = [self.kv_bufs[t] for t in range(max(0, s - 4), s + 1)]
        ne = 0
        for hp in range(8):
            par = hp % 2
            K_, Kb = kt[par], ktb[par]
            V_, Vb = vt[par], vtb[par]
            lo0 = 128 if s == 0 else 0
            SP.dma(K_[:, lo0:640], self.kTd[0, hp, :, s * T - 128 + lo0: s * T + 512], kv_reads, [Kb])
            src = self.vd[0, s * T - 128 + lo0: s * T + 512, hp * 128:(hp + 1) * 128].rearrange("(c p) f -> p c f", p=128)
            SP.dma(V_[:, lo0 // 128:5, :], src, kv_reads, [Vb])
            lo1 = 128 if s == 0 else 0
            srck = self.kTd[1, hp, :, :].rearrange("p (r j) -> p r j", r=4)[:, :, 128 * (s - 1) + lo1: 128 * (s + 1)]
            SP.dma(K_[:, 640:640 + 1024].rearrange("p (r j) -> p r j", r=4)[:, :, lo1:256], srck, kv_reads, [Kb])
            for r in range(4):
                for c2 in range(lo1 // 128, 2):
                    r0 = r * 1024 + 128 * (s - 1 + c2)
                    SP.dma(V_[:, 5 + r * 2 + c2, :], self.vd[1, r0:r0 + 128, hp * 128:(hp + 1) * 128], kv_reads, [Vb])
            na = min(128, 32 * s)
            k2 = K_[:, 640 + 1024:].rearrange("p (r j) -> p r j", r=16)
            srck = self.kTd[2, hp, :, :].rearrange("p (r j) -> p r j", r=16)[:, :, 32 * s - na: 32 * s + 32]
            SP.dma(k2[:, :, 128 - na:160], srck, kv_reads, [Kb])
            for r in range(16):
                if na > 0:
                    r0 = r * 256 + 32 * s - na
                    SP.dma(V_[0:na, 13 + r, :], self.vd[2, r0:r0 + na, hp * 128:(hp + 1) * 128], kv_reads, [Vb])
                r0 = r * 256 + 32 * s
                SP.dma(V_[0:32, 29 + r, :], self.vd[2, r0:r0 + 32, hp * 128:(hp + 1) * 128], kv_reads, [Vb])
            slot, sbuf_ = self.w_next(f"wq{l}_{hp}")
            Q_, Qb = qT[par], qTb[par]
            for g in range(3):
                pi = hp * 3 + g
                pt, pbuf = self.ps[pi % 4], self.ps_b[pi % 4]
                for kc in range(KC):
                    base = (g * 8 + kc) * 128
                    PE.op(nc.tensor.matmul, [sbuf_, self.h_b], [pbuf], pt[:, :], lhsT=slot[:, base:base + 128], rhs=self.hT[:, kc, :],
                          start=(kc == 0), stop=(kc == KC - 1))
                d = DIL[g]
                if d == 1:
                    DVE.op(nc.vector.tensor_copy, [pbuf], [Qb], out=Q_[:, g, :], in_=pt[:, :])
                else:
                    DVE.op(nc.vector.tensor_copy, [pbuf], [Qb], out=Q_[:, g, :].rearrange("p (r j) -> p r j", r=d),
                           in_=pt[:, :].rearrange("p (j r) -> p r j", r=d))
            Np, Npb = self.ps[4 + par], self.ps_b[4 + par]
            Dp, Dpb = self.ps[6 + par], self.ps_b[6 + par]
            first = [True, True]
            for g in range(3):
                d = DIL[g]
                batches = []
                if g < 2:
                    for kind in (0, 1):
                        units = []
                        for u in range(4):
                            if g == 0:
                                if kind == 0 and s == 0 and u == 0:
                                    continue
                                kcol = (u + kind) * 128
                                vch = u + kind
                            else:
                                if kind == 0 and s == 0:
                                    continue
                                kcol = 640 + u * 256 + kind * 128
                                vch = 5 + u * 2 + kind
                            units.append((u, kcol, vch))
                        if units:
                            batches.append((kind, 128, 0, 128, units))
                else:
                    na = min(128, 32 * s)
                    if na > 0:
                        batches.append((0, 32, 0, na, [(r, 640 + 1024 + r * 160 + 128 - na, 13 + r) for r in range(16)]))
                    batches.append((1, 32, 0, 32, [(r, 640 + 1024 + r * 160 + 128, 29 + r) for r in range(16)]))
                for (kind, nq, p0, nk, units) in batches:
                    for hh in range(2):
                        h = hp * 2 + hh
                        r0, r1 = hh * 64, hh * 64 + 64
                        si = ne % 3
                        ne += 1
                        Sp, Spb = self.ps[si], self.ps_b[si]
                        if hp * 0 + si >= 0:
                            pass
                        for (u, kcol, vch) in units:
                            PE.op(nc.tensor.matmul, [Kb, Qb], [Spb], Sp[p0:p0 + nk, u * nq:(u + 1) * nq],
                                  lhsT=K_[r0:r1, kcol:kcol + nk],
                                  rhs=Q_[r0:r1, g, u * nq:(u + 1) * nq], start=True, stop=True)
                        u0 = units[0][0]
                        u1 = units[-1][0] + 1
                        E_, Eb = ET[si], ETb[si]
                        P_, Pb_ = PT[si], PTb[si]
                        ACT.op(nc.scalar.activation, [Spb], [Eb], out=E_[p0:p0 + nk, u0 * nq:u1 * nq], in_=Sp[p0:p0 + nk, u0 * nq:u1 * nq],
                               func=AF.Exp, scale=HD ** -0.5)
                        if g == 2 and kind == 0 and nk < 128:
                            mk = self.masks2[0:nk, nk // 32 - 1, h, 0:32]
                        else:
                            mk = self.masks[p0:p0 + nk, g * NH + h, kind * 128: kind * 128 + nq]
                        nu = u1 - u0
                        DVE.op(nc.vector.tensor_tensor, [Eb, self.masks_b], [Pb_],
                               out=P_[p0:p0 + nk, u0 * nq:u1 * nq].rearrange("p (u q) -> p u q", u=nu),
                               in0=E_[p0:p0 + nk, u0 * nq:u1 * nq].rearrange("p (u q) -> p u q", u=nu),
                               in1=mk.unsqueeze(1).to_broadcast([nk, nu, nq]), op=ALU.mult)
                        for (u, kcol, vch) in units:
                            if d == 1:
                                oc = slice(u * 128, (u + 1) * 128)
                                No = Np[r0:r1, oc]
                                Do = Dp[r0:r1, oc]
                            else:
                                No = Np[r0:r1, :].rearrange("p (j r) -> p r j", r=d)[:, u, :]
                                Do = Dp[r0:r1, :].rearrange("p (j r) -> p r j", r=d)[:, u, :]
                            PE.op(nc.tensor.matmul, [Vb, Pb_], [Npb], No, lhsT=V_[p0:p0 + nk, vch, r0:r1], rhs=P_[p0:p0 + nk, u * nq:(u + 1) * nq],
                                  start=first[hh], stop=False, skip_group_check=True)
                            PE.op(nc.tensor.matmul, [self.ones_b, Pb_], [Dpb], Do, lhsT=self.ones[p0:p0 + nk, 0:64], rhs=P_[p0:p0 + nk, u * nq:(u + 1) * nq],
                                  start=first[hh], stop=False, skip_group_check=True)
                            first[hh] = False
            DVE.op(nc.vector.reciprocal, [Dpb], [rDb], out=rD[:, :], in_=Dp[:, :])
            DVE.op(nc.vector.tensor_tensor, [Npb, rDb], [oTb], out=oT[:, hp, :], in0=Np[:, :], in1=rD[:, :], op=ALU.mult)
        self._psi = 0
        for half in range(2):
            slot, sbuf_ = self.w_next(f"wo{l}_{half}")
            for mm in range(4):
                m = half * 4 + mm
                pi = self.psrr()
                pt, pbuf = self.ps[pi], self.ps_b[pi]
                for kc in range(KC):
                    PE.op(nc.tensor.matmul, [sbuf_, oTb], [pbuf], pt[:, :], lhsT=slot[:, kc * 512 + mm * 128: kc * 512 + mm * 128 + 128],
                          rhs=oT[:, kc, :], start=(kc == 0), stop=(kc == KC - 1))
                DVE.op(nc.vector.scalar_tensor_tensor, [pbuf, self.mod_b, xb], [xb], out=x[:, m, :], in0=pt[:, :], scalar=self.M(l, 2, m),
                       in1=x[:, m, :], op0=ALU.mult, op1=ALU.add)
        self.stage_end(es)


_CACHE = {}


def get_prog(n_tiles=NT, nstages=len(STAGES)):
    key = (n_tiles, nstages)
    if key not in _CACHE:
        p = Prog(n_tiles, nstages)
        p.build()
        _CACHE[key] = p
    return _CACHE[key]


def kernel(**inputs):
    inp = {k: np.asarray(v) for k, v in inputs.items()}
    W, A, per_core = host_prepare(inp)
    p = get_prog()
    in_maps = [{"xT": pc["xT"], "wts": W, "adaw": A, "vecs": pc["vecs"]} for pc in per_core]
    res = run_bass_kernel_spmd(p.nc, in_maps, core_ids=list(range(8)))
    out = np.stack([np.ascontiguousarray(r["outT"].T) for r in res.results], axis=0)
    return out.astype(np.float32)
```

```python
import math
from contextlib import ExitStack

import numpy as np
import concourse.bass as bass
import concourse.mybir as mybir
from concourse.bass_utils import run_bass_kernel_spmd

F32 = mybir.dt.float32
BF16 = mybir.dt.bfloat16
AF = mybir.ActivationFunctionType
ALU = mybir.AluOpType

D = 1024
S = 4096
FF = 2816
NFC = 22
T = 512
NT = S // T
KC = 8
EPS = 1e-6
POOLW = (2, 4, 8, 16)
DIL = (1, 4, 16)
NH = 16
HD = 64
NSLOT = 5
SLOTW = 4096
BIG = 30000.0


def _alibi_slopes(n):
    def pow2(m):
        start = 2.0 ** (-(2.0 ** -(math.log2(m) - 3)))
        return [start ** (i + 1) for i in range(m)]
    if math.log2(n).is_integer():
        s = pow2(n)
    else:
        c = 2 ** math.floor(math.log2(n))
        s = pow2(c) + pow2(2 * c)[0::2][: n - c]
    s = np.asarray(s, dtype=np.float32)
    return -np.sort(-s)


SLOPES = _alibi_slopes(3 * NH).reshape(3, NH)


STAGES = ["mix0", "ffn0", "mix1", "ffn1", "kv", "mix2", "ffn2", "mix3", "ffn3"]


def stage_pieces(st):
    P = []
    if st.startswith("mix"):
        l = int(st[3])
        if l < 2:
            P += [(f"win{l}_0", 4096), (f"win{l}_1", 4096), (f"wgrp{l}", 2048), (f"wout{l}_0", 4096), (f"wout{l}_1", 4096)]
        else:
            P += [(f"wq{l}_{hp}", 3072) for hp in range(8)]
            P += [(f"wo{l}_0", 4096), (f"wo{l}_1", 4096)]
    elif st == "kv":
        P += [(f"wk_{i}", 4096) for i in range(6)]
        P += [(f"wv_{i}", 4096) for i in range(6)]
    else:
        l = int(st[3])
        P += [(f"wup{l}_{i}", 4096) for i in range(11)]
        for r in range(2):
            P += [(f"wdn{l}_{r}_0", 4096), (f"wdn{l}_{r}_1", 4096), (f"wdn{l}_{r}_2", 3072)]
    return P


def piece_list(nstages=len(STAGES)):
    P = []
    for st in STAGES[:nstages]:
        P += stage_pieces(st)
    return P


def ada_piece_list():
    P = []
    for l in range(4):
        P += [(f"ada{l}_{i}", 4096) for i in range(12)]
    P += [(f"kvada_{i}", 4096) for i in range(4)]
    return P


def _offsets(pl):
    off = {}
    o = 0
    for n, w in pl:
        off[n] = (o, w)
        o += w
    return off, o


def vec_layout():
    L = {}
    o = 0

    def add(name, n):
        nonlocal o
        L[name] = (o, n)
        o += n
    for l in range(4):
        add(f"ada_b{l}", 48)
        add(f"n1g{l}", 8)
        add(f"n2g{l}", 8)
        add(f"cw{l}", 66)
        add(f"cb{l}", 22)
    for l in range(2):
        add(f"psc{l}", 8)
    add("kvg", 8)
    add("kvb", 16)
    add("fg", 8)
    add("c", 8)
    add("invc", 64)
    add("dist", 256)
    add("dist2", 96)
    return L, o


def pmaj(v):
    return np.ascontiguousarray(v.reshape(-1, 128).T)


def kmajor(W, c0, ncols):
    K = W.shape[0]
    a = W[:, c0:c0 + ncols].reshape(K // 128, 128, ncols).transpose(1, 0, 2)
    return a.reshape(128, -1)


def host_prepare(inp):
    pl = piece_list()
    off, tot = _offsets(pl)
    W = np.empty((128, tot), np.float32)

    def put(name, arr):
        o, w = off[name]
        assert arr.shape == (128, w), (name, arr.shape, w)
        W[:, o:o + w] = arr
    for l in range(4):
        if l < 2:
            for m in range(2):
                put(f"win{l}_{m}", kmajor(inp["pool_w_in"][l], m * 512, 512))
                put(f"wout{l}_{m}", kmajor(inp["pool_w_out"][l], m * 512, 512))
            g = inp["pool_w_grp"][l]
            put(f"wgrp{l}", g.reshape(4, 2, 128, 256).transpose(2, 0, 1, 3).reshape(128, 2048))
        else:
            j = l - 2
            wq = inp["attn_w_q"][j]
            for hp in range(8):
                a = np.stack([kmajor(wq, g * 1024 + hp * 128, 128).reshape(128, 8, 128) for g in range(3)], axis=1)
                put(f"wq{l}_{hp}", a.reshape(128, 3072))
            for m in range(2):
                put(f"wo{l}_{m}", kmajor(inp["attn_w_o"][j], m * 512, 512))
        if l == 2:
            for i in range(6):
                put(f"wk_{i}", kmajor(inp["w_kv"], i * 512, 512))
                put(f"wv_{i}", kmajor(inp["w_kv"], 3072 + i * 512, 512))
        wu = inp["ffn_w_up"][l]
        for i in range(11):
            parts = []
            for jj in range(2):
                fc = 2 * i + jj
                for av in range(2):
                    parts.append(kmajor(wu, av * FF + fc * 128, 128))
            put(f"wup{l}_{i}", np.concatenate(parts, axis=1))
        wd = inp["ffn_w_down"][l]
        for r in range(2):
            a = wd[:, r * 512:(r + 1) * 512].reshape(NFC, 128, 512).transpose(1, 0, 2)
            put(f"wdn{l}_{r}_0", a[:, 0:8].reshape(128, 4096))
            put(f"wdn{l}_{r}_1", a[:, 8:16].reshape(128, 4096))
            put(f"wdn{l}_{r}_2", a[:, 16:22].reshape(128, 3072))
    apl = ada_piece_list()
    aoff, atot = _offsets(apl)
    A = np.empty((128, atot), np.float32)
    for l in range(4):
        for i in range(12):
            o, w = aoff[f"ada{l}_{i}"]
            A[:, o:o + w] = kmajor(inp["ada_w"][l], i * 512, 512)
    for i in range(4):
        o, w = aoff[f"kvada_{i}"]
        A[:, o:o + w] = kmajor(inp["kv_ada_w"], i * 512, 512)
    VL, nv = vec_layout()
    shared = np.zeros((128, nv), np.float32)

    def vput(name, arr):
        o, n = VL[name]
        assert arr.shape == (128, n), (name, arr.shape)
        shared[:, o:o + n] = arr
    for l in range(4):
        vput(f"ada_b{l}", pmaj(inp["ada_b"][l]))
        vput(f"n1g{l}", pmaj(inp["norm1_g"][l]))
        vput(f"n2g{l}", pmaj(inp["norm2_g"][l]))
        cw = inp["ffn_conv_w"][l]
        vput(f"cw{l}", np.concatenate([pmaj(cw[k]) for k in range(3)], axis=1))
        vput(f"cb{l}", pmaj(inp["ffn_conv_b"][l]))
    for l in range(2):
        vput(f"psc{l}", pmaj(inp["pool_scale"][l]))
    vput("kvg", pmaj(inp["kv_norm_g"]))
    vput("kvb", pmaj(inp["kv_ada_b"]))
    vput("fg", pmaj(inp["final_g"]))
    invc = np.zeros((128, 4, 16), np.float32)
    for g, w in enumerate(POOLW):
        invc[:, g, :] = 1.0 / np.minimum(np.arange(16) + 1, w)
    vput("invc", invc.reshape(128, 64))
    k = np.arange(128)[:, None]
    q = np.arange(128)[None, :]
    dprev = (q - k + 128).astype(np.float32)
    dprev[dprev > 128] = BIG
    dcur = (q - k).astype(np.float32)
    dcur[dcur < 0] = BIG
    vput("dist", np.concatenate([dprev, dcur], axis=1))
    d2 = []
    for na in (32, 64, 96):
        dd = (q[:, 0:32] - k + na).astype(np.float32)
        dd[dd > 128] = BIG
        dd[k[:, 0] >= na, :] = BIG
        d2.append(dd)
    vput("dist2", np.concatenate(d2, axis=1))
    per_core = []
    for b in range(8):
        v = shared.copy()
        o, n = VL["c"]
        v[:, o:o + n] = pmaj(inp["c"][b])
        per_core.append({"xT": np.ascontiguousarray(inp["x"][b].T), "vecs": v})
    return W, A, per_core


class Chan:
    __slots__ = ("sem", "val")


class Buf:
    __slots__ = ("name", "w", "r")

    def __init__(self, name, seed=None):
        self.name = name
        self.w = {}
        self.r = dict(seed) if seed else {}

    def tokens(self):
        d = dict(self.w)
        for ch, v in self.r.items():
            if d.get(ch, 0) < v:
                d[ch] = v
        return d


def _merge(d, ch, v):
    if d.get(ch, 0) < v:
        d[ch] = v


class Ctx:
    def __init__(self, nc, es):
        self.nc = nc
        self.es = es
        self.nsem = 0

    def new_chan(self):
        sem = self.es.enter_context(self.nc.semaphore(f"sm{self.nsem}"))
        self.nsem += 1
        c = Chan()
        c.sem = sem
        c.val = 0
        return c


class Eng:
    EPOCH = 16000

    def __init__(self, ctx, eng, name, is_pe=False, n_dma=0):
        self.ctx = ctx
        self.e = eng
        self.name = name
        self.is_pe = is_pe
        self.chan = ctx.new_chan()
        self.waited = {}
        self.dma_pool = [ctx.new_chan() for _ in range(n_dma)]
        self.dma_i = 0
        self.n = 0

    def wait_tok(self, ch, v):
        if ch is self.chan and self.is_pe:
            return
        if self.waited.get(ch, 0) >= v:
            return
        self.e.wait_ge(ch.sem, v)
        self.waited[ch] = v

    def sync(self, reads, writes):
        for b in reads:
            for ch, v in b.w.items():
                self.wait_tok(ch, v)
        for b in writes:
            for ch, v in b.w.items():
                self.wait_tok(ch, v)
            for ch, v in b.r.items():
                self.wait_tok(ch, v)

    def op(self, fn, reads, writes, *a, **k):
        self.sync(reads, writes)
        ins = fn(*a, **k)
        if self.chan.val >= self.EPOCH:
            self.chan = self.ctx.new_chan()
        ch = self.chan
        ch.val += 1
        ins.then_inc(ch.sem, 1)
        for b in reads:
            _merge(b.r, ch, ch.val)
        for b in writes:
            _merge(b.w, ch, ch.val)
        self.n += 1
        return ins

    def dma(self, out, in_, reads, writes, **k):
        ch = self.dma_pool[self.dma_i % len(self.dma_pool)]
        self.dma_i += 1
        if ch.val:
            self.wait_tok(ch, ch.val)
        self.sync(reads, writes)
        ins = self.e.dma_start(out=out, in_=in_, **k)
        ch.val += 16
        ins.then_inc(ch.sem, 16)
        for b in reads:
            _merge(b.r, ch, ch.val)
        for b in writes:
            _merge(b.w, ch, ch.val)
        self.n += 1
        return ins

    def wait_all(self, bufs):
        for b in bufs:
            for ch, v in b.tokens().items():
                self.wait_tok(ch, v)


class Prog:
    def __init__(self, n_tiles=NT, nstages=len(STAGES), dbg=None):
        self.n_tiles = n_tiles
        self.nstages = nstages
        self.full = nstages == len(STAGES)
        self.dbg = dbg
        nc = bass.Bass("TRN2", target_bir_lowering=False)
        self.nc = nc
        self.es = ExitStack()
        self.pl = piece_list()
        self.poff, self.ptot = _offsets(self.pl)
        self.apl = ada_piece_list()
        self.aoff, self.atot = _offsets(self.apl)
        self.VL, self.nv = vec_layout()
        self.seed = {}

    def sb(self, es, name, shape, dt):
        self._nalloc = getattr(self, "_nalloc", 0) + 1
        return es.enter_context(self.nc.sbuf_tensor(f"{name}_{self._nalloc}", list(shape), dt))

    def newbuf(self, name):
        b = Buf(name, self.seed)
        self.stage_bufs.append(b)
        return b

    def stage_begin(self):
        self.stage_bufs = []
        return ExitStack()

    def stage_end(self, es):
        seed = dict(self.seed)
        for b in self.stage_bufs:
            for ch, v in b.tokens().items():
                _merge(seed, ch, v)
        self.seed = seed
        es.close()

    def w_init(self):
        self.wslots = [self.sb(self.es, f"wslot{i}", [128, SLOTW], BF16) for i in range(NSLOT)]
        self.wbufs = [Buf(f"wslot{i}") for i in range(NSLOT)]
        self.wsched = []
        self.w_issued = 0
        self.w_used = 0

    def w_issue_upto(self, idx):
        while self.w_issued <= idx and self.w_issued < len(self.wsched):
            i = self.w_issued
            src, w, name = self.wsched[i]
            slot = i % NSLOT
            self.POOL.dma(self.wslots[slot][:, 0:w], src, [], [self.wbufs[slot]], max_dma_last_dim=2048)
            self.w_issued += 1

    def w_next(self, name):
        i = self.w_used
        src, w, nm = self.wsched[i]
        assert nm == name, (nm, name)
        self.w_issue_upto(i + NSLOT - 1)
        self.w_used += 1
        slot = i % NSLOT
        return self.wslots[slot], self.wbufs[slot]

    def build(self):
        nc = self.nc
        es = self.es
        ctx = Ctx(nc, es)
        self.ctx = ctx
        nt = self.n_tiles
        self.xT = nc.dram_tensor("xT", [D, S], F32, kind="ExternalInput").ap()
        self.wts = nc.dram_tensor("wts", [128, self.ptot], F32, kind="ExternalInput").ap()
        self.adaw = nc.dram_tensor("adaw", [128, self.atot], F32, kind="ExternalInput").ap()
        self.vecs_d = nc.dram_tensor("vecs", [128, self.nv], F32, kind="ExternalInput").ap()
        self.outT = nc.dram_tensor("outT", [D, S], F32, kind="ExternalOutput").ap()
        self.kTd = nc.dram_tensor("kTd", [3, 8, 128, S], BF16, kind="Internal").ap()
        self.vd = nc.dram_tensor("vd", [3, S, D], BF16, kind="Internal").ap()
        self.kv_bufs = [Buf(f"kv{s}") for s in range(NT)]

        self.PE = Eng(ctx, nc.tensor, "pe", is_pe=True)
        self.ACT = Eng(ctx, nc.scalar, "act")
        self.DVE = Eng(ctx, nc.vector, "dve")
        self.POOL = Eng(ctx, nc.gpsimd, "pool", n_dma=NSLOT + 1)
        self.SP = Eng(ctx, nc.sync, "sp", n_dma=40)
        PE, ACT, DVE, POOL, SP = self.PE, self.ACT, self.DVE, self.POOL, self.SP

        self.vecs = self.sb(es, "vecs_sb", [128, self.nv], F32)
        self.vecs_b = Buf("vecs")
        self.modT = self.sb(es, "modT", [128, 4 * 48 + 16], F32)
        self.der = self.sb(es, "der", [128, 4 * 16 + 8], F32)
        self.mod_b = Buf("mod")
        self.ones = self.sb(es, "ones", [128, 128], BF16)
        self.ones_b = Buf("ones")
        self.condb = self.sb(es, "condb", [128, 8], BF16)
        self.cond_b = Buf("cond")
        self.xTs = [self.sb(es, f"xTs{i}", [128, KC, T], F32) for i in range(2)]
        self.x_bufs = [Buf(f"x{i}") for i in range(2)]
        self.hT = self.sb(es, "hT", [128, KC, T], BF16)
        self.h_b = Buf("hT")
        self.sq = self.sb(es, "sq", [128, KC, T], BF16)
        self.sq_b = Buf("sq")
        self.std = self.sb(es, "std", [128, T], F32)
        self.rstd = self.sb(es, "rstd", [128, T], F32)
        self.std_b = Buf("std")
        self.rstd_b = Buf("rstd")
        self.ntmp = [self.sb(es, f"ntmp{i}", [128, T], F32) for i in range(2)]
        self.ntmp_b = [Buf(f"ntmp{i}") for i in range(2)]
        self.uhalo = [self.sb(es, f"uhalo{l}", [128, KC, 16], F32) for l in range(2)]
        self.uhalo_b = [Buf(f"uhalo{l}") for l in range(2)]
        self.ahalo = [self.sb(es, f"ahalo{l}", [128, NFC, 2], F32) for l in range(4)]
        self.ahalo_b = [Buf(f"ahalo{l}") for l in range(4)]
        self.masks = self.sb(es, "masks", [128, 48, 256], BF16)
        self.masks_b = Buf("masks")
        self.masks2 = self.sb(es, "masks2", [128, 3, NH, 32], BF16)
        self.ps = [es.enter_context(nc.psum_tensor(f"ps{i}", [128, 512], F32)) for i in range(8)]
        self.ps_b = [Buf(f"ps{i}") for i in range(8)]
        self.w_init()

        for n, w in self.apl:
            o, _ = self.aoff[n]
            self.wsched.append((self.adaw[:, o:o + w], w, n))
        for s in range(nt):
            for n, w in piece_list(self.nstages):
                o, _ = self.poff[n]
                self.wsched.append((self.wts[:, o:o + w], w, n))

        self.prologue()
        for s in range(nt):
            self.tile(s)
        for b in self.out_wait:
            SP.wait_all([b])
        return nc

    def vcol(self, name, c0=0, n=1):
        o, _ = self.VL[name]
        return self.vecs[:, o + c0:o + c0 + n]

    def prologue(self):
        nc = self.nc
        PE, ACT, DVE, POOL, SP = self.PE, self.ACT, self.DVE, self.POOL, self.SP
        SP.dma(self.vecs[:, :], self.vecs_d[:, :], [], [self.vecs_b])
        DVE.op(nc.vector.memset, [], [self.ones_b], self.ones[:, :], 1.0)
        for l in range(2):
            DVE.op(nc.vector.memset, [], [self.uhalo_b[l]], self.uhalo[l][:, :, :], 0.0)
        for l in range(4):
            DVE.op(nc.vector.memset, [], [self.ahalo_b[l]], self.ahalo[l][:, :, :], 0.0)
        ACT.op(nc.scalar.activation, [self.vecs_b], [self.cond_b], out=self.condb[:, :], in_=self.vcol("c", 0, 8), func=AF.Silu)
        pb = 0
        for l in range(5):
            ncol = 48 if l < 4 else 16
            npieces = 12 if l < 4 else 4
            pt, pbuf = self.ps[pb], self.ps_b[pb]
            first = True
            for i in range(npieces):
                slot, sbuf_ = self.w_next(f"ada{l}_{i}" if l < 4 else f"kvada_{i}")
                for mm in range(4):
                    col = i * 4 + mm
                    for kc in range(KC):
                        PE.op(nc.tensor.matmul, [sbuf_, self.cond_b], [pbuf], pt[:, col:col + 1],
                              lhsT=slot[:, kc * 512 + mm * 128: kc * 512 + mm * 128 + 128], rhs=self.condb[:, kc:kc + 1],
                              start=first, stop=(kc == KC - 1))
                        first = False
            bname = f"ada_b{l}" if l < 4 else "kvb"
            DVE.op(nc.vector.tensor_tensor, [pbuf, self.vecs_b], [self.mod_b], out=self.modT[:, l * 48:l * 48 + ncol],
                   in0=pt[:, 0:ncol], in1=self.vcol(bname, 0, ncol), op=ALU.add)
            pb = (pb + 1) % 8
        for l in range(4):
            for j, (gn, sc0) in enumerate(((f"n1g{l}", 8), (f"n2g{l}", 32))):
                DVE.op(nc.vector.scalar_tensor_tensor, [self.mod_b, self.vecs_b], [self.mod_b],
                       out=self.der[:, l * 16 + j * 8: l * 16 + j * 8 + 8], in0=self.modT[:, l * 48 + sc0: l * 48 + sc0 + 8],
                       scalar=1.0, in1=self.vcol(gn, 0, 8), op0=ALU.add, op1=ALU.mult)
        DVE.op(nc.vector.scalar_tensor_tensor, [self.mod_b, self.vecs_b], [self.mod_b],
               out=self.der[:, 64:72], in0=self.modT[:, 192 + 8:192 + 16], scalar=1.0, in1=self.vcol("kvg", 0, 8),
               op0=ALU.add, op1=ALU.mult)
        if self.nstages > 5:
            for g in range(3):
                for h in range(NH):
                    ACT.op(nc.scalar.activation, [self.vecs_b], [self.masks_b], out=self.masks[:, g * NH + h, :],
                           in_=self.vcol("dist", 0, 256), func=AF.Exp, scale=-float(SLOPES[g, h]) * DIL[g])
            for i in range(3):
                for h in range(NH):
                    ACT.op(nc.scalar.activation, [self.vecs_b], [self.masks_b], out=self.masks2[:, i, h, :],
                           in_=self.vcol("dist2", i * 32, 32), func=AF.Exp, scale=-float(SLOPES[2, h]) * DIL[2])

    def A(self, l, which, c):
        return self.der[:, l * 16 + which * 8 + c: l * 16 + which * 8 + c + 1]

    def M(self, l, j, c):
        return self.modT[:, l * 48 + j * 8 + c: l * 48 + j * 8 + c + 1]

    def norm(self, x, xb, acol, bcol, out=None, out_b=None, final=False):
        nc = self.nc
        PE, ACT, DVE = self.PE, self.ACT, self.DVE
        ACT.op(nc.scalar.activation, [xb], [self.sq_b], out=self.sq[:, :, :], in_=x[:, :, :], func=AF.Square)
        pi = self.psrr()
        pt, pbuf = self.ps[pi], self.ps_b[pi]
        for c in range(KC):
            PE.op(nc.tensor.matmul, [self.sq_b, self.ones_b], [pbuf], pt[:, :], lhsT=self.ones[:, :], rhs=self.sq[:, c, :],
                  start=(c == 0), stop=(c == KC - 1))
        ACT.op(nc.scalar.activation, [pbuf, self.eps_b], [self.std_b], out=self.std[:, :], in_=pt[:, :], func=AF.Sqrt,
               scale=1.0 / D, bias=self.epsc[:, 0:1])
        DVE.op(nc.vector.reciprocal, [self.std_b], [self.rstd_b], out=self.rstd[:, :], in_=self.std[:, :])
        for c in range(KC):
            tb = self.ntmp_b[c % 2]
            tt = self.ntmp[c % 2]
            DVE.op(nc.vector.tensor_tensor, [xb, self.rstd_b], [tb], out=tt[:, :], in0=x[:, c, :], in1=self.rstd[:, :], op=ALU.mult)
            if final:
                ACT.op(nc.scalar.activation, [tb, self.vecs_b], [out_b], out=out[:, c, :], in_=tt[:, :], func=AF.Copy,
                       scale=acol(c))
            else:
                ACT.op(nc.scalar.activation, [tb, self.mod_b], [self.h_b], out=self.hT[:, c, :], in_=tt[:, :], func=AF.Identity,
                       scale=acol(c), bias=bcol(c))

    def psrr(self):
        i = self._psi
        self._psi = (self._psi + 1) % 8
        return i

    def tile(self, s):
        nc = self.nc
        PE, ACT, DVE, POOL, SP = self.PE, self.ACT, self.DVE, self.POOL, self.SP
        if s == 0:
            self._psi = 0
            self.out_wait = []
            self.epsc = self.sb(self.es, "epsc", [128, 1], F32)
            self.eps_b = Buf("eps")
            DVE.op(nc.vector.memset, [], [self.eps_b], self.epsc[:, :], EPS)
        x = self.xTs[s % 2]
        xb = self.x_bufs[s % 2]
        SP.dma(x[:, :, :], self.xT.rearrange("(c p) t -> p c t", p=128)[:, :, s * T:(s + 1) * T], [], [xb])
        for st in STAGES[:self.nstages]:
            if st == "kv":
                self.kv_stage(s, x, xb)
            elif st.startswith("mix"):
                l = int(st[3])
                if l < 2:
                    self.pool_mixer(l, s, x, xb)
                else:
                    self.attn_mixer(l, s, x, xb)
            else:
                self.ffn(int(st[3]), s, x, xb)
        es = self.stage_begin()
        o = self.sb(es, "otile", [128, KC, T], F32)
        ob = self.newbuf("otile")
        if self.full:
            self.norm(x, xb, lambda c: self.vcol("fg", c, 1), None, out=o, out_b=ob, final=True)
        else:
            ACT.op(nc.scalar.activation, [xb], [ob], out=o[:, :, :], in_=x[:, :, :], func=AF.Copy)
        SP.dma(self.outT.rearrange("(c p) t -> p c t", p=128)[:, :, s * T:(s + 1) * T], o[:, :, :], [ob], [])
        self.out_wait.append(ob)
        self.stage_end(es)

    def pool_mixer(self, l, s, x, xb):
        nc = self.nc
        PE, ACT, DVE = self.PE, self.ACT, self.DVE
        es = self.stage_begin()
        U = self.sb(es, "poolU", [128, KC, 16 + T], F32)
        Ub = self.newbuf("U")
        A_ = self.sb(es, "poolA", [128, KC, 16 + T], F32)
        Ab = self.newbuf("A")
        B_ = self.sb(es, "poolB", [128, KC, 16 + T], F32)
        Bb = self.newbuf("B")
        Pb = self.sb(es, "poolP", [128, KC, T], BF16)
        Pbb = self.newbuf("P")
        Zb = self.sb(es, "poolZ", [128, KC, T], BF16)
        Zbb = self.newbuf("Z")
        self.norm(x, xb, lambda c: self.A(l, 0, c), lambda c: self.M(l, 0, c))
        DVE.op(nc.vector.tensor_copy, [self.uhalo_b[l]], [Ub], out=U[:, :, 0:16], in_=self.uhalo[l][:, :, :])
        for half in range(2):
            slot, sbuf_ = self.w_next(f"win{l}_{half}")
            for mm in range(4):
                m = half * 4 + mm
                pi = self.psrr()
                pt, pbuf = self.ps[pi], self.ps_b[pi]
                for kc in range(KC):
                    PE.op(nc.tensor.matmul, [sbuf_, self.h_b], [pbuf], pt[:, :], lhsT=slot[:, kc * 512 + mm * 128: kc * 512 + mm * 128 + 128],
                          rhs=self.hT[:, kc, :], start=(kc == 0), stop=(kc == KC - 1))
                ACT.op(nc.scalar.activation, [pbuf], [Ub], out=U[:, m, 16:16 + T], in_=pt[:, :], func=AF.Copy)
        DVE.op(nc.vector.tensor_copy, [Ub], [self.uhalo_b[l]], out=self.uhalo[l][:, :, :], in_=U[:, :, T:T + 16])
        W_ = 16 + T
        DVE.op(nc.vector.tensor_tensor, [Ub], [Ab], out=A_[:, :, 1:W_], in0=U[:, :, 1:W_], in1=U[:, :, 0:W_ - 1], op=ALU.add)
        DVE.op(nc.vector.tensor_tensor, [Ab], [Bb], out=B_[:, 2:8, 3:W_], in0=A_[:, 2:8, 3:W_], in1=A_[:, 2:8, 1:W_ - 2], op=ALU.add)
        DVE.op(nc.vector.tensor_tensor, [Bb], [Ab], out=A_[:, 4:8, 7:W_], in0=B_[:, 4:8, 7:W_], in1=B_[:, 4:8, 3:W_ - 4], op=ALU.add)
        DVE.op(nc.vector.tensor_tensor, [Ab], [Bb], out=B_[:, 6:8, 15:W_], in0=A_[:, 6:8, 15:W_], in1=A_[:, 6:8, 7:W_ - 8], op=ALU.add)
        srcs = [(A_, Ab), (B_, Bb), (A_, Ab), (B_, Bb)]
        for g in range(4):
            St, Sb_ = srcs[g]
            DVE.op(nc.vector.scalar_tensor_tensor, [Sb_, Ub], [Pbb], out=Pb[:, 2 * g:2 * g + 2, :], in0=St[:, 2 * g:2 * g + 2, 16:16 + T],
                   scalar=1.0 / POOLW[g], in1=U[:, 2 * g:2 * g + 2, 16:16 + T], op0=ALU.mult, op1=ALU.subtract)
            if s == 0:
                o, _ = self.VL["invc"]
                for cc in range(2):
                    c = 2 * g + cc
                    tb, tt = self.ntmp_b[cc], self.ntmp[cc]
                    DVE.op(nc.vector.tensor_tensor, [Sb_, self.vecs_b], [tb], out=tt[:, 0:16], in0=St[:, c, 16:32],
                           in1=self.vecs[:, o + g * 16:o + g * 16 + 16], op=ALU.mult)
                    DVE.op(nc.vector.tensor_tensor, [tb, Ub], [Pbb], out=Pb[:, c, 0:16], in0=tt[:, 0:16], in1=U[:, c, 16:32], op=ALU.subtract)
        slot, sbuf_ = self.w_next(f"wgrp{l}")
        for g in range(4):
            for mo in range(2):
                c = 2 * g + mo
                pi = self.psrr()
                pt, pbuf = self.ps[pi], self.ps_b[pi]
                for ki in range(2):
                    base = (g * 2 + ki) * 256 + mo * 128
                    PE.op(nc.tensor.matmul, [sbuf_, Pbb], [pbuf], pt[:, :], lhsT=slot[:, base:base + 128], rhs=Pb[:, 2 * g + ki, :],
                          start=(ki == 0), stop=(ki == 1))
                ACT.op(nc.scalar.activation, [pbuf, self.vecs_b], [Zbb], out=Zb[:, c, :], in_=pt[:, :], func=AF.Copy,
                       scale=self.vcol(f"psc{l}", c, 1))
        for half in range(2):
            slot, sbuf_ = self.w_next(f"wout{l}_{half}")
            for mm in range(4):
                m = half * 4 + mm
                pi = self.psrr()
                pt, pbuf = self.ps[pi], self.ps_b[pi]
                for kc in range(KC):
                    PE.op(nc.tensor.matmul, [sbuf_, Zbb], [pbuf], pt[:, :], lhsT=slot[:, kc * 512 + mm * 128: kc * 512 + mm * 128 + 128],
                          rhs=Zb[:, kc, :], start=(kc == 0), stop=(kc == KC - 1))
                DVE.op(nc.vector.scalar_tensor_tensor, [pbuf, self.mod_b, xb], [xb], out=x[:, m, :], in0=pt[:, :], scalar=self.M(l, 2, m),
                       in1=x[:, m, :], op0=ALU.mult, op1=ALU.add)
        self.stage_end(es)

    def ffn(self, l, s, x, xb):
        nc = self.nc
        PE, ACT, DVE = self.PE, self.ACT, self.DVE
        es = self.stage_begin()
        gT = self.sb(es, "gT", [128, NFC, T], BF16)
        gb = self.newbuf("gT")
        abuf = [self.sb(es, f"abuf{i}", [128, 2 + T], F32) for i in range(2)]
        ab = [self.newbuf(f"abuf{i}") for i in range(2)]
        c1 = [self.sb(es, f"c1_{i}", [128, T], F32) for i in range(2)]
        c1b = [self.newbuf(f"c1_{i}") for i in range(2)]
        c2 = [self.sb(es, f"c2_{i}", [128, T], F32) for i in range(2)]
        c2b = [self.newbuf(f"c2_{i}") for i in range(2)]
        self.norm(x, xb, lambda c: self.A(l, 1, c), lambda c: self.M(l, 3, c))
        cwo, _ = self.VL[f"cw{l}"]
        cbo, _ = self.VL[f"cb{l}"]
        for i in range(11):
            slot, sbuf_ = self.w_next(f"wup{l}_{i}")
            for jj in range(2):
                fc = 2 * i + jj
                par = fc % 2
                pa, pab = self.ps[par * 2], self.ps_b[par * 2]
                pv, pvb = self.ps[par * 2 + 1], self.ps_b[par * 2 + 1]
                for av, (pt, pbuf) in enumerate(((pa, pab), (pv, pvb))):
                    base = (jj * 2 + av) * 1024
                    for kc in range(KC):
                        PE.op(nc.tensor.matmul, [sbuf_, self.h_b], [pbuf], pt[:, :], lhsT=slot[:, base + kc * 128: base + kc * 128 + 128],
                              rhs=self.hT[:, kc, :], start=(kc == 0), stop=(kc == KC - 1))
                A_, Ab = abuf[par], ab[par]
                ACT.op(nc.scalar.activation, [self.ahalo_b[l]], [Ab], out=A_[:, 0:2], in_=self.ahalo[l][:, fc, :], func=AF.Copy)
                ACT.op(nc.scalar.activation, [pab], [Ab], out=A_[:, 2:2 + T], in_=pa[:, :], func=AF.Copy)
                ACT.op(nc.scalar.activation, [Ab], [self.ahalo_b[l]], out=self.ahalo[l][:, fc, :], in_=A_[:, T:T + 2], func=AF.Copy)
                ACT.op(nc.scalar.activation, [pab, self.vecs_b], [c1b[par]], out=c1[par][:, :], in_=pa[:, :], func=AF.Identity,
                       scale=self.vecs[:, cwo + 2 * NFC + fc: cwo + 2 * NFC + fc + 1], bias=self.vecs[:, cbo + fc:cbo + fc + 1])
                DVE.op(nc.vector.scalar_tensor_tensor, [Ab, self.vecs_b, c1b[par]], [c2b[par]], out=c2[par][:, :], in0=A_[:, 1:1 + T],
                       scalar=self.vecs[:, cwo + NFC + fc: cwo + NFC + fc + 1], in1=c1[par][:, :], op0=ALU.mult, op1=ALU.add)
                DVE.op(nc.vector.scalar_tensor_tensor, [Ab, self.vecs_b, c2b[par]], [c1b[par]], out=c1[par][:, :], in0=A_[:, 0:T],
                       scalar=self.vecs[:, cwo + fc: cwo + fc + 1], in1=c2[par][:, :], op0=ALU.mult, op1=ALU.add)
                ACT.op(nc.scalar.activation, [c1b[par]], [c2b[par]], out=c2[par][:, :], in_=c1[par][:, :], func=AF.Silu)
                DVE.op(nc.vector.tensor_tensor, [c2b[par], pvb], [gb], out=gT[:, fc, :], in0=c2[par][:, :], in1=pv[:, :], op=ALU.mult)
        for r in range(2):
            for j in range(3):
                slot, sbuf_ = self.w_next(f"wdn{l}_{r}_{j}")
                nf = 8 if j < 2 else 6
                for fl in range(nf):
                    fc = 8 * j + fl
                    for mm in range(4):
                        PE.op(nc.tensor.matmul, [sbuf_, gb], [self.ps_b[4 + mm]], self.ps[4 + mm][:, :],
                              lhsT=slot[:, fl * 512 + mm * 128: fl * 512 + mm * 128 + 128], rhs=gT[:, fc, :],
                              start=(fc == 0), stop=(fc == NFC - 1))
            for mm in range(4):
                m = r * 4 + mm
                DVE.op(nc.vector.scalar_tensor_tensor, [self.ps_b[4 + mm], self.mod_b, xb], [xb], out=x[:, m, :], in0=self.ps[4 + mm][:, :],
                       scalar=self.M(l, 5, m), in1=x[:, m, :], op0=ALU.mult, op1=ALU.add)
        self._psi = 0
        self.stage_end(es)

    def kv_stage(self, s, x, xb):
        nc = self.nc
        PE, ACT, DVE, SP = self.PE, self.ACT, self.DVE, self.SP
        es = self.stage_begin()
        kst = [self.sb(es, f"kst{i}", [128, T], BF16) for i in range(3)]
        kstb = [self.newbuf(f"kst{i}") for i in range(3)]
        vst = [self.sb(es, f"vst{i}", [128, 1024], BF16) for i in range(2)]
        vstb = [self.newbuf(f"vst{i}") for i in range(2)]
        kvb = self.kv_bufs[s]
        self.norm(x, xb, lambda c: self.der[:, 64 + c:65 + c], lambda c: self.modT[:, 192 + c:193 + c])
        h16 = self.sb(es, "h16", [128, KC, T], BF16)
        h16b = self.newbuf("h16")
        for kc in range(KC):
            DVE.op(nc.vector.tensor_copy, [self.h_b], [h16b], out=h16[:, kc, :].rearrange("p (r j) -> p r j", r=16),
                   in_=self.hT[:, kc, :].rearrange("p (j r) -> p r j", r=16))
        n = 0
        for i in range(6):
            slot, sbuf_ = self.w_next(f"wk_{i}")
            g = i // 2
            d = DIL[g]
            for mm in range(4):
                hp = (i % 2) * 4 + mm
                pi = self.psrr()
                pt, pbuf = self.ps[pi], self.ps_b[pi]
                for kc in range(KC):
                    PE.op(nc.tensor.matmul, [sbuf_, self.h_b], [pbuf], pt[:, :], lhsT=slot[:, kc * 512 + mm * 128: kc * 512 + mm * 128 + 128],
                          rhs=self.hT[:, kc, :], start=(kc == 0), stop=(kc == KC - 1))
                kt, ktb = kst[n % 3], kstb[n % 3]
                n += 1
                nj = T // d
                if d == 1:
                    ACT.op(nc.scalar.activation, [pbuf], [ktb], out=kt[:, :], in_=pt[:, :], func=AF.Copy)
                    SP.dma(self.kTd[g, hp, :, s * T:(s + 1) * T], kt[:, :], [ktb], [kvb])
                else:
                    ACT.op(nc.scalar.activation, [pbuf], [ktb], out=kt[:, :].rearrange("p (r j) -> p r j", r=d),
                           in_=pt[:, :].rearrange("p (j r) -> p r j", r=d), func=AF.Copy)
                    dst = self.kTd[g, hp, :, :].rearrange("p (r j) -> p r j", r=d)[:, :, s * nj:(s + 1) * nj]
                    SP.dma(dst, kt[:, :].rearrange("p (r j) -> p r j", r=d), [ktb], [kvb])
        n = 0
        for i in range(6):
            slot, sbuf_ = self.w_next(f"wv_{i}")
            g = i // 2
            d = DIL[g]
            half = i % 2
            for ch in range(4):
                if d == 1:
                    cols = lambda kc: self.hT[:, kc, ch * 128:(ch + 1) * 128]
                elif d == 4:
                    cols = lambda kc: self.hT[:, kc, :].rearrange("p (j r) -> p r j", r=4)[:, ch, :]
                else:
                    cols = lambda kc: h16[:, kc, ch * 128:(ch + 1) * 128]
                pi = self.psrr()
                pt, pbuf = self.ps[pi], self.ps_b[pi]
                for kc in range(KC):
                    PE.op(nc.tensor.matmul, [sbuf_, self.h_b, h16b], [pbuf], pt[:, :], lhsT=cols(kc), rhs=slot[:, kc * 512:(kc + 1) * 512],
                          start=(kc == 0), stop=(kc == KC - 1))
                vt, vtb = vst[n % 2], vstb[n % 2]
                n += 1
                DVE.op(nc.vector.tensor_copy, [pbuf], [vtb], out=vt[:, 0:512], in_=pt[:, :])
                if d == 1:
                    r0 = s * T + ch * 128
                    SP.dma(self.vd[g, r0:r0 + 128, half * 512:(half + 1) * 512], vt[:, 0:512], [vtb], [kvb])
                elif d == 4:
                    r0 = ch * 1024 + s * 128
                    SP.dma(self.vd[g, r0:r0 + 128, half * 512:(half + 1) * 512], vt[:, 0:512], [vtb], [kvb])
                else:
                    for rl in range(4):
                        r0 = (ch * 4 + rl) * 256 + s * 32
                        SP.dma(self.vd[g, r0:r0 + 32, half * 512:(half + 1) * 512], vt[rl * 32:(rl + 1) * 32, 0:512], [vtb], [kvb])
        self.stage_end(es)

    def attn_mixer(self, l, s, x, xb):
        nc = self.nc
        PE, ACT, DVE, SP = self.PE, self.ACT, self.DVE, self.SP
        es = self.stage_begin()
        oT = self.sb(es, "oT", [128, KC, T], BF16)
        oTb = self.newbuf("oT")
        qT = [self.sb(es, f"qT{i}", [128, 3, T], BF16) for i in range(2)]
        qTb = [self.newbuf(f"qT{i}") for i in range(2)]
        KW = 640 + 1024 + 2560
        kt = [self.sb(es, f"ktile{i}", [128, KW], BF16) for i in range(2)]
        ktb = [self.newbuf(f"ktile{i}") for i in range(2)]
        NVC = 5 + 8 + 32
        vt = [self.sb(es, f"vtile{i}", [128, NVC, 128], BF16) for i in range(2)]
        vtb = [self.newbuf(f"vtile{i}") for i in range(2)]
        NB = 3
        ET = [self.sb(es, f"E{i}", [128, T], BF16) for i in range(NB)]
        ETb = [self.newbuf(f"E{i}") for i in range(NB)]
        PT = [self.sb(es, f"P{i}", [128, T], BF16) for i in range(NB)]
        PTb = [self.newbuf(f"P{i}") for i in range(NB)]
        rD = self.sb(es, "rD", [128, T], F32)
        rDb = self.newbuf("rD")
        self.norm(x, xb, lambda c: self.A(l, 0, c), lambda c: self.M(l, 0, c))
        kv_reads = [self.kv_bufs[t] for t in range(max(0, s - 4), s + 1)]
        na = min(128, 32 * s)

        def prep(hp):
            par = hp % 2
            K_, Kb = kt[par], ktb[par]
            V_, Vb = vt[par], vtb[par]
            lo0 = 128 if s == 0 else 0
            SP.dma(K_[:, lo0:640], self.kTd[0, hp, :, s * T - 128 + lo0: s * T + 512], kv_reads, [Kb])
            src_ = self.vd[0, s * T - 128 + lo0: s * T + 512, hp * 128:(hp + 1) * 128].rearrange("(c p) f -> p c f", p=128)
            SP.dma(V_[:, lo0 // 128:5, :], src_, kv_reads, [Vb])
            srck = self.kTd[1, hp, :, :].rearrange("p (r j) -> p r j", r=4)[:, :, 128 * (s - 1) + lo0: 128 * (s + 1)]
            SP.dma(K_[:, 640:640 + 1024].rearrange("p (r j) -> p r j", r=4)[:, :, lo0:256], srck, kv_reads, [Kb])
            for r in range(4):
                r0 = r * 1024 + 128 * (s - 1) + lo0
                nch = 2 - lo0 // 128
                src_ = self.vd[1, r0:r0 + 128 * nch, hp * 128:(hp + 1) * 128].rearrange("(c p) f -> p c f", p=128)
                SP.dma(V_[:, 5 + r * 2 + lo0 // 128: 5 + r * 2 + 2, :], src_, kv_reads, [Vb])
            k2 = K_[:, 640 + 1024:].rearrange("p (r j) -> p r j", r=16)
            srck = self.kTd[2, hp, :, :].rearrange("p (r j) -> p r j", r=16)[:, :, 32 * s - na: 32 * s + 32]
            SP.dma(k2[:, :, 128 - na:160], srck, kv_reads, [Kb])
            v2 = self.vd[2, :, hp * 128:(hp + 1) * 128].rearrange("(r j) f -> j r f", r=16)
            if na > 0:
                SP.dma(V_[0:na, 13:29, :], v2[32 * s - na:32 * s, :, :], kv_reads, [Vb])
            SP.dma(V_[0:32, 29:45, :], v2[32 * s:32 * s + 32, :, :], kv_reads, [Vb])
            slot, sbuf_ = self.w_next(f"wq{l}_{hp}")
            Q_, Qb = qT[par], qTb[par]
            for g in range(3):
                pt, pbuf = self.ps[3], self.ps_b[3]
                for kc in range(KC):
                    base = (g * 8 + kc) * 128
                    PE.op(nc.tensor.matmul, [sbuf_, self.h_b], [pbuf], pt[:, :], lhsT=slot[:, base:base + 128], rhs=self.hT[:, kc, :],
                          start=(kc == 0), stop=(kc == KC - 1))
                d = DIL[g]
                if d == 1:
                    ACT.op(nc.scalar.activation, [pbuf], [Qb], out=Q_[:, g, :], in_=pt[:, :], func=AF.Copy)
                else:
                    ACT.op(nc.scalar.activation, [pbuf], [Qb], out=Q_[:, g, :].rearrange("p (r j) -> p r j", r=d),
                           in_=pt[:, :].rearrange("p (j r) -> p r j", r=d), func=AF.Copy)

        units_all = []
        for hp in range(8):
            ulist = []
            for g in range(3):
                batches = []
                if g < 2:
                    for kind in (0, 1):
                        us = []
                        for u in range(4):
                            if g == 0:
                                if kind == 0 and s == 0 and u == 0:
                                    continue
                                us.append((u, (u + kind) * 128, u + kind))
                            else:
                                if kind == 0 and s == 0:
                                    continue
                                us.append((u, 640 + u * 256 + kind * 128, 5 + u * 2 + kind))
                        if us:
                            batches.append((kind, 128, 128, us))
                else:
                    if na > 0:
                        batches.append((0, 32, na, [(r, 640 + 1024 + r * 160 + 128 - na, 13 + r) for r in range(16)]))
                    batches.append((1, 32, 32, [(r, 640 + 1024 + r * 160 + 128, 29 + r) for r in range(16)]))
                for (kind, nq, nk, us) in batches:
                    for hh in range(2):
                        ulist.append(dict(hp=hp, g=g, kind=kind, nq=nq, nk=nk, us=us, hh=hh))
            ulist[0]["first_of_hp"] = True
            ulist[-1]["last_of_hp"] = True
            seen = set()
            for ud in ulist:
                if ud["hh"] not in seen:
                    ud["first_pv"] = True
                    seen.add(ud["hh"])
            units_all += ulist

        def s_phase(i, ud):
            hp, g, kind, nq, nk, us, hh = ud["hp"], ud["g"], ud["kind"], ud["nq"], ud["nk"], ud["us"], ud["hh"]
            par = hp % 2
            K_, Kb = kt[par], ktb[par]
            Q_, Qb = qT[par], qTb[par]
            h = hp * 2 + hh
            r0, r1 = hh * 64, hh * 64 + 64
            si = i % NB
            Sp, Spb = self.ps[si], self.ps_b[si]
            for (u, kcol, vch) in us:
                PE.op(nc.tensor.matmul, [Kb, Qb], [Spb], Sp[0:nk, u * nq:(u + 1) * nq], lhsT=K_[r0:r1, kcol:kcol + nk],
                      rhs=Q_[r0:r1, g, u * nq:(u + 1) * nq], start=True, stop=True)
            u0 = us[0][0]
            u1 = us[-1][0] + 1
            nu = u1 - u0
            E_, Eb = ET[si], ETb[si]
            P_, Pb_ = PT[si], PTb[si]
            ACT.op(nc.scalar.activation, [Spb], [Eb], out=E_[0:nk, u0 * nq:u1 * nq], in_=Sp[0:nk, u0 * nq:u1 * nq],
                   func=AF.Exp, scale=HD ** -0.5)
            if g == 2 and kind == 0 and nk < 128:
                mk = self.masks2[0:nk, nk // 32 - 1, h, 0:32]
            else:
                mk = self.masks[0:nk, g * NH + h, kind * 128: kind * 128 + nq]
            DVE.op(nc.vector.tensor_tensor, [Eb, self.masks_b], [Pb_],
                   out=P_[0:nk, u0 * nq:u1 * nq].rearrange("p (u q) -> p u q", u=nu),
                   in0=E_[0:nk, u0 * nq:u1 * nq].rearrange("p (u q) -> p u q", u=nu),
                   in1=mk.unsqueeze(1).to_broadcast([nk, nu, nq]), op=ALU.mult)

        def pv_phase(i, ud):
            hp, g, kind, nq, nk, us, hh = ud["hp"], ud["g"], ud["kind"], ud["nq"], ud["nk"], ud["us"], ud["hh"]
            par = hp % 2
            V_, Vb = vt[par], vtb[par]
            r0, r1 = hh * 64, hh * 64 + 64
            si = i % NB
            P_, Pb_ = PT[si], PTb[si]
            Np, Npb = self.ps[4 + par], self.ps_b[4 + par]
            Dp, Dpb = self.ps[6 + par], self.ps_b[6 + par]
            d = DIL[g]
            first = ud.get("first_pv", False)
            for (u, kcol, vch) in us:
                if d == 1:
                    No = Np[r0:r1, u * 128:(u + 1) * 128]
                    Do = Dp[r0:r1, u * 128:(u + 1) * 128]
                else:
                    No = Np[r0:r1, :].rearrange("p (j r) -> p r j", r=d)[:, u, :]
                    Do = Dp[r0:r1, :].rearrange("p (j r) -> p r j", r=d)[:, u, :]
                PE.op(nc.tensor.matmul, [Vb, Pb_], [Npb], No, lhsT=V_[0:nk, vch, r0:r1], rhs=P_[0:nk, u * nq:(u + 1) * nq],
                      start=first, stop=False, skip_group_check=True)
                PE.op(nc.tensor.matmul, [self.ones_b, Pb_], [Dpb], Do, lhsT=self.ones[0:nk, 0:64], rhs=P_[0:nk, u * nq:(u + 1) * nq],
                      start=first, stop=False, skip_group_check=True)
                first = False
            if ud.get("last_of_hp"):
                DVE.op(nc.vector.reciprocal, [Dpb], [rDb], out=rD[:, :], in_=Dp[:, :])
                DVE.op(nc.vector.tensor_tensor, [Npb, rDb], [oTb], out=oT[:, hp, :], in0=Np[:, :], in1=rD[:, :], op=ALU.mult)

        LOOK = 2
        n = len(units_all)
        prep(0)
        prep(1)
        for i in range(n + LOOK):
            if i < n:
                s_phase(i, units_all[i])
            if i - LOOK >= 0:
                ud = units_all[i - LOOK]
                pv_phase(i - LOOK, ud)
                if ud.get("last_of_hp") and ud["hp"] + 2 < 8:
                    prep(ud["hp"] + 2)
        self._psi = 0
        for half in range(2):
            slot, sbuf_ = self.w_next(f"wo{l}_{half}")
            for mm in range(4):
                m = half * 4 + mm
                pi = self.psrr()
                pt, pbuf = self.ps[pi], self.ps_b[pi]
                for kc in range(KC):
                    PE.op(nc.tensor.matmul, [sbuf_, oTb], [pbuf], pt[:, :], lhsT=slot[:, kc * 512 + mm * 128: kc * 512 + mm * 128 + 128],
                          rhs=oT[:, kc, :], start=(kc == 0), stop=(kc == KC - 1))
                DVE.op(nc.vector.scalar_tensor_tensor, [pbuf, self.mod_b, xb], [xb], out=x[:, m, :], in0=pt[:, :], scalar=self.M(l, 2, m),
                       in1=x[:, m, :], op0=ALU.mult, op1=ALU.add)
        self.stage_end(es)


_CACHE = {}


def get_prog(n_tiles=NT, nstages=len(STAGES)):
    key = (n_tiles, nstages)
    if key not in _CACHE:
        p = Prog(n_tiles, nstages)
        p.build()
        _CACHE[key] = p
    return _CACHE[key]


def kernel(**inputs):
    inp = {k: np.asarray(v) for k, v in inputs.items()}
    W, A, per_core = host_prepare(inp)
    p = get_prog()
    in_maps = [{"xT": pc["xT"], "wts": W, "adaw": A, "vecs": pc["vecs"]} for pc in per_core]
    res = run_bass_kernel_spmd(p.nc, in_maps, core_ids=list(range(8)))
    out = np.stack([np.ascontiguousarray(r["outT"].T) for r in res.results], axis=0)
    return out.astype(np.float32)
```

```python
import math
from contextlib import ExitStack

import numpy as np
import concourse.bass as bass
import concourse.mybir as mybir
from concourse.bass_utils import run_bass_kernel_spmd

F32 = mybir.dt.float32
BF16 = mybir.dt.bfloat16
AF = mybir.ActivationFunctionType
ALU = mybir.AluOpType

D = 1024
S = 4096
FF = 2816
NFC = 22
T = 512
NT = S // T
KC = 8
EPS = 1e-6
POOLW = (2, 4, 8, 16)
DIL = (1, 4, 16)
NH = 16
HD = 64
NSLOT = 5
SLOTW = 4096
BIG = 30000.0


def _alibi_slopes(n):
    def pow2(m):
        start = 2.0 ** (-(2.0 ** -(math.log2(m) - 3)))
        return [start ** (i + 1) for i in range(m)]
    if math.log2(n).is_integer():
        s = pow2(n)
    else:
        c = 2 ** math.floor(math.log2(n))
        s = pow2(c) + pow2(2 * c)[0::2][: n - c]
    s = np.asarray(s, dtype=np.float32)
    return -np.sort(-s)


SLOPES = _alibi_slopes(3 * NH).reshape(3, NH)


STAGES = ["mix0", "ffn0", "mix1", "ffn1", "kv", "mix2", "ffn2", "mix3", "ffn3"]


def stage_pieces(st):
    P = []
    if st.startswith("mix"):
        l = int(st[3])
        if l < 2:
            P += [(f"win{l}_0", 4096), (f"win{l}_1", 4096), (f"wgrp{l}", 2048), (f"wout{l}_0", 4096), (f"wout{l}_1", 4096)]
        else:
            P += [(f"wq{l}_{hp}", 3072) for hp in range(8)]
            P += [(f"wo{l}_0", 4096), (f"wo{l}_1", 4096)]
    elif st == "kv":
        P += [(f"wk_{i}", 4096) for i in range(6)]
        P += [(f"wv_{i}", 4096) for i in range(6)]
    else:
        l = int(st[3])
        P += [(f"wup{l}_{i}", 4096) for i in range(11)]
        for r in range(2):
            P += [(f"wdn{l}_{r}_0", 4096), (f"wdn{l}_{r}_1", 4096), (f"wdn{l}_{r}_2", 3072)]
    return P


def piece_list(nstages=len(STAGES)):
    P = []
    for st in STAGES[:nstages]:
        P += stage_pieces(st)
    return P


def ada_piece_list():
    P = []
    for l in range(4):
        P += [(f"ada{l}_{i}", 4096) for i in range(12)]
    P += [(f"kvada_{i}", 4096) for i in range(4)]
    return P


def _offsets(pl):
    off = {}
    o = 0
    for n, w in pl:
        off[n] = (o, w)
        o += w
    return off, o


def vec_layout():
    L = {}
    o = 0

    def add(name, n):
        nonlocal o
        L[name] = (o, n)
        o += n
    for l in range(4):
        add(f"ada_b{l}", 48)
        add(f"n1g{l}", 8)
        add(f"n2g{l}", 8)
        add(f"cw{l}", 66)
        add(f"cb{l}", 22)
    for l in range(2):
        add(f"psc{l}", 8)
    add("kvg", 8)
    add("kvb", 16)
    add("fg", 8)
    add("c", 8)
    add("invc", 64)
    add("dist", 256)
    add("dist2", 96)
    return L, o


def pmaj(v):
    return np.ascontiguousarray(v.reshape(-1, 128).T)


def kmajor(W, c0, ncols):
    K = W.shape[0]
    a = W[:, c0:c0 + ncols].reshape(K // 128, 128, ncols).transpose(1, 0, 2)
    return a.reshape(128, -1)


def host_prepare(inp):
    pl = piece_list()
    off, tot = _offsets(pl)
    W = np.empty((128, tot), np.float32)

    def put(name, arr):
        o, w = off[name]
        assert arr.shape == (128, w), (name, arr.shape, w)
        W[:, o:o + w] = arr
    for l in range(4):
        if l < 2:
            for m in range(2):
                put(f"win{l}_{m}", kmajor(inp["pool_w_in"][l], m * 512, 512))
                put(f"wout{l}_{m}", kmajor(inp["pool_w_out"][l], m * 512, 512))
            g = inp["pool_w_grp"][l]
            put(f"wgrp{l}", g.reshape(4, 2, 128, 256).transpose(2, 0, 1, 3).reshape(128, 2048))
        else:
            j = l - 2
            wq = inp["attn_w_q"][j]
            for hp in range(8):
                a = np.stack([kmajor(wq, g * 1024 + hp * 128, 128).reshape(128, 8, 128) for g in range(3)], axis=1)
                put(f"wq{l}_{hp}", a.reshape(128, 3072))
            for m in range(2):
                put(f"wo{l}_{m}", kmajor(inp["attn_w_o"][j], m * 512, 512))
        if l == 2:
            for i in range(6):
                put(f"wk_{i}", kmajor(inp["w_kv"], i * 512, 512))
                put(f"wv_{i}", kmajor(inp["w_kv"], 3072 + i * 512, 512))
        wu = inp["ffn_w_up"][l]
        for i in range(11):
            parts = []
            for jj in range(2):
                fc = 2 * i + jj
                for av in range(2):
                    parts.append(kmajor(wu, av * FF + fc * 128, 128))
            put(f"wup{l}_{i}", np.concatenate(parts, axis=1))
        wd = inp["ffn_w_down"][l]
        for r in range(2):
            a = wd[:, r * 512:(r + 1) * 512].reshape(NFC, 128, 512).transpose(1, 0, 2)
            put(f"wdn{l}_{r}_0", a[:, 0:8].reshape(128, 4096))
            put(f"wdn{l}_{r}_1", a[:, 8:16].reshape(128, 4096))
            put(f"wdn{l}_{r}_2", a[:, 16:22].reshape(128, 3072))
    apl = ada_piece_list()
    aoff, atot = _offsets(apl)
    A = np.empty((128, atot), np.float32)
    for l in range(4):
        for i in range(12):
            o, w = aoff[f"ada{l}_{i}"]
            A[:, o:o + w] = kmajor(inp["ada_w"][l], i * 512, 512)
    for i in range(4):
        o, w = aoff[f"kvada_{i}"]
        A[:, o:o + w] = kmajor(inp["kv_ada_w"], i * 512, 512)
    VL, nv = vec_layout()
    shared = np.zeros((128, nv), np.float32)

    def vput(name, arr):
        o, n = VL[name]
        assert arr.shape == (128, n), (name, arr.shape)
        shared[:, o:o + n] = arr
    for l in range(4):
        vput(f"ada_b{l}", pmaj(inp["ada_b"][l]))
        vput(f"n1g{l}", pmaj(inp["norm1_g"][l]))
        vput(f"n2g{l}", pmaj(inp["norm2_g"][l]))
        cw = inp["ffn_conv_w"][l]
        vput(f"cw{l}", np.concatenate([pmaj(cw[k]) for k in range(3)], axis=1))
        vput(f"cb{l}", pmaj(inp["ffn_conv_b"][l]))
    for l in range(2):
        vput(f"psc{l}", pmaj(inp["pool_scale"][l]))
    vput("kvg", pmaj(inp["kv_norm_g"]))
    vput("kvb", pmaj(inp["kv_ada_b"]))
    vput("fg", pmaj(inp["final_g"]))
    invc = np.zeros((128, 4, 16), np.float32)
    for g, w in enumerate(POOLW):
        invc[:, g, :] = 1.0 / np.minimum(np.arange(16) + 1, w)
    vput("invc", invc.reshape(128, 64))
    k = np.arange(128)[:, None]
    q = np.arange(128)[None, :]
    dprev = (q - k + 128).astype(np.float32)
    dprev[dprev > 128] = BIG
    dcur = (q - k).astype(np.float32)
    dcur[dcur < 0] = BIG
    vput("dist", np.concatenate([dprev, dcur], axis=1))
    d2 = []
    for na in (32, 64, 96):
        dd = (q[:, 0:32] - k + na).astype(np.float32)
        dd[dd > 128] = BIG
        dd[k[:, 0] >= na, :] = BIG
        d2.append(dd)
    vput("dist2", np.concatenate(d2, axis=1))
    per_core = []
    for b in range(8):
        v = shared.copy()
        o, n = VL["c"]
        v[:, o:o + n] = pmaj(inp["c"][b])
        per_core.append({"xT": np.ascontiguousarray(inp["x"][b].T), "vecs": v})
    return W, A, per_core


class Chan:
    __slots__ = ("sem", "val")


class Buf:
    __slots__ = ("name", "w", "r")

    def __init__(self, name, seed=None):
        self.name = name
        self.w = {}
        self.r = dict(seed) if seed else {}

    def tokens(self):
        d = dict(self.w)
        for ch, v in self.r.items():
            if d.get(ch, 0) < v:
                d[ch] = v
        return d


def _merge(d, ch, v):
    if d.get(ch, 0) < v:
        d[ch] = v


class Ctx:
    def __init__(self, nc, es):
        self.nc = nc
        self.es = es
        self.nsem = 0

    def new_chan(self):
        sem = self.es.enter_context(self.nc.semaphore(f"sm{self.nsem}"))
        self.nsem += 1
        c = Chan()
        c.sem = sem
        c.val = 0
        return c


class Eng:
    EPOCH = 16000

    def __init__(self, ctx, eng, name, is_pe=False, n_dma=0):
        self.ctx = ctx
        self.e = eng
        self.name = name
        self.is_pe = is_pe
        self.chan = ctx.new_chan()
        self.waited = {}
        self.dma_pool = [ctx.new_chan() for _ in range(n_dma)]
        self.dma_i = 0
        self.n = 0

    def wait_tok(self, ch, v):
        if ch is self.chan and self.is_pe:
            return
        if self.waited.get(ch, 0) >= v:
            return
        self.e.wait_ge(ch.sem, v)
        self.waited[ch] = v

    def sync(self, reads, writes):
        for b in reads:
            for ch, v in b.w.items():
                self.wait_tok(ch, v)
        for b in writes:
            for ch, v in b.w.items():
                self.wait_tok(ch, v)
            for ch, v in b.r.items():
                self.wait_tok(ch, v)

    def op(self, fn, reads, writes, *a, **k):
        self.sync(reads, writes)
        ins = fn(*a, **k)
        if self.chan.val >= self.EPOCH:
            self.chan = self.ctx.new_chan()
        ch = self.chan
        ch.val += 1
        ins.then_inc(ch.sem, 1)
        for b in reads:
            _merge(b.r, ch, ch.val)
        for b in writes:
            _merge(b.w, ch, ch.val)
        self.n += 1
        return ins

    def dma(self, out, in_, reads, writes, **k):
        ch = self.dma_pool[self.dma_i % len(self.dma_pool)]
        self.dma_i += 1
        if ch.val:
            self.wait_tok(ch, ch.val)
        self.sync(reads, writes)
        ins = self.e.dma_start(out=out, in_=in_, **k)
        ch.val += 16
        ins.then_inc(ch.sem, 16)
        for b in reads:
            _merge(b.r, ch, ch.val)
        for b in writes:
            _merge(b.w, ch, ch.val)
        self.n += 1
        return ins

    def wait_all(self, bufs):
        for b in bufs:
            for ch, v in b.tokens().items():
                self.wait_tok(ch, v)


class Prog:
    def __init__(self, n_tiles=NT, nstages=len(STAGES), dbg=None):
        self.n_tiles = n_tiles
        self.nstages = nstages
        self.full = nstages == len(STAGES)
        self.dbg = dbg
        nc = bass.Bass("TRN2", target_bir_lowering=False)
        self.nc = nc
        self.es = ExitStack()
        self.pl = piece_list()
        self.poff, self.ptot = _offsets(self.pl)
        self.apl = ada_piece_list()
        self.aoff, self.atot = _offsets(self.apl)
        self.VL, self.nv = vec_layout()
        self.seed = {}

    def sb(self, es, name, shape, dt):
        self._nalloc = getattr(self, "_nalloc", 0) + 1
        return es.enter_context(self.nc.sbuf_tensor(f"{name}_{self._nalloc}", list(shape), dt))

    def newbuf(self, name):
        b = Buf(name, self.seed)
        self.stage_bufs.append(b)
        return b

    def stage_begin(self):
        self.stage_bufs = []
        return ExitStack()

    def stage_end(self, es):
        seed = dict(self.seed)
        for b in self.stage_bufs:
            for ch, v in b.tokens().items():
                _merge(seed, ch, v)
        self.seed = seed
        es.close()

    def w_init(self):
        self.wslots = [self.sb(self.es, f"wslot{i}", [128, SLOTW], BF16) for i in range(NSLOT)]
        self.wbufs = [Buf(f"wslot{i}") for i in range(NSLOT)]
        self.wsched = []
        self.w_issued = 0
        self.w_used = 0

    def w_issue_upto(self, idx):
        while self.w_issued <= idx and self.w_issued < len(self.wsched):
            i = self.w_issued
            src, w, name = self.wsched[i]
            slot = i % NSLOT
            self.POOL.dma(self.wslots[slot][:, 0:w], src, [], [self.wbufs[slot]], max_dma_last_dim=2048)
            self.w_issued += 1

    def w_next(self, name):
        i = self.w_used
        src, w, nm = self.wsched[i]
        assert nm == name, (nm, name)
        self.w_issue_upto(i + NSLOT - 1)
        self.w_used += 1
        slot = i % NSLOT
        return self.wslots[slot], self.wbufs[slot]

    def build(self):
        nc = self.nc
        es = self.es
        ctx = Ctx(nc, es)
        self.ctx = ctx
        nt = self.n_tiles
        self.xT = nc.dram_tensor("xT", [D, S], F32, kind="ExternalInput").ap()
        self.wts = nc.dram_tensor("wts", [128, self.ptot], F32, kind="ExternalInput").ap()
        self.adaw = nc.dram_tensor("adaw", [128, self.atot], F32, kind="ExternalInput").ap()
        self.vecs_d = nc.dram_tensor("vecs", [128, self.nv], F32, kind="ExternalInput").ap()
        self.outT = nc.dram_tensor("outT", [D, S], F32, kind="ExternalOutput").ap()
        self.kTd = nc.dram_tensor("kTd", [3, 8, 128, S], BF16, kind="Internal").ap()
        self.vd = nc.dram_tensor("vd", [3, S, D], BF16, kind="Internal").ap()
        self.kv_bufs = [Buf(f"kv{s}") for s in range(NT)]

        self.PE = Eng(ctx, nc.tensor, "pe", is_pe=True)
        self.ACT = Eng(ctx, nc.scalar, "act")
        self.DVE = Eng(ctx, nc.vector, "dve")
        self.POOL = Eng(ctx, nc.gpsimd, "pool", n_dma=NSLOT + 1)
        self.SP = Eng(ctx, nc.sync, "sp", n_dma=40)
        PE, ACT, DVE, POOL, SP = self.PE, self.ACT, self.DVE, self.POOL, self.SP

        self.vecs = self.sb(es, "vecs_sb", [128, self.nv], F32)
        self.vecs_b = Buf("vecs")
        self.modT = self.sb(es, "modT", [128, 4 * 48 + 16], F32)
        self.der = self.sb(es, "der", [128, 4 * 16 + 8], F32)
        self.mod_b = Buf("mod")
        self.ones = self.sb(es, "ones", [128, 128], BF16)
        self.ones_b = Buf("ones")
        self.condb = self.sb(es, "condb", [128, 8], BF16)
        self.cond_b = Buf("cond")
        self.xTs = [self.sb(es, f"xTs{i}", [128, KC, T], F32) for i in range(2)]
        self.x_bufs = [Buf(f"x{i}") for i in range(2)]
        self.hT = self.sb(es, "hT", [128, KC, T], BF16)
        self.h_b = Buf("hT")
        self.sq = self.sb(es, "sq", [128, KC, T], BF16)
        self.sq_b = Buf("sq")
        self.std = self.sb(es, "std", [128, T], F32)
        self.rstd = self.sb(es, "rstd", [128, T], F32)
        self.std_b = Buf("std")
        self.rstd_b = Buf("rstd")
        self.ntmp = [self.sb(es, f"ntmp{i}", [128, T], F32) for i in range(2)]
        self.ntmp_b = [Buf(f"ntmp{i}") for i in range(2)]
        self.uhalo = [self.sb(es, f"uhalo{l}", [128, KC, 16], F32) for l in range(2)]
        self.uhalo_b = [Buf(f"uhalo{l}") for l in range(2)]
        self.ahalo = [self.sb(es, f"ahalo{l}", [128, NFC, 2], F32) for l in range(4)]
        self.ahalo_b = [Buf(f"ahalo{l}") for l in range(4)]
        self.masks = self.sb(es, "masks", [128, 48, 256], BF16)
        self.masks_b = Buf("masks")
        self.masks2 = self.sb(es, "masks2", [128, 3, NH, 32], BF16)
        self.ps = [es.enter_context(nc.psum_tensor(f"ps{i}", [128, 512], F32)) for i in range(8)]
        self.ps_b = [Buf(f"ps{i}") for i in range(8)]
        self.w_init()

        def ada_sched(l):
            for n, w in self.apl:
                if n.startswith(f"ada{l}_") or (l == 4 and n.startswith("kvada")):
                    o, _ = self.aoff[n]
                    self.wsched.append((self.adaw[:, o:o + w], w, n))
        for s in range(nt):
            for st in STAGES[:self.nstages]:
                if s == 0 and st.startswith("mix"):
                    ada_sched(int(st[3]))
                if s == 0 and st == "kv":
                    ada_sched(4)
                for n, w in stage_pieces(st):
                    o, _ = self.poff[n]
                    self.wsched.append((self.wts[:, o:o + w], w, n))

        self.prologue()
        for s in range(nt):
            self.tile(s)
        for b in self.out_wait:
            SP.wait_all([b])
        return nc

    def vcol(self, name, c0=0, n=1):
        o, _ = self.VL[name]
        return self.vecs[:, o + c0:o + c0 + n]

    def prologue(self):
        nc = self.nc
        PE, ACT, DVE, POOL, SP = self.PE, self.ACT, self.DVE, self.POOL, self.SP
        SP.dma(self.vecs[:, :], self.vecs_d[:, :], [], [self.vecs_b])
        DVE.op(nc.vector.memset, [], [self.ones_b], self.ones[:, :], 1.0)
        for l in range(2):
            DVE.op(nc.vector.memset, [], [self.uhalo_b[l]], self.uhalo[l][:, :, :], 0.0)
        for l in range(4):
            DVE.op(nc.vector.memset, [], [self.ahalo_b[l]], self.ahalo[l][:, :, :], 0.0)
        ACT.op(nc.scalar.activation, [self.vecs_b], [self.cond_b], out=self.condb[:, :], in_=self.vcol("c", 0, 8), func=AF.Silu)
        if self.nstages > 5:
            for g in range(3):
                for h in range(NH):
                    ACT.op(nc.scalar.activation, [self.vecs_b], [self.masks_b], out=self.masks[:, g * NH + h, :],
                           in_=self.vcol("dist", 0, 256), func=AF.Exp, scale=-float(SLOPES[g, h]) * DIL[g])
            for i in range(3):
                for h in range(NH):
                    ACT.op(nc.scalar.activation, [self.vecs_b], [self.masks_b], out=self.masks2[:, i, h, :],
                           in_=self.vcol("dist2", i * 32, 32), func=AF.Exp, scale=-float(SLOPES[2, h]) * DIL[2])

    def compute_mod(self, l):
        nc = self.nc
        PE, ACT, DVE, POOL, SP = self.PE, self.ACT, self.DVE, self.POOL, self.SP
        pb = self.psrr()
        if True:
            ncol = 48 if l < 4 else 16
            npieces = 12 if l < 4 else 4
            pt, pbuf = self.ps[pb], self.ps_b[pb]
            first = True
            for i in range(npieces):
                slot, sbuf_ = self.w_next(f"ada{l}_{i}" if l < 4 else f"kvada_{i}")
                for mm in range(4):
                    col = i * 4 + mm
                    for kc in range(KC):
                        PE.op(nc.tensor.matmul, [sbuf_, self.cond_b], [pbuf], pt[:, col:col + 1],
                              lhsT=slot[:, kc * 512 + mm * 128: kc * 512 + mm * 128 + 128], rhs=self.condb[:, kc:kc + 1],
                              start=first, stop=(kc == KC - 1))
                        first = False
            bname = f"ada_b{l}" if l < 4 else "kvb"
            DVE.op(nc.vector.tensor_tensor, [pbuf, self.vecs_b], [self.mod_b], out=self.modT[:, l * 48:l * 48 + ncol],
                   in0=pt[:, 0:ncol], in1=self.vcol(bname, 0, ncol), op=ALU.add)
        if l < 4:
            for j, (gn, sc0) in enumerate(((f"n1g{l}", 8), (f"n2g{l}", 32))):
                DVE.op(nc.vector.scalar_tensor_tensor, [self.mod_b, self.vecs_b], [self.mod_b],
                       out=self.der[:, l * 16 + j * 8: l * 16 + j * 8 + 8], in0=self.modT[:, l * 48 + sc0: l * 48 + sc0 + 8],
                       scalar=1.0, in1=self.vcol(gn, 0, 8), op0=ALU.add, op1=ALU.mult)
        else:
            DVE.op(nc.vector.scalar_tensor_tensor, [self.mod_b, self.vecs_b], [self.mod_b],
                   out=self.der[:, 64:72], in0=self.modT[:, 192 + 8:192 + 16], scalar=1.0, in1=self.vcol("kvg", 0, 8),
                   op0=ALU.add, op1=ALU.mult)

    def A(self, l, which, c):
        return self.der[:, l * 16 + which * 8 + c: l * 16 + which * 8 + c + 1]

    def M(self, l, j, c):
        return self.modT[:, l * 48 + j * 8 + c: l * 48 + j * 8 + c + 1]

    def norm(self, x, xb, acol, bcol, out=None, out_b=None, final=False):
        nc = self.nc
        PE, ACT, DVE = self.PE, self.ACT, self.DVE
        ACT.op(nc.scalar.activation, [xb], [self.sq_b], out=self.sq[:, :, :], in_=x[:, :, :], func=AF.Square)
        pi = self.psrr()
        pt, pbuf = self.ps[pi], self.ps_b[pi]
        for c in range(KC):
            PE.op(nc.tensor.matmul, [self.sq_b, self.ones_b], [pbuf], pt[:, :], lhsT=self.ones[:, :], rhs=self.sq[:, c, :],
                  start=(c == 0), stop=(c == KC - 1))
        ACT.op(nc.scalar.activation, [pbuf, self.eps_b], [self.std_b], out=self.std[:, :], in_=pt[:, :], func=AF.Sqrt,
               scale=1.0 / D, bias=self.epsc[:, 0:1])
        DVE.op(nc.vector.reciprocal, [self.std_b], [self.rstd_b], out=self.rstd[:, :], in_=self.std[:, :])
        for c in range(KC):
            tb = self.ntmp_b[c % 2]
            tt = self.ntmp[c % 2]
            DVE.op(nc.vector.tensor_tensor, [xb, self.rstd_b], [tb], out=tt[:, :], in0=x[:, c, :], in1=self.rstd[:, :], op=ALU.mult)
            if final:
                ACT.op(nc.scalar.activation, [tb, self.vecs_b], [out_b], out=out[:, c, :], in_=tt[:, :], func=AF.Copy,
                       scale=acol(c))
            else:
                ACT.op(nc.scalar.activation, [tb, self.mod_b], [self.h_b], out=self.hT[:, c, :], in_=tt[:, :], func=AF.Identity,
                       scale=acol(c), bias=bcol(c))

    def psrr(self):
        i = self._psi
        self._psi = (self._psi + 1) % 8
        return i

    def tile(self, s):
        nc = self.nc
        PE, ACT, DVE, POOL, SP = self.PE, self.ACT, self.DVE, self.POOL, self.SP
        if s == 0:
            self._psi = 0
            self.out_wait = []
            self.epsc = self.sb(self.es, "epsc", [128, 1], F32)
            self.eps_b = Buf("eps")
            DVE.op(nc.vector.memset, [], [self.eps_b], self.epsc[:, :], EPS)
        x = self.xTs[s % 2]
        xb = self.x_bufs[s % 2]
        for s2 in ([0, 1] if s == 0 else [s + 1]):
            if s2 < self.n_tiles:
                SP.dma(self.xTs[s2 % 2][:, :, :], self.xT.rearrange("(c p) t -> p c t", p=128)[:, :, s2 * T:(s2 + 1) * T], [], [self.x_bufs[s2 % 2]])
        for st in STAGES[:self.nstages]:
            if s == 0 and st.startswith("mix"):
                self.compute_mod(int(st[3]))
            if s == 0 and st == "kv":
                self.compute_mod(4)
            if st == "kv":
                self.kv_stage(s, x, xb)
            elif st.startswith("mix"):
                l = int(st[3])
                if l < 2:
                    self.pool_mixer(l, s, x, xb)
                else:
                    self.attn_mixer(l, s, x, xb)
            else:
                self.ffn(int(st[3]), s, x, xb)
        es = self.stage_begin()
        o = self.sb(es, "otile", [128, KC, T], F32)
        ob = self.newbuf("otile")
        if self.full:
            self.norm(x, xb, lambda c: self.vcol("fg", c, 1), None, out=o, out_b=ob, final=True)
        else:
            ACT.op(nc.scalar.activation, [xb], [ob], out=o[:, :, :], in_=x[:, :, :], func=AF.Copy)
        SP.dma(self.outT.rearrange("(c p) t -> p c t", p=128)[:, :, s * T:(s + 1) * T], o[:, :, :], [ob], [])
        self.out_wait.append(ob)
        self.stage_end(es)

    def pool_mixer(self, l, s, x, xb):
        nc = self.nc
        PE, ACT, DVE = self.PE, self.ACT, self.DVE
        es = self.stage_begin()
        U = self.sb(es, "poolU", [128, KC, 16 + T], F32)
        Ub = self.newbuf("U")
        A_ = self.sb(es, "poolA", [128, KC, 16 + T], F32)
        Ab = self.newbuf("A")
        B_ = self.sb(es, "poolB", [128, KC, 16 + T], F32)
        Bb = self.newbuf("B")
        Pb = self.sb(es, "poolP", [128, KC, T], BF16)
        Pbb = self.newbuf("P")
        Zb = self.sb(es, "poolZ", [128, KC, T], BF16)
        Zbb = self.newbuf("Z")
        self.norm(x, xb, lambda c: self.A(l, 0, c), lambda c: self.M(l, 0, c))
        DVE.op(nc.vector.tensor_copy, [self.uhalo_b[l]], [Ub], out=U[:, :, 0:16], in_=self.uhalo[l][:, :, :])
        for half in range(2):
            slot, sbuf_ = self.w_next(f"win{l}_{half}")
            for mm in range(4):
                m = half * 4 + mm
                pi = self.psrr()
                pt, pbuf = self.ps[pi], self.ps_b[pi]
                for kc in range(KC):
                    PE.op(nc.tensor.matmul, [sbuf_, self.h_b], [pbuf], pt[:, :], lhsT=slot[:, kc * 512 + mm * 128: kc * 512 + mm * 128 + 128],
                          rhs=self.hT[:, kc, :], start=(kc == 0), stop=(kc == KC - 1))
                ACT.op(nc.scalar.activation, [pbuf], [Ub], out=U[:, m, 16:16 + T], in_=pt[:, :], func=AF.Copy)
        DVE.op(nc.vector.tensor_copy, [Ub], [self.uhalo_b[l]], out=self.uhalo[l][:, :, :], in_=U[:, :, T:T + 16])
        W_ = 16 + T
        DVE.op(nc.vector.tensor_tensor, [Ub], [Ab], out=A_[:, :, 1:W_], in0=U[:, :, 1:W_], in1=U[:, :, 0:W_ - 1], op=ALU.add)
        DVE.op(nc.vector.tensor_tensor, [Ab], [Bb], out=B_[:, 2:8, 3:W_], in0=A_[:, 2:8, 3:W_], in1=A_[:, 2:8, 1:W_ - 2], op=ALU.add)
        DVE.op(nc.vector.tensor_tensor, [Bb], [Ab], out=A_[:, 4:8, 7:W_], in0=B_[:, 4:8, 7:W_], in1=B_[:, 4:8, 3:W_ - 4], op=ALU.add)
        DVE.op(nc.vector.tensor_tensor, [Ab], [Bb], out=B_[:, 6:8, 15:W_], in0=A_[:, 6:8, 15:W_], in1=A_[:, 6:8, 7:W_ - 8], op=ALU.add)
        srcs = [(A_, Ab), (B_, Bb), (A_, Ab), (B_, Bb)]
        for g in range(4):
            St, Sb_ = srcs[g]
            DVE.op(nc.vector.scalar_tensor_tensor, [Sb_, Ub], [Pbb], out=Pb[:, 2 * g:2 * g + 2, :], in0=St[:, 2 * g:2 * g + 2, 16:16 + T],
                   scalar=1.0 / POOLW[g], in1=U[:, 2 * g:2 * g + 2, 16:16 + T], op0=ALU.mult, op1=ALU.subtract)
            if s == 0:
                o, _ = self.VL["invc"]
                for cc in range(2):
                    c = 2 * g + cc
                    tb, tt = self.ntmp_b[cc], self.ntmp[cc]
                    DVE.op(nc.vector.tensor_tensor, [Sb_, self.vecs_b], [tb], out=tt[:, 0:16], in0=St[:, c, 16:32],
                           in1=self.vecs[:, o + g * 16:o + g * 16 + 16], op=ALU.mult)
                    DVE.op(nc.vector.tensor_tensor, [tb, Ub], [Pbb], out=Pb[:, c, 0:16], in0=tt[:, 0:16], in1=U[:, c, 16:32], op=ALU.subtract)
        slot, sbuf_ = self.w_next(f"wgrp{l}")
        for g in range(4):
            for mo in range(2):
                c = 2 * g + mo
                pi = self.psrr()
                pt, pbuf = self.ps[pi], self.ps_b[pi]
                for ki in range(2):
                    base = (g * 2 + ki) * 256 + mo * 128
                    PE.op(nc.tensor.matmul, [sbuf_, Pbb], [pbuf], pt[:, :], lhsT=slot[:, base:base + 128], rhs=Pb[:, 2 * g + ki, :],
                          start=(ki == 0), stop=(ki == 1))
                ACT.op(nc.scalar.activation, [pbuf, self.vecs_b], [Zbb], out=Zb[:, c, :], in_=pt[:, :], func=AF.Copy,
                       scale=self.vcol(f"psc{l}", c, 1))
        for half in range(2):
            slot, sbuf_ = self.w_next(f"wout{l}_{half}")
            for mm in range(4):
                m = half * 4 + mm
                pi = self.psrr()
                pt, pbuf = self.ps[pi], self.ps_b[pi]
                for kc in range(KC):
                    PE.op(nc.tensor.matmul, [sbuf_, Zbb], [pbuf], pt[:, :], lhsT=slot[:, kc * 512 + mm * 128: kc * 512 + mm * 128 + 128],
                          rhs=Zb[:, kc, :], start=(kc == 0), stop=(kc == KC - 1))
                DVE.op(nc.vector.scalar_tensor_tensor, [pbuf, self.mod_b, xb], [xb], out=x[:, m, :], in0=pt[:, :], scalar=self.M(l, 2, m),
                       in1=x[:, m, :], op0=ALU.mult, op1=ALU.add)
        self.stage_end(es)

    def ffn(self, l, s, x, xb):
        nc = self.nc
        PE, ACT, DVE = self.PE, self.ACT, self.DVE
        es = self.stage_begin()
        gT = self.sb(es, "gT", [128, NFC, T], BF16)
        gb = self.newbuf("gT")
        abuf = [self.sb(es, f"abuf{i}", [128, 2 + T], F32) for i in range(2)]
        ab = [self.newbuf(f"abuf{i}") for i in range(2)]
        c1 = [self.sb(es, f"c1_{i}", [128, T], F32) for i in range(2)]
        c1b = [self.newbuf(f"c1_{i}") for i in range(2)]
        c2 = [self.sb(es, f"c2_{i}", [128, T], F32) for i in range(2)]
        c2b = [self.newbuf(f"c2_{i}") for i in range(2)]
        vsb = [self.sb(es, f"vsb{i}", [128, T], F32) for i in range(2)]
        vsbb = [self.newbuf(f"vsb{i}") for i in range(2)]
        self.norm(x, xb, lambda c: self.A(l, 1, c), lambda c: self.M(l, 3, c))
        cwo, _ = self.VL[f"cw{l}"]
        cbo, _ = self.VL[f"cb{l}"]
        for i in range(11):
            slot, sbuf_ = self.w_next(f"wup{l}_{i}")
            for jj in range(2):
                fc = 2 * i + jj
                par = fc % 2
                pa, pab = self.ps[par * 2], self.ps_b[par * 2]
                pv, pvb = self.ps[par * 2 + 1], self.ps_b[par * 2 + 1]
                for av, (pt, pbuf) in enumerate(((pa, pab), (pv, pvb))):
                    base = (jj * 2 + av) * 1024
                    for kc in range(KC):
                        PE.op(nc.tensor.matmul, [sbuf_, self.h_b], [pbuf], pt[:, :], lhsT=slot[:, base + kc * 128: base + kc * 128 + 128],
                              rhs=self.hT[:, kc, :], start=(kc == 0), stop=(kc == KC - 1))
                A_, Ab = abuf[par], ab[par]
                ACT.op(nc.scalar.activation, [self.ahalo_b[l]], [Ab], out=A_[:, 0:2], in_=self.ahalo[l][:, fc, :], func=AF.Copy)
                ACT.op(nc.scalar.activation, [pab], [Ab], out=A_[:, 2:2 + T], in_=pa[:, :], func=AF.Copy)
                ACT.op(nc.scalar.activation, [Ab], [self.ahalo_b[l]], out=self.ahalo[l][:, fc, :], in_=A_[:, T:T + 2], func=AF.Copy)
                ACT.op(nc.scalar.activation, [pab, self.vecs_b], [c1b[par]], out=c1[par][:, :], in_=pa[:, :], func=AF.Identity,
                       scale=self.vecs[:, cwo + 2 * NFC + fc: cwo + 2 * NFC + fc + 1], bias=self.vecs[:, cbo + fc:cbo + fc + 1])
                ACT.op(nc.scalar.activation, [pvb], [vsbb[par]], out=vsb[par][:, :], in_=pv[:, :], func=AF.Copy)
                DVE.op(nc.vector.scalar_tensor_tensor, [Ab, self.vecs_b, c1b[par]], [c2b[par]], out=c2[par][:, :], in0=A_[:, 1:1 + T],
                       scalar=self.vecs[:, cwo + NFC + fc: cwo + NFC + fc + 1], in1=c1[par][:, :], op0=ALU.mult, op1=ALU.add)
                DVE.op(nc.vector.scalar_tensor_tensor, [Ab, self.vecs_b, c2b[par]], [c1b[par]], out=c1[par][:, :], in0=A_[:, 0:T],
                       scalar=self.vecs[:, cwo + fc: cwo + fc + 1], in1=c2[par][:, :], op0=ALU.mult, op1=ALU.add)
                ACT.op(nc.scalar.activation, [c1b[par]], [c2b[par]], out=c2[par][:, :], in_=c1[par][:, :], func=AF.Silu)
                DVE.op(nc.vector.tensor_tensor, [c2b[par], vsbb[par]], [gb], out=gT[:, fc, :], in0=c2[par][:, :], in1=vsb[par][:, :], op=ALU.mult)
        for r in range(2):
            b0 = 4 if r == 0 else 0
            for j in range(3):
                slot, sbuf_ = self.w_next(f"wdn{l}_{r}_{j}")
                nf = 8 if j < 2 else 6
                for fl in range(nf):
                    fc = 8 * j + fl
                    for mm in range(4):
                        PE.op(nc.tensor.matmul, [sbuf_, gb], [self.ps_b[b0 + mm]], self.ps[b0 + mm][:, :],
                              lhsT=slot[:, fl * 512 + mm * 128: fl * 512 + mm * 128 + 128], rhs=gT[:, fc, :],
                              start=(fc == 0), stop=(fc == NFC - 1))
            for mm in range(4):
                m = r * 4 + mm
                DVE.op(nc.vector.scalar_tensor_tensor, [self.ps_b[b0 + mm], self.mod_b, xb], [xb], out=x[:, m, :], in0=self.ps[b0 + mm][:, :],
                       scalar=self.M(l, 5, m), in1=x[:, m, :], op0=ALU.mult, op1=ALU.add)
        self._psi = 4
        self.stage_end(es)

    def kv_stage(self, s, x, xb):
        nc = self.nc
        PE, ACT, DVE, SP = self.PE, self.ACT, self.DVE, self.SP
        es = self.stage_begin()
        kst = [self.sb(es, f"kst{i}", [128, T], BF16) for i in range(6)]
        kstb = [self.newbuf(f"kst{i}") for i in range(6)]
        vst = [self.sb(es, f"vst{i}", [128, 512], BF16) for i in range(4)]
        vstb = [self.newbuf(f"vst{i}") for i in range(4)]
        kvb = self.kv_bufs[s]
        self.norm(x, xb, lambda c: self.der[:, 64 + c:65 + c], lambda c: self.modT[:, 192 + c:193 + c])
        h16 = self.sb(es, "h16", [128, KC, T], BF16)
        h16b = self.newbuf("h16")
        for kc in range(KC):
            DVE.op(nc.vector.tensor_copy, [self.h_b], [h16b], out=h16[:, kc, :].rearrange("p (r j) -> p r j", r=16),
                   in_=self.hT[:, kc, :].rearrange("p (j r) -> p r j", r=16))
        n = 0
        for i in range(6):
            slot, sbuf_ = self.w_next(f"wk_{i}")
            g = i // 2
            d = DIL[g]
            for mm in range(4):
                hp = (i % 2) * 4 + mm
                pi = self.psrr()
                pt, pbuf = self.ps[pi], self.ps_b[pi]
                for kc in range(KC):
                    PE.op(nc.tensor.matmul, [sbuf_, self.h_b], [pbuf], pt[:, :], lhsT=slot[:, kc * 512 + mm * 128: kc * 512 + mm * 128 + 128],
                          rhs=self.hT[:, kc, :], start=(kc == 0), stop=(kc == KC - 1))
                kt, ktb = kst[n % 6], kstb[n % 6]
                n += 1
                nj = T // d
                if d == 1:
                    ACT.op(nc.scalar.activation, [pbuf], [ktb], out=kt[:, :], in_=pt[:, :], func=AF.Copy)
                    SP.dma(self.kTd[g, hp, :, s * T:(s + 1) * T], kt[:, :], [ktb], [kvb])
                else:
                    ACT.op(nc.scalar.activation, [pbuf], [ktb], out=kt[:, :].rearrange("p (r j) -> p r j", r=d),
                           in_=pt[:, :].rearrange("p (j r) -> p r j", r=d), func=AF.Copy)
                    dst = self.kTd[g, hp, :, :].rearrange("p (r j) -> p r j", r=d)[:, :, s * nj:(s + 1) * nj]
                    SP.dma(dst, kt[:, :].rearrange("p (r j) -> p r j", r=d), [ktb], [kvb])
        n = 0
        for i in range(6):
            slot, sbuf_ = self.w_next(f"wv_{i}")
            g = i // 2
            d = DIL[g]
            half = i % 2
            for ch in range(4):
                if d == 1:
                    cols = lambda kc: self.hT[:, kc, ch * 128:(ch + 1) * 128]
                elif d == 4:
                    cols = lambda kc: self.hT[:, kc, :].rearrange("p (j r) -> p r j", r=4)[:, ch, :]
                else:
                    cols = lambda kc: h16[:, kc, ch * 128:(ch + 1) * 128]
                pi = self.psrr()
                pt, pbuf = self.ps[pi], self.ps_b[pi]
                for kc in range(KC):
                    PE.op(nc.tensor.matmul, [sbuf_, self.h_b, h16b], [pbuf], pt[:, :], lhsT=cols(kc), rhs=slot[:, kc * 512:(kc + 1) * 512],
                          start=(kc == 0), stop=(kc == KC - 1))
                vt, vtb = vst[n % 4], vstb[n % 4]
                n += 1
                DVE.op(nc.vector.tensor_copy, [pbuf], [vtb], out=vt[:, 0:512], in_=pt[:, :])
                if d == 1:
                    r0 = s * T + ch * 128
                    SP.dma(self.vd[g, r0:r0 + 128, half * 512:(half + 1) * 512], vt[:, 0:512], [vtb], [kvb])
                elif d == 4:
                    r0 = ch * 1024 + s * 128
                    SP.dma(self.vd[g, r0:r0 + 128, half * 512:(half + 1) * 512], vt[:, 0:512], [vtb], [kvb])
                else:
                    for rl in range(4):
                        r0 = (ch * 4 + rl) * 256 + s * 32
                        SP.dma(self.vd[g, r0:r0 + 32, half * 512:(half + 1) * 512], vt[rl * 32:(rl + 1) * 32, 0:512], [vtb], [kvb])
        self.stage_end(es)

    def attn_mixer(self, l, s, x, xb):
        nc = self.nc
        PE, ACT, DVE, SP = self.PE, self.ACT, self.DVE, self.SP
        es = self.stage_begin()
        oT = self.sb(es, "oT", [128, KC, T], BF16)
        oTb = self.newbuf("oT")
        qT = [self.sb(es, f"qT{i}", [128, 3, T], BF16) for i in range(2)]
        qTb = [self.newbuf(f"qT{i}") for i in range(2)]
        KW = 640 + 1024 + 2560
        kt = [self.sb(es, f"ktile{i}", [128, KW], BF16) for i in range(2)]
        ktb = [self.newbuf(f"ktile{i}") for i in range(2)]
        NVC = 5 + 8 + 32
        vt = [self.sb(es, f"vtile{i}", [128, NVC, 128], BF16) for i in range(2)]
        vtb = [self.newbuf(f"vtile{i}") for i in range(2)]
        NB = 3
        ET = [self.sb(es, f"E{i}", [128, T], BF16) for i in range(NB)]
        ETb = [self.newbuf(f"E{i}") for i in range(NB)]
        PT = [self.sb(es, f"P{i}", [128, T], BF16) for i in range(NB)]
        PTb = [self.newbuf(f"P{i}") for i in range(NB)]
        rD = self.sb(es, "rD", [128, T], F32)
        rDb = self.newbuf("rD")
        self.norm(x, xb, lambda c: self.A(l, 0, c), lambda c: self.M(l, 0, c))
        kv_reads = [self.kv_bufs[t] for t in range(max(0, s - 4), s + 1)]
        na = min(128, 32 * s)

        def prep(hp):
            par = hp % 2
            K_, Kb = kt[par], ktb[par]
            V_, Vb = vt[par], vtb[par]
            lo0 = 128 if s == 0 else 0
            SP.dma(K_[:, lo0:640], self.kTd[0, hp, :, s * T - 128 + lo0: s * T + 512], kv_reads, [Kb])
            src_ = self.vd[0, s * T - 128 + lo0: s * T + 512, hp * 128:(hp + 1) * 128].rearrange("(c p) f -> p c f", p=128)
            SP.dma(V_[:, lo0 // 128:5, :], src_, kv_reads, [Vb])
            srck = self.kTd[1, hp, :, :].rearrange("p (r j) -> p r j", r=4)[:, :, 128 * (s - 1) + lo0: 128 * (s + 1)]
            SP.dma(K_[:, 640:640 + 1024].rearrange("p (r j) -> p r j", r=4)[:, :, lo0:256], srck, kv_reads, [Kb])
            for r in range(4):
                r0 = r * 1024 + 128 * (s - 1) + lo0
                nch = 2 - lo0 // 128
                src_ = self.vd[1, r0:r0 + 128 * nch, hp * 128:(hp + 1) * 128].rearrange("(c p) f -> p c f", p=128)
                SP.dma(V_[:, 5 + r * 2 + lo0 // 128: 5 + r * 2 + 2, :], src_, kv_reads, [Vb])
            k2 = K_[:, 640 + 1024:].rearrange("p (r j) -> p r j", r=16)
            srck = self.kTd[2, hp, :, :].rearrange("p (r j) -> p r j", r=16)[:, :, 32 * s - na: 32 * s + 32]
            SP.dma(k2[:, :, 128 - na:160], srck, kv_reads, [Kb])
            v2 = self.vd[2, :, hp * 128:(hp + 1) * 128].rearrange("(r j) f -> j r f", r=16)
            if na > 0:
                SP.dma(V_[0:na, 13:29, :], v2[32 * s - na:32 * s, :, :], kv_reads, [Vb])
            SP.dma(V_[0:32, 29:45, :], v2[32 * s:32 * s + 32, :, :], kv_reads, [Vb])
            slot, sbuf_ = self.w_next(f"wq{l}_{hp}")
            Q_, Qb = qT[par], qTb[par]
            for g in range(3):
                pt, pbuf = self.ps[3], self.ps_b[3]
                for kc in range(KC):
                    base = (g * 8 + kc) * 128
                    PE.op(nc.tensor.matmul, [sbuf_, self.h_b], [pbuf], pt[:, :], lhsT=slot[:, base:base + 128], rhs=self.hT[:, kc, :],
                          start=(kc == 0), stop=(kc == KC - 1))
                d = DIL[g]
                if d == 1:
                    ACT.op(nc.scalar.activation, [pbuf], [Qb], out=Q_[:, g, :], in_=pt[:, :], func=AF.Copy)
                else:
                    ACT.op(nc.scalar.activation, [pbuf], [Qb], out=Q_[:, g, :].rearrange("p (r j) -> p r j", r=d),
                           in_=pt[:, :].rearrange("p (j r) -> p r j", r=d), func=AF.Copy)

        units_all = []
        for hp in range(8):
            ulist = []
            for g in range(3):
                batches = []
                if g < 2:
                    for kind in (0, 1):
                        us = []
                        for u in range(4):
                            if g == 0:
                                if kind == 0 and s == 0 and u == 0:
                                    continue
                                us.append((u, (u + kind) * 128, u + kind))
                            else:
                                if kind == 0 and s == 0:
                                    continue
                                us.append((u, 640 + u * 256 + kind * 128, 5 + u * 2 + kind))
                        if us:
                            batches.append((kind, 128, 128, us))
                else:
                    if na > 0:
                        batches.append((0, 32, na, [(r, 640 + 1024 + r * 160 + 128 - na, 13 + r) for r in range(16)]))
                    batches.append((1, 32, 32, [(r, 640 + 1024 + r * 160 + 128, 29 + r) for r in range(16)]))
                for (kind, nq, nk, us) in batches:
                    for hh in range(2):
                        ulist.append(dict(hp=hp, g=g, kind=kind, nq=nq, nk=nk, us=us, hh=hh))
            ulist[0]["first_of_hp"] = True
            ulist[-1]["last_of_hp"] = True
            seen = set()
            for ud in ulist:
                if ud["hh"] not in seen:
                    ud["first_pv"] = True
                    seen.add(ud["hh"])
            units_all += ulist

        def s_phase(i, ud):
            hp, g, kind, nq, nk, us, hh = ud["hp"], ud["g"], ud["kind"], ud["nq"], ud["nk"], ud["us"], ud["hh"]
            par = hp % 2
            K_, Kb = kt[par], ktb[par]
            Q_, Qb = qT[par], qTb[par]
            h = hp * 2 + hh
            r0, r1 = hh * 64, hh * 64 + 64
            si = i % NB
            Sp, Spb = self.ps[si], self.ps_b[si]
            for (u, kcol, vch) in us:
                PE.op(nc.tensor.matmul, [Kb, Qb], [Spb], Sp[0:nk, u * nq:(u + 1) * nq], lhsT=K_[r0:r1, kcol:kcol + nk],
                      rhs=Q_[r0:r1, g, u * nq:(u + 1) * nq], start=True, stop=True)
            u0 = us[0][0]
            u1 = us[-1][0] + 1
            nu = u1 - u0
            E_, Eb = ET[si], ETb[si]
            P_, Pb_ = PT[si], PTb[si]
            ACT.op(nc.scalar.activation, [Spb], [Eb], out=E_[0:nk, u0 * nq:u1 * nq], in_=Sp[0:nk, u0 * nq:u1 * nq],
                   func=AF.Exp, scale=HD ** -0.5)
            if g == 2 and kind == 0 and nk < 128:
                mk = self.masks2[0:nk, nk // 32 - 1, h, 0:32]
            else:
                mk = self.masks[0:nk, g * NH + h, kind * 128: kind * 128 + nq]
            DVE.op(nc.vector.tensor_tensor, [Eb, self.masks_b], [Pb_],
                   out=P_[0:nk, u0 * nq:u1 * nq].rearrange("p (u q) -> p u q", u=nu),
                   in0=E_[0:nk, u0 * nq:u1 * nq].rearrange("p (u q) -> p u q", u=nu),
                   in1=mk.unsqueeze(1).to_broadcast([nk, nu, nq]), op=ALU.mult)

        def pv_phase(i, ud):
            hp, g, kind, nq, nk, us, hh = ud["hp"], ud["g"], ud["kind"], ud["nq"], ud["nk"], ud["us"], ud["hh"]
            par = hp % 2
            V_, Vb = vt[par], vtb[par]
            r0, r1 = hh * 64, hh * 64 + 64
            si = i % NB
            P_, Pb_ = PT[si], PTb[si]
            Np, Npb = self.ps[4 + par], self.ps_b[4 + par]
            Dp, Dpb = self.ps[6 + par], self.ps_b[6 + par]
            d = DIL[g]
            first = ud.get("first_pv", False)
            for (u, kcol, vch) in us:
                if d == 1:
                    No = Np[r0:r1, u * 128:(u + 1) * 128]
                    Do = Dp[r0:r1, u * 128:(u + 1) * 128]
                else:
                    No = Np[r0:r1, :].rearrange("p (j r) -> p r j", r=d)[:, u, :]
                    Do = Dp[r0:r1, :].rearrange("p (j r) -> p r j", r=d)[:, u, :]
                PE.op(nc.tensor.matmul, [Vb, Pb_], [Npb], No, lhsT=V_[0:nk, vch, r0:r1], rhs=P_[0:nk, u * nq:(u + 1) * nq],
                      start=first, stop=False, skip_group_check=True)
                PE.op(nc.tensor.matmul, [self.ones_b, Pb_], [Dpb], Do, lhsT=self.ones[0:nk, 0:64], rhs=P_[0:nk, u * nq:(u + 1) * nq],
                      start=first, stop=False, skip_group_check=True)
                first = False
            if ud.get("last_of_hp"):
                DVE.op(nc.vector.reciprocal, [Dpb], [rDb], out=rD[:, :], in_=Dp[:, :])
                DVE.op(nc.vector.tensor_tensor, [Npb, rDb], [oTb], out=oT[:, hp, :], in0=Np[:, :], in1=rD[:, :], op=ALU.mult)

        LOOK = 2
        n = len(units_all)
        prep(0)
        prep(1)
        for i in range(n + LOOK):
            if i < n:
                s_phase(i, units_all[i])
            if i - LOOK >= 0:
                ud = units_all[i - LOOK]
                pv_phase(i - LOOK, ud)
                if ud.get("last_of_hp") and ud["hp"] + 2 < 8:
                    prep(ud["hp"] + 2)
        self._psi = 0
        for half in range(2):
            slot, sbuf_ = self.w_next(f"wo{l}_{half}")
            for mm in range(4):
                m = half * 4 + mm
                pi = self.psrr()
                pt, pbuf = self.ps[pi], self.ps_b[pi]
                for kc in range(KC):
                    PE.op(nc.tensor.matmul, [sbuf_, oTb], [pbuf], pt[:, :], lhsT=slot[:, kc * 512 + mm * 128: kc * 512 + mm * 128 + 128],
                          rhs=oT[:, kc, :], start=(kc == 0), stop=(kc == KC - 1))
                DVE.op(nc.vector.scalar_tensor_tensor, [pbuf, self.mod_b, xb], [xb], out=x[:, m, :], in0=pt[:, :], scalar=self.M(l, 2, m),
                       in1=x[:, m, :], op0=ALU.mult, op1=ALU.add)
        self.stage_end(es)


_CACHE = {}


def get_prog(n_tiles=NT, nstages=len(STAGES)):
    key = (n_tiles, nstages)
    if key not in _CACHE:
        p = Prog(n_tiles, nstages)
        p.build()
        _CACHE[key] = p
    return _CACHE[key]


def kernel(**inputs):
    inp = {k: np.asarray(v) for k, v in inputs.items()}
    W, A, per_core = host_prepare(inp)
    p = get_prog()
    in_maps = [{"xT": pc["xT"], "wts": W, "adaw": A, "vecs": pc["vecs"]} for pc in per_core]
    res = run_bass_kernel_spmd(p.nc, in_maps, core_ids=list(range(8)))
    out = np.stack([np.ascontiguousarray(r["outT"].T) for r in res.results], axis=0)
    return out.astype(np.float32)
```

```python
import math
from contextlib import ExitStack

import numpy as np
import concourse.bass as bass
import concourse.mybir as mybir
from concourse.bass_utils import run_bass_kernel_spmd

F32 = mybir.dt.float32
BF16 = mybir.dt.bfloat16
AF = mybir.ActivationFunctionType
ALU = mybir.AluOpType

D = 1024
S = 4096
FF = 2816
NFC = 22
T = 512
NT = S // T
KC = 8
EPS = 1e-6
POOLW = (2, 4, 8, 16)
DIL = (1, 4, 16)
NH = 16
HD = 64
NSLOT = 5
SLOTW = 4096
BIG = 30000.0


def _alibi_slopes(n):
    def pow2(m):
        start = 2.0 ** (-(2.0 ** -(math.log2(m) - 3)))
        return [start ** (i + 1) for i in range(m)]
    if math.log2(n).is_integer():
        s = pow2(n)
    else:
        c = 2 ** math.floor(math.log2(n))
        s = pow2(c) + pow2(2 * c)[0::2][: n - c]
    s = np.asarray(s, dtype=np.float32)
    return -np.sort(-s)


SLOPES = _alibi_slopes(3 * NH).reshape(3, NH)


STAGES = ["mix0", "ffn0", "mix1", "ffn1", "kv", "mix2", "ffn2", "mix3", "ffn3"]


def stage_pieces(st):
    P = []
    if st.startswith("mix"):
        l = int(st[3])
        if l < 2:
            P += [(f"win{l}_0", 4096), (f"win{l}_1", 4096), (f"wgrp{l}", 2048), (f"wout{l}_0", 4096), (f"wout{l}_1", 4096)]
        else:
            P += [(f"wq{l}_{hp}", 3072) for hp in range(8)]
            P += [(f"wo{l}_0", 4096), (f"wo{l}_1", 4096)]
    elif st == "kv":
        P += [(f"wk_{i}", 4096) for i in range(6)]
        P += [(f"wv_{i}", 4096) for i in range(6)]
    else:
        l = int(st[3])
        P += [(f"wup{l}_{i}", 4096) for i in range(11)]
        for r in range(2):
            P += [(f"wdn{l}_{r}_0", 4096), (f"wdn{l}_{r}_1", 4096), (f"wdn{l}_{r}_2", 3072)]
    return P


def piece_list(nstages=len(STAGES)):
    P = []
    for st in STAGES[:nstages]:
        P += stage_pieces(st)
    return P


def ada_piece_list():
    P = []
    for l in range(4):
        P += [(f"ada{l}_{i}", 4096) for i in range(12)]
    P += [(f"kvada_{i}", 4096) for i in range(4)]
    return P


def _offsets(pl):
    off = {}
    o = 0
    for n, w in pl:
        off[n] = (o, w)
        o += w
    return off, o


def vec_layout():
    L = {}
    o = 0

    def add(name, n):
        nonlocal o
        L[name] = (o, n)
        o += n
    for l in range(4):
        add(f"ada_b{l}", 48)
        add(f"n1g{l}", 8)
        add(f"n2g{l}", 8)
        add(f"cw{l}", 66)
        add(f"cb{l}", 22)
    for l in range(2):
        add(f"psc{l}", 8)
    add("kvg", 8)
    add("kvb", 16)
    add("fg", 8)
    add("c", 8)
    add("invc", 64)
    add("dist", 256)
    add("dist2", 96)
    return L, o


def pmaj(v):
    return np.ascontiguousarray(v.reshape(-1, 128).T)


def kmajor(W, c0, ncols):
    K = W.shape[0]
    a = W[:, c0:c0 + ncols].reshape(K // 128, 128, ncols).transpose(1, 0, 2)
    return a.reshape(128, -1)


def host_prepare(inp):
    pl = piece_list()
    off, tot = _offsets(pl)
    W = np.empty((128, tot), np.float32)

    def put(name, arr):
        o, w = off[name]
        assert arr.shape == (128, w), (name, arr.shape, w)
        W[:, o:o + w] = arr
    for l in range(4):
        if l < 2:
            for m in range(2):
                put(f"win{l}_{m}", kmajor(inp["pool_w_in"][l], m * 512, 512))
                put(f"wout{l}_{m}", kmajor(inp["pool_w_out"][l], m * 512, 512))
            g = inp["pool_w_grp"][l]
            put(f"wgrp{l}", g.reshape(4, 2, 128, 256).transpose(2, 0, 1, 3).reshape(128, 2048))
        else:
            j = l - 2
            wq = inp["attn_w_q"][j]
            for hp in range(8):
                a = np.stack([kmajor(wq, g * 1024 + hp * 128, 128).reshape(128, 8, 128) for g in range(3)], axis=1)
                put(f"wq{l}_{hp}", a.reshape(128, 3072))
            for m in range(2):
                put(f"wo{l}_{m}", kmajor(inp["attn_w_o"][j], m * 512, 512))
        if l == 2:
            for i in range(6):
                put(f"wk_{i}", kmajor(inp["w_kv"], i * 512, 512))
                put(f"wv_{i}", kmajor(inp["w_kv"], 3072 + i * 512, 512))
        wu = inp["ffn_w_up"][l]
        for i in range(11):
            parts = []
            for jj in range(2):
                fc = 2 * i + jj
                for av in range(2):
                    parts.append(kmajor(wu, av * FF + fc * 128, 128))
            put(f"wup{l}_{i}", np.concatenate(parts, axis=1))
        wd = inp["ffn_w_down"][l]
        for r in range(2):
            a = wd[:, r * 512:(r + 1) * 512].reshape(NFC, 128, 512).transpose(1, 0, 2)
            put(f"wdn{l}_{r}_0", a[:, 0:8].reshape(128, 4096))
            put(f"wdn{l}_{r}_1", a[:, 8:16].reshape(128, 4096))
            put(f"wdn{l}_{r}_2", a[:, 16:22].reshape(128, 3072))
    apl = ada_piece_list()
    aoff, atot = _offsets(apl)
    A = np.empty((128, atot), np.float32)
    for l in range(4):
        for i in range(12):
            o, w = aoff[f"ada{l}_{i}"]
            A[:, o:o + w] = kmajor(inp["ada_w"][l], i * 512, 512)
    for i in range(4):
        o, w = aoff[f"kvada_{i}"]
        A[:, o:o + w] = kmajor(inp["kv_ada_w"], i * 512, 512)
    VL, nv = vec_layout()
    shared = np.zeros((128, nv), np.float32)

    def vput(name, arr):
        o, n = VL[name]
        assert arr.shape == (128, n), (name, arr.shape)
        shared[:, o:o + n] = arr
    for l in range(4):
        vput(f"ada_b{l}", pmaj(inp["ada_b"][l]))
        vput(f"n1g{l}", pmaj(inp["norm1_g"][l]))
        vput(f"n2g{l}", pmaj(inp["norm2_g"][l]))
        cw = inp["ffn_conv_w"][l]
        vput(f"cw{l}", np.concatenate([pmaj(cw[k]) for k in range(3)], axis=1))
        vput(f"cb{l}", pmaj(inp["ffn_conv_b"][l]))
    for l in range(2):
        vput(f"psc{l}", pmaj(inp["pool_scale"][l]))
    vput("kvg", pmaj(inp["kv_norm_g"]))
    vput("kvb", pmaj(inp["kv_ada_b"]))
    vput("fg", pmaj(inp["final_g"]))
    invc = np.zeros((128, 4, 16), np.float32)
    for g, w in enumerate(POOLW):
        invc[:, g, :] = 1.0 / np.minimum(np.arange(16) + 1, w)
    vput("invc", invc.reshape(128, 64))
    k = np.arange(128)[:, None]
    q = np.arange(128)[None, :]
    dprev = (q - k + 128).astype(np.float32)
    dprev[dprev > 128] = BIG
    dcur = (q - k).astype(np.float32)
    dcur[dcur < 0] = BIG
    vput("dist", np.concatenate([dprev, dcur], axis=1))
    d2 = []
    for na in (32, 64, 96):
        dd = (q[:, 0:32] - k + na).astype(np.float32)
        dd[dd > 128] = BIG
        dd[k[:, 0] >= na, :] = BIG
        d2.append(dd)
    vput("dist2", np.concatenate(d2, axis=1))
    per_core = []
    for b in range(8):
        v = shared.copy()
        o, n = VL["c"]
        v[:, o:o + n] = pmaj(inp["c"][b])
        per_core.append({"xT": np.ascontiguousarray(inp["x"][b].T), "vecs": v})
    return W, A, per_core


class Chan:
    __slots__ = ("sem", "val")


class Buf:
    __slots__ = ("name", "w", "r")

    def __init__(self, name, seed=None):
        self.name = name
        self.w = {}
        self.r = dict(seed) if seed else {}

    def tokens(self):
        d = dict(self.w)
        for ch, v in self.r.items():
            if d.get(ch, 0) < v:
                d[ch] = v
        return d


def _merge(d, ch, v):
    if d.get(ch, 0) < v:
        d[ch] = v


class Ctx:
    def __init__(self, nc, es):
        self.nc = nc
        self.es = es
        self.nsem = 0

    def new_chan(self):
        sem = self.es.enter_context(self.nc.semaphore(f"sm{self.nsem}"))
        self.nsem += 1
        c = Chan()
        c.sem = sem
        c.val = 0
        return c


class Eng:
    EPOCH = 16000

    def __init__(self, ctx, eng, name, is_pe=False, n_dma=0):
        self.ctx = ctx
        self.e = eng
        self.name = name
        self.is_pe = is_pe
        self.chan = ctx.new_chan()
        self.waited = {}
        self.dma_pool = [ctx.new_chan() for _ in range(n_dma)]
        self.dma_i = 0
        self.n = 0

    def wait_tok(self, ch, v):
        if ch is self.chan and self.is_pe:
            return
        if self.waited.get(ch, 0) >= v:
            return
        self.e.wait_ge(ch.sem, v)
        self.waited[ch] = v

    def sync(self, reads, writes):
        for b in reads:
            for ch, v in b.w.items():
                self.wait_tok(ch, v)
        for b in writes:
            for ch, v in b.w.items():
                self.wait_tok(ch, v)
            for ch, v in b.r.items():
                self.wait_tok(ch, v)

    def op(self, fn, reads, writes, *a, **k):
        self.sync(reads, writes)
        ins = fn(*a, **k)
        if self.chan.val >= self.EPOCH:
            self.chan = self.ctx.new_chan()
        ch = self.chan
        ch.val += 1
        ins.then_inc(ch.sem, 1)
        for b in reads:
            _merge(b.r, ch, ch.val)
        for b in writes:
            _merge(b.w, ch, ch.val)
        self.n += 1
        return ins

    def dma(self, out, in_, reads, writes, **k):
        ch = self.dma_pool[self.dma_i % len(self.dma_pool)]
        self.dma_i += 1
        if ch.val:
            self.wait_tok(ch, ch.val)
        self.sync(reads, writes)
        ins = self.e.dma_start(out=out, in_=in_, **k)
        ch.val += 16
        ins.then_inc(ch.sem, 16)
        for b in reads:
            _merge(b.r, ch, ch.val)
        for b in writes:
            _merge(b.w, ch, ch.val)
        self.n += 1
        return ins

    def wait_all(self, bufs):
        for b in bufs:
            for ch, v in b.tokens().items():
                self.wait_tok(ch, v)


class Prog:
    def __init__(self, n_tiles=NT, nstages=len(STAGES), dbg=None):
        self.n_tiles = n_tiles
        self.nstages = nstages
        self.full = nstages == len(STAGES)
        self.dbg = dbg
        nc = bass.Bass("TRN2", target_bir_lowering=False)
        self.nc = nc
        self.es = ExitStack()
        self.pl = piece_list()
        self.poff, self.ptot = _offsets(self.pl)
        self.apl = ada_piece_list()
        self.aoff, self.atot = _offsets(self.apl)
        self.VL, self.nv = vec_layout()
        self.seed = {}

    def sb(self, es, name, shape, dt):
        self._nalloc = getattr(self, "_nalloc", 0) + 1
        return es.enter_context(self.nc.sbuf_tensor(f"{name}_{self._nalloc}", list(shape), dt))

    def newbuf(self, name):
        b = Buf(name, self.seed)
        self.stage_bufs.append(b)
        return b

    def stage_begin(self):
        self.stage_bufs = []
        return ExitStack()

    def stage_end(self, es):
        seed = dict(self.seed)
        for b in self.stage_bufs:
            for ch, v in b.tokens().items():
                _merge(seed, ch, v)
        self.seed = seed
        es.close()

    def w_init(self):
        self.wslots = [self.sb(self.es, f"wslot{i}", [128, SLOTW], BF16) for i in range(NSLOT)]
        self.wbufs = [Buf(f"wslot{i}") for i in range(NSLOT)]
        self.wsched = []
        self.w_issued = 0
        self.w_used = 0

    def w_issue_upto(self, idx):
        while self.w_issued <= idx and self.w_issued < len(self.wsched):
            i = self.w_issued
            src, w, name = self.wsched[i]
            slot = i % NSLOT
            self.POOL.dma(self.wslots[slot][:, 0:w], src, [], [self.wbufs[slot]], max_dma_last_dim=2048)
            self.w_issued += 1

    def w_next(self, name):
        i = self.w_used
        src, w, nm = self.wsched[i]
        assert nm == name, (nm, name)
        self.w_issue_upto(i + NSLOT - 1)
        self.w_used += 1
        slot = i % NSLOT
        return self.wslots[slot], self.wbufs[slot]

    def build(self):
        nc = self.nc
        es = self.es
        ctx = Ctx(nc, es)
        self.ctx = ctx
        nt = self.n_tiles
        self.xT = nc.dram_tensor("xT", [D, S], F32, kind="ExternalInput").ap()
        self.wts = nc.dram_tensor("wts", [128, self.ptot], F32, kind="ExternalInput").ap()
        self.adaw = nc.dram_tensor("adaw", [128, self.atot], F32, kind="ExternalInput").ap()
        self.vecs_d = nc.dram_tensor("vecs", [128, self.nv], F32, kind="ExternalInput").ap()
        self.outT = nc.dram_tensor("outT", [D, S], F32, kind="ExternalOutput").ap()
        self.kTd = nc.dram_tensor("kTd", [3, 8, 128, S], BF16, kind="Internal").ap()
        self.vd = nc.dram_tensor("vd", [3, S, D], BF16, kind="Internal").ap()
        self.kv_bufs = [Buf(f"kv{s}") for s in range(NT)]

        self.PE = Eng(ctx, nc.tensor, "pe", is_pe=True)
        self.ACT = Eng(ctx, nc.scalar, "act")
        self.DVE = Eng(ctx, nc.vector, "dve")
        self.POOL = Eng(ctx, nc.gpsimd, "pool", n_dma=NSLOT + 1)
        self.SP = Eng(ctx, nc.sync, "sp", n_dma=40)
        PE, ACT, DVE, POOL, SP = self.PE, self.ACT, self.DVE, self.POOL, self.SP

        self.vecs = self.sb(es, "vecs_sb", [128, self.nv], F32)
        self.vecs_b = Buf("vecs")
        self.modT = self.sb(es, "modT", [128, 4 * 48 + 16], F32)
        self.der = self.sb(es, "der", [128, 4 * 16 + 8], F32)
        self.mod_b = Buf("mod")
        self.ones = self.sb(es, "ones", [128, 128], BF16)
        self.ones_b = Buf("ones")
        self.condb = self.sb(es, "condb", [128, 8], BF16)
        self.cond_b = Buf("cond")
        self.xTs = [self.sb(es, f"xTs{i}", [128, KC, T], F32) for i in range(2)]
        self.x_bufs = [[Buf(f"x{i}_{c}") for c in range(KC)] for i in range(2)]
        self.hT = self.sb(es, "hT", [128, KC, T], BF16)
        self.h_bs = [Buf(f"hT{c}") for c in range(KC)]
        self.sq = self.sb(es, "sq", [128, KC, T], BF16)
        self.sq_bs = [Buf(f"sq{c}") for c in range(KC)]
        self.ms_ready = None
        self.std = self.sb(es, "std", [128, T], F32)
        self.rstd = self.sb(es, "rstd", [128, T], F32)
        self.std_b = Buf("std")
        self.rstd_b = Buf("rstd")
        self.ntmp = [self.sb(es, f"ntmp{i}", [128, T], F32) for i in range(2)]
        self.ntmp_b = [Buf(f"ntmp{i}") for i in range(2)]
        self.uhalo = [self.sb(es, f"uhalo{l}", [128, KC, 16], F32) for l in range(2)]
        self.uhalo_b = [Buf(f"uhalo{l}") for l in range(2)]
        self.ahalo = [self.sb(es, f"ahalo{l}", [128, NFC, 2], F32) for l in range(4)]
        self.ahalo_b = [Buf(f"ahalo{l}") for l in range(4)]
        self.masks = self.sb(es, "masks", [128, 48, 256], BF16)
        self.masks_b = Buf("masks")
        self.masks2 = self.sb(es, "masks2", [128, 3, NH, 32], BF16)
        self.ps = [es.enter_context(nc.psum_tensor(f"ps{i}", [128, 512], F32)) for i in range(8)]
        self.ps_b = [Buf(f"ps{i}") for i in range(8)]
        self.w_init()

        def ada_sched(l):
            for n, w in self.apl:
                if n.startswith(f"ada{l}_") or (l == 4 and n.startswith("kvada")):
                    o, _ = self.aoff[n]
                    self.wsched.append((self.adaw[:, o:o + w], w, n))
        for s in range(nt):
            for st in STAGES[:self.nstages]:
                if s == 0 and st.startswith("mix"):
                    ada_sched(int(st[3]))
                if s == 0 and st == "kv":
                    ada_sched(4)
                for n, w in stage_pieces(st):
                    o, _ = self.poff[n]
                    self.wsched.append((self.wts[:, o:o + w], w, n))

        self.prologue()
        for s in range(nt):
            self.tile(s)
        for b in self.out_wait:
            SP.wait_all([b])
        return nc

    def vcol(self, name, c0=0, n=1):
        o, _ = self.VL[name]
        return self.vecs[:, o + c0:o + c0 + n]

    def prologue(self):
        nc = self.nc
        PE, ACT, DVE, POOL, SP = self.PE, self.ACT, self.DVE, self.POOL, self.SP
        SP.dma(self.vecs[:, :], self.vecs_d[:, :], [], [self.vecs_b])
        DVE.op(nc.vector.memset, [], [self.ones_b], self.ones[:, :], 1.0)
        for l in range(2):
            DVE.op(nc.vector.memset, [], [self.uhalo_b[l]], self.uhalo[l][:, :, :], 0.0)
        for l in range(4):
            DVE.op(nc.vector.memset, [], [self.ahalo_b[l]], self.ahalo[l][:, :, :], 0.0)
        ACT.op(nc.scalar.activation, [self.vecs_b], [self.cond_b], out=self.condb[:, :], in_=self.vcol("c", 0, 8), func=AF.Silu)
        if self.nstages > 5:
            for g in range(3):
                for h in range(NH):
                    ACT.op(nc.scalar.activation, [self.vecs_b], [self.masks_b], out=self.masks[:, g * NH + h, :],
                           in_=self.vcol("dist", 0, 256), func=AF.Exp, scale=-float(SLOPES[g, h]) * DIL[g])
            for i in range(3):
                for h in range(NH):
                    ACT.op(nc.scalar.activation, [self.vecs_b], [self.masks_b], out=self.masks2[:, i, h, :],
                           in_=self.vcol("dist2", i * 32, 32), func=AF.Exp, scale=-float(SLOPES[2, h]) * DIL[2])

    def compute_mod(self, l):
        nc = self.nc
        PE, ACT, DVE, POOL, SP = self.PE, self.ACT, self.DVE, self.POOL, self.SP
        pb = self.psrr()
        if True:
            ncol = 48 if l < 4 else 16
            npieces = 12 if l < 4 else 4
            pt, pbuf = self.ps[pb], self.ps_b[pb]
            first = True
            for i in range(npieces):
                slot, sbuf_ = self.w_next(f"ada{l}_{i}" if l < 4 else f"kvada_{i}")
                for mm in range(4):
                    col = i * 4 + mm
                    for kc in range(KC):
                        PE.op(nc.tensor.matmul, [sbuf_, self.cond_b], [pbuf], pt[:, col:col + 1],
                              lhsT=slot[:, kc * 512 + mm * 128: kc * 512 + mm * 128 + 128], rhs=self.condb[:, kc:kc + 1],
                              start=first, stop=(kc == KC - 1))
                        first = False
            bname = f"ada_b{l}" if l < 4 else "kvb"
            DVE.op(nc.vector.tensor_tensor, [pbuf, self.vecs_b], [self.mod_b], out=self.modT[:, l * 48:l * 48 + ncol],
                   in0=pt[:, 0:ncol], in1=self.vcol(bname, 0, ncol), op=ALU.add)
        if l < 4:
            for j, (gn, sc0) in enumerate(((f"n1g{l}", 8), (f"n2g{l}", 32))):
                DVE.op(nc.vector.scalar_tensor_tensor, [self.mod_b, self.vecs_b], [self.mod_b],
                       out=self.der[:, l * 16 + j * 8: l * 16 + j * 8 + 8], in0=self.modT[:, l * 48 + sc0: l * 48 + sc0 + 8],
                       scalar=1.0, in1=self.vcol(gn, 0, 8), op0=ALU.add, op1=ALU.mult)
        else:
            DVE.op(nc.vector.scalar_tensor_tensor, [self.mod_b, self.vecs_b], [self.mod_b],
                   out=self.der[:, 64:72], in0=self.modT[:, 192 + 8:192 + 16], scalar=1.0, in1=self.vcol("kvg", 0, 8),
                   op0=ALU.add, op1=ALU.mult)

    def A(self, l, which, c):
        return self.der[:, l * 16 + which * 8 + c: l * 16 + which * 8 + c + 1]

    def M(self, l, j, c):
        return self.modT[:, l * 48 + j * 8 + c: l * 48 + j * 8 + c + 1]

    def norm(self, x, xb, acol, bcol, out=None, out_b=None, final=False, keep=False, reuse=False):
        nc = self.nc
        PE, ACT, DVE = self.PE, self.ACT, self.DVE
        if not reuse:
            if self.ms_ready is None:
                for c in range(KC):
                    ACT.op(nc.scalar.activation, [xb[c]], [self.sq_bs[c]], out=self.sq[:, c, :], in_=x[:, c, :], func=AF.Square)
                self.emit_ms()
            pt, pbuf = self.ms_ready
            self.ms_ready = None
            ACT.op(nc.scalar.activation, [pbuf, self.eps_b], [self.std_b], out=self.std[:, :], in_=pt[:, :], func=AF.Sqrt,
                   scale=1.0 / D, bias=self.epsc[:, 0:1])
            if keep:
                DVE.op(nc.vector.reciprocal, [self.std_b], [self.rstd_b], out=self.rstd[:, :], in_=self.std[:, :])
                self.rstd_cur = (self.rstd, self.rstd_b)
            else:
                DVE.op(nc.vector.reciprocal, [self.std_b], [pbuf], out=pt[:, :], in_=self.std[:, :])
                self.rstd_cur = (pt, pbuf)
        rt, rb = self.rstd_cur
        for c in range(KC):
            tb = self.ntmp_b[c % 2]
            tt = self.ntmp[c % 2]
            DVE.op(nc.vector.tensor_tensor, [xb[c], rb], [tb], out=tt[:, :], in0=x[:, c, :], in1=rt[:, :], op=ALU.mult)
            if final:
                ACT.op(nc.scalar.activation, [tb, self.vecs_b], [out_b], out=out[:, c, :], in_=tt[:, :], func=AF.Copy,
                       scale=acol(c))
            else:
                ACT.op(nc.scalar.activation, [tb, self.mod_b], [self.h_bs[c]], out=self.hT[:, c, :], in_=tt[:, :], func=AF.Identity,
                       scale=acol(c), bias=bcol(c))

    def emit_ms(self):
        nc = self.nc
        pi = self.psrr()
        pt, pbuf = self.ps[pi], self.ps_b[pi]
        for c in range(KC):
            self.PE.op(nc.tensor.matmul, [self.sq_bs[c], self.ones_b], [pbuf], pt[:, :], lhsT=self.ones[:, :], rhs=self.sq[:, c, :],
                       start=(c == 0), stop=(c == KC - 1))
        self.ms_ready = (pt, pbuf)

    def residual(self, x, xb, m, pt, pbuf, gcol):
        nc = self.nc
        self.DVE.op(nc.vector.scalar_tensor_tensor, [pbuf, self.mod_b, xb[m]], [xb[m]], out=x[:, m, :], in0=pt[:, :], scalar=gcol,
                    in1=x[:, m, :], op0=ALU.mult, op1=ALU.add)
        self.ACT.op(nc.scalar.activation, [xb[m]], [self.sq_bs[m]], out=self.sq[:, m, :], in_=x[:, m, :], func=AF.Square)

    def psrr(self):
        i = self._psi
        self._psi = (self._psi + 1) % 8
        return i

    def tile(self, s):
        nc = self.nc
        PE, ACT, DVE, POOL, SP = self.PE, self.ACT, self.DVE, self.POOL, self.SP
        if s == 0:
            self._psi = 0
            self.out_wait = []
            self.epsc = self.sb(self.es, "epsc", [128, 1], F32)
            self.eps_b = Buf("eps")
            DVE.op(nc.vector.memset, [], [self.eps_b], self.epsc[:, :], EPS)
        x = self.xTs[s % 2]
        xb = self.x_bufs[s % 2]
        self.ms_ready = None
        for s2 in ([0, 1] if s == 0 else [s + 1]):
            if s2 < self.n_tiles:
                SP.dma(self.xTs[s2 % 2][:, :, :], self.xT.rearrange("(c p) t -> p c t", p=128)[:, :, s2 * T:(s2 + 1) * T], [], self.x_bufs[s2 % 2])
        for st in STAGES[:self.nstages]:
            if s == 0 and st.startswith("mix"):
                self.compute_mod(int(st[3]))
            if s == 0 and st == "kv":
                self.compute_mod(4)
            if st == "kv":
                self.kv_stage(s, x, xb)
            elif st.startswith("mix"):
                l = int(st[3])
                if l < 2:
                    self.pool_mixer(l, s, x, xb)
                else:
                    self.attn_mixer(l, s, x, xb)
            else:
                self.ffn(int(st[3]), s, x, xb)
        es = self.stage_begin()
        o = self.sb(es, "otile", [128, KC, T], F32)
        ob = self.newbuf("otile")
        if self.full:
            self.norm(x, xb, lambda c: self.vcol("fg", c, 1), None, out=o, out_b=ob, final=True)
        else:
            ACT.op(nc.scalar.activation, xb, [ob], out=o[:, :, :], in_=x[:, :, :], func=AF.Copy)
        SP.dma(self.outT.rearrange("(c p) t -> p c t", p=128)[:, :, s * T:(s + 1) * T], o[:, :, :], [ob], [])
        self.out_wait.append(ob)
        self.stage_end(es)

    def pool_mixer(self, l, s, x, xb):
        nc = self.nc
        PE, ACT, DVE = self.PE, self.ACT, self.DVE
        es = self.stage_begin()
        U = self.sb(es, "poolU", [128, KC, 16 + T], F32)
        Ub = self.newbuf("U")
        A_ = self.sb(es, "poolA", [128, KC, 16 + T], F32)
        Ab = self.newbuf("A")
        B_ = self.sb(es, "poolB", [128, KC, 16 + T], F32)
        Bb = self.newbuf("B")
        Pb = self.sb(es, "poolP", [128, KC, T], BF16)
        Pbb = self.newbuf("P")
        Zb = self.sb(es, "poolZ", [128, KC, T], BF16)
        Zbb = self.newbuf("Z")
        self.norm(x, xb, lambda c: self.A(l, 0, c), lambda c: self.M(l, 0, c))
        DVE.op(nc.vector.tensor_copy, [self.uhalo_b[l]], [Ub], out=U[:, :, 0:16], in_=self.uhalo[l][:, :, :])
        for half in range(2):
            slot, sbuf_ = self.w_next(f"win{l}_{half}")
            for mm in range(4):
                m = half * 4 + mm
                pi = self.psrr()
                pt, pbuf = self.ps[pi], self.ps_b[pi]
                for kc in range(KC):
                    PE.op(nc.tensor.matmul, [sbuf_, self.h_bs[kc]], [pbuf], pt[:, :], lhsT=slot[:, kc * 512 + mm * 128: kc * 512 + mm * 128 + 128],
                          rhs=self.hT[:, kc, :], start=(kc == 0), stop=(kc == KC - 1))
                ACT.op(nc.scalar.activation, [pbuf], [Ub], out=U[:, m, 16:16 + T], in_=pt[:, :], func=AF.Copy)
        DVE.op(nc.vector.tensor_copy, [Ub], [self.uhalo_b[l]], out=self.uhalo[l][:, :, :], in_=U[:, :, T:T + 16])
        W_ = 16 + T
        DVE.op(nc.vector.tensor_tensor, [Ub], [Ab], out=A_[:, :, 1:W_], in0=U[:, :, 1:W_], in1=U[:, :, 0:W_ - 1], op=ALU.add)
        DVE.op(nc.vector.tensor_tensor, [Ab], [Bb], out=B_[:, 2:8, 3:W_], in0=A_[:, 2:8, 3:W_], in1=A_[:, 2:8, 1:W_ - 2], op=ALU.add)
        DVE.op(nc.vector.tensor_tensor, [Bb], [Ab], out=A_[:, 4:8, 7:W_], in0=B_[:, 4:8, 7:W_], in1=B_[:, 4:8, 3:W_ - 4], op=ALU.add)
        DVE.op(nc.vector.tensor_tensor, [Ab], [Bb], out=B_[:, 6:8, 15:W_], in0=A_[:, 6:8, 15:W_], in1=A_[:, 6:8, 7:W_ - 8], op=ALU.add)
        srcs = [(A_, Ab), (B_, Bb), (A_, Ab), (B_, Bb)]
        for g in range(4):
            St, Sb_ = srcs[g]
            DVE.op(nc.vector.scalar_tensor_tensor, [Sb_, Ub], [Pbb], out=Pb[:, 2 * g:2 * g + 2, :], in0=St[:, 2 * g:2 * g + 2, 16:16 + T],
                   scalar=1.0 / POOLW[g], in1=U[:, 2 * g:2 * g + 2, 16:16 + T], op0=ALU.mult, op1=ALU.subtract)
            if s == 0:
                o, _ = self.VL["invc"]
                for cc in range(2):
                    c = 2 * g + cc
                    tb, tt = self.ntmp_b[cc], self.ntmp[cc]
                    DVE.op(nc.vector.tensor_tensor, [Sb_, self.vecs_b], [tb], out=tt[:, 0:16], in0=St[:, c, 16:32],
                           in1=self.vecs[:, o + g * 16:o + g * 16 + 16], op=ALU.mult)
                    DVE.op(nc.vector.tensor_tensor, [tb, Ub], [Pbb], out=Pb[:, c, 0:16], in0=tt[:, 0:16], in1=U[:, c, 16:32], op=ALU.subtract)
        slot, sbuf_ = self.w_next(f"wgrp{l}")
        for g in range(4):
            for mo in range(2):
                c = 2 * g + mo
                pi = self.psrr()
                pt, pbuf = self.ps[pi], self.ps_b[pi]
                for ki in range(2):
                    base = (g * 2 + ki) * 256 + mo * 128
                    PE.op(nc.tensor.matmul, [sbuf_, Pbb], [pbuf], pt[:, :], lhsT=slot[:, base:base + 128], rhs=Pb[:, 2 * g + ki, :],
                          start=(ki == 0), stop=(ki == 1))
                ACT.op(nc.scalar.activation, [pbuf, self.vecs_b], [Zbb], out=Zb[:, c, :], in_=pt[:, :], func=AF.Copy,
                       scale=self.vcol(f"psc{l}", c, 1))
        for half in range(2):
            slot, sbuf_ = self.w_next(f"wout{l}_{half}")
            for mm in range(4):
                m = half * 4 + mm
                pi = self.psrr()
                pt, pbuf = self.ps[pi], self.ps_b[pi]
                for kc in range(KC):
                    PE.op(nc.tensor.matmul, [sbuf_, Zbb], [pbuf], pt[:, :], lhsT=slot[:, kc * 512 + mm * 128: kc * 512 + mm * 128 + 128],
                          rhs=Zb[:, kc, :], start=(kc == 0), stop=(kc == KC - 1))
                self.residual(x, xb, m, pt, pbuf, self.M(l, 2, m))
        self.emit_ms()
        self.stage_end(es)

    def ffn(self, l, s, x, xb):
        nc = self.nc
        PE, ACT, DVE = self.PE, self.ACT, self.DVE
        es = self.stage_begin()
        gT = self.sb(es, "gT", [128, NFC, T], BF16)
        gb = self.newbuf("gT")
        abuf = [self.sb(es, f"abuf{i}", [128, 2 + T], F32) for i in range(2)]
        ab = [self.newbuf(f"abuf{i}") for i in range(2)]
        c1 = [self.sb(es, f"c1_{i}", [128, T], F32) for i in range(2)]
        c1b = [self.newbuf(f"c1_{i}") for i in range(2)]
        c2 = [self.sb(es, f"c2_{i}", [128, T], F32) for i in range(2)]
        c2b = [self.newbuf(f"c2_{i}") for i in range(2)]
        vsb = [self.sb(es, f"vsb{i}", [128, T], F32) for i in range(2)]
        vsbb = [self.newbuf(f"vsb{i}") for i in range(2)]
        self.norm(x, xb, lambda c: self.A(l, 1, c), lambda c: self.M(l, 3, c))
        cwo, _ = self.VL[f"cw{l}"]
        cbo, _ = self.VL[f"cb{l}"]
        for i in range(11):
            slot, sbuf_ = self.w_next(f"wup{l}_{i}")
            for jj in range(2):
                fc = 2 * i + jj
                par = fc % 2
                pa, pab = self.ps[par * 2], self.ps_b[par * 2]
                pv, pvb = self.ps[par * 2 + 1], self.ps_b[par * 2 + 1]
                for av, (pt, pbuf) in enumerate(((pa, pab), (pv, pvb))):
                    base = (jj * 2 + av) * 1024
                    for kc in range(KC):
                        PE.op(nc.tensor.matmul, [sbuf_, self.h_bs[kc]], [pbuf], pt[:, :], lhsT=slot[:, base + kc * 128: base + kc * 128 + 128],
                              rhs=self.hT[:, kc, :], start=(kc == 0), stop=(kc == KC - 1))
                A_, Ab = abuf[par], ab[par]
                ACT.op(nc.scalar.activation, [self.ahalo_b[l]], [Ab], out=A_[:, 0:2], in_=self.ahalo[l][:, fc, :], func=AF.Copy)
                ACT.op(nc.scalar.activation, [pab], [Ab], out=A_[:, 2:2 + T], in_=pa[:, :], func=AF.Copy)
                ACT.op(nc.scalar.activation, [Ab], [self.ahalo_b[l]], out=self.ahalo[l][:, fc, :], in_=A_[:, T:T + 2], func=AF.Copy)
                ACT.op(nc.scalar.activation, [pab, self.vecs_b], [c1b[par]], out=c1[par][:, :], in_=pa[:, :], func=AF.Identity,
                       scale=self.vecs[:, cwo + 2 * NFC + fc: cwo + 2 * NFC + fc + 1], bias=self.vecs[:, cbo + fc:cbo + fc + 1])
                ACT.op(nc.scalar.activation, [pvb], [vsbb[par]], out=vsb[par][:, :], in_=pv[:, :], func=AF.Copy)
                DVE.op(nc.vector.scalar_tensor_tensor, [Ab, self.vecs_b, c1b[par]], [c2b[par]], out=c2[par][:, :], in0=A_[:, 1:1 + T],
                       scalar=self.vecs[:, cwo + NFC + fc: cwo + NFC + fc + 1], in1=c1[par][:, :], op0=ALU.mult, op1=ALU.add)
                DVE.op(nc.vector.scalar_tensor_tensor, [Ab, self.vecs_b, c2b[par]], [c1b[par]], out=c1[par][:, :], in0=A_[:, 0:T],
                       scalar=self.vecs[:, cwo + fc: cwo + fc + 1], in1=c2[par][:, :], op0=ALU.mult, op1=ALU.add)
                ACT.op(nc.scalar.activation, [c1b[par]], [c2b[par]], out=c2[par][:, :], in_=c1[par][:, :], func=AF.Silu)
                DVE.op(nc.vector.tensor_tensor, [c2b[par], vsbb[par]], [gb], out=gT[:, fc, :], in0=c2[par][:, :], in1=vsb[par][:, :], op=ALU.mult)
        for r in range(2):
            b0 = 4 if r == 0 else 0
            for j in range(3):
                slot, sbuf_ = self.w_next(f"wdn{l}_{r}_{j}")
                nf = 8 if j < 2 else 6
                for fl in range(nf):
                    fc = 8 * j + fl
                    for mm in range(4):
                        PE.op(nc.tensor.matmul, [sbuf_, gb], [self.ps_b[b0 + mm]], self.ps[b0 + mm][:, :],
                              lhsT=slot[:, fl * 512 + mm * 128: fl * 512 + mm * 128 + 128], rhs=gT[:, fc, :],
                              start=(fc == 0), stop=(fc == NFC - 1))
            for mm in range(4):
                m = r * 4 + mm
                self.residual(x, xb, m, self.ps[b0 + mm], self.ps_b[b0 + mm], self.M(l, 5, m))
        self._psi = 4
        self.emit_ms()
        self.stage_end(es)

    def kv_stage(self, s, x, xb):
        nc = self.nc
        PE, ACT, DVE, SP = self.PE, self.ACT, self.DVE, self.SP
        es = self.stage_begin()
        kst = [self.sb(es, f"kst{i}", [128, T], BF16) for i in range(6)]
        kstb = [self.newbuf(f"kst{i}") for i in range(6)]
        vst = [self.sb(es, f"vst{i}", [128, 512], BF16) for i in range(4)]
        vstb = [self.newbuf(f"vst{i}") for i in range(4)]
        kvb = self.kv_bufs[s]
        self.norm(x, xb, lambda c: self.der[:, 64 + c:65 + c], lambda c: self.modT[:, 192 + c:193 + c], keep=True)
        h16 = self.sb(es, "h16", [128, KC, T], BF16)
        h16b = self.newbuf("h16")
        for kc in range(KC):
            DVE.op(nc.vector.tensor_copy, [self.h_bs[kc]], [h16b], out=h16[:, kc, :].rearrange("p (r j) -> p r j", r=16),
                   in_=self.hT[:, kc, :].rearrange("p (j r) -> p r j", r=16))
        n = 0
        for i in range(6):
            slot, sbuf_ = self.w_next(f"wk_{i}")
            g = i // 2
            d = DIL[g]
            for mm in range(4):
                hp = (i % 2) * 4 + mm
                pi = self.psrr()
                pt, pbuf = self.ps[pi], self.ps_b[pi]
                for kc in range(KC):
                    PE.op(nc.tensor.matmul, [sbuf_, self.h_bs[kc]], [pbuf], pt[:, :], lhsT=slot[:, kc * 512 + mm * 128: kc * 512 + mm * 128 + 128],
                          rhs=self.hT[:, kc, :], start=(kc == 0), stop=(kc == KC - 1))
                kt, ktb = kst[n % 6], kstb[n % 6]
                n += 1
                nj = T // d
                if d == 1:
                    ACT.op(nc.scalar.activation, [pbuf], [ktb], out=kt[:, :], in_=pt[:, :], func=AF.Copy)
                    SP.dma(self.kTd[g, hp, :, s * T:(s + 1) * T], kt[:, :], [ktb], [kvb])
                else:
                    ACT.op(nc.scalar.activation, [pbuf], [ktb], out=kt[:, :].rearrange("p (r j) -> p r j", r=d),
                           in_=pt[:, :].rearrange("p (j r) -> p r j", r=d), func=AF.Copy)
                    dst = self.kTd[g, hp, :, :].rearrange("p (r j) -> p r j", r=d)[:, :, s * nj:(s + 1) * nj]
                    SP.dma(dst, kt[:, :].rearrange("p (r j) -> p r j", r=d), [ktb], [kvb])
        n = 0
        for i in range(6):
            slot, sbuf_ = self.w_next(f"wv_{i}")
            g = i // 2
            d = DIL[g]
            half = i % 2
            for ch in range(4):
                if d == 1:
                    cols = lambda kc: self.hT[:, kc, ch * 128:(ch + 1) * 128]
                elif d == 4:
                    cols = lambda kc: self.hT[:, kc, :].rearrange("p (j r) -> p r j", r=4)[:, ch, :]
                else:
                    cols = lambda kc: h16[:, kc, ch * 128:(ch + 1) * 128]
                pi = self.psrr()
                pt, pbuf = self.ps[pi], self.ps_b[pi]
                for kc in range(KC):
                    PE.op(nc.tensor.matmul, [sbuf_, self.h_bs[kc], h16b], [pbuf], pt[:, :], lhsT=cols(kc), rhs=slot[:, kc * 512:(kc + 1) * 512],
                          start=(kc == 0), stop=(kc == KC - 1))
                vt, vtb = vst[n % 4], vstb[n % 4]
                n += 1
                DVE.op(nc.vector.tensor_copy, [pbuf], [vtb], out=vt[:, 0:512], in_=pt[:, :])
                if d == 1:
                    r0 = s * T + ch * 128
                    SP.dma(self.vd[g, r0:r0 + 128, half * 512:(half + 1) * 512], vt[:, 0:512], [vtb], [kvb])
                elif d == 4:
                    r0 = ch * 1024 + s * 128
                    SP.dma(self.vd[g, r0:r0 + 128, half * 512:(half + 1) * 512], vt[:, 0:512], [vtb], [kvb])
                else:
                    for rl in range(4):
                        r0 = (ch * 4 + rl) * 256 + s * 32
                        SP.dma(self.vd[g, r0:r0 + 32, half * 512:(half + 1) * 512], vt[rl * 32:(rl + 1) * 32, 0:512], [vtb], [kvb])
        self.stage_end(es)

    def attn_mixer(self, l, s, x, xb):
        nc = self.nc
        PE, ACT, DVE, SP = self.PE, self.ACT, self.DVE, self.SP
        es = self.stage_begin()
        oT = self.sb(es, "oT", [128, KC, T], BF16)
        oTb = self.newbuf("oT")
        qT = [self.sb(es, f"qT{i}", [128, 3, T], BF16) for i in range(2)]
        qTb = [self.newbuf(f"qT{i}") for i in range(2)]
        KW = 640 + 1024 + 2560
        kt = [self.sb(es, f"ktile{i}", [128, KW], BF16) for i in range(2)]
        ktb = [self.newbuf(f"ktile{i}") for i in range(2)]
        NVC = 5 + 8 + 32
        vt = [self.sb(es, f"vtile{i}", [128, NVC, 128], BF16) for i in range(2)]
        vtb = [self.newbuf(f"vtile{i}") for i in range(2)]
        NB = 3
        ET = [self.sb(es, f"E{i}", [128, T], BF16) for i in range(NB)]
        ETb = [self.newbuf(f"E{i}") for i in range(NB)]
        PT = [self.sb(es, f"P{i}", [128, T], BF16) for i in range(NB)]
        PTb = [self.newbuf(f"P{i}") for i in range(NB)]
        rD = self.sb(es, "rD", [128, T], F32)
        rDb = self.newbuf("rD")
        self.norm(x, xb, lambda c: self.A(l, 0, c), lambda c: self.M(l, 0, c), reuse=(l == 2))
        kv_reads = [self.kv_bufs[t] for t in range(max(0, s - 4), s + 1)]
        na = min(128, 32 * s)

        def prep(hp):
            par = hp % 2
            K_, Kb = kt[par], ktb[par]
            V_, Vb = vt[par], vtb[par]
            lo0 = 128 if s == 0 else 0
            SP.dma(K_[:, lo0:640], self.kTd[0, hp, :, s * T - 128 + lo0: s * T + 512], kv_reads, [Kb])
            src_ = self.vd[0, s * T - 128 + lo0: s * T + 512, hp * 128:(hp + 1) * 128].rearrange("(c p) f -> p c f", p=128)
            SP.dma(V_[:, lo0 // 128:5, :], src_, kv_reads, [Vb])
            srck = self.kTd[1, hp, :, :].rearrange("p (r j) -> p r j", r=4)[:, :, 128 * (s - 1) + lo0: 128 * (s + 1)]
            SP.dma(K_[:, 640:640 + 1024].rearrange("p (r j) -> p r j", r=4)[:, :, lo0:256], srck, kv_reads, [Kb])
            for r in range(4):
                r0 = r * 1024 + 128 * (s - 1) + lo0
                nch = 2 - lo0 // 128
                src_ = self.vd[1, r0:r0 + 128 * nch, hp * 128:(hp + 1) * 128].rearrange("(c p) f -> p c f", p=128)
                SP.dma(V_[:, 5 + r * 2 + lo0 // 128: 5 + r * 2 + 2, :], src_, kv_reads, [Vb])
            k2 = K_[:, 640 + 1024:].rearrange("p (r j) -> p r j", r=16)
            srck = self.kTd[2, hp, :, :].rearrange("p (r j) -> p r j", r=16)[:, :, 32 * s - na: 32 * s + 32]
            SP.dma(k2[:, :, 128 - na:160], srck, kv_reads, [Kb])
            v2 = self.vd[2, :, hp * 128:(hp + 1) * 128].rearrange("(r j) f -> j r f", r=16)
            if na > 0:
                SP.dma(V_[0:na, 13:29, :], v2[32 * s - na:32 * s, :, :], kv_reads, [Vb])
            SP.dma(V_[0:32, 29:45, :], v2[32 * s:32 * s + 32, :, :], kv_reads, [Vb])
            slot, sbuf_ = self.w_next(f"wq{l}_{hp}")
            Q_, Qb = qT[par], qTb[par]
            for g in range(3):
                pt, pbuf = self.ps[3], self.ps_b[3]
                for kc in range(KC):
                    base = (g * 8 + kc) * 128
                    PE.op(nc.tensor.matmul, [sbuf_, self.h_bs[kc]], [pbuf], pt[:, :], lhsT=slot[:, base:base + 128], rhs=self.hT[:, kc, :],
                          start=(kc == 0), stop=(kc == KC - 1))
                d = DIL[g]
                if d == 1:
                    ACT.op(nc.scalar.activation, [pbuf], [Qb], out=Q_[:, g, :], in_=pt[:, :], func=AF.Copy)
                else:
                    ACT.op(nc.scalar.activation, [pbuf], [Qb], out=Q_[:, g, :].rearrange("p (r j) -> p r j", r=d),
                           in_=pt[:, :].rearrange("p (j r) -> p r j", r=d), func=AF.Copy)

        units_all = []
        for hp in range(8):
            ulist = []
            for g in range(3):
                batches = []
                if g < 2:
                    for kind in (0, 1):
                        us = []
                        for u in range(4):
                            if g == 0:
                                if kind == 0 and s == 0 and u == 0:
                                    continue
                                us.append((u, (u + kind) * 128, u + kind))
                            else:
                                if kind == 0 and s == 0:
                                    continue
                                us.append((u, 640 + u * 256 + kind * 128, 5 + u * 2 + kind))
                        if us:
                            batches.append((kind, 128, 128, us))
                else:
                    if na > 0:
                        batches.append((0, 32, na, [(r, 640 + 1024 + r * 160 + 128 - na, 13 + r) for r in range(16)]))
                    batches.append((1, 32, 32, [(r, 640 + 1024 + r * 160 + 128, 29 + r) for r in range(16)]))
                for (kind, nq, nk, us) in batches:
                    for hh in range(2):
                        ulist.append(dict(hp=hp, g=g, kind=kind, nq=nq, nk=nk, us=us, hh=hh))
            ulist[0]["first_of_hp"] = True
            ulist[-1]["last_of_hp"] = True
            seen = set()
            for ud in ulist:
                if ud["hh"] not in seen:
                    ud["first_pv"] = True
                    seen.add(ud["hh"])
            units_all += ulist

        def s_phase(i, ud):
            hp, g, kind, nq, nk, us, hh = ud["hp"], ud["g"], ud["kind"], ud["nq"], ud["nk"], ud["us"], ud["hh"]
            par = hp % 2
            K_, Kb = kt[par], ktb[par]
            Q_, Qb = qT[par], qTb[par]
            h = hp * 2 + hh
            r0, r1 = hh * 64, hh * 64 + 64
            si = i % NB
            Sp, Spb = self.ps[si], self.ps_b[si]
            for (u, kcol, vch) in us:
                PE.op(nc.tensor.matmul, [Kb, Qb], [Spb], Sp[0:nk, u * nq:(u + 1) * nq], lhsT=K_[r0:r1, kcol:kcol + nk],
                      rhs=Q_[r0:r1, g, u * nq:(u + 1) * nq], start=True, stop=True)
            u0 = us[0][0]
            u1 = us[-1][0] + 1
            nu = u1 - u0
            E_, Eb = ET[si], ETb[si]
            P_, Pb_ = PT[si], PTb[si]
            ACT.op(nc.scalar.activation, [Spb], [Eb], out=E_[0:nk, u0 * nq:u1 * nq], in_=Sp[0:nk, u0 * nq:u1 * nq],
                   func=AF.Exp, scale=HD ** -0.5)
            if g == 2 and kind == 0 and nk < 128:
                mk = self.masks2[0:nk, nk // 32 - 1, h, 0:32]
            else:
                mk = self.masks[0:nk, g * NH + h, kind * 128: kind * 128 + nq]
            DVE.op(nc.vector.tensor_tensor, [Eb, self.masks_b], [Pb_],
                   out=P_[0:nk, u0 * nq:u1 * nq].rearrange("p (u q) -> p u q", u=nu),
                   in0=E_[0:nk, u0 * nq:u1 * nq].rearrange("p (u q) -> p u q", u=nu),
                   in1=mk.unsqueeze(1).to_broadcast([nk, nu, nq]), op=ALU.mult)

        def pv_phase(i, ud):
            hp, g, kind, nq, nk, us, hh = ud["hp"], ud["g"], ud["kind"], ud["nq"], ud["nk"], ud["us"], ud["hh"]
            par = hp % 2
            V_, Vb = vt[par], vtb[par]
            r0, r1 = hh * 64, hh * 64 + 64
            si = i % NB
            P_, Pb_ = PT[si], PTb[si]
            Np, Npb = self.ps[4 + par], self.ps_b[4 + par]
            Dp, Dpb = self.ps[6 + par], self.ps_b[6 + par]
            d = DIL[g]
            first = ud.get("first_pv", False)
            for (u, kcol, vch) in us:
                if d == 1:
                    No = Np[r0:r1, u * 128:(u + 1) * 128]
                    Do = Dp[r0:r1, u * 128:(u + 1) * 128]
                else:
                    No = Np[r0:r1, :].rearrange("p (j r) -> p r j", r=d)[:, u, :]
                    Do = Dp[r0:r1, :].rearrange("p (j r) -> p r j", r=d)[:, u, :]
                PE.op(nc.tensor.matmul, [Vb, Pb_], [Npb], No, lhsT=V_[0:nk, vch, r0:r1], rhs=P_[0:nk, u * nq:(u + 1) * nq],
                      start=first, stop=False, skip_group_check=True)
                PE.op(nc.tensor.matmul, [self.ones_b, Pb_], [Dpb], Do, lhsT=self.ones[0:nk, 0:64], rhs=P_[0:nk, u * nq:(u + 1) * nq],
                      start=first, stop=False, skip_group_check=True)
                first = False
            if ud.get("last_of_hp"):
                DVE.op(nc.vector.reciprocal, [Dpb], [rDb], out=rD[:, :], in_=Dp[:, :])
                DVE.op(nc.vector.tensor_tensor, [Npb, rDb], [oTb], out=oT[:, hp, :], in0=Np[:, :], in1=rD[:, :], op=ALU.mult)

        LOOK = 2
        n = len(units_all)
        prep(0)
        prep(1)
        for i in range(n + LOOK):
            if i < n:
                s_phase(i, units_all[i])
            if i - LOOK >= 0:
                ud = units_all[i - LOOK]
                pv_phase(i - LOOK, ud)
                if ud.get("last_of_hp") and ud["hp"] + 2 < 8:
                    prep(ud["hp"] + 2)
        self._psi = 0
        for half in range(2):
            slot, sbuf_ = self.w_next(f"wo{l}_{half}")
            for mm in range(4):
                m = half * 4 + mm
                pi = self.psrr()
                pt, pbuf = self.ps[pi], self.ps_b[pi]
                for kc in range(KC):
                    PE.op(nc.tensor.matmul, [sbuf_, oTb], [pbuf], pt[:, :], lhsT=slot[:, kc * 512 + mm * 128: kc * 512 + mm * 128 + 128],
                          rhs=oT[:, kc, :], start=(kc == 0), stop=(kc == KC - 1))
                self.residual(x, xb, m, pt, pbuf, self.M(l, 2, m))
        self.emit_ms()
        self.stage_end(es)


_CACHE = {}


def get_prog(n_tiles=NT, nstages=len(STAGES)):
    key = (n_tiles, nstages)
    if key not in _CACHE:
        p = Prog(n_tiles, nstages)
        p.build()
        _CACHE[key] = p
    return _CACHE[key]


def kernel(**inputs):
    inp = {k: np.asarray(v) for k, v in inputs.items()}
    W, A, per_core = host_prepare(inp)
    p = get_prog()
    in_maps = [{"xT": pc["xT"], "wts": W, "adaw": A, "vecs": pc["vecs"]} for pc in per_core]
    res = run_bass_kernel_spmd(p.nc, in_maps, core_ids=list(range(8)))
    out = np.stack([np.ascontiguousarray(r["outT"].T) for r in res.results], axis=0)
    return out.astype(np.float32)
```

```python
import math
from contextlib import ExitStack

import numpy as np
import concourse.bass as bass
import concourse.mybir as mybir
from concourse.bass_utils import run_bass_kernel_spmd

F32 = mybir.dt.float32
BF16 = mybir.dt.bfloat16
AF = mybir.ActivationFunctionType
ALU = mybir.AluOpType

D = 1024
S = 4096
FF = 2816
NFC = 22
T = 512
NT = S // T
KC = 8
EPS = 1e-6
POOLW = (2, 4, 8, 16)
DIL = (1, 4, 16)
NH = 16
HD = 64
NSLOT = 5
SLOTW = 4096
BIG = 30000.0


def _alibi_slopes(n):
    def pow2(m):
        start = 2.0 ** (-(2.0 ** -(math.log2(m) - 3)))
        return [start ** (i + 1) for i in range(m)]
    if math.log2(n).is_integer():
        s = pow2(n)
    else:
        c = 2 ** math.floor(math.log2(n))
        s = pow2(c) + pow2(2 * c)[0::2][: n - c]
    s = np.asarray(s, dtype=np.float32)
    return -np.sort(-s)


SLOPES = _alibi_slopes(3 * NH).reshape(3, NH)


STAGES = ["mix0", "ffn0", "mix1", "ffn1", "kv", "mix2", "ffn2", "mix3", "ffn3"]


def stage_pieces(st):
    P = []
    if st.startswith("mix"):
        l = int(st[3])
        if l < 2:
            P += [(f"win{l}_0", 4096), (f"win{l}_1", 4096), (f"wgrp{l}", 2048), (f"wout{l}_0", 4096), (f"wout{l}_1", 4096)]
        else:
            P += [(f"wq{l}_{hp}", 3072) for hp in range(8)]
            P += [(f"wo{l}_0", 4096), (f"wo{l}_1", 4096)]
    elif st == "kv":
        P += [(f"wk_{i}", 4096) for i in range(6)]
        P += [(f"wv_{i}", 4096) for i in range(6)]
    else:
        l = int(st[3])
        P += [(f"wup{l}_{i}", 4096) for i in range(11)]
        for r in range(2):
            P += [(f"wdn{l}_{r}_0", 4096), (f"wdn{l}_{r}_1", 4096), (f"wdn{l}_{r}_2", 3072)]
    return P


def piece_list(nstages=len(STAGES)):
    P = []
    for st in STAGES[:nstages]:
        P += stage_pieces(st)
    return P


def ada_piece_list():
    P = []
    for l in range(4):
        P += [(f"ada{l}_{i}", 4096) for i in range(12)]
    P += [(f"kvada_{i}", 4096) for i in range(4)]
    return P


def _offsets(pl):
    off = {}
    o = 0
    for n, w in pl:
        off[n] = (o, w)
        o += w
    return off, o


def vec_layout():
    L = {}
    o = 0

    def add(name, n):
        nonlocal o
        L[name] = (o, n)
        o += n
    for l in range(4):
        add(f"ada_b{l}", 48)
        add(f"n1g{l}", 8)
        add(f"n2g{l}", 8)
        add(f"cw{l}", 66)
        add(f"cb{l}", 22)
    for l in range(2):
        add(f"psc{l}", 8)
    add("kvg", 8)
    add("kvb", 16)
    add("fg", 8)
    add("c", 8)
    add("invc", 64)
    add("dist", 256)
    add("dist2", 96)
    return L, o


def pmaj(v):
    return np.ascontiguousarray(v.reshape(-1, 128).T)


def kmajor(W, c0, ncols):
    K = W.shape[0]
    a = W[:, c0:c0 + ncols].reshape(K // 128, 128, ncols).transpose(1, 0, 2)
    return a.reshape(128, -1)


def host_prepare(inp):
    pl = piece_list()
    off, tot = _offsets(pl)
    W = np.empty((128, tot), np.float32)

    def put(name, arr):
        o, w = off[name]
        assert arr.shape == (128, w), (name, arr.shape, w)
        W[:, o:o + w] = arr
    for l in range(4):
        if l < 2:
            for m in range(2):
                put(f"win{l}_{m}", kmajor(inp["pool_w_in"][l], m * 512, 512))
                put(f"wout{l}_{m}", kmajor(inp["pool_w_out"][l], m * 512, 512))
            g = inp["pool_w_grp"][l]
            put(f"wgrp{l}", g.reshape(4, 2, 128, 256).transpose(2, 0, 1, 3).reshape(128, 2048))
        else:
            j = l - 2
            wq = inp["attn_w_q"][j]
            for hp in range(8):
                a = np.stack([kmajor(wq, g * 1024 + hp * 128, 128).reshape(128, 8, 128) for g in range(3)], axis=1)
                put(f"wq{l}_{hp}", a.reshape(128, 3072))
            for m in range(2):
                put(f"wo{l}_{m}", kmajor(inp["attn_w_o"][j], m * 512, 512))
        if l == 2:
            for i in range(6):
                put(f"wk_{i}", kmajor(inp["w_kv"], i * 512, 512))
                put(f"wv_{i}", kmajor(inp["w_kv"], 3072 + i * 512, 512))
        wu = inp["ffn_w_up"][l]
        for i in range(11):
            parts = []
            for jj in range(2):
                fc = 2 * i + jj
                for av in range(2):
                    parts.append(kmajor(wu, av * FF + fc * 128, 128))
            put(f"wup{l}_{i}", np.concatenate(parts, axis=1))
        wd = inp["ffn_w_down"][l]
        for r in range(2):
            a = wd[:, r * 512:(r + 1) * 512].reshape(NFC, 128, 512).transpose(1, 0, 2)
            put(f"wdn{l}_{r}_0", a[:, 0:8].reshape(128, 4096))
            put(f"wdn{l}_{r}_1", a[:, 8:16].reshape(128, 4096))
            put(f"wdn{l}_{r}_2", a[:, 16:22].reshape(128, 3072))
    apl = ada_piece_list()
    aoff, atot = _offsets(apl)
    A = np.empty((128, atot), np.float32)
    for l in range(4):
        for i in range(12):
            o, w = aoff[f"ada{l}_{i}"]
            A[:, o:o + w] = kmajor(inp["ada_w"][l], i * 512, 512)
    for i in range(4):
        o, w = aoff[f"kvada_{i}"]
        A[:, o:o + w] = kmajor(inp["kv_ada_w"], i * 512, 512)
    VL, nv = vec_layout()
    shared = np.zeros((128, nv), np.float32)

    def vput(name, arr):
        o, n = VL[name]
        assert arr.shape == (128, n), (name, arr.shape)
        shared[:, o:o + n] = arr
    for l in range(4):
        vput(f"ada_b{l}", pmaj(inp["ada_b"][l]))
        vput(f"n1g{l}", pmaj(inp["norm1_g"][l]))
        vput(f"n2g{l}", pmaj(inp["norm2_g"][l]))
        cw = inp["ffn_conv_w"][l]
        vput(f"cw{l}", np.concatenate([pmaj(cw[k]) for k in range(3)], axis=1))
        vput(f"cb{l}", pmaj(inp["ffn_conv_b"][l]))
    for l in range(2):
        vput(f"psc{l}", pmaj(inp["pool_scale"][l]))
    vput("kvg", pmaj(inp["kv_norm_g"]))
    vput("kvb", pmaj(inp["kv_ada_b"]))
    vput("fg", pmaj(inp["final_g"]))
    invc = np.zeros((128, 4, 16), np.float32)
    for g, w in enumerate(POOLW):
        invc[:, g, :] = 1.0 / np.minimum(np.arange(16) + 1, w)
    vput("invc", invc.reshape(128, 64))
    k = np.arange(128)[:, None]
    q = np.arange(128)[None, :]
    dprev = (q - k + 128).astype(np.float32)
    dprev[dprev > 128] = BIG
    dcur = (q - k).astype(np.float32)
    dcur[dcur < 0] = BIG
    vput("dist", np.concatenate([dprev, dcur], axis=1))
    d2 = []
    for na in (32, 64, 96):
        dd = (q[:, 0:32] - k + na).astype(np.float32)
        dd[dd > 128] = BIG
        dd[k[:, 0] >= na, :] = BIG
        d2.append(dd)
    vput("dist2", np.concatenate(d2, axis=1))
    per_core = []
    for b in range(8):
        v = shared.copy()
        o, n = VL["c"]
        v[:, o:o + n] = pmaj(inp["c"][b])
        per_core.append({"xT": np.ascontiguousarray(inp["x"][b].T), "vecs": v})
    return W, A, per_core


class Chan:
    __slots__ = ("sem", "val")


class Buf:
    __slots__ = ("name", "w", "r")

    def __init__(self, name, seed=None):
        self.name = name
        self.w = {}
        self.r = dict(seed) if seed else {}

    def tokens(self):
        d = dict(self.w)
        for ch, v in self.r.items():
            if d.get(ch, 0) < v:
                d[ch] = v
        return d


def _merge(d, ch, v):
    if d.get(ch, 0) < v:
        d[ch] = v


class Ctx:
    def __init__(self, nc, es):
        self.nc = nc
        self.es = es
        self.nsem = 0

    def new_chan(self):
        sem = self.es.enter_context(self.nc.semaphore(f"sm{self.nsem}"))
        self.nsem += 1
        c = Chan()
        c.sem = sem
        c.val = 0
        return c


class Eng:
    EPOCH = 16000

    def __init__(self, ctx, eng, name, is_pe=False, n_dma=0):
        self.ctx = ctx
        self.e = eng
        self.name = name
        self.is_pe = is_pe
        self.chan = ctx.new_chan()
        self.waited = {}
        self.dma_pool = [ctx.new_chan() for _ in range(n_dma)]
        self.dma_i = 0
        self.n = 0

    def wait_tok(self, ch, v):
        if ch is self.chan and self.is_pe:
            return
        if self.waited.get(ch, 0) >= v:
            return
        self.e.wait_ge(ch.sem, v)
        self.waited[ch] = v

    def sync(self, reads, writes):
        for b in reads:
            for ch, v in b.w.items():
                self.wait_tok(ch, v)
        for b in writes:
            for ch, v in b.w.items():
                self.wait_tok(ch, v)
            for ch, v in b.r.items():
                self.wait_tok(ch, v)

    def op(self, fn, reads, writes, *a, **k):
        self.sync(reads, writes)
        ins = fn(*a, **k)
        if self.chan.val >= self.EPOCH:
            self.chan = self.ctx.new_chan()
        ch = self.chan
        ch.val += 1
        ins.then_inc(ch.sem, 1)
        for b in reads:
            _merge(b.r, ch, ch.val)
        for b in writes:
            _merge(b.w, ch, ch.val)
        self.n += 1
        return ins

    def dma(self, out, in_, reads, writes, **k):
        ch = self.dma_pool[self.dma_i % len(self.dma_pool)]
        self.dma_i += 1
        if ch.val:
            self.wait_tok(ch, ch.val)
        self.sync(reads, writes)
        ins = self.e.dma_start(out=out, in_=in_, **k)
        ch.val += 16
        ins.then_inc(ch.sem, 16)
        for b in reads:
            _merge(b.r, ch, ch.val)
        for b in writes:
            _merge(b.w, ch, ch.val)
        self.n += 1
        return ins

    def wait_all(self, bufs):
        for b in bufs:
            for ch, v in b.tokens().items():
                self.wait_tok(ch, v)


class Prog:
    def __init__(self, n_tiles=NT, nstages=len(STAGES), dbg=None):
        self.n_tiles = n_tiles
        self.nstages = nstages
        self.full = nstages == len(STAGES)
        self.dbg = dbg
        nc = bass.Bass("TRN2", target_bir_lowering=False)
        self.nc = nc
        self.es = ExitStack()
        self.pl = piece_list()
        self.poff, self.ptot = _offsets(self.pl)
        self.apl = ada_piece_list()
        self.aoff, self.atot = _offsets(self.apl)
        self.VL, self.nv = vec_layout()
        self.seed = {}

    def sb(self, es, name, shape, dt):
        self._nalloc = getattr(self, "_nalloc", 0) + 1
        return es.enter_context(self.nc.sbuf_tensor(f"{name}_{self._nalloc}", list(shape), dt))

    def newbuf(self, name):
        b = Buf(name, self.seed)
        self.stage_bufs.append(b)
        return b

    def stage_begin(self):
        self.stage_bufs = []
        return ExitStack()

    def stage_end(self, es):
        seed = dict(self.seed)
        for b in self.stage_bufs:
            for ch, v in b.tokens().items():
                _merge(seed, ch, v)
        self.seed = seed
        es.close()

    def w_init(self):
        self.wslots = [self.sb(self.es, f"wslot{i}", [128, SLOTW], BF16) for i in range(NSLOT)]
        self.wbufs = [Buf(f"wslot{i}") for i in range(NSLOT)]
        self.wsched = []
        self.w_issued = 0
        self.w_used = 0

    def w_issue_upto(self, idx):
        while self.w_issued <= idx and self.w_issued < len(self.wsched):
            i = self.w_issued
            src, w, name = self.wsched[i]
            slot = i % NSLOT
            self.POOL.dma(self.wslots[slot][:, 0:w], src, [], [self.wbufs[slot]], max_dma_last_dim=2048)
            self.w_issued += 1

    def w_next(self, name):
        i = self.w_used
        src, w, nm = self.wsched[i]
        assert nm == name, (nm, name)
        self.w_issue_upto(i + NSLOT - 1)
        self.w_used += 1
        slot = i % NSLOT
        return self.wslots[slot], self.wbufs[slot]

    def build(self):
        nc = self.nc
        es = self.es
        ctx = Ctx(nc, es)
        self.ctx = ctx
        nt = self.n_tiles
        self.xT = nc.dram_tensor("xT", [D, S], F32, kind="ExternalInput").ap()
        self.wts = nc.dram_tensor("wts", [128, self.ptot], F32, kind="ExternalInput").ap()
        self.adaw = nc.dram_tensor("adaw", [128, self.atot], F32, kind="ExternalInput").ap()
        self.vecs_d = nc.dram_tensor("vecs", [128, self.nv], F32, kind="ExternalInput").ap()
        self.outT = nc.dram_tensor("outT", [D, S], F32, kind="ExternalOutput").ap()
        self.kTd = nc.dram_tensor("kTd", [3, 8, 128, S], BF16, kind="Internal").ap()
        self.vd = nc.dram_tensor("vd", [3, S, D], BF16, kind="Internal").ap()
        self.kv_bufs = [Buf(f"kv{s}") for s in range(NT)]

        self.PE = Eng(ctx, nc.tensor, "pe", is_pe=True)
        self.ACT = Eng(ctx, nc.scalar, "act")
        self.DVE = Eng(ctx, nc.vector, "dve")
        self.POOL = Eng(ctx, nc.gpsimd, "pool", n_dma=NSLOT + 1)
        self.SP = Eng(ctx, nc.sync, "sp", n_dma=40)
        PE, ACT, DVE, POOL, SP = self.PE, self.ACT, self.DVE, self.POOL, self.SP

        self.vecs = self.sb(es, "vecs_sb", [128, self.nv], F32)
        self.vecs_b = Buf("vecs")
        self.modT = self.sb(es, "modT", [128, 4 * 48 + 16], F32)
        self.der = self.sb(es, "der", [128, 4 * 16 + 8], F32)
        self.mod_b = Buf("mod")
        self.ones = self.sb(es, "ones", [128, 128], BF16)
        self.ones_b = Buf("ones")
        self.condb = self.sb(es, "condb", [128, 8], BF16)
        self.cond_b = Buf("cond")
        self.xTs = [self.sb(es, f"xTs{i}", [128, KC, T], F32) for i in range(2)]
        self.x_bufs = [[Buf(f"x{i}_{c}") for c in range(KC)] for i in range(2)]
        self.hT = self.sb(es, "hT", [128, KC, T], BF16)
        self.h_bs = [Buf(f"hT{c}") for c in range(KC)]
        self.sq = self.sb(es, "sq", [128, KC, T], BF16)
        self.sq_bs = [Buf(f"sq{c}") for c in range(KC)]
        self.ms_ready = None
        self.std = self.sb(es, "std", [128, T], F32)
        self.rstd = self.sb(es, "rstd", [128, T], F32)
        self.rstd2 = self.sb(es, "rstd2", [128, T], F32)
        self.rstd2_b = Buf("rstd2")
        self.std_b = Buf("std")
        self.rstd_b = Buf("rstd")
        self.ntmp = [self.sb(es, f"ntmp{i}", [128, T], F32) for i in range(2)]
        self.ntmp_b = [Buf(f"ntmp{i}") for i in range(2)]
        self.uhalo = [self.sb(es, f"uhalo{l}", [128, KC, 16], F32) for l in range(2)]
        self.uhalo_b = [Buf(f"uhalo{l}") for l in range(2)]
        self.ahalo = [self.sb(es, f"ahalo{l}", [128, NFC, 2], F32) for l in range(4)]
        self.ahalo_b = [Buf(f"ahalo{l}") for l in range(4)]
        self.masks = self.sb(es, "masks", [128, 48, 256], BF16)
        self.masks_b = Buf("masks")
        self.masks2 = self.sb(es, "masks2", [128, 3, NH, 32], BF16)
        self.ps = [es.enter_context(nc.psum_tensor(f"ps{i}", [128, 512], F32)) for i in range(8)]
        self.ps_b = [Buf(f"ps{i}") for i in range(8)]
        self.w_init()

        def ada_sched(l):
            for n, w in self.apl:
                if n.startswith(f"ada{l}_") or (l == 4 and n.startswith("kvada")):
                    o, _ = self.aoff[n]
                    self.wsched.append((self.adaw[:, o:o + w], w, n))
        stages = STAGES[:self.nstages]
        nP = min(5, len(stages))
        order = []
        for s in range(nt):
            order += [(s, st) for st in stages[:nP]]
            if s >= 1:
                order += [(s - 1, st) for st in stages[nP:]] + [(s - 1, "out")]
        order += [(nt - 1, st) for st in stages[nP:]] + [(nt - 1, "out")]
        for (s, st) in order:
            if st == "out":
                continue
            if s == 0 and st.startswith("mix"):
                ada_sched(int(st[3]))
            if s == 0 and st == "kv":
                ada_sched(4)
            for n, w in stage_pieces(st):
                o, _ = self.poff[n]
                self.wsched.append((self.wts[:, o:o + w], w, n))

        self.prologue()
        self.tile_init()
        for (s, st) in order:
            self.run_stage(s, st)
        for b in self.out_wait:
            SP.wait_all([b])
        return nc

    def vcol(self, name, c0=0, n=1):
        o, _ = self.VL[name]
        return self.vecs[:, o + c0:o + c0 + n]

    def prologue(self):
        nc = self.nc
        PE, ACT, DVE, POOL, SP = self.PE, self.ACT, self.DVE, self.POOL, self.SP
        SP.dma(self.vecs[:, :], self.vecs_d[:, :], [], [self.vecs_b])
        DVE.op(nc.vector.memset, [], [self.ones_b], self.ones[:, :], 1.0)
        for l in range(2):
            DVE.op(nc.vector.memset, [], [self.uhalo_b[l]], self.uhalo[l][:, :, :], 0.0)
        for l in range(4):
            DVE.op(nc.vector.memset, [], [self.ahalo_b[l]], self.ahalo[l][:, :, :], 0.0)
        ACT.op(nc.scalar.activation, [self.vecs_b], [self.cond_b], out=self.condb[:, :], in_=self.vcol("c", 0, 8), func=AF.Silu)
        if self.nstages > 5:
            for g in range(3):
                for h in range(NH):
                    ACT.op(nc.scalar.activation, [self.vecs_b], [self.masks_b], out=self.masks[:, g * NH + h, :],
                           in_=self.vcol("dist", 0, 256), func=AF.Exp, scale=-float(SLOPES[g, h]) * DIL[g])
            for i in range(3):
                for h in range(NH):
                    ACT.op(nc.scalar.activation, [self.vecs_b], [self.masks_b], out=self.masks2[:, i, h, :],
                           in_=self.vcol("dist2", i * 32, 32), func=AF.Exp, scale=-float(SLOPES[2, h]) * DIL[2])

    def compute_mod(self, l):
        nc = self.nc
        PE, ACT, DVE, POOL, SP = self.PE, self.ACT, self.DVE, self.POOL, self.SP
        pb = self.psrr()
        if True:
            ncol = 48 if l < 4 else 16
            npieces = 12 if l < 4 else 4
            pt, pbuf = self.ps[pb], self.ps_b[pb]
            first = True
            for i in range(npieces):
                slot, sbuf_ = self.w_next(f"ada{l}_{i}" if l < 4 else f"kvada_{i}")
                for mm in range(4):
                    col = i * 4 + mm
                    for kc in range(KC):
                        PE.op(nc.tensor.matmul, [sbuf_, self.cond_b], [pbuf], pt[:, col:col + 1],
                              lhsT=slot[:, kc * 512 + mm * 128: kc * 512 + mm * 128 + 128], rhs=self.condb[:, kc:kc + 1],
                              start=first, stop=(kc == KC - 1))
                        first = False
            bname = f"ada_b{l}" if l < 4 else "kvb"
            DVE.op(nc.vector.tensor_tensor, [pbuf, self.vecs_b], [self.mod_b], out=self.modT[:, l * 48:l * 48 + ncol],
                   in0=pt[:, 0:ncol], in1=self.vcol(bname, 0, ncol), op=ALU.add)
        if l < 4:
            for j, (gn, sc0) in enumerate(((f"n1g{l}", 8), (f"n2g{l}", 32))):
                DVE.op(nc.vector.scalar_tensor_tensor, [self.mod_b, self.vecs_b], [self.mod_b],
                       out=self.der[:, l * 16 + j * 8: l * 16 + j * 8 + 8], in0=self.modT[:, l * 48 + sc0: l * 48 + sc0 + 8],
                       scalar=1.0, in1=self.vcol(gn, 0, 8), op0=ALU.add, op1=ALU.mult)
        else:
            DVE.op(nc.vector.scalar_tensor_tensor, [self.mod_b, self.vecs_b], [self.mod_b],
                   out=self.der[:, 64:72], in0=self.modT[:, 192 + 8:192 + 16], scalar=1.0, in1=self.vcol("kvg", 0, 8),
                   op0=ALU.add, op1=ALU.mult)

    def A(self, l, which, c):
        return self.der[:, l * 16 + which * 8 + c: l * 16 + which * 8 + c + 1]

    def M(self, l, j, c):
        return self.modT[:, l * 48 + j * 8 + c: l * 48 + j * 8 + c + 1]

    def norm(self, x, xb, acol, bcol, out=None, out_b=None, final=False, keep=None, reuse=None):
        nc = self.nc
        PE, ACT, DVE = self.PE, self.ACT, self.DVE
        if reuse is not None:
            self.rstd_cur = ((self.rstd, self.rstd_b), (self.rstd2, self.rstd2_b))[reuse]
        else:
            if self.ms_ready is None:
                for c in range(KC):
                    ACT.op(nc.scalar.activation, [xb[c]], [self.sq_bs[c]], out=self.sq[:, c, :], in_=x[:, c, :], func=AF.Square)
                self.emit_ms()
            pt, pbuf = self.ms_ready
            self.ms_ready = None
            ACT.op(nc.scalar.activation, [pbuf, self.eps_b], [self.std_b], out=self.std[:, :], in_=pt[:, :], func=AF.Sqrt,
                   scale=1.0 / D, bias=self.epsc[:, 0:1])
            if keep is not None:
                kt_, kb_ = ((self.rstd, self.rstd_b), (self.rstd2, self.rstd2_b))[keep]
                DVE.op(nc.vector.reciprocal, [self.std_b], [kb_], out=kt_[:, :], in_=self.std[:, :])
                self.rstd_cur = (kt_, kb_)
            else:
                DVE.op(nc.vector.reciprocal, [self.std_b], [pbuf], out=pt[:, :], in_=self.std[:, :])
                self.rstd_cur = (pt, pbuf)
        rt, rb = self.rstd_cur
        for c in range(KC):
            tb = self.ntmp_b[c % 2]
            tt = self.ntmp[c % 2]
            DVE.op(nc.vector.tensor_tensor, [xb[c], rb], [tb], out=tt[:, :], in0=x[:, c, :], in1=rt[:, :], op=ALU.mult)
            if final:
                ACT.op(nc.scalar.activation, [tb, self.vecs_b], [out_b], out=out[:, c, :], in_=tt[:, :], func=AF.Copy,
                       scale=acol(c))
            else:
                ACT.op(nc.scalar.activation, [tb, self.mod_b], [self.h_bs[c]], out=self.hT[:, c, :], in_=tt[:, :], func=AF.Identity,
                       scale=acol(c), bias=bcol(c))

    def emit_ms(self):
        nc = self.nc
        pi = self.psrr()
        pt, pbuf = self.ps[pi], self.ps_b[pi]
        for c in range(KC):
            self.PE.op(nc.tensor.matmul, [self.sq_bs[c], self.ones_b], [pbuf], pt[:, :], lhsT=self.ones[:, :], rhs=self.sq[:, c, :],
                       start=(c == 0), stop=(c == KC - 1))
        self.ms_ready = (pt, pbuf)

    def residual(self, x, xb, m, pt, pbuf, gcol):
        nc = self.nc
        self.DVE.op(nc.vector.scalar_tensor_tensor, [pbuf, self.mod_b, xb[m]], [xb[m]], out=x[:, m, :], in0=pt[:, :], scalar=gcol,
                    in1=x[:, m, :], op0=ALU.mult, op1=ALU.add)
        self.ACT.op(nc.scalar.activation, [xb[m]], [self.sq_bs[m]], out=self.sq[:, m, :], in_=x[:, m, :], func=AF.Square)

    def psrr(self):
        i = self._psi
        self._psi = (self._psi + 1) % 8
        return i

    def tile_init(self):
        nc = self.nc
        self._psi = 0
        self.out_wait = []
        self.epsc = self.sb(self.es, "epsc", [128, 1], F32)
        self.eps_b = Buf("eps")
        self.DVE.op(nc.vector.memset, [], [self.eps_b], self.epsc[:, :], EPS)
        self.x_loaded = set()
        self.load_x(0)
        self.load_x(1)

    def load_x(self, s2):
        if s2 >= self.n_tiles or s2 in self.x_loaded:
            return
        self.x_loaded.add(s2)
        src_ = self.xT.rearrange("(c p) t -> p c t", p=128)
        for c in range(KC):
            self.SP.dma(self.xTs[s2 % 2][:, c, :], src_[:, c, s2 * T:(s2 + 1) * T], [], [self.x_bufs[s2 % 2][c]])

    def run_stage(self, s, st):
        nc = self.nc
        PE, ACT, DVE, POOL, SP = self.PE, self.ACT, self.DVE, self.POOL, self.SP
        x = self.xTs[s % 2]
        xb = self.x_bufs[s % 2]
        if st == "mix0":
            self.load_x(s)
            self.ms_ready = None
        if s == 0 and st.startswith("mix"):
            self.compute_mod(int(st[3]))
        if s == 0 and st == "kv":
            self.compute_mod(4)
        if st == "kv":
            self.kv_stage(s, x, xb)
        elif st.startswith("mix"):
            l = int(st[3])
            if l < 2:
                self.pool_mixer(l, s, x, xb)
            else:
                self.attn_mixer(l, s, x, xb)
        elif st.startswith("ffn"):
            self.ffn(int(st[3]), s, x, xb)
        else:
            es = self.stage_begin()
            o = self.sb(es, "otile", [128, KC, T], F32)
            ob = self.newbuf("otile")
            if self.full:
                self.norm(x, xb, lambda c: self.vcol("fg", c, 1), None, out=o, out_b=ob, final=True)
            else:
                self.ms_ready = None
                ACT.op(nc.scalar.activation, xb, [ob], out=o[:, :, :], in_=x[:, :, :], func=AF.Copy)
            SP.dma(self.outT.rearrange("(c p) t -> p c t", p=128)[:, :, s * T:(s + 1) * T], o[:, :, :], [ob], [])
            self.out_wait.append(ob)
            self.stage_end(es)
            self.load_x(s + 2)

    def pool_mixer(self, l, s, x, xb):
        nc = self.nc
        PE, ACT, DVE = self.PE, self.ACT, self.DVE
        es = self.stage_begin()
        U = self.sb(es, "poolU", [128, KC, 16 + T], F32)
        Ub = self.newbuf("U")
        A_ = self.sb(es, "poolA", [128, KC, 16 + T], F32)
        Ab = self.newbuf("A")
        B_ = self.sb(es, "poolB", [128, KC, 16 + T], F32)
        Bb = self.newbuf("B")
        Pb = self.sb(es, "poolP", [128, KC, T], BF16)
        Pbb = self.newbuf("P")
        Zb = self.sb(es, "poolZ", [128, KC, T], BF16)
        Zbb = self.newbuf("Z")
        self.norm(x, xb, lambda c: self.A(l, 0, c), lambda c: self.M(l, 0, c))
        DVE.op(nc.vector.tensor_copy, [self.uhalo_b[l]], [Ub], out=U[:, :, 0:16], in_=self.uhalo[l][:, :, :])
        for half in range(2):
            slot, sbuf_ = self.w_next(f"win{l}_{half}")
            for mm in range(4):
                m = half * 4 + mm
                pi = self.psrr()
                pt, pbuf = self.ps[pi], self.ps_b[pi]
                for kc in range(KC):
                    PE.op(nc.tensor.matmul, [sbuf_, self.h_bs[kc]], [pbuf], pt[:, :], lhsT=slot[:, kc * 512 + mm * 128: kc * 512 + mm * 128 + 128],
                          rhs=self.hT[:, kc, :], start=(kc == 0), stop=(kc == KC - 1))
                ACT.op(nc.scalar.activation, [pbuf], [Ub], out=U[:, m, 16:16 + T], in_=pt[:, :], func=AF.Copy)
        DVE.op(nc.vector.tensor_copy, [Ub], [self.uhalo_b[l]], out=self.uhalo[l][:, :, :], in_=U[:, :, T:T + 16])
        W_ = 16 + T
        DVE.op(nc.vector.tensor_tensor, [Ub], [Ab], out=A_[:, :, 1:W_], in0=U[:, :, 1:W_], in1=U[:, :, 0:W_ - 1], op=ALU.add)
        DVE.op(nc.vector.tensor_tensor, [Ab], [Bb], out=B_[:, 2:8, 3:W_], in0=A_[:, 2:8, 3:W_], in1=A_[:, 2:8, 1:W_ - 2], op=ALU.add)
        DVE.op(nc.vector.tensor_tensor, [Bb], [Ab], out=A_[:, 4:8, 7:W_], in0=B_[:, 4:8, 7:W_], in1=B_[:, 4:8, 3:W_ - 4], op=ALU.add)
        DVE.op(nc.vector.tensor_tensor, [Ab], [Bb], out=B_[:, 6:8, 15:W_], in0=A_[:, 6:8, 15:W_], in1=A_[:, 6:8, 7:W_ - 8], op=ALU.add)
        srcs = [(A_, Ab), (B_, Bb), (A_, Ab), (B_, Bb)]
        for g in range(4):
            St, Sb_ = srcs[g]
            DVE.op(nc.vector.scalar_tensor_tensor, [Sb_, Ub], [Pbb], out=Pb[:, 2 * g:2 * g + 2, :], in0=St[:, 2 * g:2 * g + 2, 16:16 + T],
                   scalar=1.0 / POOLW[g], in1=U[:, 2 * g:2 * g + 2, 16:16 + T], op0=ALU.mult, op1=ALU.subtract)
            if s == 0:
                o, _ = self.VL["invc"]
                for cc in range(2):
                    c = 2 * g + cc
                    tb, tt = self.ntmp_b[cc], self.ntmp[cc]
                    DVE.op(nc.vector.tensor_tensor, [Sb_, self.vecs_b], [tb], out=tt[:, 0:16], in0=St[:, c, 16:32],
                           in1=self.vecs[:, o + g * 16:o + g * 16 + 16], op=ALU.mult)
                    DVE.op(nc.vector.tensor_tensor, [tb, Ub], [Pbb], out=Pb[:, c, 0:16], in0=tt[:, 0:16], in1=U[:, c, 16:32], op=ALU.subtract)
        slot, sbuf_ = self.w_next(f"wgrp{l}")
        for g in range(4):
            for mo in range(2):
                c = 2 * g + mo
                pi = self.psrr()
                pt, pbuf = self.ps[pi], self.ps_b[pi]
                for ki in range(2):
                    base = (g * 2 + ki) * 256 + mo * 128
                    PE.op(nc.tensor.matmul, [sbuf_, Pbb], [pbuf], pt[:, :], lhsT=slot[:, base:base + 128], rhs=Pb[:, 2 * g + ki, :],
                          start=(ki == 0), stop=(ki == 1))
                ACT.op(nc.scalar.activation, [pbuf, self.vecs_b], [Zbb], out=Zb[:, c, :], in_=pt[:, :], func=AF.Copy,
                       scale=self.vcol(f"psc{l}", c, 1))
        for half in range(2):
            slot, sbuf_ = self.w_next(f"wout{l}_{half}")
            for mm in range(4):
                m = half * 4 + mm
                pi = self.psrr()
                pt, pbuf = self.ps[pi], self.ps_b[pi]
                for kc in range(KC):
                    PE.op(nc.tensor.matmul, [sbuf_, Zbb], [pbuf], pt[:, :], lhsT=slot[:, kc * 512 + mm * 128: kc * 512 + mm * 128 + 128],
                          rhs=Zb[:, kc, :], start=(kc == 0), stop=(kc == KC - 1))
                self.residual(x, xb, m, pt, pbuf, self.M(l, 2, m))
        self.emit_ms()
        self.stage_end(es)

    def ffn(self, l, s, x, xb):
        nc = self.nc
        PE, ACT, DVE = self.PE, self.ACT, self.DVE
        es = self.stage_begin()
        gT = self.sb(es, "gT", [128, NFC, T], BF16)
        gbs = [self.newbuf(f"gT{i}") for i in range(NFC)]
        abuf = [self.sb(es, f"abuf{i}", [128, 2 + T], F32) for i in range(2)]
        ab = [self.newbuf(f"abuf{i}") for i in range(2)]
        c1 = [self.sb(es, f"c1_{i}", [128, T], F32) for i in range(2)]
        c1b = [self.newbuf(f"c1_{i}") for i in range(2)]
        c2 = [self.sb(es, f"c2_{i}", [128, T], F32) for i in range(2)]
        c2b = [self.newbuf(f"c2_{i}") for i in range(2)]
        vsb = [self.sb(es, f"vsb{i}", [128, T], F32) for i in range(2)]
        vsbb = [self.newbuf(f"vsb{i}") for i in range(2)]
        self.norm(x, xb, lambda c: self.A(l, 1, c), lambda c: self.M(l, 3, c))
        cwo, _ = self.VL[f"cw{l}"]
        cbo, _ = self.VL[f"cb{l}"]
        for i in range(11):
            slot, sbuf_ = self.w_next(f"wup{l}_{i}")
            for jj in range(2):
                fc = 2 * i + jj
                par = fc % 2
                pa, pab = self.ps[par * 2], self.ps_b[par * 2]
                pv, pvb = self.ps[par * 2 + 1], self.ps_b[par * 2 + 1]
                for av, (pt, pbuf) in enumerate(((pa, pab), (pv, pvb))):
                    base = (jj * 2 + av) * 1024
                    for kc in range(KC):
                        PE.op(nc.tensor.matmul, [sbuf_, self.h_bs[kc]], [pbuf], pt[:, :], lhsT=slot[:, base + kc * 128: base + kc * 128 + 128],
                              rhs=self.hT[:, kc, :], start=(kc == 0), stop=(kc == KC - 1))
                A_, Ab = abuf[par], ab[par]
                ACT.op(nc.scalar.activation, [self.ahalo_b[l]], [Ab], out=A_[:, 0:2], in_=self.ahalo[l][:, fc, :], func=AF.Copy)
                ACT.op(nc.scalar.activation, [pab], [Ab], out=A_[:, 2:2 + T], in_=pa[:, :], func=AF.Copy)
                ACT.op(nc.scalar.activation, [Ab], [self.ahalo_b[l]], out=self.ahalo[l][:, fc, :], in_=A_[:, T:T + 2], func=AF.Copy)
                ACT.op(nc.scalar.activation, [pab, self.vecs_b], [c1b[par]], out=c1[par][:, :], in_=pa[:, :], func=AF.Identity,
                       scale=self.vecs[:, cwo + 2 * NFC + fc: cwo + 2 * NFC + fc + 1], bias=self.vecs[:, cbo + fc:cbo + fc + 1])
                ACT.op(nc.scalar.activation, [pvb], [vsbb[par]], out=vsb[par][:, :], in_=pv[:, :], func=AF.Copy)
                DVE.op(nc.vector.scalar_tensor_tensor, [Ab, self.vecs_b, c1b[par]], [c2b[par]], out=c2[par][:, :], in0=A_[:, 1:1 + T],
                       scalar=self.vecs[:, cwo + NFC + fc: cwo + NFC + fc + 1], in1=c1[par][:, :], op0=ALU.mult, op1=ALU.add)
                DVE.op(nc.vector.scalar_tensor_tensor, [Ab, self.vecs_b, c2b[par]], [c1b[par]], out=c1[par][:, :], in0=A_[:, 0:T],
                       scalar=self.vecs[:, cwo + fc: cwo + fc + 1], in1=c2[par][:, :], op0=ALU.mult, op1=ALU.add)
                ACT.op(nc.scalar.activation, [c1b[par]], [c2b[par]], out=c2[par][:, :], in_=c1[par][:, :], func=AF.Silu)
                DVE.op(nc.vector.tensor_tensor, [c2b[par], vsbb[par]], [gbs[fc]], out=gT[:, fc, :], in0=c2[par][:, :], in1=vsb[par][:, :], op=ALU.mult)
        for r in range(2):
            b0 = 4 if r == 0 else 0
            for j in range(3):
                slot, sbuf_ = self.w_next(f"wdn{l}_{r}_{j}")
                nf = 8 if j < 2 else 6
                for fl in range(nf):
                    fc = 8 * j + fl
                    for mm in range(4):
                        PE.op(nc.tensor.matmul, [sbuf_, gbs[fc]], [self.ps_b[b0 + mm]], self.ps[b0 + mm][:, :],
                              lhsT=slot[:, fl * 512 + mm * 128: fl * 512 + mm * 128 + 128], rhs=gT[:, fc, :],
                              start=(fc == 0), stop=(fc == NFC - 1))
            for mm in range(4):
                m = r * 4 + mm
                self.residual(x, xb, m, self.ps[b0 + mm], self.ps_b[b0 + mm], self.M(l, 5, m))
        self._psi = 4
        self.emit_ms()
        self.stage_end(es)

    def kv_stage(self, s, x, xb):
        nc = self.nc
        PE, ACT, DVE, SP = self.PE, self.ACT, self.DVE, self.SP
        es = self.stage_begin()
        kst = [self.sb(es, f"kst{i}", [128, T], BF16) for i in range(6)]
        kstb = [self.newbuf(f"kst{i}") for i in range(6)]
        vst = [self.sb(es, f"vst{i}", [128, 512], BF16) for i in range(4)]
        vstb = [self.newbuf(f"vst{i}") for i in range(4)]
        kvb = self.kv_bufs[s]
        self.norm(x, xb, lambda c: self.der[:, 64 + c:65 + c], lambda c: self.modT[:, 192 + c:193 + c], keep=s % 2)
        h16 = self.sb(es, "h16", [128, KC, T], BF16)
        h16b = self.newbuf("h16")
        for kc in range(KC):
            DVE.op(nc.vector.tensor_copy, [self.h_bs[kc]], [h16b], out=h16[:, kc, :].rearrange("p (r j) -> p r j", r=16),
                   in_=self.hT[:, kc, :].rearrange("p (j r) -> p r j", r=16))
        n = 0
        for i in range(6):
            slot, sbuf_ = self.w_next(f"wk_{i}")
            g = i // 2
            d = DIL[g]
            for mm in range(4):
                hp = (i % 2) * 4 + mm
                pi = self.psrr()
                pt, pbuf = self.ps[pi], self.ps_b[pi]
                for kc in range(KC):
                    PE.op(nc.tensor.matmul, [sbuf_, self.h_bs[kc]], [pbuf], pt[:, :], lhsT=slot[:, kc * 512 + mm * 128: kc * 512 + mm * 128 + 128],
                          rhs=self.hT[:, kc, :], start=(kc == 0), stop=(kc == KC - 1))
                kt, ktb = kst[n % 6], kstb[n % 6]
                n += 1
                nj = T // d
                if d == 1:
                    ACT.op(nc.scalar.activation, [pbuf], [ktb], out=kt[:, :], in_=pt[:, :], func=AF.Copy)
                    SP.dma(self.kTd[g, hp, :, s * T:(s + 1) * T], kt[:, :], [ktb], [kvb])
                else:
                    ACT.op(nc.scalar.activation, [pbuf], [ktb], out=kt[:, :].rearrange("p (r j) -> p r j", r=d),
                           in_=pt[:, :].rearrange("p (j r) -> p r j", r=d), func=AF.Copy)
                    dst = self.kTd[g, hp, :, :].rearrange("p (r j) -> p r j", r=d)[:, :, s * nj:(s + 1) * nj]
                    SP.dma(dst, kt[:, :].rearrange("p (r j) -> p r j", r=d), [ktb], [kvb])
        n = 0
        for i in range(6):
            slot, sbuf_ = self.w_next(f"wv_{i}")
            g = i // 2
            d = DIL[g]
            half = i % 2
            for ch in range(4):
                if d == 1:
                    cols = lambda kc: self.hT[:, kc, ch * 128:(ch + 1) * 128]
                elif d == 4:
                    cols = lambda kc: self.hT[:, kc, :].rearrange("p (j r) -> p r j", r=4)[:, ch, :]
                else:
                    cols = lambda kc: h16[:, kc, ch * 128:(ch + 1) * 128]
                pi = self.psrr()
                pt, pbuf = self.ps[pi], self.ps_b[pi]
                for kc in range(KC):
                    PE.op(nc.tensor.matmul, [sbuf_, self.h_bs[kc], h16b], [pbuf], pt[:, :], lhsT=cols(kc), rhs=slot[:, kc * 512:(kc + 1) * 512],
                          start=(kc == 0), stop=(kc == KC - 1))
                vt, vtb = vst[n % 4], vstb[n % 4]
                n += 1
                DVE.op(nc.vector.tensor_copy, [pbuf], [vtb], out=vt[:, 0:512], in_=pt[:, :])
                if d == 1:
                    r0 = s * T + ch * 128
                    SP.dma(self.vd[g, r0:r0 + 128, half * 512:(half + 1) * 512], vt[:, 0:512], [vtb], [kvb])
                elif d == 4:
                    r0 = ch * 1024 + s * 128
                    SP.dma(self.vd[g, r0:r0 + 128, half * 512:(half + 1) * 512], vt[:, 0:512], [vtb], [kvb])
                else:
                    for rl in range(4):
                        r0 = (ch * 4 + rl) * 256 + s * 32
                        SP.dma(self.vd[g, r0:r0 + 32, half * 512:(half + 1) * 512], vt[rl * 32:(rl + 1) * 32, 0:512], [vtb], [kvb])
        self.stage_end(es)

    def attn_mixer(self, l, s, x, xb):
        nc = self.nc
        PE, ACT, DVE, SP = self.PE, self.ACT, self.DVE, self.SP
        es = self.stage_begin()
        oT = self.sb(es, "oT", [128, KC, T], BF16)
        oTb = self.newbuf("oT")
        qT = [self.sb(es, f"qT{i}", [128, 3, T], BF16) for i in range(2)]
        qTb = [self.newbuf(f"qT{i}") for i in range(2)]
        KW = 640 + 1024 + 2560
        kt = [self.sb(es, f"ktile{i}", [128, KW], BF16) for i in range(2)]
        ktb = [self.newbuf(f"ktile{i}") for i in range(2)]
        NVC = 5 + 8 + 32
        vt = [self.sb(es, f"vtile{i}", [128, NVC, 128], BF16) for i in range(2)]
        vtb = [self.newbuf(f"vtile{i}") for i in range(2)]
        NB = 3
        ET = [self.sb(es, f"E{i}", [128, T], BF16) for i in range(NB)]
        ETb = [self.newbuf(f"E{i}") for i in range(NB)]
        PT = [self.sb(es, f"P{i}", [128, T], BF16) for i in range(NB)]
        PTb = [self.newbuf(f"P{i}") for i in range(NB)]
        rD = self.sb(es, "rD", [128, T], F32)
        rDb = self.newbuf("rD")
        self.norm(x, xb, lambda c: self.A(l, 0, c), lambda c: self.M(l, 0, c), reuse=(s % 2 if l == 2 else None))
        kv_reads = [self.kv_bufs[t] for t in range(max(0, s - 4), s + 1)]
        na = min(128, 32 * s)

        def prep(hp):
            par = hp % 2
            K_, Kb = kt[par], ktb[par]
            V_, Vb = vt[par], vtb[par]
            lo0 = 128 if s == 0 else 0
            SP.dma(K_[:, lo0:640], self.kTd[0, hp, :, s * T - 128 + lo0: s * T + 512], kv_reads, [Kb])
            src_ = self.vd[0, s * T - 128 + lo0: s * T + 512, hp * 128:(hp + 1) * 128].rearrange("(c p) f -> p c f", p=128)
            SP.dma(V_[:, lo0 // 128:5, :], src_, kv_reads, [Vb])
            srck = self.kTd[1, hp, :, :].rearrange("p (r j) -> p r j", r=4)[:, :, 128 * (s - 1) + lo0: 128 * (s + 1)]
            SP.dma(K_[:, 640:640 + 1024].rearrange("p (r j) -> p r j", r=4)[:, :, lo0:256], srck, kv_reads, [Kb])
            for r in range(4):
                r0 = r * 1024 + 128 * (s - 1) + lo0
                nch = 2 - lo0 // 128
                src_ = self.vd[1, r0:r0 + 128 * nch, hp * 128:(hp + 1) * 128].rearrange("(c p) f -> p c f", p=128)
                SP.dma(V_[:, 5 + r * 2 + lo0 // 128: 5 + r * 2 + 2, :], src_, kv_reads, [Vb])
            k2 = K_[:, 640 + 1024:].rearrange("p (r j) -> p r j", r=16)
            srck = self.kTd[2, hp, :, :].rearrange("p (r j) -> p r j", r=16)[:, :, 32 * s - na: 32 * s + 32]
            SP.dma(k2[:, :, 128 - na:160], srck, kv_reads, [Kb])
            v2 = self.vd[2, :, hp * 128:(hp + 1) * 128].rearrange("(r j) f -> j r f", r=16)
            if na > 0:
                SP.dma(V_[0:na, 13:29, :], v2[32 * s - na:32 * s, :, :], kv_reads, [Vb])
            SP.dma(V_[0:32, 29:45, :], v2[32 * s:32 * s + 32, :, :], kv_reads, [Vb])
            slot, sbuf_ = self.w_next(f"wq{l}_{hp}")
            Q_, Qb = qT[par], qTb[par]
            for g in range(3):
                pt, pbuf = self.ps[3], self.ps_b[3]
                for kc in range(KC):
                    base = (g * 8 + kc) * 128
                    PE.op(nc.tensor.matmul, [sbuf_, self.h_bs[kc]], [pbuf], pt[:, :], lhsT=slot[:, base:base + 128], rhs=self.hT[:, kc, :],
                          start=(kc == 0), stop=(kc == KC - 1))
                d = DIL[g]
                if d == 1:
                    ACT.op(nc.scalar.activation, [pbuf], [Qb], out=Q_[:, g, :], in_=pt[:, :], func=AF.Copy)
                else:
                    ACT.op(nc.scalar.activation, [pbuf], [Qb], out=Q_[:, g, :].rearrange("p (r j) -> p r j", r=d),
                           in_=pt[:, :].rearrange("p (j r) -> p r j", r=d), func=AF.Copy)

        units_all = []
        for hp in range(8):
            ulist = []
            for g in range(3):
                batches = []
                if g < 2:
                    for kind in (0, 1):
                        us = []
                        for u in range(4):
                            if g == 0:
                                if kind == 0 and s == 0 and u == 0:
                                    continue
                                us.append((u, (u + kind) * 128, u + kind))
                            else:
                                if kind == 0 and s == 0:
                                    continue
                                us.append((u, 640 + u * 256 + kind * 128, 5 + u * 2 + kind))
                        if us:
                            batches.append((kind, 128, 128, us))
                else:
                    if na > 0:
                        batches.append((0, 32, na, [(r, 640 + 1024 + r * 160 + 128 - na, 13 + r) for r in range(16)]))
                    batches.append((1, 32, 32, [(r, 640 + 1024 + r * 160 + 128, 29 + r) for r in range(16)]))
                for (kind, nq, nk, us) in batches:
                    for hh in range(2):
                        ulist.append(dict(hp=hp, g=g, kind=kind, nq=nq, nk=nk, us=us, hh=hh))
            ulist[0]["first_of_hp"] = True
            ulist[-1]["last_of_hp"] = True
            seen = set()
            for ud in ulist:
                if ud["hh"] not in seen:
                    ud["first_pv"] = True
                    seen.add(ud["hh"])
            units_all += ulist

        def s_phase(i, ud):
            hp, g, kind, nq, nk, us, hh = ud["hp"], ud["g"], ud["kind"], ud["nq"], ud["nk"], ud["us"], ud["hh"]
            par = hp % 2
            K_, Kb = kt[par], ktb[par]
            Q_, Qb = qT[par], qTb[par]
            h = hp * 2 + hh
            r0, r1 = hh * 64, hh * 64 + 64
            si = i % NB
            Sp, Spb = self.ps[si], self.ps_b[si]
            for (u, kcol, vch) in us:
                PE.op(nc.tensor.matmul, [Kb, Qb], [Spb], Sp[0:nk, u * nq:(u + 1) * nq], lhsT=K_[r0:r1, kcol:kcol + nk],
                      rhs=Q_[r0:r1, g, u * nq:(u + 1) * nq], start=True, stop=True)
            u0 = us[0][0]
            u1 = us[-1][0] + 1
            nu = u1 - u0
            E_, Eb = ET[si], ETb[si]
            P_, Pb_ = PT[si], PTb[si]
            ACT.op(nc.scalar.activation, [Spb], [Eb], out=E_[0:nk, u0 * nq:u1 * nq], in_=Sp[0:nk, u0 * nq:u1 * nq],
                   func=AF.Exp, scale=HD ** -0.5)
            if g == 2 and kind == 0 and nk < 128:
                mk = self.masks2[0:nk, nk // 32 - 1, h, 0:32]
            else:
                mk = self.masks[0:nk, g * NH + h, kind * 128: kind * 128 + nq]
            DVE.op(nc.vector.tensor_tensor, [Eb, self.masks_b], [Pb_],
                   out=P_[0:nk, u0 * nq:u1 * nq].rearrange("p (u q) -> p u q", u=nu),
                   in0=E_[0:nk, u0 * nq:u1 * nq].rearrange("p (u q) -> p u q", u=nu),
                   in1=mk.unsqueeze(1).to_broadcast([nk, nu, nq]), op=ALU.mult)

        def pv_phase(i, ud):
            hp, g, kind, nq, nk, us, hh = ud["hp"], ud["g"], ud["kind"], ud["nq"], ud["nk"], ud["us"], ud["hh"]
            par = hp % 2
            V_, Vb = vt[par], vtb[par]
            r0, r1 = hh * 64, hh * 64 + 64
            si = i % NB
            P_, Pb_ = PT[si], PTb[si]
            Np, Npb = self.ps[4 + par], self.ps_b[4 + par]
            Dp, Dpb = self.ps[6 + par], self.ps_b[6 + par]
            d = DIL[g]
            first = ud.get("first_pv", False)
            for (u, kcol, vch) in us:
                if d == 1:
                    No = Np[r0:r1, u * 128:(u + 1) * 128]
                    Do = Dp[r0:r1, u * 128:(u + 1) * 128]
                else:
                    No = Np[r0:r1, :].rearrange("p (j r) -> p r j", r=d)[:, u, :]
                    Do = Dp[r0:r1, :].rearrange("p (j r) -> p r j", r=d)[:, u, :]
                PE.op(nc.tensor.matmul, [Vb, Pb_], [Npb], No, lhsT=V_[0:nk, vch, r0:r1], rhs=P_[0:nk, u * nq:(u + 1) * nq],
                      start=first, stop=False, skip_group_check=True)
                PE.op(nc.tensor.matmul, [self.ones_b, Pb_], [Dpb], Do, lhsT=self.ones[0:nk, 0:64], rhs=P_[0:nk, u * nq:(u + 1) * nq],
                      start=first, stop=False, skip_group_check=True)
                first = False
            if ud.get("last_of_hp"):
                DVE.op(nc.vector.reciprocal, [Dpb], [rDb], out=rD[:, :], in_=Dp[:, :])
                DVE.op(nc.vector.tensor_tensor, [Npb, rDb], [oTb], out=oT[:, hp, :], in0=Np[:, :], in1=rD[:, :], op=ALU.mult)

        LOOK = 2
        n = len(units_all)
        prep(0)
        prep(1)
        for i in range(n + LOOK):
            if i < n:
                s_phase(i, units_all[i])
            if i - LOOK >= 0:
                ud = units_all[i - LOOK]
                pv_phase(i - LOOK, ud)
                if ud.get("last_of_hp") and ud["hp"] + 2 < 8:
                    prep(ud["hp"] + 2)
        self._psi = 0
        for half in range(2):
            slot, sbuf_ = self.w_next(f"wo{l}_{half}")
            for mm in range(4):
                m = half * 4 + mm
                pi = self.psrr()
                pt, pbuf = self.ps[pi], self.ps_b[pi]
                for kc in range(KC):
                    PE.op(nc.tensor.matmul, [sbuf_, oTb], [pbuf], pt[:, :], lhsT=slot[:, kc * 512 + mm * 128: kc * 512 + mm * 128 + 128],
                          rhs=oT[:, kc, :], start=(kc == 0), stop=(kc == KC - 1))
                self.residual(x, xb, m, pt, pbuf, self.M(l, 2, m))
        self.emit_ms()
        self.stage_end(es)


_CACHE = {}


def get_prog(n_tiles=NT, nstages=len(STAGES)):
    key = (n_tiles, nstages)
    if key not in _CACHE:
        p = Prog(n_tiles, nstages)
        p.build()
        _CACHE[key] = p
    return _CACHE[key]


def kernel(**inputs):
    inp = {k: np.asarray(v) for k, v in inputs.items()}
    W, A, per_core = host_prepare(inp)
    p = get_prog()
    in_maps = [{"xT": pc["xT"], "wts": W, "adaw": A, "vecs": pc["vecs"]} for pc in per_core]
    res = run_bass_kernel_spmd(p.nc, in_maps, core_ids=list(range(8)))
    out = np.stack([np.ascontiguousarray(r["outT"].T) for r in res.results], axis=0)
    return out.astype(np.float32)
```

```python
import math
from contextlib import ExitStack

import numpy as np
import concourse.bass as bass
import concourse.mybir as mybir
from concourse.bass_utils import run_bass_kernel_spmd

F32 = mybir.dt.float32
BF16 = mybir.dt.bfloat16
AF = mybir.ActivationFunctionType
ALU = mybir.AluOpType

D = 1024
S = 4096
FF = 2816
NFC = 22
T = 512
NT = S // T
KC = 8
EPS = 1e-6
POOLW = (2, 4, 8, 16)
DIL = (1, 4, 16)
NH = 16
HD = 64
NSLOT = 5
SLOTW = 4096
BIG = 30000.0


def _alibi_slopes(n):
    def pow2(m):
        start = 2.0 ** (-(2.0 ** -(math.log2(m) - 3)))
        return [start ** (i + 1) for i in range(m)]
    if math.log2(n).is_integer():
        s = pow2(n)
    else:
        c = 2 ** math.floor(math.log2(n))
        s = pow2(c) + pow2(2 * c)[0::2][: n - c]
    s = np.asarray(s, dtype=np.float32)
    return -np.sort(-s)


SLOPES = _alibi_slopes(3 * NH).reshape(3, NH)


STAGES = ["mix0", "ffn0", "mix1", "ffn1", "kv", "mix2", "ffn2", "mix3", "ffn3"]


def stage_pieces(st):
    P = []
    if st.startswith("mix"):
        l = int(st[3])
        if l < 2:
            P += [(f"win{l}_0", 4096), (f"win{l}_1", 4096), (f"wgrp{l}", 2048), (f"wout{l}_0", 4096), (f"wout{l}_1", 4096)]
        else:
            P += [(f"wq{l}_{hp}", 3072) for hp in range(8)]
            P += [(f"wo{l}_0", 4096), (f"wo{l}_1", 4096)]
    elif st == "kv":
        P += [(f"wk_{i}", 4096) for i in range(6)]
        P += [(f"wv_{i}", 4096) for i in range(6)]
    else:
        l = int(st[3])
        P += [(f"wup{l}_{i}", 4096) for i in range(11)]
        for r in range(2):
            P += [(f"wdn{l}_{r}_0", 4096), (f"wdn{l}_{r}_1", 4096), (f"wdn{l}_{r}_2", 3072)]
    return P


def piece_list(nstages=len(STAGES)):
    P = []
    for st in STAGES[:nstages]:
        P += stage_pieces(st)
    return P


def ada_piece_list():
    P = []
    for l in range(4):
        P += [(f"ada{l}_{i}", 4096) for i in range(12)]
    P += [(f"kvada_{i}", 4096) for i in range(4)]
    return P


def _offsets(pl):
    off = {}
    o = 0
    for n, w in pl:
        off[n] = (o, w)
        o += w
    return off, o


def vec_layout():
    L = {}
    o = 0

    def add(name, n):
        nonlocal o
        L[name] = (o, n)
        o += n
    for l in range(4):
        add(f"ada_b{l}", 48)
        add(f"n1g{l}", 8)
        add(f"n2g{l}", 8)
        add(f"cw{l}", 66)
        add(f"cb{l}", 22)
    for l in range(2):
        add(f"psc{l}", 8)
    add("kvg", 8)
    add("kvb", 16)
    add("fg", 8)
    add("c", 8)
    add("invc", 64)
    add("dist", 256)
    add("dist2", 96)
    return L, o


def pmaj(v):
    return np.ascontiguousarray(v.reshape(-1, 128).T)


def kmajor(W, c0, ncols):
    K = W.shape[0]
    a = W[:, c0:c0 + ncols].reshape(K // 128, 128, ncols).transpose(1, 0, 2)
    return a.reshape(128, -1)


def host_prepare(inp):
    pl = piece_list()
    off, tot = _offsets(pl)
    W = np.empty((128, tot), np.float32)

    def put(name, arr):
        o, w = off[name]
        assert arr.shape == (128, w), (name, arr.shape, w)
        W[:, o:o + w] = arr
    for l in range(4):
        if l < 2:
            for m in range(2):
                put(f"win{l}_{m}", kmajor(inp["pool_w_in"][l], m * 512, 512))
                put(f"wout{l}_{m}", kmajor(inp["pool_w_out"][l], m * 512, 512))
            g = inp["pool_w_grp"][l]
            put(f"wgrp{l}", g.reshape(4, 2, 128, 256).transpose(2, 0, 1, 3).reshape(128, 2048))
        else:
            j = l - 2
            wq = inp["attn_w_q"][j]
            for hp in range(8):
                a = np.stack([kmajor(wq, g * 1024 + hp * 128, 128).reshape(128, 8, 128) for g in range(3)], axis=1)
                put(f"wq{l}_{hp}", a.reshape(128, 3072))
            for m in range(2):
                put(f"wo{l}_{m}", kmajor(inp["attn_w_o"][j], m * 512, 512))
        if l == 2:
            for i in range(6):
                put(f"wk_{i}", kmajor(inp["w_kv"], i * 512, 512))
                put(f"wv_{i}", kmajor(inp["w_kv"], 3072 + i * 512, 512))
        wu = inp["ffn_w_up"][l]
        for i in range(11):
            parts = []
            for jj in range(2):
                fc = 2 * i + jj
                for av in range(2):
                    parts.append(kmajor(wu, av * FF + fc * 128, 128))
            put(f"wup{l}_{i}", np.concatenate(parts, axis=1))
        wd = inp["ffn_w_down"][l]
        for r in range(2):
            a = wd[:, r * 512:(r + 1) * 512].reshape(NFC, 128, 512).transpose(1, 0, 2)
            put(f"wdn{l}_{r}_0", a[:, 0:8].reshape(128, 4096))
            put(f"wdn{l}_{r}_1", a[:, 8:16].reshape(128, 4096))
            put(f"wdn{l}_{r}_2", a[:, 16:22].reshape(128, 3072))
    apl = ada_piece_list()
    aoff, atot = _offsets(apl)
    A = np.empty((128, atot), np.float32)
    for l in range(4):
        for i in range(12):
            o, w = aoff[f"ada{l}_{i}"]
            A[:, o:o + w] = kmajor(inp["ada_w"][l], i * 512, 512)
    for i in range(4):
        o, w = aoff[f"kvada_{i}"]
        A[:, o:o + w] = kmajor(inp["kv_ada_w"], i * 512, 512)
    VL, nv = vec_layout()
    shared = np.zeros((128, nv), np.float32)

    def vput(name, arr):
        o, n = VL[name]
        assert arr.shape == (128, n), (name, arr.shape)
        shared[:, o:o + n] = arr
    for l in range(4):
        vput(f"ada_b{l}", pmaj(inp["ada_b"][l]))
        vput(f"n1g{l}", pmaj(inp["norm1_g"][l]))
        vput(f"n2g{l}", pmaj(inp["norm2_g"][l]))
        cw = inp["ffn_conv_w"][l]
        vput(f"cw{l}", np.concatenate([pmaj(cw[k]) for k in range(3)], axis=1))
        vput(f"cb{l}", pmaj(inp["ffn_conv_b"][l]))
    for l in range(2):
        vput(f"psc{l}", pmaj(inp["pool_scale"][l]))
    vput("kvg", pmaj(inp["kv_norm_g"]))
    vput("kvb", pmaj(inp["kv_ada_b"]))
    vput("fg", pmaj(inp["final_g"]))
    invc = np.zeros((128, 4, 16), np.float32)
    for g, w in enumerate(POOLW):
        invc[:, g, :] = 1.0 / np.minimum(np.arange(16) + 1, w)
    vput("invc", invc.reshape(128, 64))
    k = np.arange(128)[:, None]
    q = np.arange(128)[None, :]
    dprev = (q - k + 128).astype(np.float32)
    dprev[dprev > 128] = BIG
    dcur = (q - k).astype(np.float32)
    dcur[dcur < 0] = BIG
    vput("dist", np.concatenate([dprev, dcur], axis=1))
    d2 = []
    for na in (32, 64, 96):
        dd = (q[:, 0:32] - k + na).astype(np.float32)
        dd[dd > 128] = BIG
        dd[k[:, 0] >= na, :] = BIG
        d2.append(dd)
    vput("dist2", np.concatenate(d2, axis=1))
    per_core = []
    for b in range(8):
        v = shared.copy()
        o, n = VL["c"]
        v[:, o:o + n] = pmaj(inp["c"][b])
        per_core.append({"xT": np.ascontiguousarray(inp["x"][b].T), "vecs": v})
    return W, A, per_core


class Chan:
    __slots__ = ("sem", "val")


class Buf:
    __slots__ = ("name", "w", "r")

    def __init__(self, name, seed=None):
        self.name = name
        self.w = {}
        self.r = dict(seed) if seed else {}

    def tokens(self):
        d = dict(self.w)
        for ch, v in self.r.items():
            if d.get(ch, 0) < v:
                d[ch] = v
        return d


def _merge(d, ch, v):
    if d.get(ch, 0) < v:
        d[ch] = v


class Ctx:
    def __init__(self, nc, es):
        self.nc = nc
        self.es = es
        self.nsem = 0

    def new_chan(self):
        sem = self.es.enter_context(self.nc.semaphore(f"sm{self.nsem}"))
        self.nsem += 1
        c = Chan()
        c.sem = sem
        c.val = 0
        return c


class Eng:
    EPOCH = 16000

    def __init__(self, ctx, eng, name, is_pe=False, n_dma=0):
        self.ctx = ctx
        self.e = eng
        self.name = name
        self.is_pe = is_pe
        self.chan = ctx.new_chan()
        self.waited = {}
        self.dma_pool = [ctx.new_chan() for _ in range(n_dma)]
        self.dma_i = 0
        self.n = 0

    def wait_tok(self, ch, v):
        if ch is self.chan and self.is_pe:
            return
        if self.waited.get(ch, 0) >= v:
            return
        self.e.wait_ge(ch.sem, v)
        self.waited[ch] = v

    def sync(self, reads, writes):
        for b in reads:
            for ch, v in b.w.items():
                self.wait_tok(ch, v)
        for b in writes:
            for ch, v in b.w.items():
                self.wait_tok(ch, v)
            for ch, v in b.r.items():
                self.wait_tok(ch, v)

    def op(self, fn, reads, writes, *a, **k):
        self.sync(reads, writes)
        ins = fn(*a, **k)
        if self.chan.val >= self.EPOCH:
            self.chan = self.ctx.new_chan()
        ch = self.chan
        ch.val += 1
        ins.then_inc(ch.sem, 1)
        for b in reads:
            _merge(b.r, ch, ch.val)
        for b in writes:
            _merge(b.w, ch, ch.val)
        self.n += 1
        return ins

    def dma(self, out, in_, reads, writes, **k):
        ch = self.dma_pool[self.dma_i % len(self.dma_pool)]
        self.dma_i += 1
        if ch.val:
            self.wait_tok(ch, ch.val)
        self.sync(reads, writes)
        ins = self.e.dma_start(out=out, in_=in_, **k)
        ch.val += 16
        ins.then_inc(ch.sem, 16)
        for b in reads:
            _merge(b.r, ch, ch.val)
        for b in writes:
            _merge(b.w, ch, ch.val)
        self.n += 1
        return ins

    def wait_all(self, bufs):
        for b in bufs:
            for ch, v in b.tokens().items():
                self.wait_tok(ch, v)


class Prog:
    def __init__(self, n_tiles=NT, nstages=len(STAGES), dbg=None):
        self.n_tiles = n_tiles
        self.nstages = nstages
        self.full = nstages == len(STAGES)
        self.dbg = dbg
        nc = bass.Bass("TRN2", target_bir_lowering=False)
        self.nc = nc
        self.es = ExitStack()
        self.pl = piece_list()
        self.poff, self.ptot = _offsets(self.pl)
        self.apl = ada_piece_list()
        self.aoff, self.atot = _offsets(self.apl)
        self.VL, self.nv = vec_layout()
        self.seed = {}

    def sb(self, es, name, shape, dt):
        self._nalloc = getattr(self, "_nalloc", 0) + 1
        return es.enter_context(self.nc.sbuf_tensor(f"{name}_{self._nalloc}", list(shape), dt))

    def newbuf(self, name):
        b = Buf(name, self.seed)
        self.stage_bufs.append(b)
        return b

    def stage_begin(self):
        self.stage_bufs = []
        return ExitStack()

    def stage_end(self, es):
        seed = dict(self.seed)
        for b in self.stage_bufs:
            for ch, v in b.tokens().items():
                _merge(seed, ch, v)
        self.seed = seed
        es.close()

    def w_init(self):
        self.wslots = [self.sb(self.es, f"wslot{i}", [128, SLOTW], BF16) for i in range(NSLOT)]
        self.wbufs = [Buf(f"wslot{i}") for i in range(NSLOT)]
        self.wsched = []
        self.w_issued = 0
        self.w_used = 0

    def w_issue_upto(self, idx):
        while self.w_issued <= idx and self.w_issued < len(self.wsched):
            i = self.w_issued
            src, w, name = self.wsched[i]
            slot = i % NSLOT
            self.POOL.dma(self.wslots[slot][:, 0:w], src, [], [self.wbufs[slot]], max_dma_last_dim=2048)
            self.w_issued += 1

    def w_next(self, name):
        i = self.w_used
        src, w, nm = self.wsched[i]
        assert nm == name, (nm, name)
        self.w_issue_upto(i + NSLOT - 1)
        self.w_used += 1
        slot = i % NSLOT
        return self.wslots[slot], self.wbufs[slot]

    def build(self):
        nc = self.nc
        es = self.es
        ctx = Ctx(nc, es)
        self.ctx = ctx
        nt = self.n_tiles
        self.xT = nc.dram_tensor("xT", [D, S], F32, kind="ExternalInput").ap()
        self.wts = nc.dram_tensor("wts", [128, self.ptot], F32, kind="ExternalInput").ap()
        self.adaw = nc.dram_tensor("adaw", [128, self.atot], F32, kind="ExternalInput").ap()
        self.vecs_d = nc.dram_tensor("vecs", [128, self.nv], F32, kind="ExternalInput").ap()
        self.outT = nc.dram_tensor("outT", [D, S], F32, kind="ExternalOutput").ap()
        self.kTd = nc.dram_tensor("kTd", [3, 8, 128, S], BF16, kind="Internal").ap()
        self.vd = nc.dram_tensor("vd", [3, S, D], BF16, kind="Internal").ap()
        self.kv_bufs = [Buf(f"kv{s}") for s in range(NT)]

        self.PE = Eng(ctx, nc.tensor, "pe", is_pe=True)
        self.ACT = Eng(ctx, nc.scalar, "act")
        self.DVE = Eng(ctx, nc.vector, "dve")
        self.POOL = Eng(ctx, nc.gpsimd, "pool", n_dma=NSLOT + 1)
        self.SP = Eng(ctx, nc.sync, "sp", n_dma=40)
        PE, ACT, DVE, POOL, SP = self.PE, self.ACT, self.DVE, self.POOL, self.SP

        self.vecs = self.sb(es, "vecs_sb", [128, self.nv], F32)
        self.vecs_b = Buf("vecs")
        self.modT = self.sb(es, "modT", [128, 4 * 48 + 16], F32)
        self.der = self.sb(es, "der", [128, 4 * 16 + 8], F32)
        self.mod_b = Buf("mod")
        self.ones = self.sb(es, "ones", [128, 128], BF16)
        self.ones_b = Buf("ones")
        self.condb = self.sb(es, "condb", [128, 8], BF16)
        self.cond_b = Buf("cond")
        self.xTs = [self.sb(es, f"xTs{i}", [128, KC, T], F32) for i in range(2)]
        self.x_bufs = [[Buf(f"x{i}_{c}") for c in range(KC)] for i in range(2)]
        self.hT = self.sb(es, "hT", [128, KC, T], BF16)
        self.h_bs = [Buf(f"hT{c}") for c in range(KC)]
        self.sq = self.sb(es, "sq", [128, KC, T], BF16)
        self.sq_bs = [Buf(f"sq{c}") for c in range(KC)]
        self.ms_ready = None
        self.std = self.sb(es, "std", [128, T], F32)
        self.rstd = self.sb(es, "rstd", [128, T], F32)
        self.rstd2 = self.sb(es, "rstd2", [128, T], F32)
        self.rstd2_b = Buf("rstd2")
        self.std_b = Buf("std")
        self.rstd_b = Buf("rstd")
        self.ntmp = [self.sb(es, f"ntmp{i}", [128, T], F32) for i in range(2)]
        self.ntmp_b = [Buf(f"ntmp{i}") for i in range(2)]
        self.uhalo = [self.sb(es, f"uhalo{l}", [128, KC, 16], F32) for l in range(2)]
        self.uhalo_b = [Buf(f"uhalo{l}") for l in range(2)]
        self.ahalo = [self.sb(es, f"ahalo{l}", [128, NFC, 2], F32) for l in range(4)]
        self.ahalo_b = [Buf(f"ahalo{l}") for l in range(4)]
        self.masks = self.sb(es, "masks", [128, 48, 256], BF16)
        self.masks_b = Buf("masks")
        self.masks2 = self.sb(es, "masks2", [128, 3, NH, 32], BF16)
        self.ps = [es.enter_context(nc.psum_tensor(f"ps{i}", [128, 512], F32)) for i in range(8)]
        self.ps_b = [Buf(f"ps{i}") for i in range(8)]
        self.w_init()

        def ada_sched(l):
            for n, w in self.apl:
                if n.startswith(f"ada{l}_") or (l == 4 and n.startswith("kvada")):
                    o, _ = self.aoff[n]
                    self.wsched.append((self.adaw[:, o:o + w], w, n))
        stages = STAGES[:self.nstages]
        nP = min(5, len(stages))
        order = []
        for s in range(nt):
            order += [(s, st) for st in stages[:nP]]
            if s >= 1:
                order += [(s - 1, st) for st in stages[nP:]] + [(s - 1, "out")]
        order += [(nt - 1, st) for st in stages[nP:]] + [(nt - 1, "out")]
        for (s, st) in order:
            if st == "out":
                continue
            if s == 0 and st.startswith("mix"):
                ada_sched(int(st[3]))
            if s == 0 and st == "kv":
                ada_sched(4)
            for n, w in stage_pieces(st):
                o, _ = self.poff[n]
                self.wsched.append((self.wts[:, o:o + w], w, n))

        self.prologue()
        self.tile_init()
        for (s, st) in order:
            self.run_stage(s, st)
        for b in self.out_wait:
            SP.wait_all([b])
        return nc

    def vcol(self, name, c0=0, n=1):
        o, _ = self.VL[name]
        return self.vecs[:, o + c0:o + c0 + n]

    def prologue(self):
        nc = self.nc
        PE, ACT, DVE, POOL, SP = self.PE, self.ACT, self.DVE, self.POOL, self.SP
        SP.dma(self.vecs[:, :], self.vecs_d[:, :], [], [self.vecs_b])
        DVE.op(nc.vector.memset, [], [self.ones_b], self.ones[:, :], 1.0)
        for l in range(2):
            DVE.op(nc.vector.memset, [], [self.uhalo_b[l]], self.uhalo[l][:, :, :], 0.0)
        for l in range(4):
            DVE.op(nc.vector.memset, [], [self.ahalo_b[l]], self.ahalo[l][:, :, :], 0.0)
        ACT.op(nc.scalar.activation, [self.vecs_b], [self.cond_b], out=self.condb[:, :], in_=self.vcol("c", 0, 8), func=AF.Silu)
        if self.nstages > 5:
            for g in range(3):
                for h in range(NH):
                    ACT.op(nc.scalar.activation, [self.vecs_b], [self.masks_b], out=self.masks[:, g * NH + h, :],
                           in_=self.vcol("dist", 0, 256), func=AF.Exp, scale=-float(SLOPES[g, h]) * DIL[g])
            for i in range(3):
                for h in range(NH):
                    ACT.op(nc.scalar.activation, [self.vecs_b], [self.masks_b], out=self.masks2[:, i, h, :],
                           in_=self.vcol("dist2", i * 32, 32), func=AF.Exp, scale=-float(SLOPES[2, h]) * DIL[2])

    def compute_mod(self, l):
        nc = self.nc
        PE, ACT, DVE, POOL, SP = self.PE, self.ACT, self.DVE, self.POOL, self.SP
        pb = self.psrr()
        if True:
            ncol = 48 if l < 4 else 16
            npieces = 12 if l < 4 else 4
            pt, pbuf = self.ps[pb], self.ps_b[pb]
            first = True
            for i in range(npieces):
                slot, sbuf_ = self.w_next(f"ada{l}_{i}" if l < 4 else f"kvada_{i}")
                for mm in range(4):
                    col = i * 4 + mm
                    for kc in range(KC):
                        PE.op(nc.tensor.matmul, [sbuf_, self.cond_b], [pbuf], pt[:, col:col + 1],
                              lhsT=slot[:, kc * 512 + mm * 128: kc * 512 + mm * 128 + 128], rhs=self.condb[:, kc:kc + 1],
                              start=first, stop=(kc == KC - 1))
                        first = False
            bname = f"ada_b{l}" if l < 4 else "kvb"
            DVE.op(nc.vector.tensor_tensor, [pbuf, self.vecs_b], [self.mod_b], out=self.modT[:, l * 48:l * 48 + ncol],
                   in0=pt[:, 0:ncol], in1=self.vcol(bname, 0, ncol), op=ALU.add)
        if l < 4:
            for j, (gn, sc0) in enumerate(((f"n1g{l}", 8), (f"n2g{l}", 32))):
                DVE.op(nc.vector.scalar_tensor_tensor, [self.mod_b, self.vecs_b], [self.mod_b],
                       out=self.der[:, l * 16 + j * 8: l * 16 + j * 8 + 8], in0=self.modT[:, l * 48 + sc0: l * 48 + sc0 + 8],
                       scalar=1.0, in1=self.vcol(gn, 0, 8), op0=ALU.add, op1=ALU.mult)
        else:
            DVE.op(nc.vector.scalar_tensor_tensor, [self.mod_b, self.vecs_b], [self.mod_b],
                   out=self.der[:, 64:72], in0=self.modT[:, 192 + 8:192 + 16], scalar=1.0, in1=self.vcol("kvg", 0, 8),
                   op0=ALU.add, op1=ALU.mult)

    def A(self, l, which, c):
        return self.der[:, l * 16 + which * 8 + c: l * 16 + which * 8 + c + 1]

    def M(self, l, j, c):
        return self.modT[:, l * 48 + j * 8 + c: l * 48 + j * 8 + c + 1]

    def norm(self, x, xb, acol, bcol, out=None, out_b=None, final=False, keep=None, reuse=None):
        nc = self.nc
        PE, ACT, DVE = self.PE, self.ACT, self.DVE
        if reuse is not None:
            self.rstd_cur = ((self.rstd, self.rstd_b), (self.rstd2, self.rstd2_b))[reuse]
        else:
            if self.ms_ready is None:
                for c in range(KC):
                    ACT.op(nc.scalar.activation, [xb[c]], [self.sq_bs[c]], out=self.sq[:, c, :], in_=x[:, c, :], func=AF.Square)
                self.emit_ms()
            pt, pbuf = self.ms_ready
            self.ms_ready = None
            ACT.op(nc.scalar.activation, [pbuf, self.eps_b], [self.std_b], out=self.std[:, :], in_=pt[:, :], func=AF.Sqrt,
                   scale=1.0 / D, bias=self.epsc[:, 0:1])
            if keep is not None:
                kt_, kb_ = ((self.rstd, self.rstd_b), (self.rstd2, self.rstd2_b))[keep]
                DVE.op(nc.vector.reciprocal, [self.std_b], [kb_], out=kt_[:, :], in_=self.std[:, :])
                self.rstd_cur = (kt_, kb_)
            else:
                DVE.op(nc.vector.reciprocal, [self.std_b], [pbuf], out=pt[:, :], in_=self.std[:, :])
                self.rstd_cur = (pt, pbuf)
        rt, rb = self.rstd_cur
        for c in range(KC):
            tb = self.ntmp_b[c % 2]
            tt = self.ntmp[c % 2]
            DVE.op(nc.vector.tensor_tensor, [xb[c], rb], [tb], out=tt[:, :], in0=x[:, c, :], in1=rt[:, :], op=ALU.mult)
            if final:
                ACT.op(nc.scalar.activation, [tb, self.vecs_b], [out_b], out=out[:, c, :], in_=tt[:, :], func=AF.Copy,
                       scale=acol(c))
            else:
                ACT.op(nc.scalar.activation, [tb, self.mod_b], [self.h_bs[c]], out=self.hT[:, c, :], in_=tt[:, :], func=AF.Identity,
                       scale=acol(c), bias=bcol(c))

    def emit_ms(self):
        nc = self.nc
        pi = self.psrr()
        pt, pbuf = self.ps[pi], self.ps_b[pi]
        for c in range(KC):
            self.PE.op(nc.tensor.matmul, [self.sq_bs[c], self.ones_b], [pbuf], pt[:, :], lhsT=self.ones[:, :], rhs=self.sq[:, c, :],
                       start=(c == 0), stop=(c == KC - 1))
        self.ms_ready = (pt, pbuf)

    def residual(self, x, xb, m, pt, pbuf, gcol):
        nc = self.nc
        self.DVE.op(nc.vector.scalar_tensor_tensor, [pbuf, self.mod_b, xb[m]], [xb[m]], out=x[:, m, :], in0=pt[:, :], scalar=gcol,
                    in1=x[:, m, :], op0=ALU.mult, op1=ALU.add)
        self.ACT.op(nc.scalar.activation, [xb[m]], [self.sq_bs[m]], out=self.sq[:, m, :], in_=x[:, m, :], func=AF.Square)

    def psrr(self):
        i = self._psi
        self._psi = (self._psi + 1) % 8
        return i

    def tile_init(self):
        nc = self.nc
        self._psi = 0
        self.out_wait = []
        self.epsc = self.sb(self.es, "epsc", [128, 1], F32)
        self.eps_b = Buf("eps")
        self.DVE.op(nc.vector.memset, [], [self.eps_b], self.epsc[:, :], EPS)
        self.x_loaded = set()
        self.load_x(0)
        self.load_x(1)

    def load_x(self, s2):
        if s2 >= self.n_tiles or s2 in self.x_loaded:
            return
        self.x_loaded.add(s2)
        src_ = self.xT.rearrange("(c p) t -> p c t", p=128)
        self.SP.dma(self.xTs[s2 % 2][:, :, :], src_[:, :, s2 * T:(s2 + 1) * T], [], self.x_bufs[s2 % 2])

    def run_stage(self, s, st):
        nc = self.nc
        PE, ACT, DVE, POOL, SP = self.PE, self.ACT, self.DVE, self.POOL, self.SP
        x = self.xTs[s % 2]
        xb = self.x_bufs[s % 2]
        if st == "mix0":
            self.load_x(s)
            self.ms_ready = None
        if s == 0 and st.startswith("mix"):
            self.compute_mod(int(st[3]))
        if s == 0 and st == "kv":
            self.compute_mod(4)
        if st == "kv":
            self.kv_stage(s, x, xb)
        elif st.startswith("mix"):
            l = int(st[3])
            if l < 2:
                self.pool_mixer(l, s, x, xb)
            else:
                self.attn_mixer(l, s, x, xb)
        elif st.startswith("ffn"):
            self.ffn(int(st[3]), s, x, xb)
        else:
            es = self.stage_begin()
            o = self.sb(es, "otile", [128, KC, T], F32)
            ob = self.newbuf("otile")
            if self.full:
                self.norm(x, xb, lambda c: self.vcol("fg", c, 1), None, out=o, out_b=ob, final=True)
            else:
                self.ms_ready = None
                ACT.op(nc.scalar.activation, xb, [ob], out=o[:, :, :], in_=x[:, :, :], func=AF.Copy)
            SP.dma(self.outT.rearrange("(c p) t -> p c t", p=128)[:, :, s * T:(s + 1) * T], o[:, :, :], [ob], [])
            self.out_wait.append(ob)
            self.stage_end(es)
            self.load_x(s + 2)

    def pool_mixer(self, l, s, x, xb):
        nc = self.nc
        PE, ACT, DVE = self.PE, self.ACT, self.DVE
        es = self.stage_begin()
        U = self.sb(es, "poolU", [128, KC, 16 + T], F32)
        Ub = self.newbuf("U")
        A_ = self.sb(es, "poolA", [128, KC, 16 + T], F32)
        Ab = self.newbuf("A")
        B_ = self.sb(es, "poolB", [128, KC, 16 + T], F32)
        Bb = self.newbuf("B")
        Pb = self.sb(es, "poolP", [128, KC, T], BF16)
        Pbb = self.newbuf("P")
        Zb = self.sb(es, "poolZ", [128, KC, T], BF16)
        Zbb = self.newbuf("Z")
        self.norm(x, xb, lambda c: self.A(l, 0, c), lambda c: self.M(l, 0, c))
        DVE.op(nc.vector.tensor_copy, [self.uhalo_b[l]], [Ub], out=U[:, :, 0:16], in_=self.uhalo[l][:, :, :])
        for half in range(2):
            slot, sbuf_ = self.w_next(f"win{l}_{half}")
            for mm in range(4):
                m = half * 4 + mm
                pi = self.psrr()
                pt, pbuf = self.ps[pi], self.ps_b[pi]
                for kc in range(KC):
                    PE.op(nc.tensor.matmul, [sbuf_, self.h_bs[kc]], [pbuf], pt[:, :], lhsT=slot[:, kc * 512 + mm * 128: kc * 512 + mm * 128 + 128],
                          rhs=self.hT[:, kc, :], start=(kc == 0), stop=(kc == KC - 1))
                ACT.op(nc.scalar.activation, [pbuf], [Ub], out=U[:, m, 16:16 + T], in_=pt[:, :], func=AF.Copy)
        DVE.op(nc.vector.tensor_copy, [Ub], [self.uhalo_b[l]], out=self.uhalo[l][:, :, :], in_=U[:, :, T:T + 16])
        W_ = 16 + T
        DVE.op(nc.vector.tensor_tensor, [Ub], [Ab], out=A_[:, :, 1:W_], in0=U[:, :, 1:W_], in1=U[:, :, 0:W_ - 1], op=ALU.add)
        DVE.op(nc.vector.tensor_tensor, [Ab], [Bb], out=B_[:, 2:8, 3:W_], in0=A_[:, 2:8, 3:W_], in1=A_[:, 2:8, 1:W_ - 2], op=ALU.add)
        DVE.op(nc.vector.tensor_tensor, [Bb], [Ab], out=A_[:, 4:8, 7:W_], in0=B_[:, 4:8, 7:W_], in1=B_[:, 4:8, 3:W_ - 4], op=ALU.add)
        DVE.op(nc.vector.tensor_tensor, [Ab], [Bb], out=B_[:, 6:8, 15:W_], in0=A_[:, 6:8, 15:W_], in1=A_[:, 6:8, 7:W_ - 8], op=ALU.add)
        srcs = [(A_, Ab), (B_, Bb), (A_, Ab), (B_, Bb)]
        for g in range(4):
            St, Sb_ = srcs[g]
            DVE.op(nc.vector.scalar_tensor_tensor, [Sb_, Ub], [Pbb], out=Pb[:, 2 * g:2 * g + 2, :], in0=St[:, 2 * g:2 * g + 2, 16:16 + T],
                   scalar=1.0 / POOLW[g], in1=U[:, 2 * g:2 * g + 2, 16:16 + T], op0=ALU.mult, op1=ALU.subtract)
            if s == 0:
                o, _ = self.VL["invc"]
                for cc in range(2):
                    c = 2 * g + cc
                    tb, tt = self.ntmp_b[cc], self.ntmp[cc]
                    DVE.op(nc.vector.tensor_tensor, [Sb_, self.vecs_b], [tb], out=tt[:, 0:16], in0=St[:, c, 16:32],
                           in1=self.vecs[:, o + g * 16:o + g * 16 + 16], op=ALU.mult)
                    DVE.op(nc.vector.tensor_tensor, [tb, Ub], [Pbb], out=Pb[:, c, 0:16], in0=tt[:, 0:16], in1=U[:, c, 16:32], op=ALU.subtract)
        slot, sbuf_ = self.w_next(f"wgrp{l}")
        for g in range(4):
            for mo in range(2):
                c = 2 * g + mo
                pi = self.psrr()
                pt, pbuf = self.ps[pi], self.ps_b[pi]
                for ki in range(2):
                    base = (g * 2 + ki) * 256 + mo * 128
                    PE.op(nc.tensor.matmul, [sbuf_, Pbb], [pbuf], pt[:, :], lhsT=slot[:, base:base + 128], rhs=Pb[:, 2 * g + ki, :],
                          start=(ki == 0), stop=(ki == 1))
                ACT.op(nc.scalar.activation, [pbuf, self.vecs_b], [Zbb], out=Zb[:, c, :], in_=pt[:, :], func=AF.Copy,
                       scale=self.vcol(f"psc{l}", c, 1))
        for half in range(2):
            slot, sbuf_ = self.w_next(f"wout{l}_{half}")
            for mm in range(4):
                m = half * 4 + mm
                pi = self.psrr()
                pt, pbuf = self.ps[pi], self.ps_b[pi]
                for kc in range(KC):
                    PE.op(nc.tensor.matmul, [sbuf_, Zbb], [pbuf], pt[:, :], lhsT=slot[:, kc * 512 + mm * 128: kc * 512 + mm * 128 + 128],
                          rhs=Zb[:, kc, :], start=(kc == 0), stop=(kc == KC - 1))
                self.residual(x, xb, m, pt, pbuf, self.M(l, 2, m))
        self.emit_ms()
        self.stage_end(es)

    def ffn(self, l, s, x, xb):
        nc = self.nc
        PE, ACT, DVE = self.PE, self.ACT, self.DVE
        es = self.stage_begin()
        gT = self.sb(es, "gT", [128, NFC, T], BF16)
        gbs = [self.newbuf(f"gT{i}") for i in range(NFC)]
        abuf = [self.sb(es, f"abuf{i}", [128, 2 + T], F32) for i in range(2)]
        ab = [self.newbuf(f"abuf{i}") for i in range(2)]
        c1 = [self.sb(es, f"c1_{i}", [128, T], F32) for i in range(2)]
        c1b = [self.newbuf(f"c1_{i}") for i in range(2)]
        c2 = [self.sb(es, f"c2_{i}", [128, T], F32) for i in range(2)]
        c2b = [self.newbuf(f"c2_{i}") for i in range(2)]
        vsb = [self.sb(es, f"vsb{i}", [128, T], F32) for i in range(2)]
        vsbb = [self.newbuf(f"vsb{i}") for i in range(2)]
        self.norm(x, xb, lambda c: self.A(l, 1, c), lambda c: self.M(l, 3, c))
        cwo, _ = self.VL[f"cw{l}"]
        cbo, _ = self.VL[f"cb{l}"]
        for i in range(11):
            slot, sbuf_ = self.w_next(f"wup{l}_{i}")
            for jj in range(2):
                fc = 2 * i + jj
                par = fc % 2
                pa, pab = self.ps[par * 2], self.ps_b[par * 2]
                pv, pvb = self.ps[par * 2 + 1], self.ps_b[par * 2 + 1]
                for av, (pt, pbuf) in enumerate(((pa, pab), (pv, pvb))):
                    base = (jj * 2 + av) * 1024
                    for kc in range(KC):
                        PE.op(nc.tensor.matmul, [sbuf_, self.h_bs[kc]], [pbuf], pt[:, :], lhsT=slot[:, base + kc * 128: base + kc * 128 + 128],
                              rhs=self.hT[:, kc, :], start=(kc == 0), stop=(kc == KC - 1))
                A_, Ab = abuf[par], ab[par]
                ACT.op(nc.scalar.activation, [self.ahalo_b[l]], [Ab], out=A_[:, 0:2], in_=self.ahalo[l][:, fc, :], func=AF.Copy)
                ACT.op(nc.scalar.activation, [pab], [Ab], out=A_[:, 2:2 + T], in_=pa[:, :], func=AF.Copy)
                ACT.op(nc.scalar.activation, [Ab], [self.ahalo_b[l]], out=self.ahalo[l][:, fc, :], in_=A_[:, T:T + 2], func=AF.Copy)
                ACT.op(nc.scalar.activation, [pab, self.vecs_b], [c1b[par]], out=c1[par][:, :], in_=pa[:, :], func=AF.Identity,
                       scale=self.vecs[:, cwo + 2 * NFC + fc: cwo + 2 * NFC + fc + 1], bias=self.vecs[:, cbo + fc:cbo + fc + 1])
                ACT.op(nc.scalar.activation, [pvb], [vsbb[par]], out=vsb[par][:, :], in_=pv[:, :], func=AF.Copy)
                DVE.op(nc.vector.scalar_tensor_tensor, [Ab, self.vecs_b, c1b[par]], [c2b[par]], out=c2[par][:, :], in0=A_[:, 1:1 + T],
                       scalar=self.vecs[:, cwo + NFC + fc: cwo + NFC + fc + 1], in1=c1[par][:, :], op0=ALU.mult, op1=ALU.add)
                DVE.op(nc.vector.scalar_tensor_tensor, [Ab, self.vecs_b, c2b[par]], [c1b[par]], out=c1[par][:, :], in0=A_[:, 0:T],
                       scalar=self.vecs[:, cwo + fc: cwo + fc + 1], in1=c2[par][:, :], op0=ALU.mult, op1=ALU.add)
                ACT.op(nc.scalar.activation, [c1b[par]], [c2b[par]], out=c2[par][:, :], in_=c1[par][:, :], func=AF.Silu)
                DVE.op(nc.vector.tensor_tensor, [c2b[par], vsbb[par]], [gbs[fc]], out=gT[:, fc, :], in0=c2[par][:, :], in1=vsb[par][:, :], op=ALU.mult)
        for r in range(2):
            b0 = 4 if r == 0 else 0
            for j in range(3):
                slot, sbuf_ = self.w_next(f"wdn{l}_{r}_{j}")
                nf = 8 if j < 2 else 6
                for fl in range(nf):
                    fc = 8 * j + fl
                    for mm in range(4):
                        PE.op(nc.tensor.matmul, [sbuf_, gbs[fc]], [self.ps_b[b0 + mm]], self.ps[b0 + mm][:, :],
                              lhsT=slot[:, fl * 512 + mm * 128: fl * 512 + mm * 128 + 128], rhs=gT[:, fc, :],
                              start=(fc == 0), stop=(fc == NFC - 1))
            for mm in range(4):
                m = r * 4 + mm
                self.residual(x, xb, m, self.ps[b0 + mm], self.ps_b[b0 + mm], self.M(l, 5, m))
        self._psi = 4
        self.emit_ms()
        self.stage_end(es)

    def kv_stage(self, s, x, xb):
        nc = self.nc
        PE, ACT, DVE, SP = self.PE, self.ACT, self.DVE, self.SP
        es = self.stage_begin()
        kst = [self.sb(es, f"kst{i}", [128, T], BF16) for i in range(6)]
        kstb = [self.newbuf(f"kst{i}") for i in range(6)]
        vst = [self.sb(es, f"vst{i}", [128, 512], BF16) for i in range(4)]
        vstb = [self.newbuf(f"vst{i}") for i in range(4)]
        kvb = self.kv_bufs[s]
        self.norm(x, xb, lambda c: self.der[:, 64 + c:65 + c], lambda c: self.modT[:, 192 + c:193 + c], keep=s % 2)
        h16 = self.sb(es, "h16", [128, KC, T], BF16)
        h16b = self.newbuf("h16")
        for kc in range(KC):
            DVE.op(nc.vector.tensor_copy, [self.h_bs[kc]], [h16b], out=h16[:, kc, :].rearrange("p (r j) -> p r j", r=16),
                   in_=self.hT[:, kc, :].rearrange("p (j r) -> p r j", r=16))
        n = 0
        for i in range(6):
            slot, sbuf_ = self.w_next(f"wk_{i}")
            g = i // 2
            d = DIL[g]
            for mm in range(4):
                hp = (i % 2) * 4 + mm
                pi = self.psrr()
                pt, pbuf = self.ps[pi], self.ps_b[pi]
                for kc in range(KC):
                    PE.op(nc.tensor.matmul, [sbuf_, self.h_bs[kc]], [pbuf], pt[:, :], lhsT=slot[:, kc * 512 + mm * 128: kc * 512 + mm * 128 + 128],
                          rhs=self.hT[:, kc, :], start=(kc == 0), stop=(kc == KC - 1))
                kt, ktb = kst[n % 6], kstb[n % 6]
                n += 1
                nj = T // d
                if d == 1:
                    ACT.op(nc.scalar.activation, [pbuf], [ktb], out=kt[:, :], in_=pt[:, :], func=AF.Copy)
                    SP.dma(self.kTd[g, hp, :, s * T:(s + 1) * T], kt[:, :], [ktb], [kvb])
                else:
                    ACT.op(nc.scalar.activation, [pbuf], [ktb], out=kt[:, :].rearrange("p (r j) -> p r j", r=d),
                           in_=pt[:, :].rearrange("p (j r) -> p r j", r=d), func=AF.Copy)
                    dst = self.kTd[g, hp, :, :].rearrange("p (r j) -> p r j", r=d)[:, :, s * nj:(s + 1) * nj]
                    SP.dma(dst, kt[:, :].rearrange("p (r j) -> p r j", r=d), [ktb], [kvb])
        n = 0
        for i in range(6):
            slot, sbuf_ = self.w_next(f"wv_{i}")
            g = i // 2
            d = DIL[g]
            half = i % 2
            for ch in range(4):
                if d == 1:
                    cols = lambda kc: self.hT[:, kc, ch * 128:(ch + 1) * 128]
                elif d == 4:
                    cols = lambda kc: self.hT[:, kc, :].rearrange("p (j r) -> p r j", r=4)[:, ch, :]
                else:
                    cols = lambda kc: h16[:, kc, ch * 128:(ch + 1) * 128]
                pi = self.psrr()
                pt, pbuf = self.ps[pi], self.ps_b[pi]
                for kc in range(KC):
                    PE.op(nc.tensor.matmul, [sbuf_, self.h_bs[kc], h16b], [pbuf], pt[:, :], lhsT=cols(kc), rhs=slot[:, kc * 512:(kc + 1) * 512],
                          start=(kc == 0), stop=(kc == KC - 1))
                vt, vtb = vst[n % 4], vstb[n % 4]
                n += 1
                DVE.op(nc.vector.tensor_copy, [pbuf], [vtb], out=vt[:, 0:512], in_=pt[:, :])
                if d == 1:
                    r0 = s * T + ch * 128
                    SP.dma(self.vd[g, r0:r0 + 128, half * 512:(half + 1) * 512], vt[:, 0:512], [vtb], [kvb])
                elif d == 4:
                    r0 = ch * 1024 + s * 128
                    SP.dma(self.vd[g, r0:r0 + 128, half * 512:(half + 1) * 512], vt[:, 0:512], [vtb], [kvb])
                else:
                    dst = self.vd[g, :, half * 512:(half + 1) * 512].rearrange("(r j) f -> r j f", r=16)[ch * 4:(ch + 1) * 4, s * 32:(s + 1) * 32, :]
                    SP.dma(dst, vt[:, 0:512], [vtb], [kvb])
        self.stage_end(es)

    def attn_mixer(self, l, s, x, xb):
        nc = self.nc
        PE, ACT, DVE, SP = self.PE, self.ACT, self.DVE, self.SP
        es = self.stage_begin()
        oT = self.sb(es, "oT", [128, KC, T], BF16)
        oTb = self.newbuf("oT")
        qT = [self.sb(es, f"qT{i}", [128, 3, T], BF16) for i in range(2)]
        qTb = [self.newbuf(f"qT{i}") for i in range(2)]
        KW = 640 + 1024 + 2560
        kt = [self.sb(es, f"ktile{i}", [128, KW], BF16) for i in range(2)]
        ktb = [self.newbuf(f"ktile{i}") for i in range(2)]
        NVC = 5 + 8 + 32
        vt = [self.sb(es, f"vtile{i}", [128, NVC, 128], BF16) for i in range(2)]
        vtb = [self.newbuf(f"vtile{i}") for i in range(2)]
        NB = 3
        ET = [self.sb(es, f"E{i}", [128, T], BF16) for i in range(NB)]
        ETb = [self.newbuf(f"E{i}") for i in range(NB)]
        PT = [self.sb(es, f"P{i}", [128, T], BF16) for i in range(NB)]
        PTb = [self.newbuf(f"P{i}") for i in range(NB)]
        rD = self.sb(es, "rD", [128, T], F32)
        rDb = self.newbuf("rD")
        self.norm(x, xb, lambda c: self.A(l, 0, c), lambda c: self.M(l, 0, c), reuse=(s % 2 if l == 2 else None))
        kv_reads = [self.kv_bufs[t] for t in range(max(0, s - 4), s + 1)]
        na = min(128, 32 * s)

        def prep(hp):
            par = hp % 2
            K_, Kb = kt[par], ktb[par]
            V_, Vb = vt[par], vtb[par]
            lo0 = 128 if s == 0 else 0
            SP.dma(K_[:, lo0:640], self.kTd[0, hp, :, s * T - 128 + lo0: s * T + 512], kv_reads, [Kb])
            src_ = self.vd[0, s * T - 128 + lo0: s * T + 512, hp * 128:(hp + 1) * 128].rearrange("(c p) f -> p c f", p=128)
            SP.dma(V_[:, lo0 // 128:5, :], src_, kv_reads, [Vb])
            srck = self.kTd[1, hp, :, :].rearrange("p (r j) -> p r j", r=4)[:, :, 128 * (s - 1) + lo0: 128 * (s + 1)]
            SP.dma(K_[:, 640:640 + 1024].rearrange("p (r j) -> p r j", r=4)[:, :, lo0:256], srck, kv_reads, [Kb])
            c0 = lo0 // 128
            v1 = self.vd[1, :, hp * 128:(hp + 1) * 128].rearrange("(r c p) f -> p r c f", r=4, p=128)
            for c2 in range(c0, 2):
                SP.dma(V_[:, 5:13, :].rearrange("p (r c) f -> p r c f", r=4)[:, :, c2, :], v1[:, :, s - 1 + c2, :], kv_reads, [Vb])
            k2 = K_[:, 640 + 1024:].rearrange("p (r j) -> p r j", r=16)
            srck = self.kTd[2, hp, :, :].rearrange("p (r j) -> p r j", r=16)[:, :, 32 * s - na: 32 * s + 32]
            SP.dma(k2[:, :, 128 - na:160], srck, kv_reads, [Kb])
            v2 = self.vd[2, :, hp * 128:(hp + 1) * 128].rearrange("(r j) f -> j r f", r=16)
            if na > 0:
                SP.dma(V_[0:na, 13:29, :], v2[32 * s - na:32 * s, :, :], kv_reads, [Vb])
            SP.dma(V_[0:32, 29:45, :], v2[32 * s:32 * s + 32, :, :], kv_reads, [Vb])
            slot, sbuf_ = self.w_next(f"wq{l}_{hp}")
            Q_, Qb = qT[par], qTb[par]
            for g in range(3):
                pt, pbuf = self.ps[3], self.ps_b[3]
                for kc in range(KC):
                    base = (g * 8 + kc) * 128
                    PE.op(nc.tensor.matmul, [sbuf_, self.h_bs[kc]], [pbuf], pt[:, :], lhsT=slot[:, base:base + 128], rhs=self.hT[:, kc, :],
                          start=(kc == 0), stop=(kc == KC - 1))
                d = DIL[g]
                if d == 1:
                    ACT.op(nc.scalar.activation, [pbuf], [Qb], out=Q_[:, g, :], in_=pt[:, :], func=AF.Copy)
                else:
                    ACT.op(nc.scalar.activation, [pbuf], [Qb], out=Q_[:, g, :].rearrange("p (r j) -> p r j", r=d),
                           in_=pt[:, :].rearrange("p (j r) -> p r j", r=d), func=AF.Copy)

        units_all = []
        for hp in range(8):
            ulist = []
            for g in range(3):
                batches = []
                if g < 2:
                    for kind in (0, 1):
                        us = []
                        for u in range(4):
                            if g == 0:
                                if kind == 0 and s == 0 and u == 0:
                                    continue
                                us.append((u, (u + kind) * 128, u + kind))
                            else:
                                if kind == 0 and s == 0:
                                    continue
                                us.append((u, 640 + u * 256 + kind * 128, 5 + u * 2 + kind))
                        if us:
                            batches.append((kind, 128, 128, us))
                else:
                    if na > 0:
                        batches.append((0, 32, na, [(r, 640 + 1024 + r * 160 + 128 - na, 13 + r) for r in range(16)]))
                    batches.append((1, 32, 32, [(r, 640 + 1024 + r * 160 + 128, 29 + r) for r in range(16)]))
                for (kind, nq, nk, us) in batches:
                    for hh in range(2):
                        ulist.append(dict(hp=hp, g=g, kind=kind, nq=nq, nk=nk, us=us, hh=hh))
            ulist[0]["first_of_hp"] = True
            ulist[-1]["last_of_hp"] = True
            seen = set()
            for ud in ulist:
                if ud["hh"] not in seen:
                    ud["first_pv"] = True
                    seen.add(ud["hh"])
            units_all += ulist

        def s_phase(i, ud):
            hp, g, kind, nq, nk, us, hh = ud["hp"], ud["g"], ud["kind"], ud["nq"], ud["nk"], ud["us"], ud["hh"]
            par = hp % 2
            K_, Kb = kt[par], ktb[par]
            Q_, Qb = qT[par], qTb[par]
            h = hp * 2 + hh
            r0, r1 = hh * 64, hh * 64 + 64
            si = i % NB
            Sp, Spb = self.ps[si], self.ps_b[si]
            for (u, kcol, vch) in us:
                PE.op(nc.tensor.matmul, [Kb, Qb], [Spb], Sp[0:nk, u * nq:(u + 1) * nq], lhsT=K_[r0:r1, kcol:kcol + nk],
                      rhs=Q_[r0:r1, g, u * nq:(u + 1) * nq], start=True, stop=True)
            u0 = us[0][0]
            u1 = us[-1][0] + 1
            nu = u1 - u0
            E_, Eb = ET[si], ETb[si]
            P_, Pb_ = PT[si], PTb[si]
            ACT.op(nc.scalar.activation, [Spb], [Eb], out=E_[0:nk, u0 * nq:u1 * nq], in_=Sp[0:nk, u0 * nq:u1 * nq],
                   func=AF.Exp, scale=HD ** -0.5)
            if g == 2 and kind == 0 and nk < 128:
                mk = self.masks2[0:nk, nk // 32 - 1, h, 0:32]
            else:
                mk = self.masks[0:nk, g * NH + h, kind * 128: kind * 128 + nq]
            DVE.op(nc.vector.tensor_tensor, [Eb, self.masks_b], [Pb_],
                   out=P_[0:nk, u0 * nq:u1 * nq].rearrange("p (u q) -> p u q", u=nu),
                   in0=E_[0:nk, u0 * nq:u1 * nq].rearrange("p (u q) -> p u q", u=nu),
                   in1=mk.unsqueeze(1).to_broadcast([nk, nu, nq]), op=ALU.mult)

        def pv_phase(i, ud):
            hp, g, kind, nq, nk, us, hh = ud["hp"], ud["g"], ud["kind"], ud["nq"], ud["nk"], ud["us"], ud["hh"]
            par = hp % 2
            V_, Vb = vt[par], vtb[par]
            r0, r1 = hh * 64, hh * 64 + 64
            si = i % NB
            P_, Pb_ = PT[si], PTb[si]
            Np, Npb = self.ps[4 + par], self.ps_b[4 + par]
            Dp, Dpb = self.ps[6 + par], self.ps_b[6 + par]
            d = DIL[g]
            first = ud.get("first_pv", False)
            for (u, kcol, vch) in us:
                if d == 1:
                    No = Np[r0:r1, u * 128:(u + 1) * 128]
                    Do = Dp[r0:r1, u * 128:(u + 1) * 128]
                else:
                    No = Np[r0:r1, :].rearrange("p (j r) -> p r j", r=d)[:, u, :]
                    Do = Dp[r0:r1, :].rearrange("p (j r) -> p r j", r=d)[:, u, :]
                PE.op(nc.tensor.matmul, [Vb, Pb_], [Npb], No, lhsT=V_[0:nk, vch, r0:r1], rhs=P_[0:nk, u * nq:(u + 1) * nq],
                      start=first, stop=False, skip_group_check=True)
                PE.op(nc.tensor.matmul, [self.ones_b, Pb_], [Dpb], Do, lhsT=self.ones[0:nk, 0:64], rhs=P_[0:nk, u * nq:(u + 1) * nq],
                      start=first, stop=False, skip_group_check=True)
                first = False
            if ud.get("last_of_hp"):
                DVE.op(nc.vector.reciprocal, [Dpb], [rDb], out=rD[:, :], in_=Dp[:, :])
                DVE.op(nc.vector.tensor_tensor, [Npb, rDb], [oTb], out=oT[:, hp, :], in0=Np[:, :], in1=rD[:, :], op=ALU.mult)

        LOOK = 2
        n = len(units_all)
        prep(0)
        prep(1)
        for i in range(n + LOOK):
            if i < n:
                s_phase(i, units_all[i])
            if i - LOOK >= 0:
                ud = units_all[i - LOOK]
                pv_phase(i - LOOK, ud)
                if ud.get("last_of_hp") and ud["hp"] + 2 < 8:
                    prep(ud["hp"] + 2)
        self._psi = 0
        for half in range(2):
            slot, sbuf_ = self.w_next(f"wo{l}_{half}")
            for mm in range(4):
                m = half * 4 + mm
                pi = self.psrr()
                pt, pbuf = self.ps[pi], self.ps_b[pi]
                for kc in range(KC):
                    PE.op(nc.tensor.matmul, [sbuf_, oTb], [pbuf], pt[:, :], lhsT=slot[:, kc * 512 + mm * 128: kc * 512 + mm * 128 + 128],
                          rhs=oT[:, kc, :], start=(kc == 0), stop=(kc == KC - 1))
                self.residual(x, xb, m, pt, pbuf, self.M(l, 2, m))
        self.emit_ms()
        self.stage_end(es)


_CACHE = {}


def get_prog(n_tiles=NT, nstages=len(STAGES)):
    key = (n_tiles, nstages)
    if key not in _CACHE:
        p = Prog(n_tiles, nstages)
        p.build()
        _CACHE[key] = p
    return _CACHE[key]


def kernel(**inputs):
    inp = {k: np.asarray(v) for k, v in inputs.items()}
    W, A, per_core = host_prepare(inp)
    p = get_prog()
    in_maps = [{"xT": pc["xT"], "wts": W, "adaw": A, "vecs": pc["vecs"]} for pc in per_core]
    res = run_bass_kernel_spmd(p.nc, in_maps, core_ids=list(range(8)))
    out = np.stack([np.ascontiguousarray(r["outT"].T) for r in res.results], axis=0)
    return out.astype(np.float32)
```

```python
import math
from contextlib import ExitStack

import numpy as np
import concourse.bass as bass
import concourse.mybir as mybir
from concourse.bass_utils import run_bass_kernel_spmd

F32 = mybir.dt.float32
BF16 = mybir.dt.bfloat16
AF = mybir.ActivationFunctionType
ALU = mybir.AluOpType

D = 1024
S = 4096
FF = 2816
NFC = 22
T = 512
NT = S // T
KC = 8
EPS = 1e-6
POOLW = (2, 4, 8, 16)
DIL = (1, 4, 16)
NH = 16
HD = 64
NSLOT = 5
SLOTW = 4096
BIG = 30000.0


def _alibi_slopes(n):
    def pow2(m):
        start = 2.0 ** (-(2.0 ** -(math.log2(m) - 3)))
        return [start ** (i + 1) for i in range(m)]
    if math.log2(n).is_integer():
        s = pow2(n)
    else:
        c = 2 ** math.floor(math.log2(n))
        s = pow2(c) + pow2(2 * c)[0::2][: n - c]
    s = np.asarray(s, dtype=np.float32)
    return -np.sort(-s)


SLOPES = _alibi_slopes(3 * NH).reshape(3, NH)


STAGES = ["mix0", "ffn0", "mix1", "ffn1", "kv", "mix2", "ffn2", "mix3", "ffn3"]


def stage_pieces(st):
    P = []
    if st.startswith("mix"):
        l = int(st[3])
        if l < 2:
            P += [(f"win{l}_0", 4096), (f"win{l}_1", 4096), (f"wgrp{l}", 2048), (f"wout{l}_0", 4096), (f"wout{l}_1", 4096)]
        else:
            P += [(f"wq{l}_{hp}", 3072) for hp in range(8)]
            P += [(f"wo{l}_0", 4096), (f"wo{l}_1", 4096)]
    elif st == "kv":
        P += [(f"wk_{i}", 4096) for i in (4, 5, 0, 1, 2, 3)]
        P += [(f"wv_{i}", 4096) for i in range(6)]
    else:
        l = int(st[3])
        P += [(f"wup{l}_{i}", 4096) for i in range(11)]
        for r in range(2):
            P += [(f"wdn{l}_{r}_0", 4096), (f"wdn{l}_{r}_1", 4096), (f"wdn{l}_{r}_2", 3072)]
    return P


def piece_list(nstages=len(STAGES)):
    P = []
    for st in STAGES[:nstages]:
        P += stage_pieces(st)
    return P


def ada_piece_list():
    P = []
    for l in range(4):
        P += [(f"ada{l}_{i}", 4096) for i in range(12)]
    P += [(f"kvada_{i}", 4096) for i in range(4)]
    return P


def _offsets(pl):
    off = {}
    o = 0
    for n, w in pl:
        off[n] = (o, w)
        o += w
    return off, o


def vec_layout():
    L = {}
    o = 0

    def add(name, n):
        nonlocal o
        L[name] = (o, n)
        o += n
    for l in range(4):
        add(f"ada_b{l}", 48)
        add(f"n1g{l}", 8)
        add(f"n2g{l}", 8)
        add(f"cw{l}", 66)
        add(f"cb{l}", 22)
    for l in range(2):
        add(f"psc{l}", 8)
    add("kvg", 8)
    add("kvb", 16)
    add("fg", 8)
    add("c", 8)
    add("invc", 64)
    add("dist", 256)
    add("dist2", 96)
    return L, o


def pmaj(v):
    return np.ascontiguousarray(v.reshape(-1, 128).T)


def kmajor(W, c0, ncols):
    K = W.shape[0]
    a = W[:, c0:c0 + ncols].reshape(K // 128, 128, ncols).transpose(1, 0, 2)
    return a.reshape(128, -1)


def host_prepare(inp):
    pl = piece_list()
    off, tot = _offsets(pl)
    W = np.empty((128, tot), np.float32)

    def put(name, arr):
        o, w = off[name]
        assert arr.shape == (128, w), (name, arr.shape, w)
        W[:, o:o + w] = arr
    for l in range(4):
        if l < 2:
            for m in range(2):
                put(f"win{l}_{m}", kmajor(inp["pool_w_in"][l], m * 512, 512))
                put(f"wout{l}_{m}", kmajor(inp["pool_w_out"][l], m * 512, 512))
            g = inp["pool_w_grp"][l]
            put(f"wgrp{l}", g.reshape(4, 2, 128, 256).transpose(2, 0, 1, 3).reshape(128, 2048))
        else:
            j = l - 2
            wq = inp["attn_w_q"][j]
            for hp in range(8):
                a = np.stack([kmajor(wq, g * 1024 + hp * 128, 128).reshape(128, 8, 128) for g in range(3)], axis=1)
                put(f"wq{l}_{hp}", a.reshape(128, 3072))
            for m in range(2):
                put(f"wo{l}_{m}", kmajor(inp["attn_w_o"][j], m * 512, 512))
        if l == 2:
            for i in range(6):
                put(f"wk_{i}", kmajor(inp["w_kv"], i * 512, 512))
                put(f"wv_{i}", kmajor(inp["w_kv"], 3072 + i * 512, 512))
        wu = inp["ffn_w_up"][l]
        for i in range(11):
            parts = []
            for jj in range(2):
                fc = 2 * i + jj
                for av in range(2):
                    parts.append(kmajor(wu, av * FF + fc * 128, 128))
            put(f"wup{l}_{i}", np.concatenate(parts, axis=1))
        wd = inp["ffn_w_down"][l]
        for r in range(2):
            a = wd[:, r * 512:(r + 1) * 512].reshape(NFC, 128, 512).transpose(1, 0, 2)
            put(f"wdn{l}_{r}_0", a[:, 0:8].reshape(128, 4096))
            put(f"wdn{l}_{r}_1", a[:, 8:16].reshape(128, 4096))
            put(f"wdn{l}_{r}_2", a[:, 16:22].reshape(128, 3072))
    apl = ada_piece_list()
    aoff, atot = _offsets(apl)
    A = np.empty((128, atot), np.float32)
    for l in range(4):
        for i in range(12):
            o, w = aoff[f"ada{l}_{i}"]
            A[:, o:o + w] = kmajor(inp["ada_w"][l], i * 512, 512)
    for i in range(4):
        o, w = aoff[f"kvada_{i}"]
        A[:, o:o + w] = kmajor(inp["kv_ada_w"], i * 512, 512)
    VL, nv = vec_layout()
    shared = np.zeros((128, nv), np.float32)

    def vput(name, arr):
        o, n = VL[name]
        assert arr.shape == (128, n), (name, arr.shape)
        shared[:, o:o + n] = arr
    for l in range(4):
        vput(f"ada_b{l}", pmaj(inp["ada_b"][l]))
        vput(f"n1g{l}", pmaj(inp["norm1_g"][l]))
        vput(f"n2g{l}", pmaj(inp["norm2_g"][l]))
        cw = inp["ffn_conv_w"][l]
        vput(f"cw{l}", np.concatenate([pmaj(cw[k]) for k in range(3)], axis=1))
        vput(f"cb{l}", pmaj(inp["ffn_conv_b"][l]))
    for l in range(2):
        vput(f"psc{l}", pmaj(inp["pool_scale"][l]))
    vput("kvg", pmaj(inp["kv_norm_g"]))
    vput("kvb", pmaj(inp["kv_ada_b"]))
    vput("fg", pmaj(inp["final_g"]))
    invc = np.zeros((128, 4, 16), np.float32)
    for g, w in enumerate(POOLW):
        invc[:, g, :] = 1.0 / np.minimum(np.arange(16) + 1, w)
    vput("invc", invc.reshape(128, 64))
    k = np.arange(128)[:, None]
    q = np.arange(128)[None, :]
    dprev = (q - k + 128).astype(np.float32)
    dprev[dprev > 128] = BIG
    dcur = (q - k).astype(np.float32)
    dcur[dcur < 0] = BIG
    vput("dist", np.concatenate([dprev, dcur], axis=1))
    d2 = []
    for na in (32, 64, 96):
        dd = (q[:, 0:32] - k + na).astype(np.float32)
        dd[dd > 128] = BIG
        dd[k[:, 0] >= na, :] = BIG
        d2.append(dd)
    vput("dist2", np.concatenate(d2, axis=1))
    per_core = []
    for b in range(8):
        v = shared.copy()
        o, n = VL["c"]
        v[:, o:o + n] = pmaj(inp["c"][b])
        per_core.append({"xT": np.ascontiguousarray(inp["x"][b].T), "vecs": v})
    return W, A, per_core


class Chan:
    __slots__ = ("sem", "val")


class Buf:
    __slots__ = ("name", "w", "r")

    def __init__(self, name, seed=None):
        self.name = name
        self.w = {}
        self.r = dict(seed) if seed else {}

    def tokens(self):
        d = dict(self.w)
        for ch, v in self.r.items():
            if d.get(ch, 0) < v:
                d[ch] = v
        return d


def _merge(d, ch, v):
    if d.get(ch, 0) < v:
        d[ch] = v


class Ctx:
    def __init__(self, nc, es):
        self.nc = nc
        self.es = es
        self.nsem = 0

    def new_chan(self):
        sem = self.es.enter_context(self.nc.semaphore(f"sm{self.nsem}"))
        self.nsem += 1
        c = Chan()
        c.sem = sem
        c.val = 0
        return c


class Eng:
    EPOCH = 16000

    def __init__(self, ctx, eng, name, is_pe=False, n_dma=0):
        self.ctx = ctx
        self.e = eng
        self.name = name
        self.is_pe = is_pe
        self.chan = ctx.new_chan()
        self.waited = {}
        self.dma_pool = [ctx.new_chan() for _ in range(n_dma)]
        self.dma_i = 0
        self.n = 0

    def wait_tok(self, ch, v):
        if ch is self.chan and self.is_pe:
            return
        if self.waited.get(ch, 0) >= v:
            return
        self.e.wait_ge(ch.sem, v)
        self.waited[ch] = v

    def sync(self, reads, writes):
        for b in reads:
            for ch, v in b.w.items():
                self.wait_tok(ch, v)
        for b in writes:
            for ch, v in b.w.items():
                self.wait_tok(ch, v)
            for ch, v in b.r.items():
                self.wait_tok(ch, v)

    def op(self, fn, reads, writes, *a, **k):
        self.sync(reads, writes)
        ins = fn(*a, **k)
        if self.chan.val >= self.EPOCH:
            self.chan = self.ctx.new_chan()
        ch = self.chan
        ch.val += 1
        ins.then_inc(ch.sem, 1)
        for b in reads:
            _merge(b.r, ch, ch.val)
        for b in writes:
            _merge(b.w, ch, ch.val)
        self.n += 1
        return ins

    def dma(self, out, in_, reads, writes, **k):
        ch = self.dma_pool[self.dma_i % len(self.dma_pool)]
        self.dma_i += 1
        if ch.val:
            self.wait_tok(ch, ch.val)
        self.sync(reads, writes)
        ins = self.e.dma_start(out=out, in_=in_, **k)
        ch.val += 16
        ins.then_inc(ch.sem, 16)
        for b in reads:
            _merge(b.r, ch, ch.val)
        for b in writes:
            _merge(b.w, ch, ch.val)
        self.n += 1
        return ins

    def wait_all(self, bufs):
        for b in bufs:
            for ch, v in b.tokens().items():
                self.wait_tok(ch, v)


class Prog:
    def __init__(self, n_tiles=NT, nstages=len(STAGES), dbg=None):
        self.n_tiles = n_tiles
        self.nstages = nstages
        self.full = nstages == len(STAGES)
        self.dbg = dbg
        nc = bass.Bass("TRN2", target_bir_lowering=False)
        self.nc = nc
        self.es = ExitStack()
        self.pl = piece_list()
        self.poff, self.ptot = _offsets(self.pl)
        self.apl = ada_piece_list()
        self.aoff, self.atot = _offsets(self.apl)
        self.VL, self.nv = vec_layout()
        self.seed = {}

    def sb(self, es, name, shape, dt):
        self._nalloc = getattr(self, "_nalloc", 0) + 1
        return es.enter_context(self.nc.sbuf_tensor(f"{name}_{self._nalloc}", list(shape), dt))

    def newbuf(self, name):
        b = Buf(name, self.seed)
        self.stage_bufs.append(b)
        return b

    def stage_begin(self):
        self.stage_bufs = []
        return ExitStack()

    def stage_end(self, es):
        seed = dict(self.seed)
        for b in self.stage_bufs:
            for ch, v in b.tokens().items():
                _merge(seed, ch, v)
        self.seed = seed
        es.close()

    def w_init(self):
        self.wslots = [self.sb(self.es, f"wslot{i}", [128, SLOTW], BF16) for i in range(NSLOT)]
        self.wbufs = [Buf(f"wslot{i}") for i in range(NSLOT)]
        self.wsched = []
        self.w_issued = 0
        self.w_used = 0

    def w_issue_upto(self, idx):
        while self.w_issued <= idx and self.w_issued < len(self.wsched):
            i = self.w_issued
            src, w, name = self.wsched[i]
            slot = i % NSLOT
            self.POOL.dma(self.wslots[slot][:, 0:w], src, [], [self.wbufs[slot]], max_dma_last_dim=2048)
            self.w_issued += 1

    def w_next(self, name):
        i = self.w_used
        src, w, nm = self.wsched[i]
        assert nm == name, (nm, name)
        self.w_issue_upto(i + NSLOT - 1)
        self.w_used += 1
        slot = i % NSLOT
        return self.wslots[slot], self.wbufs[slot]

    def build(self):
        nc = self.nc
        es = self.es
        ctx = Ctx(nc, es)
        self.ctx = ctx
        nt = self.n_tiles
        self.xT = nc.dram_tensor("xT", [D, S], F32, kind="ExternalInput").ap()
        self.wts = nc.dram_tensor("wts", [128, self.ptot], F32, kind="ExternalInput").ap()
        self.adaw = nc.dram_tensor("adaw", [128, self.atot], F32, kind="ExternalInput").ap()
        self.vecs_d = nc.dram_tensor("vecs", [128, self.nv], F32, kind="ExternalInput").ap()
        self.outT = nc.dram_tensor("outT", [D, S], F32, kind="ExternalOutput").ap()
        self.kTd = nc.dram_tensor("kTd", [3, 8, 128, S], BF16, kind="Internal").ap()
        self.vd = nc.dram_tensor("vd", [3, S, D], BF16, kind="Internal").ap()
        self.kT2d = nc.dram_tensor("kT2d", [8, 128, NT, T], BF16, kind="Internal").ap()
        self.kv_bufs = [Buf(f"kv{s}") for s in range(NT)]

        self.PE = Eng(ctx, nc.tensor, "pe", is_pe=True)
        self.ACT = Eng(ctx, nc.scalar, "act")
        self.DVE = Eng(ctx, nc.vector, "dve")
        self.POOL = Eng(ctx, nc.gpsimd, "pool", n_dma=NSLOT + 1)
        self.SP = Eng(ctx, nc.sync, "sp", n_dma=40)
        PE, ACT, DVE, POOL, SP = self.PE, self.ACT, self.DVE, self.POOL, self.SP

        self.vecs = self.sb(es, "vecs_sb", [128, self.nv], F32)
        self.vecs_b = Buf("vecs")
        self.modT = self.sb(es, "modT", [128, 4 * 48 + 16], F32)
        self.der = self.sb(es, "der", [128, 4 * 16 + 8], F32)
        self.mod_b = Buf("mod")
        self.ones = self.sb(es, "ones", [128, 128], BF16)
        self.ones_b = Buf("ones")
        self.condb = self.sb(es, "condb", [128, 8], BF16)
        self.cond_b = Buf("cond")
        self.xTs = [self.sb(es, f"xTs{i}", [128, KC, T], F32) for i in range(2)]
        self.x_bufs = [[Buf(f"x{i}_{c}") for c in range(KC)] for i in range(2)]
        self.hT = self.sb(es, "hT", [128, KC, T], BF16)
        self.h_bs = [Buf(f"hT{c}") for c in range(KC)]
        self.sq = self.sb(es, "sq", [128, KC, T], BF16)
        self.sq_bs = [Buf(f"sq{c}") for c in range(KC)]
        self.ms_ready = None
        self.std = self.sb(es, "std", [128, T], F32)
        self.rstd = self.sb(es, "rstd", [128, T], F32)
        self.rstd2 = self.sb(es, "rstd2", [128, T], F32)
        self.rstd2_b = Buf("rstd2")
        self.std_b = Buf("std")
        self.rstd_b = Buf("rstd")
        self.ntmp = [self.sb(es, f"ntmp{i}", [128, T], F32) for i in range(2)]
        self.ntmp_b = [Buf(f"ntmp{i}") for i in range(2)]
        self.uhalo = [self.sb(es, f"uhalo{l}", [128, KC, 16], F32) for l in range(2)]
        self.uhalo_b = [Buf(f"uhalo{l}") for l in range(2)]
        self.ahalo = [self.sb(es, f"ahalo{l}", [128, NFC, 2], F32) for l in range(4)]
        self.ahalo_b = [Buf(f"ahalo{l}") for l in range(4)]
        self.masks = self.sb(es, "masks", [128, 48, 256], BF16)
        self.masks_b = Buf("masks")
        self.masks2 = self.sb(es, "masks2", [128, 3, NH, 32], BF16)
        self.ps = [es.enter_context(nc.psum_tensor(f"ps{i}", [128, 512], F32)) for i in range(8)]
        self.ps_b = [Buf(f"ps{i}") for i in range(8)]
        self.w_init()

        def ada_sched(l):
            for n, w in self.apl:
                if n.startswith(f"ada{l}_") or (l == 4 and n.startswith("kvada")):
                    o, _ = self.aoff[n]
                    self.wsched.append((self.adaw[:, o:o + w], w, n))
        stages = STAGES[:self.nstages]
        nP = min(5, len(stages))
        order = []
        for s in range(nt):
            order += [(s, st) for st in stages[:nP]]
            if s >= 1:
                order += [(s - 1, st) for st in stages[nP:]] + [(s - 1, "out")]
        order += [(nt - 1, st) for st in stages[nP:]] + [(nt - 1, "out")]
        for (s, st) in order:
            if st == "out":
                continue
            if s == 0 and st.startswith("mix"):
                ada_sched(int(st[3]))
            if s == 0 and st == "kv":
                ada_sched(4)
            for n, w in stage_pieces(st):
                o, _ = self.poff[n]
                self.wsched.append((self.wts[:, o:o + w], w, n))

        self.prologue()
        self.tile_init()
        for (s, st) in order:
            self.run_stage(s, st)
        for b in self.out_wait:
            SP.wait_all([b])
        return nc

    def vcol(self, name, c0=0, n=1):
        o, _ = self.VL[name]
        return self.vecs[:, o + c0:o + c0 + n]

    def prologue(self):
        nc = self.nc
        PE, ACT, DVE, POOL, SP = self.PE, self.ACT, self.DVE, self.POOL, self.SP
        SP.dma(self.vecs[:, :], self.vecs_d[:, :], [], [self.vecs_b])
        DVE.op(nc.vector.memset, [], [self.ones_b], self.ones[:, :], 1.0)
        for l in range(2):
            DVE.op(nc.vector.memset, [], [self.uhalo_b[l]], self.uhalo[l][:, :, :], 0.0)
        for l in range(4):
            DVE.op(nc.vector.memset, [], [self.ahalo_b[l]], self.ahalo[l][:, :, :], 0.0)
        ACT.op(nc.scalar.activation, [self.vecs_b], [self.cond_b], out=self.condb[:, :], in_=self.vcol("c", 0, 8), func=AF.Silu)
        if self.nstages > 5:
            for g in range(3):
                for h in range(NH):
                    ACT.op(nc.scalar.activation, [self.vecs_b], [self.masks_b], out=self.masks[:, g * NH + h, :],
                           in_=self.vcol("dist", 0, 256), func=AF.Exp, scale=-float(SLOPES[g, h]) * DIL[g])
            for i in range(3):
                for h in range(NH):
                    ACT.op(nc.scalar.activation, [self.vecs_b], [self.masks_b], out=self.masks2[:, i, h, :],
                           in_=self.vcol("dist2", i * 32, 32), func=AF.Exp, scale=-float(SLOPES[2, h]) * DIL[2])

    def compute_mod(self, l):
        nc = self.nc
        PE, ACT, DVE, POOL, SP = self.PE, self.ACT, self.DVE, self.POOL, self.SP
        pb = self.psrr()
        if True:
            ncol = 48 if l < 4 else 16
            npieces = 12 if l < 4 else 4
            pt, pbuf = self.ps[pb], self.ps_b[pb]
            first = True
            for i in range(npieces):
                slot, sbuf_ = self.w_next(f"ada{l}_{i}" if l < 4 else f"kvada_{i}")
                for mm in range(4):
                    col = i * 4 + mm
                    for kc in range(KC):
                        PE.op(nc.tensor.matmul, [sbuf_, self.cond_b], [pbuf], pt[:, col:col + 1],
                              lhsT=slot[:, kc * 512 + mm * 128: kc * 512 + mm * 128 + 128], rhs=self.condb[:, kc:kc + 1],
                              start=first, stop=(kc == KC - 1))
                        first = False
            bname = f"ada_b{l}" if l < 4 else "kvb"
            DVE.op(nc.vector.tensor_tensor, [pbuf, self.vecs_b], [self.mod_b], out=self.modT[:, l * 48:l * 48 + ncol],
                   in0=pt[:, 0:ncol], in1=self.vcol(bname, 0, ncol), op=ALU.add)
        if l < 4:
            for j, (gn, sc0) in enumerate(((f"n1g{l}", 8), (f"n2g{l}", 32))):
                DVE.op(nc.vector.scalar_tensor_tensor, [self.mod_b, self.vecs_b], [self.mod_b],
                       out=self.der[:, l * 16 + j * 8: l * 16 + j * 8 + 8], in0=self.modT[:, l * 48 + sc0: l * 48 + sc0 + 8],
                       scalar=1.0, in1=self.vcol(gn, 0, 8), op0=ALU.add, op1=ALU.mult)
        else:
            DVE.op(nc.vector.scalar_tensor_tensor, [self.mod_b, self.vecs_b], [self.mod_b],
                   out=self.der[:, 64:72], in0=self.modT[:, 192 + 8:192 + 16], scalar=1.0, in1=self.vcol("kvg", 0, 8),
                   op0=ALU.add, op1=ALU.mult)

    def A(self, l, which, c):
        return self.der[:, l * 16 + which * 8 + c: l * 16 + which * 8 + c + 1]

    def M(self, l, j, c):
        return self.modT[:, l * 48 + j * 8 + c: l * 48 + j * 8 + c + 1]

    def norm(self, x, xb, acol, bcol, out=None, out_b=None, final=False, keep=None, reuse=None):
        nc = self.nc
        PE, ACT, DVE = self.PE, self.ACT, self.DVE
        if reuse is not None:
            self.rstd_cur = ((self.rstd, self.rstd_b), (self.rstd2, self.rstd2_b))[reuse]
        else:
            if self.ms_ready is None:
                for c in range(KC):
                    ACT.op(nc.scalar.activation, [xb[c]], [self.sq_bs[c]], out=self.sq[:, c, :], in_=x[:, c, :], func=AF.Square)
                self.emit_ms()
            pt, pbuf = self.ms_ready
            self.ms_ready = None
            ACT.op(nc.scalar.activation, [pbuf, self.eps_b], [self.std_b], out=self.std[:, :], in_=pt[:, :], func=AF.Sqrt,
                   scale=1.0 / D, bias=self.epsc[:, 0:1])
            if keep is not None:
                kt_, kb_ = ((self.rstd, self.rstd_b), (self.rstd2, self.rstd2_b))[keep]
                DVE.op(nc.vector.reciprocal, [self.std_b], [kb_], out=kt_[:, :], in_=self.std[:, :])
                self.rstd_cur = (kt_, kb_)
            else:
                DVE.op(nc.vector.reciprocal, [self.std_b], [pbuf], out=pt[:, :], in_=self.std[:, :])
                self.rstd_cur = (pt, pbuf)
        rt, rb = self.rstd_cur
        for c in range(KC):
            tb = self.ntmp_b[c % 2]
            tt = self.ntmp[c % 2]
            DVE.op(nc.vector.tensor_tensor, [xb[c], rb], [tb], out=tt[:, :], in0=x[:, c, :], in1=rt[:, :], op=ALU.mult)
            if final:
                ACT.op(nc.scalar.activation, [tb, self.vecs_b], [out_b], out=out[:, c, :], in_=tt[:, :], func=AF.Copy,
                       scale=acol(c))
            else:
                ACT.op(nc.scalar.activation, [tb, self.mod_b], [self.h_bs[c]], out=self.hT[:, c, :], in_=tt[:, :], func=AF.Identity,
                       scale=acol(c), bias=bcol(c))

    def emit_ms(self):
        nc = self.nc
        pi = self.psrr()
        pt, pbuf = self.ps[pi], self.ps_b[pi]
        for c in range(KC):
            self.PE.op(nc.tensor.matmul, [self.sq_bs[c], self.ones_b], [pbuf], pt[:, :], lhsT=self.ones[:, :], rhs=self.sq[:, c, :],
                       start=(c == 0), stop=(c == KC - 1))
        self.ms_ready = (pt, pbuf)

    def residual(self, x, xb, m, pt, pbuf, gcol):
        nc = self.nc
        self.DVE.op(nc.vector.scalar_tensor_tensor, [pbuf, self.mod_b, xb[m]], [xb[m]], out=x[:, m, :], in0=pt[:, :], scalar=gcol,
                    in1=x[:, m, :], op0=ALU.mult, op1=ALU.add)
        self.ACT.op(nc.scalar.activation, [xb[m]], [self.sq_bs[m]], out=self.sq[:, m, :], in_=x[:, m, :], func=AF.Square)

    def psrr(self):
        i = self._psi
        self._psi = (self._psi + 1) % 8
        return i

    def tile_init(self):
        nc = self.nc
        self._psi = 0
        self.out_wait = []
        self.epsc = self.sb(self.es, "epsc", [128, 1], F32)
        self.eps_b = Buf("eps")
        self.DVE.op(nc.vector.memset, [], [self.eps_b], self.epsc[:, :], EPS)
        self.x_loaded = set()
        self.load_x(0)
        self.load_x(1)

    def load_x(self, s2):
        if s2 >= self.n_tiles or s2 in self.x_loaded:
            return
        self.x_loaded.add(s2)
        src_ = self.xT.rearrange("(c p) t -> p c t", p=128)
        self.SP.dma(self.xTs[s2 % 2][:, :, :], src_[:, :, s2 * T:(s2 + 1) * T], [], self.x_bufs[s2 % 2])

    def run_stage(self, s, st):
        nc = self.nc
        PE, ACT, DVE, POOL, SP = self.PE, self.ACT, self.DVE, self.POOL, self.SP
        x = self.xTs[s % 2]
        xb = self.x_bufs[s % 2]
        if st == "mix0":
            self.load_x(s)
            self.ms_ready = None
        if s == 0 and st.startswith("mix"):
            self.compute_mod(int(st[3]))
        if s == 0 and st == "kv":
            self.compute_mod(4)
        if st == "kv":
            self.kv_stage(s, x, xb)
        elif st.startswith("mix"):
            l = int(st[3])
            if l < 2:
                self.pool_mixer(l, s, x, xb)
            else:
                self.attn_mixer(l, s, x, xb)
        elif st.startswith("ffn"):
            self.ffn(int(st[3]), s, x, xb)
        else:
            es = self.stage_begin()
            o = self.sb(es, "otile", [128, KC, T], F32)
            ob = self.newbuf("otile")
            if self.full:
                self.norm(x, xb, lambda c: self.vcol("fg", c, 1), None, out=o, out_b=ob, final=True)
            else:
                self.ms_ready = None
                ACT.op(nc.scalar.activation, xb, [ob], out=o[:, :, :], in_=x[:, :, :], func=AF.Copy)
            SP.dma(self.outT.rearrange("(c p) t -> p c t", p=128)[:, :, s * T:(s + 1) * T], o[:, :, :], [ob], [])
            self.out_wait.append(ob)
            self.stage_end(es)
            self.load_x(s + 2)

    def pool_mixer(self, l, s, x, xb):
        nc = self.nc
        PE, ACT, DVE = self.PE, self.ACT, self.DVE
        es = self.stage_begin()
        U = self.sb(es, "poolU", [128, KC, 16 + T], F32)
        Ub = self.newbuf("U")
        A_ = self.sb(es, "poolA", [128, KC, 16 + T], F32)
        Ab = self.newbuf("A")
        B_ = self.sb(es, "poolB", [128, KC, 16 + T], F32)
        Bb = self.newbuf("B")
        Pb = self.sb(es, "poolP", [128, KC, T], BF16)
        Pbb = self.newbuf("P")
        Zb = self.sb(es, "poolZ", [128, KC, T], BF16)
        Zbb = self.newbuf("Z")
        self.norm(x, xb, lambda c: self.A(l, 0, c), lambda c: self.M(l, 0, c))
        DVE.op(nc.vector.tensor_copy, [self.uhalo_b[l]], [Ub], out=U[:, :, 0:16], in_=self.uhalo[l][:, :, :])
        for half in range(2):
            slot, sbuf_ = self.w_next(f"win{l}_{half}")
            for mm in range(4):
                m = half * 4 + mm
                pi = self.psrr()
                pt, pbuf = self.ps[pi], self.ps_b[pi]
                for kc in range(KC):
                    PE.op(nc.tensor.matmul, [sbuf_, self.h_bs[kc]], [pbuf], pt[:, :], lhsT=slot[:, kc * 512 + mm * 128: kc * 512 + mm * 128 + 128],
                          rhs=self.hT[:, kc, :], start=(kc == 0), stop=(kc == KC - 1))
                ACT.op(nc.scalar.activation, [pbuf], [Ub], out=U[:, m, 16:16 + T], in_=pt[:, :], func=AF.Copy)
        DVE.op(nc.vector.tensor_copy, [Ub], [self.uhalo_b[l]], out=self.uhalo[l][:, :, :], in_=U[:, :, T:T + 16])
        W_ = 16 + T
        DVE.op(nc.vector.tensor_tensor, [Ub], [Ab], out=A_[:, :, 1:W_], in0=U[:, :, 1:W_], in1=U[:, :, 0:W_ - 1], op=ALU.add)
        DVE.op(nc.vector.tensor_tensor, [Ab], [Bb], out=B_[:, 2:8, 3:W_], in0=A_[:, 2:8, 3:W_], in1=A_[:, 2:8, 1:W_ - 2], op=ALU.add)
        DVE.op(nc.vector.tensor_tensor, [Bb], [Ab], out=A_[:, 4:8, 7:W_], in0=B_[:, 4:8, 7:W_], in1=B_[:, 4:8, 3:W_ - 4], op=ALU.add)
        DVE.op(nc.vector.tensor_tensor, [Ab], [Bb], out=B_[:, 6:8, 15:W_], in0=A_[:, 6:8, 15:W_], in1=A_[:, 6:8, 7:W_ - 8], op=ALU.add)
        srcs = [(A_, Ab), (B_, Bb), (A_, Ab), (B_, Bb)]
        for g in range(4):
            St, Sb_ = srcs[g]
            DVE.op(nc.vector.scalar_tensor_tensor, [Sb_, Ub], [Pbb], out=Pb[:, 2 * g:2 * g + 2, :], in0=St[:, 2 * g:2 * g + 2, 16:16 + T],
                   scalar=1.0 / POOLW[g], in1=U[:, 2 * g:2 * g + 2, 16:16 + T], op0=ALU.mult, op1=ALU.subtract)
            if s == 0:
                o, _ = self.VL["invc"]
                for cc in range(2):
                    c = 2 * g + cc
                    tb, tt = self.ntmp_b[cc], self.ntmp[cc]
                    DVE.op(nc.vector.tensor_tensor, [Sb_, self.vecs_b], [tb], out=tt[:, 0:16], in0=St[:, c, 16:32],
                           in1=self.vecs[:, o + g * 16:o + g * 16 + 16], op=ALU.mult)
                    DVE.op(nc.vector.tensor_tensor, [tb, Ub], [Pbb], out=Pb[:, c, 0:16], in0=tt[:, 0:16], in1=U[:, c, 16:32], op=ALU.subtract)
        slot, sbuf_ = self.w_next(f"wgrp{l}")
        for g in range(4):
            for mo in range(2):
                c = 2 * g + mo
                pi = self.psrr()
                pt, pbuf = self.ps[pi], self.ps_b[pi]
                for ki in range(2):
                    base = (g * 2 + ki) * 256 + mo * 128
                    PE.op(nc.tensor.matmul, [sbuf_, Pbb], [pbuf], pt[:, :], lhsT=slot[:, base:base + 128], rhs=Pb[:, 2 * g + ki, :],
                          start=(ki == 0), stop=(ki == 1))
                ACT.op(nc.scalar.activation, [pbuf, self.vecs_b], [Zbb], out=Zb[:, c, :], in_=pt[:, :], func=AF.Copy,
                       scale=self.vcol(f"psc{l}", c, 1))
        for half in range(2):
            slot, sbuf_ = self.w_next(f"wout{l}_{half}")
            for mm in range(4):
                m = half * 4 + mm
                pi = self.psrr()
                pt, pbuf = self.ps[pi], self.ps_b[pi]
                for kc in range(KC):
                    PE.op(nc.tensor.matmul, [sbuf_, Zbb], [pbuf], pt[:, :], lhsT=slot[:, kc * 512 + mm * 128: kc * 512 + mm * 128 + 128],
                          rhs=Zb[:, kc, :], start=(kc == 0), stop=(kc == KC - 1))
                self.residual(x, xb, m, pt, pbuf, self.M(l, 2, m))
        self.emit_ms()
        self.stage_end(es)

    def ffn(self, l, s, x, xb):
        nc = self.nc
        PE, ACT, DVE = self.PE, self.ACT, self.DVE
        es = self.stage_begin()
        gT = self.sb(es, "gT", [128, NFC, T], BF16)
        gbs = [self.newbuf(f"gT{i}") for i in range(NFC)]
        c1 = [self.sb(es, f"c1_{i}", [128, T], F32) for i in range(2)]
        c1b = [self.newbuf(f"c1_{i}") for i in range(2)]
        c2 = [self.sb(es, f"c2_{i}", [128, T], F32) for i in range(2)]
        c2b = [self.newbuf(f"c2_{i}") for i in range(2)]
        vsb = [self.sb(es, f"vsb{i}", [128, T], F32) for i in range(2)]
        vsbb = [self.newbuf(f"vsb{i}") for i in range(2)]
        self.norm(x, xb, lambda c: self.A(l, 1, c), lambda c: self.M(l, 3, c))
        cwo, _ = self.VL[f"cw{l}"]
        cbo, _ = self.VL[f"cb{l}"]
        for i in range(11):
            slot, sbuf_ = self.w_next(f"wup{l}_{i}")
            for jj in range(2):
                fc = 2 * i + jj
                par = fc % 2
                pa, pab = self.ps[par * 2], self.ps_b[par * 2]
                pv, pvb = self.ps[par * 2 + 1], self.ps_b[par * 2 + 1]
                for av, (pt, pbuf) in enumerate(((pa, pab), (pv, pvb))):
                    base = (jj * 2 + av) * 1024
                    for kc in range(KC):
                        PE.op(nc.tensor.matmul, [sbuf_, self.h_bs[kc]], [pbuf], pt[:, :], lhsT=slot[:, base + kc * 128: base + kc * 128 + 128],
                              rhs=self.hT[:, kc, :], start=(kc == 0), stop=(kc == KC - 1))
                hl = self.ahalo[l]
                hb = self.ahalo_b[l]
                w0 = self.vecs[:, cwo + fc: cwo + fc + 1]
                w1 = self.vecs[:, cwo + NFC + fc: cwo + NFC + fc + 1]
                w2 = self.vecs[:, cwo + 2 * NFC + fc: cwo + 2 * NFC + fc + 1]
                ACT.op(nc.scalar.activation, [pab, self.vecs_b], [c1b[par]], out=c1[par][:, :], in_=pa[:, :], func=AF.Identity,
                       scale=w2, bias=self.vecs[:, cbo + fc:cbo + fc + 1])
                ACT.op(nc.scalar.activation, [pvb], [vsbb[par]], out=vsb[par][:, :], in_=pv[:, :], func=AF.Copy)
                DVE.op(nc.vector.scalar_tensor_tensor, [pab, self.vecs_b, c1b[par]], [c2b[par]], out=c2[par][:, 1:T], in0=pa[:, 0:T - 1],
                       scalar=w1, in1=c1[par][:, 1:T], op0=ALU.mult, op1=ALU.add)
                DVE.op(nc.vector.scalar_tensor_tensor, [hb, self.vecs_b, c1b[par]], [c2b[par]], out=c2[par][:, 0:1], in0=hl[:, fc, 1:2],
                       scalar=w1, in1=c1[par][:, 0:1], op0=ALU.mult, op1=ALU.add)
                DVE.op(nc.vector.scalar_tensor_tensor, [pab, self.vecs_b, c2b[par]], [c1b[par]], out=c1[par][:, 2:T], in0=pa[:, 0:T - 2],
                       scalar=w0, in1=c2[par][:, 2:T], op0=ALU.mult, op1=ALU.add)
                DVE.op(nc.vector.scalar_tensor_tensor, [hb, self.vecs_b, c2b[par]], [c1b[par]], out=c1[par][:, 0:2], in0=hl[:, fc, 0:2],
                       scalar=w0, in1=c2[par][:, 0:2], op0=ALU.mult, op1=ALU.add)
                DVE.op(nc.vector.tensor_copy, [pab], [hb], out=hl[:, fc, :], in_=pa[:, T - 2:T])
                ACT.op(nc.scalar.activation, [c1b[par]], [c2b[par]], out=c2[par][:, :], in_=c1[par][:, :], func=AF.Silu)
                DVE.op(nc.vector.tensor_tensor, [c2b[par], vsbb[par]], [gbs[fc]], out=gT[:, fc, :], in0=c2[par][:, :], in1=vsb[par][:, :], op=ALU.mult)
        for r in range(2):
            b0 = 4 if r == 0 else 0
            for j in range(3):
                slot, sbuf_ = self.w_next(f"wdn{l}_{r}_{j}")
                nf = 8 if j < 2 else 6
                for fl in range(nf):
                    fc = 8 * j + fl
                    for mm in range(4):
                        PE.op(nc.tensor.matmul, [sbuf_, gbs[fc]], [self.ps_b[b0 + mm]], self.ps[b0 + mm][:, :],
                              lhsT=slot[:, fl * 512 + mm * 128: fl * 512 + mm * 128 + 128], rhs=gT[:, fc, :],
                              start=(fc == 0), stop=(fc == NFC - 1))
            for mm in range(4):
                m = r * 4 + mm
                self.residual(x, xb, m, self.ps[b0 + mm], self.ps_b[b0 + mm], self.M(l, 5, m))
        self._psi = 4
        self.emit_ms()
        self.stage_end(es)

    def kv_stage(self, s, x, xb):
        nc = self.nc
        PE, ACT, DVE, SP = self.PE, self.ACT, self.DVE, self.SP
        es = self.stage_begin()
        kst = [self.sb(es, f"kst{i}", [128, T], BF16) for i in range(8)]
        kstb = [self.newbuf(f"kst{i}") for i in range(8)]
        vst = [self.sb(es, f"vst{i}", [128, 512], BF16) for i in range(12)]
        vstb = [self.newbuf(f"vst{i}") for i in range(12)]
        kvb = self.kv_bufs[s]
        self.norm(x, xb, lambda c: self.der[:, 64 + c:65 + c], lambda c: self.modT[:, 192 + c:193 + c], keep=s % 2)
        h16 = self.sb(es, "h16", [128, KC, T], BF16)
        h16b = self.newbuf("h16")
        for kc in range(KC):
            DVE.op(nc.vector.tensor_copy, [self.h_bs[kc]], [h16b], out=h16[:, kc, :].rearrange("p (r j) -> p r j", r=16),
                   in_=self.hT[:, kc, :].rearrange("p (j r) -> p r j", r=16))
        n = 0
        for i in (4, 5, 0, 1, 2, 3):
            slot, sbuf_ = self.w_next(f"wk_{i}")
            g = i // 2
            d = DIL[g]
            for mm in range(4):
                hp = (i % 2) * 4 + mm
                pi = self.psrr()
                pt, pbuf = self.ps[pi], self.ps_b[pi]
                for kc in range(KC):
                    PE.op(nc.tensor.matmul, [sbuf_, self.h_bs[kc]], [pbuf], pt[:, :], lhsT=slot[:, kc * 512 + mm * 128: kc * 512 + mm * 128 + 128],
                          rhs=self.hT[:, kc, :], start=(kc == 0), stop=(kc == KC - 1))
                kt, ktb = kst[n % 8], kstb[n % 8]
                n += 1
                nj = T // d
                if d == 1:
                    ACT.op(nc.scalar.activation, [pbuf], [ktb], out=kt[:, :], in_=pt[:, :], func=AF.Copy)
                    SP.dma(self.kTd[g, hp, :, s * T:(s + 1) * T], kt[:, :], [ktb], [kvb])
                else:
                    ACT.op(nc.scalar.activation, [pbuf], [ktb], out=kt[:, :].rearrange("p (r j) -> p r j", r=d),
                           in_=pt[:, :].rearrange("p (j r) -> p r j", r=d), func=AF.Copy)
                    if d == 16:
                        SP.dma(self.kT2d[hp, :, s, :], kt[:, :], [ktb], [kvb])
                    else:
                        dst = self.kTd[g, hp, :, :].rearrange("p (r j) -> p r j", r=d)[:, :, s * nj:(s + 1) * nj]
                        SP.dma(dst, kt[:, :].rearrange("p (r j) -> p r j", r=d), [ktb], [kvb])
        n = 0
        for i in range(6):
            slot, sbuf_ = self.w_next(f"wv_{i}")
            g = i // 2
            d = DIL[g]
            half = i % 2
            for ch in range(4):
                if d == 1:
                    cols = lambda kc: self.hT[:, kc, ch * 128:(ch + 1) * 128]
                elif d == 4:
                    cols = lambda kc: self.hT[:, kc, :].rearrange("p (j r) -> p r j", r=4)[:, ch, :]
                else:
                    cols = lambda kc: h16[:, kc, ch * 128:(ch + 1) * 128]
                pi = self.psrr()
                pt, pbuf = self.ps[pi], self.ps_b[pi]
                for kc in range(KC):
                    PE.op(nc.tensor.matmul, [sbuf_, self.h_bs[kc], h16b], [pbuf], pt[:, :], lhsT=cols(kc), rhs=slot[:, kc * 512:(kc + 1) * 512],
                          start=(kc == 0), stop=(kc == KC - 1))
                vt, vtb = vst[n % 12], vstb[n % 12]
                n += 1
                DVE.op(nc.vector.tensor_copy, [pbuf], [vtb], out=vt[:, 0:512], in_=pt[:, :])
                if d == 1:
                    r0 = s * T + ch * 128
                    SP.dma(self.vd[g, r0:r0 + 128, half * 512:(half + 1) * 512], vt[:, 0:512], [vtb], [kvb])
                elif d == 4:
                    r0 = ch * 1024 + s * 128
                    SP.dma(self.vd[g, r0:r0 + 128, half * 512:(half + 1) * 512], vt[:, 0:512], [vtb], [kvb])
                else:
                    dst = self.vd[g, :, half * 512:(half + 1) * 512].rearrange("(r j) f -> r j f", r=16)[ch * 4:(ch + 1) * 4, s * 32:(s + 1) * 32, :]
                    SP.dma(dst, vt[:, 0:512], [vtb], [kvb])
        self.stage_end(es)

    def attn_mixer(self, l, s, x, xb):
        nc = self.nc
        PE, ACT, DVE, SP = self.PE, self.ACT, self.DVE, self.SP
        es = self.stage_begin()
        oT = self.sb(es, "oT", [128, KC, T], BF16)
        oTb = self.newbuf("oT")
        qT = [self.sb(es, f"qT{i}", [128, 3, T], BF16) for i in range(2)]
        qTb = [self.newbuf(f"qT{i}") for i in range(2)]
        KW = 640 + 1024 + 2560
        kt = [self.sb(es, f"ktile{i}", [128, KW], BF16) for i in range(2)]
        ktb = [self.newbuf(f"ktile{i}") for i in range(2)]
        kraw = [self.sb(es, f"kraw{i}", [128, 5, T], BF16) for i in range(2)]
        krawb = [self.newbuf(f"kraw{i}") for i in range(2)]
        NVC = 5 + 8 + 32
        vt = [self.sb(es, f"vtile{i}", [128, NVC, 128], BF16) for i in range(2)]
        vtb = [self.newbuf(f"vtile{i}") for i in range(2)]
        NB = 3
        ET = [self.sb(es, f"E{i}", [128, T], BF16) for i in range(NB)]
        ETb = [self.newbuf(f"E{i}") for i in range(NB)]
        PT = [self.sb(es, f"P{i}", [128, T], BF16) for i in range(NB)]
        PTb = [self.newbuf(f"P{i}") for i in range(NB)]
        rD = self.sb(es, "rD", [128, T], F32)
        rDb = self.newbuf("rD")
        self.norm(x, xb, lambda c: self.A(l, 0, c), lambda c: self.M(l, 0, c), reuse=(s % 2 if l == 2 else None))
        kv_reads = [self.kv_bufs[t] for t in range(max(0, s - 4), s + 1)]
        na = min(128, 32 * s)

        ntl = na // 32 + 1

        def load_raw(hp):
            SP.dma(kraw[hp % 2][:, 0:ntl, :], self.kT2d[hp, :, s - ntl + 1:s + 1, :], kv_reads, [krawb[hp % 2]])

        load_raw(0)
        load_raw(1)

        def prep(hp):
            par = hp % 2
            K_, Kb = kt[par], ktb[par]
            V_, Vb = vt[par], vtb[par]
            lo0 = 128 if s == 0 else 0
            SP.dma(K_[:, lo0:640], self.kTd[0, hp, :, s * T - 128 + lo0: s * T + 512], kv_reads, [Kb])
            src_ = self.vd[0, s * T - 128 + lo0: s * T + 512, hp * 128:(hp + 1) * 128].rearrange("(c p) f -> p c f", p=128)
            SP.dma(V_[:, lo0 // 128:5, :], src_, kv_reads, [Vb])
            srck = self.kTd[1, hp, :, :].rearrange("p (r j) -> p r j", r=4)[:, :, 128 * (s - 1) + lo0: 128 * (s + 1)]
            SP.dma(K_[:, 640:640 + 1024].rearrange("p (r j) -> p r j", r=4)[:, :, lo0:256], srck, kv_reads, [Kb])
            c0 = lo0 // 128
            v1 = self.vd[1, :, hp * 128:(hp + 1) * 128].rearrange("(r c p) f -> p r c f", r=4, p=128)
            for c2 in range(c0, 2):
                SP.dma(V_[:, 5:13, :].rearrange("p (r c) f -> p r c f", r=4)[:, :, c2, :], v1[:, :, s - 1 + c2, :], kv_reads, [Vb])
            k2 = K_[:, 640 + 1024:].rearrange("p (r j) -> p r j", r=16)
            R_, Rb = kraw[par], krawb[par]
            ACT.op(nc.scalar.activation, [Rb], [Kb], out=k2[:, :, 128 - na:160].rearrange("p r (t j) -> p r t j", j=32),
                   in_=R_[:, 0:ntl, :].rearrange("p t (r j) -> p r t j", r=16), func=AF.Copy)
            if hp + 2 < 8:
                load_raw(hp + 2)
            v2 = self.vd[2, :, hp * 128:(hp + 1) * 128].rearrange("(r j) f -> j r f", r=16)
            if na > 0:
                SP.dma(V_[0:na, 13:29, :], v2[32 * s - na:32 * s, :, :], kv_reads, [Vb])
            SP.dma(V_[0:32, 29:45, :], v2[32 * s:32 * s + 32, :, :], kv_reads, [Vb])
            slot, sbuf_ = self.w_next(f"wq{l}_{hp}")
            Q_, Qb = qT[par], qTb[par]
            for g in range(3):
                pt, pbuf = self.ps[3], self.ps_b[3]
                for kc in range(KC):
                    base = (g * 8 + kc) * 128
                    PE.op(nc.tensor.matmul, [sbuf_, self.h_bs[kc]], [pbuf], pt[:, :], lhsT=slot[:, base:base + 128], rhs=self.hT[:, kc, :],
                          start=(kc == 0), stop=(kc == KC - 1))
                d = DIL[g]
                if d == 1:
                    ACT.op(nc.scalar.activation, [pbuf], [Qb], out=Q_[:, g, :], in_=pt[:, :], func=AF.Copy)
                else:
                    ACT.op(nc.scalar.activation, [pbuf], [Qb], out=Q_[:, g, :].rearrange("p (r j) -> p r j", r=d),
                           in_=pt[:, :].rearrange("p (j r) -> p r j", r=d), func=AF.Copy)

        units_all = []
        for hp in range(8):
            ulist = []
            for g in range(3):
                batches = []
                if g < 2:
                    for kind in (0, 1):
                        us = []
                        for u in range(4):
                            if g == 0:
                                if kind == 0 and s == 0 and u == 0:
                                    continue
                                us.append((u, (u + kind) * 128, u + kind))
                            else:
                                if kind == 0 and s == 0:
                                    continue
                                us.append((u, 640 + u * 256 + kind * 128, 5 + u * 2 + kind))
                        if us:
                            batches.append((kind, 128, 128, us))
                else:
                    if na > 0:
                        batches.append((0, 32, na, [(r, 640 + 1024 + r * 160 + 128 - na, 13 + r) for r in range(16)]))
                    batches.append((1, 32, 32, [(r, 640 + 1024 + r * 160 + 128, 29 + r) for r in range(16)]))
                for (kind, nq, nk, us) in batches:
                    for hh in range(2):
                        ulist.append(dict(hp=hp, g=g, kind=kind, nq=nq, nk=nk, us=us, hh=hh))
            ulist[0]["first_of_hp"] = True
            ulist[-1]["last_of_hp"] = True
            seen = set()
            for ud in ulist:
                if ud["hh"] not in seen:
                    ud["first_pv"] = True
                    seen.add(ud["hh"])
            units_all += ulist

        def s_phase(i, ud):
            hp, g, kind, nq, nk, us, hh = ud["hp"], ud["g"], ud["kind"], ud["nq"], ud["nk"], ud["us"], ud["hh"]
            par = hp % 2
            K_, Kb = kt[par], ktb[par]
            Q_, Qb = qT[par], qTb[par]
            h = hp * 2 + hh
            r0, r1 = hh * 64, hh * 64 + 64
            si = i % NB
            Sp, Spb = self.ps[si], self.ps_b[si]
            for (u, kcol, vch) in us:
                PE.op(nc.tensor.matmul, [Kb, Qb], [Spb], Sp[0:nk, u * nq:(u + 1) * nq], lhsT=K_[r0:r1, kcol:kcol + nk],
                      rhs=Q_[r0:r1, g, u * nq:(u + 1) * nq], start=True, stop=True)
            u0 = us[0][0]
            u1 = us[-1][0] + 1
            nu = u1 - u0
            E_, Eb = ET[si], ETb[si]
            P_, Pb_ = PT[si], PTb[si]
            ACT.op(nc.scalar.activation, [Spb], [Eb], out=E_[0:nk, u0 * nq:u1 * nq], in_=Sp[0:nk, u0 * nq:u1 * nq],
                   func=AF.Exp, scale=HD ** -0.5)
            if g == 2 and kind == 0 and nk < 128:
                mk = self.masks2[0:nk, nk // 32 - 1, h, 0:32]
            else:
                mk = self.masks[0:nk, g * NH + h, kind * 128: kind * 128 + nq]
            DVE.op(nc.vector.tensor_tensor, [Eb, self.masks_b], [Pb_],
                   out=P_[0:nk, u0 * nq:u1 * nq].rearrange("p (u q) -> p u q", u=nu),
                   in0=E_[0:nk, u0 * nq:u1 * nq].rearrange("p (u q) -> p u q", u=nu),
                   in1=mk.unsqueeze(1).to_broadcast([nk, nu, nq]), op=ALU.mult)

        def pv_phase(i, ud):
            hp, g, kind, nq, nk, us, hh = ud["hp"], ud["g"], ud["kind"], ud["nq"], ud["nk"], ud["us"], ud["hh"]
            par = hp % 2
            V_, Vb = vt[par], vtb[par]
            r0, r1 = hh * 64, hh * 64 + 64
            si = i % NB
            P_, Pb_ = PT[si], PTb[si]
            Np, Npb = self.ps[4 + par], self.ps_b[4 + par]
            Dp, Dpb = self.ps[6 + par], self.ps_b[6 + par]
            d = DIL[g]
            first = ud.get("first_pv", False)
            for (u, kcol, vch) in us:
                if d == 1:
                    No = Np[r0:r1, u * 128:(u + 1) * 128]
                    Do = Dp[r0:r1, u * 128:(u + 1) * 128]
                else:
                    No = Np[r0:r1, :].rearrange("p (j r) -> p r j", r=d)[:, u, :]
                    Do = Dp[r0:r1, :].rearrange("p (j r) -> p r j", r=d)[:, u, :]
                PE.op(nc.tensor.matmul, [Vb, Pb_], [Npb], No, lhsT=V_[0:nk, vch, r0:r1], rhs=P_[0:nk, u * nq:(u + 1) * nq],
                      start=first, stop=False, skip_group_check=True)
                PE.op(nc.tensor.matmul, [self.ones_b, Pb_], [Dpb], Do, lhsT=self.ones[0:nk, 0:64], rhs=P_[0:nk, u * nq:(u + 1) * nq],
                      start=first, stop=False, skip_group_check=True)
                first = False
            if ud.get("last_of_hp"):
                DVE.op(nc.vector.reciprocal, [Dpb], [rDb], out=rD[:, :], in_=Dp[:, :])
                DVE.op(nc.vector.tensor_tensor, [Npb, rDb], [oTb], out=oT[:, hp, :], in0=Np[:, :], in1=rD[:, :], op=ALU.mult)

        LOOK = 2
        n = len(units_all)
        prep(0)
        prep(1)
        for i in range(n + LOOK):
            if i < n:
                s_phase(i, units_all[i])
            if i - LOOK >= 0:
                ud = units_all[i - LOOK]
                pv_phase(i - LOOK, ud)
                if ud.get("last_of_hp") and ud["hp"] + 2 < 8:
                    prep(ud["hp"] + 2)
        self._psi = 0
        for half in range(2):
            slot, sbuf_ = self.w_next(f"wo{l}_{half}")
            for mm in range(4):
                m = half * 4 + mm
                pi = self.psrr()
                pt, pbuf = self.ps[pi], self.ps_b[pi]
                for kc in range(KC):
                    PE.op(nc.tensor.matmul, [sbuf_, oTb], [pbuf], pt[:, :], lhsT=slot[:, kc * 512 + mm * 128: kc * 512 + mm * 128 + 128],
                          rhs=oT[:, kc, :], start=(kc == 0), stop=(kc == KC - 1))
                self.residual(x, xb, m, pt, pbuf, self.M(l, 2, m))
        self.emit_ms()
        self.stage_end(es)


_CACHE = {}


def get_prog(n_tiles=NT, nstages=len(STAGES)):
    key = (n_tiles, nstages)
    if key not in _CACHE:
        p = Prog(n_tiles, nstages)
        p.build()
        _CACHE[key] = p
    return _CACHE[key]


def kernel(**inputs):
    inp = {k: np.asarray(v) for k, v in inputs.items()}
    W, A, per_core = host_prepare(inp)
    p = get_prog()
    in_maps = [{"xT": pc["xT"], "wts": W, "adaw": A, "vecs": pc["vecs"]} for pc in per_core]
    res = run_bass_kernel_spmd(p.nc, in_maps, core_ids=list(range(8)))
    out = np.stack([np.ascontiguousarray(r["outT"].T) for r in res.results], axis=0)
    return out.astype(np.float32)
```

```python
import math
from contextlib import ExitStack

import numpy as np
import concourse.bass as bass
import concourse.mybir as mybir
from concourse.bass_utils import run_bass_kernel_spmd

F32 = mybir.dt.float32
BF16 = mybir.dt.bfloat16
AF = mybir.ActivationFunctionType
ALU = mybir.AluOpType

D = 1024
S = 4096
FF = 2816
NFC = 22
T = 512
NT = S // T
KC = 8
EPS = 1e-6
POOLW = (2, 4, 8, 16)
DIL = (1, 4, 16)
NH = 16
HD = 64
NSLOT = 5
SLOTW = 4096
BIG = 30000.0


def _alibi_slopes(n):
    def pow2(m):
        start = 2.0 ** (-(2.0 ** -(math.log2(m) - 3)))
        return [start ** (i + 1) for i in range(m)]
    if math.log2(n).is_integer():
        s = pow2(n)
    else:
        c = 2 ** math.floor(math.log2(n))
        s = pow2(c) + pow2(2 * c)[0::2][: n - c]
    s = np.asarray(s, dtype=np.float32)
    return -np.sort(-s)


SLOPES = _alibi_slopes(3 * NH).reshape(3, NH)


STAGES = ["mix0", "ffn0", "mix1", "ffn1", "kv", "mix2", "ffn2", "mix3", "ffn3"]


def stage_pieces(st):
    P = []
    if st.startswith("mix"):
        l = int(st[3])
        if l < 2:
            P += [(f"win{l}_0", 4096), (f"win{l}_1", 4096), (f"wgrp{l}", 2048), (f"wout{l}_0", 4096), (f"wout{l}_1", 4096)]
        else:
            P += [(f"wq{l}_{hp}", 3072) for hp in range(8)]
            P += [(f"wo{l}_0", 4096), (f"wo{l}_1", 4096)]
    elif st == "kv":
        P += [(f"wk_{i}", 4096) for i in (4, 5, 0, 1, 2, 3)]
        P += [(f"wv_{i}", 4096) for i in range(6)]
    else:
        l = int(st[3])
        P += [(f"wup{l}_{i}", 4096) for i in range(11)]
        for r in range(2):
            P += [(f"wdn{l}_{r}_0", 4096), (f"wdn{l}_{r}_1", 4096), (f"wdn{l}_{r}_2", 3072)]
    return P


def piece_list(nstages=len(STAGES)):
    P = []
    for st in STAGES[:nstages]:
        P += stage_pieces(st)
    return P


def ada_piece_list():
    P = []
    for l in range(4):
        P += [(f"ada{l}_{i}", 4096) for i in range(12)]
    P += [(f"kvada_{i}", 4096) for i in range(4)]
    return P


def _offsets(pl):
    off = {}
    o = 0
    for n, w in pl:
        off[n] = (o, w)
        o += w
    return off, o


def vec_layout():
    L = {}
    o = 0

    def add(name, n):
        nonlocal o
        L[name] = (o, n)
        o += n
    for l in range(4):
        add(f"ada_b{l}", 48)
        add(f"n1g{l}", 8)
        add(f"n2g{l}", 8)
        add(f"cw{l}", 66)
        add(f"cb{l}", 22)
    for l in range(2):
        add(f"psc{l}", 8)
    add("kvg", 8)
    add("kvb", 16)
    add("fg", 8)
    add("c", 8)
    add("invc", 64)
    add("dist", 256)
    add("dist2", 96)
    return L, o


def pmaj(v):
    return np.ascontiguousarray(v.reshape(-1, 128).T)


def kmajor(W, c0, ncols):
    K = W.shape[0]
    a = W[:, c0:c0 + ncols].reshape(K // 128, 128, ncols).transpose(1, 0, 2)
    return a.reshape(128, -1)


def host_prepare(inp):
    pl = piece_list()
    off, tot = _offsets(pl)
    W = np.empty((128, tot), np.float32)

    def put(name, arr):
        o, w = off[name]
        assert arr.shape == (128, w), (name, arr.shape, w)
        W[:, o:o + w] = arr
    for l in range(4):
        if l < 2:
            for m in range(2):
                put(f"win{l}_{m}", kmajor(inp["pool_w_in"][l], m * 512, 512))
                put(f"wout{l}_{m}", kmajor(inp["pool_w_out"][l], m * 512, 512))
            g = inp["pool_w_grp"][l]
            put(f"wgrp{l}", g.reshape(4, 2, 128, 256).transpose(2, 0, 1, 3).reshape(128, 2048))
        else:
            j = l - 2
            wq = inp["attn_w_q"][j]
            for hp in range(8):
                a = np.stack([kmajor(wq, g * 1024 + hp * 128, 128).reshape(128, 8, 128) for g in range(3)], axis=1)
                put(f"wq{l}_{hp}", a.reshape(128, 3072))
            for m in range(2):
                put(f"wo{l}_{m}", kmajor(inp["attn_w_o"][j], m * 512, 512))
        if l == 2:
            for i in range(6):
                put(f"wk_{i}", kmajor(inp["w_kv"], i * 512, 512))
                put(f"wv_{i}", kmajor(inp["w_kv"], 3072 + i * 512, 512))
        wu = inp["ffn_w_up"][l]
        for i in range(11):
            parts = []
            for jj in range(2):
                fc = 2 * i + jj
                for av in range(2):
                    parts.append(kmajor(wu, av * FF + fc * 128, 128))
            put(f"wup{l}_{i}", np.concatenate(parts, axis=1))
        wd = inp["ffn_w_down"][l]
        for r in range(2):
            a = wd[:, r * 512:(r + 1) * 512].reshape(NFC, 128, 512).transpose(1, 0, 2)
            put(f"wdn{l}_{r}_0", a[:, 0:8].reshape(128, 4096))
            put(f"wdn{l}_{r}_1", a[:, 8:16].reshape(128, 4096))
            put(f"wdn{l}_{r}_2", a[:, 16:22].reshape(128, 3072))
    apl = ada_piece_list()
    aoff, atot = _offsets(apl)
    A = np.empty((128, atot), np.float32)
    for l in range(4):
        for i in range(12):
            o, w = aoff[f"ada{l}_{i}"]
            A[:, o:o + w] = kmajor(inp["ada_w"][l], i * 512, 512)
    for i in range(4):
        o, w = aoff[f"kvada_{i}"]
        A[:, o:o + w] = kmajor(inp["kv_ada_w"], i * 512, 512)
    VL, nv = vec_layout()
    shared = np.zeros((128, nv), np.float32)

    def vput(name, arr):
        o, n = VL[name]
        assert arr.shape == (128, n), (name, arr.shape)
        shared[:, o:o + n] = arr
    for l in range(4):
        vput(f"ada_b{l}", pmaj(inp["ada_b"][l]))
        vput(f"n1g{l}", pmaj(inp["norm1_g"][l]))
        vput(f"n2g{l}", pmaj(inp["norm2_g"][l]))
        cw = inp["ffn_conv_w"][l]
        vput(f"cw{l}", np.concatenate([pmaj(cw[k]) for k in range(3)], axis=1))
        vput(f"cb{l}", pmaj(inp["ffn_conv_b"][l]))
    for l in range(2):
        vput(f"psc{l}", pmaj(inp["pool_scale"][l]))
    vput("kvg", pmaj(inp["kv_norm_g"]))
    vput("kvb", pmaj(inp["kv_ada_b"]))
    vput("fg", pmaj(inp["final_g"]))
    invc = np.zeros((128, 4, 16), np.float32)
    for g, w in enumerate(POOLW):
        invc[:, g, :] = 1.0 / np.minimum(np.arange(16) + 1, w)
    vput("invc", invc.reshape(128, 64))
    k = np.arange(128)[:, None]
    q = np.arange(128)[None, :]
    dprev = (q - k + 128).astype(np.float32)
    dprev[dprev > 128] = BIG
    dcur = (q - k).astype(np.float32)
    dcur[dcur < 0] = BIG
    vput("dist", np.concatenate([dprev, dcur], axis=1))
    d2 = []
    for na in (32, 64, 96):
        dd = (q[:, 0:32] - k + na).astype(np.float32)
        dd[dd > 128] = BIG
        dd[k[:, 0] >= na, :] = BIG
        d2.append(dd)
    vput("dist2", np.concatenate(d2, axis=1))
    per_core = []
    for b in range(8):
        v = shared.copy()
        o, n = VL["c"]
        v[:, o:o + n] = pmaj(inp["c"][b])
        per_core.append({"xT": np.ascontiguousarray(inp["x"][b].T), "vecs": v})
    return W, A, per_core


class Chan:
    __slots__ = ("sem", "val")


class Buf:
    __slots__ = ("name", "w", "r")

    def __init__(self, name, seed=None):
        self.name = name
        self.w = {}
        self.r = dict(seed) if seed else {}

    def tokens(self):
        d = dict(self.w)
        for ch, v in self.r.items():
            if d.get(ch, 0) < v:
                d[ch] = v
        return d


def _merge(d, ch, v):
    if d.get(ch, 0) < v:
        d[ch] = v


class Ctx:
    def __init__(self, nc, es):
        self.nc = nc
        self.es = es
        self.nsem = 0

    def new_chan(self):
        sem = self.es.enter_context(self.nc.semaphore(f"sm{self.nsem}"))
        self.nsem += 1
        c = Chan()
        c.sem = sem
        c.val = 0
        return c


class Eng:
    EPOCH = 16000

    def __init__(self, ctx, eng, name, is_pe=False, n_dma=0):
        self.ctx = ctx
        self.e = eng
        self.name = name
        self.is_pe = is_pe
        self.chan = ctx.new_chan()
        self.waited = {}
        self.dma_pool = [ctx.new_chan() for _ in range(n_dma)]
        self.dma_i = 0
        self.n = 0

    def wait_tok(self, ch, v):
        if ch is self.chan and self.is_pe:
            return
        if self.waited.get(ch, 0) >= v:
            return
        self.e.wait_ge(ch.sem, v)
        self.waited[ch] = v

    def sync(self, reads, writes):
        for b in reads:
            for ch, v in b.w.items():
                self.wait_tok(ch, v)
        for b in writes:
            for ch, v in b.w.items():
                self.wait_tok(ch, v)
            for ch, v in b.r.items():
                self.wait_tok(ch, v)

    def op(self, fn, reads, writes, *a, **k):
        self.sync(reads, writes)
        ins = fn(*a, **k)
        if self.chan.val >= self.EPOCH:
            self.chan = self.ctx.new_chan()
        ch = self.chan
        ch.val += 1
        ins.then_inc(ch.sem, 1)
        for b in reads:
            _merge(b.r, ch, ch.val)
        for b in writes:
            _merge(b.w, ch, ch.val)
        self.n += 1
        return ins

    def dma(self, out, in_, reads, writes, **k):
        ch = self.dma_pool[self.dma_i % len(self.dma_pool)]
        self.dma_i += 1
        if ch.val:
            self.wait_tok(ch, ch.val)
        self.sync(reads, writes)
        ins = self.e.dma_start(out=out, in_=in_, **k)
        ch.val += 16
        ins.then_inc(ch.sem, 16)
        for b in reads:
            _merge(b.r, ch, ch.val)
        for b in writes:
            _merge(b.w, ch, ch.val)
        self.n += 1
        return ins

    def wait_all(self, bufs):
        for b in bufs:
            for ch, v in b.tokens().items():
                self.wait_tok(ch, v)


class Prog:
    def __init__(self, n_tiles=NT, nstages=len(STAGES), dbg=None):
        self.n_tiles = n_tiles
        self.nstages = nstages
        self.full = nstages == len(STAGES)
        self.dbg = dbg
        nc = bass.Bass("TRN2", target_bir_lowering=False)
        self.nc = nc
        self.es = ExitStack()
        self.pl = piece_list()
        self.poff, self.ptot = _offsets(self.pl)
        self.apl = ada_piece_list()
        self.aoff, self.atot = _offsets(self.apl)
        self.VL, self.nv = vec_layout()
        self.seed = {}

    def sb(self, es, name, shape, dt):
        self._nalloc = getattr(self, "_nalloc", 0) + 1
        return es.enter_context(self.nc.sbuf_tensor(f"{name}_{self._nalloc}", list(shape), dt))

    def newbuf(self, name):
        b = Buf(name, self.seed)
        self.stage_bufs.append(b)
        return b

    def stage_begin(self):
        self.stage_bufs = []
        return ExitStack()

    def stage_end(self, es):
        seed = dict(self.seed)
        for b in self.stage_bufs:
            for ch, v in b.tokens().items():
                _merge(seed, ch, v)
        self.seed = seed
        es.close()

    def w_init(self):
        self.wslots = [self.sb(self.es, f"wslot{i}", [128, SLOTW], BF16) for i in range(NSLOT)]
        self.wbufs = [Buf(f"wslot{i}") for i in range(NSLOT)]
        self.wsched = []
        self.w_issued = 0
        self.w_used = 0

    def w_issue_upto(self, idx):
        while self.w_issued <= idx and self.w_issued < len(self.wsched):
            i = self.w_issued
            src, w, name = self.wsched[i]
            slot = i % NSLOT
            self.POOL.dma(self.wslots[slot][:, 0:w], src, [], [self.wbufs[slot]], max_dma_last_dim=2048)
            self.w_issued += 1

    def w_next(self, name):
        i = self.w_used
        src, w, nm = self.wsched[i]
        assert nm == name, (nm, name)
        self.w_issue_upto(i + NSLOT - 1)
        self.w_used += 1
        slot = i % NSLOT
        return self.wslots[slot], self.wbufs[slot]

    def build(self):
        nc = self.nc
        es = self.es
        ctx = Ctx(nc, es)
        self.ctx = ctx
        nt = self.n_tiles
        self.xT = nc.dram_tensor("xT", [D, S], F32, kind="ExternalInput").ap()
        self.wts = nc.dram_tensor("wts", [128, self.ptot], F32, kind="ExternalInput").ap()
        self.adaw = nc.dram_tensor("adaw", [128, self.atot], F32, kind="ExternalInput").ap()
        self.vecs_d = nc.dram_tensor("vecs", [128, self.nv], F32, kind="ExternalInput").ap()
        self.outT = nc.dram_tensor("outT", [D, S], F32, kind="ExternalOutput").ap()
        self.kTd = nc.dram_tensor("kTd", [3, 8, 128, S], BF16, kind="Internal").ap()
        self.vd = nc.dram_tensor("vd", [3, S, D], BF16, kind="Internal").ap()
        self.kT2d = nc.dram_tensor("kT2d", [8, 128, NT, T], BF16, kind="Internal").ap()
        self.kv_bufs = [Buf(f"kv{s}") for s in range(NT)]

        self.PE = Eng(ctx, nc.tensor, "pe", is_pe=True)
        self.ACT = Eng(ctx, nc.scalar, "act")
        self.DVE = Eng(ctx, nc.vector, "dve")
        self.POOL = Eng(ctx, nc.gpsimd, "pool", n_dma=NSLOT + 1)
        self.SP = Eng(ctx, nc.sync, "sp", n_dma=40)
        PE, ACT, DVE, POOL, SP = self.PE, self.ACT, self.DVE, self.POOL, self.SP

        self.vecs = self.sb(es, "vecs_sb", [128, self.nv], F32)
        self.vecs_b = Buf("vecs")
        self.modT = self.sb(es, "modT", [128, 4 * 48 + 16], F32)
        self.der = self.sb(es, "der", [128, 4 * 16 + 8], F32)
        self.mod_b = Buf("mod")
        self.ones = self.sb(es, "ones", [128, 128], BF16)
        self.ones_b = Buf("ones")
        self.condb = self.sb(es, "condb", [128, 8], BF16)
        self.cond_b = Buf("cond")
        self.xTs = [self.sb(es, f"xTs{i}", [128, KC, T], F32) for i in range(2)]
        self.x_bufs = [[Buf(f"x{i}_{c}") for c in range(KC)] for i in range(2)]
        self.hT = self.sb(es, "hT", [128, KC, T], BF16)
        self.h_bs = [Buf(f"hT{c}") for c in range(KC)]
        self.sq = self.sb(es, "sq", [128, KC, T], BF16)
        self.sq_bs = [Buf(f"sq{c}") for c in range(KC)]
        self.ms_ready = None
        self.std = self.sb(es, "std", [128, T], F32)
        self.rstd = self.sb(es, "rstd", [128, T], F32)
        self.rstd2 = self.sb(es, "rstd2", [128, T], F32)
        self.rstd2_b = Buf("rstd2")
        self.std_b = Buf("std")
        self.rstd_b = Buf("rstd")
        self.ntmp = [self.sb(es, f"ntmp{i}", [128, T], F32) for i in range(2)]
        self.ntmp_b = [Buf(f"ntmp{i}") for i in range(2)]
        self.uhalo = [self.sb(es, f"uhalo{l}", [128, KC, 16], F32) for l in range(2)]
        self.uhalo_b = [Buf(f"uhalo{l}") for l in range(2)]
        self.ahalo = [self.sb(es, f"ahalo{l}", [128, NFC, 2], F32) for l in range(4)]
        self.ahalo_b = [Buf(f"ahalo{l}") for l in range(4)]
        self.masks = self.sb(es, "masks", [128, 48, 256], BF16)
        self.masks_b = Buf("masks")
        self.masks2 = self.sb(es, "masks2", [128, 3, NH, 32], BF16)
        self.ps = [es.enter_context(nc.psum_tensor(f"ps{i}", [128, 512], F32)) for i in range(8)]
        self.ps_b = [Buf(f"ps{i}") for i in range(8)]
        self.w_init()

        def ada_sched(l):
            for n, w in self.apl:
                if n.startswith(f"ada{l}_") or (l == 4 and n.startswith("kvada")):
                    o, _ = self.aoff[n]
                    self.wsched.append((self.adaw[:, o:o + w], w, n))
        stages = STAGES[:self.nstages]
        nP = min(5, len(stages))
        order = []
        for s in range(nt):
            order += [(s, st) for st in stages[:nP]]
            if s >= 1:
                order += [(s - 1, st) for st in stages[nP:]] + [(s - 1, "out")]
        order += [(nt - 1, st) for st in stages[nP:]] + [(nt - 1, "out")]
        for (s, st) in order:
            if st == "out":
                continue
            if s == 0 and st.startswith("mix"):
                ada_sched(int(st[3]))
            if s == 0 and st == "kv":
                ada_sched(4)
            for n, w in stage_pieces(st):
                o, _ = self.poff[n]
                self.wsched.append((self.wts[:, o:o + w], w, n))

        self.prologue()
        self.tile_init()
        for (s, st) in order:
            self.run_stage(s, st)
        for b in self.out_wait:
            SP.wait_all([b])
        return nc

    def vcol(self, name, c0=0, n=1):
        o, _ = self.VL[name]
        return self.vecs[:, o + c0:o + c0 + n]

    def prologue(self):
        nc = self.nc
        PE, ACT, DVE, POOL, SP = self.PE, self.ACT, self.DVE, self.POOL, self.SP
        SP.dma(self.vecs[:, :], self.vecs_d[:, :], [], [self.vecs_b])
        DVE.op(nc.vector.memset, [], [self.ones_b], self.ones[:, :], 1.0)
        for l in range(2):
            DVE.op(nc.vector.memset, [], [self.uhalo_b[l]], self.uhalo[l][:, :, :], 0.0)
        for l in range(4):
            DVE.op(nc.vector.memset, [], [self.ahalo_b[l]], self.ahalo[l][:, :, :], 0.0)
        ACT.op(nc.scalar.activation, [self.vecs_b], [self.cond_b], out=self.condb[:, :], in_=self.vcol("c", 0, 8), func=AF.Silu)
        if self.nstages > 5:
            for g in range(3):
                for h in range(NH):
                    ACT.op(nc.scalar.activation, [self.vecs_b], [self.masks_b], out=self.masks[:, g * NH + h, :],
                           in_=self.vcol("dist", 0, 256), func=AF.Exp, scale=-float(SLOPES[g, h]) * DIL[g])
            for i in range(3):
                for h in range(NH):
                    ACT.op(nc.scalar.activation, [self.vecs_b], [self.masks_b], out=self.masks2[:, i, h, :],
                           in_=self.vcol("dist2", i * 32, 32), func=AF.Exp, scale=-float(SLOPES[2, h]) * DIL[2])

    def compute_mod(self, l):
        nc = self.nc
        PE, ACT, DVE, POOL, SP = self.PE, self.ACT, self.DVE, self.POOL, self.SP
        pb = self.psrr()
        if True:
            ncol = 48 if l < 4 else 16
            npieces = 12 if l < 4 else 4
            pt, pbuf = self.ps[pb], self.ps_b[pb]
            first = True
            for i in range(npieces):
                slot, sbuf_ = self.w_next(f"ada{l}_{i}" if l < 4 else f"kvada_{i}")
                for mm in range(4):
                    col = i * 4 + mm
                    for kc in range(KC):
                        PE.op(nc.tensor.matmul, [sbuf_, self.cond_b], [pbuf], pt[:, col:col + 1],
                              lhsT=slot[:, kc * 512 + mm * 128: kc * 512 + mm * 128 + 128], rhs=self.condb[:, kc:kc + 1],
                              start=first, stop=(kc == KC - 1))
                        first = False
            bname = f"ada_b{l}" if l < 4 else "kvb"
            DVE.op(nc.vector.tensor_tensor, [pbuf, self.vecs_b], [self.mod_b], out=self.modT[:, l * 48:l * 48 + ncol],
                   in0=pt[:, 0:ncol], in1=self.vcol(bname, 0, ncol), op=ALU.add)
        if l < 4:
            for j, (gn, sc0) in enumerate(((f"n1g{l}", 8), (f"n2g{l}", 32))):
                DVE.op(nc.vector.scalar_tensor_tensor, [self.mod_b, self.vecs_b], [self.mod_b],
                       out=self.der[:, l * 16 + j * 8: l * 16 + j * 8 + 8], in0=self.modT[:, l * 48 + sc0: l * 48 + sc0 + 8],
                       scalar=1.0, in1=self.vcol(gn, 0, 8), op0=ALU.add, op1=ALU.mult)
        else:
            DVE.op(nc.vector.scalar_tensor_tensor, [self.mod_b, self.vecs_b], [self.mod_b],
                   out=self.der[:, 64:72], in0=self.modT[:, 192 + 8:192 + 16], scalar=1.0, in1=self.vcol("kvg", 0, 8),
                   op0=ALU.add, op1=ALU.mult)

    def A(self, l, which, c):
        return self.der[:, l * 16 + which * 8 + c: l * 16 + which * 8 + c + 1]

    def M(self, l, j, c):
        return self.modT[:, l * 48 + j * 8 + c: l * 48 + j * 8 + c + 1]

    def norm(self, x, xb, acol, bcol, out=None, out_b=None, final=False, keep=None, reuse=None):
        nc = self.nc
        PE, ACT, DVE = self.PE, self.ACT, self.DVE
        if reuse is not None:
            self.rstd_cur = ((self.rstd, self.rstd_b), (self.rstd2, self.rstd2_b))[reuse]
        else:
            if self.ms_ready is None:
                for c in range(KC):
                    ACT.op(nc.scalar.activation, [xb[c]], [self.sq_bs[c]], out=self.sq[:, c, :], in_=x[:, c, :], func=AF.Square)
                self.emit_ms()
            pt, pbuf = self.ms_ready
            self.ms_ready = None
            ACT.op(nc.scalar.activation, [pbuf, self.eps_b], [self.std_b], out=self.std[:, :], in_=pt[:, :], func=AF.Sqrt,
                   scale=1.0 / D, bias=self.epsc[:, 0:1])
            if keep is not None:
                kt_, kb_ = ((self.rstd, self.rstd_b), (self.rstd2, self.rstd2_b))[keep]
                DVE.op(nc.vector.reciprocal, [self.std_b], [kb_], out=kt_[:, :], in_=self.std[:, :])
                self.rstd_cur = (kt_, kb_)
            else:
                DVE.op(nc.vector.reciprocal, [self.std_b], [pbuf], out=pt[:, :], in_=self.std[:, :])
                self.rstd_cur = (pt, pbuf)
        rt, rb = self.rstd_cur
        for c in range(KC):
            tb = self.ntmp_b[c % 2]
            tt = self.ntmp[c % 2]
            DVE.op(nc.vector.tensor_tensor, [xb[c], rb], [tb], out=tt[:, :], in0=x[:, c, :], in1=rt[:, :], op=ALU.mult)
            if final:
                ACT.op(nc.scalar.activation, [tb, self.vecs_b], [out_b], out=out[:, c, :], in_=tt[:, :], func=AF.Copy,
                       scale=acol(c))
            else:
                ACT.op(nc.scalar.activation, [tb, self.mod_b], [self.h_bs[c]], out=self.hT[:, c, :], in_=tt[:, :], func=AF.Identity,
                       scale=acol(c), bias=bcol(c))

    def emit_ms(self):
        nc = self.nc
        pi = self.psrr()
        pt, pbuf = self.ps[pi], self.ps_b[pi]
        for c in range(KC):
            self.PE.op(nc.tensor.matmul, [self.sq_bs[c], self.ones_b], [pbuf], pt[:, :], lhsT=self.ones[:, :], rhs=self.sq[:, c, :],
                       start=(c == 0), stop=(c == KC - 1))
        self.ms_ready = (pt, pbuf)

    def residual(self, x, xb, m, pt, pbuf, gcol):
        nc = self.nc
        self.DVE.op(nc.vector.scalar_tensor_tensor, [pbuf, self.mod_b, xb[m]], [xb[m]], out=x[:, m, :], in0=pt[:, :], scalar=gcol,
                    in1=x[:, m, :], op0=ALU.mult, op1=ALU.add)
        self.ACT.op(nc.scalar.activation, [xb[m]], [self.sq_bs[m]], out=self.sq[:, m, :], in_=x[:, m, :], func=AF.Square)

    def psrr(self):
        i = self._psi
        self._psi = (self._psi + 1) % 8
        return i

    def tile_init(self):
        nc = self.nc
        self._psi = 0
        self.out_wait = []
        self.epsc = self.sb(self.es, "epsc", [128, 1], F32)
        self.eps_b = Buf("eps")
        self.DVE.op(nc.vector.memset, [], [self.eps_b], self.epsc[:, :], EPS)
        self.x_loaded = set()
        self.load_x(0)
        self.load_x(1)

    def load_x(self, s2):
        if s2 >= self.n_tiles or s2 in self.x_loaded:
            return
        self.x_loaded.add(s2)
        src_ = self.xT.rearrange("(c p) t -> p c t", p=128)
        self.SP.dma(self.xTs[s2 % 2][:, :, :], src_[:, :, s2 * T:(s2 + 1) * T], [], self.x_bufs[s2 % 2])

    def run_stage(self, s, st):
        nc = self.nc
        PE, ACT, DVE, POOL, SP = self.PE, self.ACT, self.DVE, self.POOL, self.SP
        x = self.xTs[s % 2]
        xb = self.x_bufs[s % 2]
        if st == "mix0":
            self.load_x(s)
            self.ms_ready = None
        if s == 0 and st.startswith("mix"):
            self.compute_mod(int(st[3]))
        if s == 0 and st == "kv":
            self.compute_mod(4)
        if st == "kv":
            self.kv_stage(s, x, xb)
        elif st.startswith("mix"):
            l = int(st[3])
            if l < 2:
                self.pool_mixer(l, s, x, xb)
            else:
                self.attn_mixer(l, s, x, xb)
        elif st.startswith("ffn"):
            self.ffn(int(st[3]), s, x, xb)
        else:
            es = self.stage_begin()
            o = self.sb(es, "otile", [128, KC, T], F32)
            ob = self.newbuf("otile")
            if self.full:
                self.norm(x, xb, lambda c: self.vcol("fg", c, 1), None, out=o, out_b=ob, final=True)
            else:
                self.ms_ready = None
                ACT.op(nc.scalar.activation, xb, [ob], out=o[:, :, :], in_=x[:, :, :], func=AF.Copy)
            SP.dma(self.outT.rearrange("(c p) t -> p c t", p=128)[:, :, s * T:(s + 1) * T], o[:, :, :], [ob], [])
            self.out_wait.append(ob)
            self.stage_end(es)
            self.load_x(s + 2)

    def pool_mixer(self, l, s, x, xb):
        nc = self.nc
        PE, ACT, DVE = self.PE, self.ACT, self.DVE
        es = self.stage_begin()
        U = self.sb(es, "poolU", [128, KC, 16 + T], F32)
        Ub = self.newbuf("U")
        A_ = self.sb(es, "poolA", [128, KC, 16 + T], F32)
        Ab = self.newbuf("A")
        B_ = self.sb(es, "poolB", [128, KC, 16 + T], F32)
        Bb = self.newbuf("B")
        Pb = self.sb(es, "poolP", [128, KC, T], BF16)
        Pbb = self.newbuf("P")
        Zb = self.sb(es, "poolZ", [128, KC, T], BF16)
        Zbb = self.newbuf("Z")
        self.norm(x, xb, lambda c: self.A(l, 0, c), lambda c: self.M(l, 0, c))
        DVE.op(nc.vector.tensor_copy, [self.uhalo_b[l]], [Ub], out=U[:, :, 0:16], in_=self.uhalo[l][:, :, :])
        for half in range(2):
            slot, sbuf_ = self.w_next(f"win{l}_{half}")
            for mm in range(4):
                m = half * 4 + mm
                pi = self.psrr()
                pt, pbuf = self.ps[pi], self.ps_b[pi]
                for kc in range(KC):
                    PE.op(nc.tensor.matmul, [sbuf_, self.h_bs[kc]], [pbuf], pt[:, :], lhsT=slot[:, kc * 512 + mm * 128: kc * 512 + mm * 128 + 128],
                          rhs=self.hT[:, kc, :], start=(kc == 0), stop=(kc == KC - 1))
                ACT.op(nc.scalar.activation, [pbuf], [Ub], out=U[:, m, 16:16 + T], in_=pt[:, :], func=AF.Copy)
        DVE.op(nc.vector.tensor_copy, [Ub], [self.uhalo_b[l]], out=self.uhalo[l][:, :, :], in_=U[:, :, T:T + 16])
        W_ = 16 + T
        DVE.op(nc.vector.tensor_tensor, [Ub], [Ab], out=A_[:, :, 1:W_], in0=U[:, :, 1:W_], in1=U[:, :, 0:W_ - 1], op=ALU.add)
        DVE.op(nc.vector.tensor_tensor, [Ab], [Bb], out=B_[:, 2:8, 3:W_], in0=A_[:, 2:8, 3:W_], in1=A_[:, 2:8, 1:W_ - 2], op=ALU.add)
        DVE.op(nc.vector.tensor_tensor, [Bb], [Ab], out=A_[:, 4:8, 7:W_], in0=B_[:, 4:8, 7:W_], in1=B_[:, 4:8, 3:W_ - 4], op=ALU.add)
        DVE.op(nc.vector.tensor_tensor, [Ab], [Bb], out=B_[:, 6:8, 15:W_], in0=A_[:, 6:8, 15:W_], in1=A_[:, 6:8, 7:W_ - 8], op=ALU.add)
        srcs = [(A_, Ab), (B_, Bb), (A_, Ab), (B_, Bb)]
        for g in range(4):
            St, Sb_ = srcs[g]
            DVE.op(nc.vector.scalar_tensor_tensor, [Sb_, Ub], [Pbb], out=Pb[:, 2 * g:2 * g + 2, :], in0=St[:, 2 * g:2 * g + 2, 16:16 + T],
                   scalar=1.0 / POOLW[g], in1=U[:, 2 * g:2 * g + 2, 16:16 + T], op0=ALU.mult, op1=ALU.subtract)
            if s == 0:
                o, _ = self.VL["invc"]
                for cc in range(2):
                    c = 2 * g + cc
                    tb, tt = self.ntmp_b[cc], self.ntmp[cc]
                    DVE.op(nc.vector.tensor_tensor, [Sb_, self.vecs_b], [tb], out=tt[:, 0:16], in0=St[:, c, 16:32],
                           in1=self.vecs[:, o + g * 16:o + g * 16 + 16], op=ALU.mult)
                    DVE.op(nc.vector.tensor_tensor, [tb, Ub], [Pbb], out=Pb[:, c, 0:16], in0=tt[:, 0:16], in1=U[:, c, 16:32], op=ALU.subtract)
        slot, sbuf_ = self.w_next(f"wgrp{l}")
        for g in range(4):
            for mo in range(2):
                c = 2 * g + mo
                pi = self.psrr()
                pt, pbuf = self.ps[pi], self.ps_b[pi]
                for ki in range(2):
                    base = (g * 2 + ki) * 256 + mo * 128
                    PE.op(nc.tensor.matmul, [sbuf_, Pbb], [pbuf], pt[:, :], lhsT=slot[:, base:base + 128], rhs=Pb[:, 2 * g + ki, :],
                          start=(ki == 0), stop=(ki == 1))
                ACT.op(nc.scalar.activation, [pbuf, self.vecs_b], [Zbb], out=Zb[:, c, :], in_=pt[:, :], func=AF.Copy,
                       scale=self.vcol(f"psc{l}", c, 1))
        for half in range(2):
            slot, sbuf_ = self.w_next(f"wout{l}_{half}")
            for mm in range(4):
                m = half * 4 + mm
                pi = self.psrr()
                pt, pbuf = self.ps[pi], self.ps_b[pi]
                for kc in range(KC):
                    PE.op(nc.tensor.matmul, [sbuf_, Zbb], [pbuf], pt[:, :], lhsT=slot[:, kc * 512 + mm * 128: kc * 512 + mm * 128 + 128],
                          rhs=Zb[:, kc, :], start=(kc == 0), stop=(kc == KC - 1))
                self.residual(x, xb, m, pt, pbuf, self.M(l, 2, m))
        self.emit_ms()
        self.stage_end(es)

    def ffn(self, l, s, x, xb):
        nc = self.nc
        PE, ACT, DVE = self.PE, self.ACT, self.DVE
        es = self.stage_begin()
        gT = self.sb(es, "gT", [128, NFC, T], BF16)
        gbs = [self.newbuf(f"gT{i}") for i in range(NFC)]
        c1 = [self.sb(es, f"c1_{i}", [128, T], F32) for i in range(2)]
        c1b = [self.newbuf(f"c1_{i}") for i in range(2)]
        c2 = [self.sb(es, f"c2_{i}", [128, T], F32) for i in range(2)]
        c2b = [self.newbuf(f"c2_{i}") for i in range(2)]
        vsb = [self.sb(es, f"vsb{i}", [128, T], F32) for i in range(2)]
        vsbb = [self.newbuf(f"vsb{i}") for i in range(2)]
        self.norm(x, xb, lambda c: self.A(l, 1, c), lambda c: self.M(l, 3, c))
        cwo, _ = self.VL[f"cw{l}"]
        cbo, _ = self.VL[f"cb{l}"]
        for i in range(11):
            slot, sbuf_ = self.w_next(f"wup{l}_{i}")
            for jj in range(2):
                fc = 2 * i + jj
                par = fc % 2
                pa, pab = self.ps[par * 2], self.ps_b[par * 2]
                pv, pvb = self.ps[par * 2 + 1], self.ps_b[par * 2 + 1]
                for av, (pt, pbuf) in enumerate(((pa, pab), (pv, pvb))):
                    base = (jj * 2 + av) * 1024
                    for kc in range(KC):
                        PE.op(nc.tensor.matmul, [sbuf_, self.h_bs[kc]], [pbuf], pt[:, :], lhsT=slot[:, base + kc * 128: base + kc * 128 + 128],
                              rhs=self.hT[:, kc, :], start=(kc == 0), stop=(kc == KC - 1))
                hl = self.ahalo[l]
                hb = self.ahalo_b[l]
                w0 = self.vecs[:, cwo + fc: cwo + fc + 1]
                w1 = self.vecs[:, cwo + NFC + fc: cwo + NFC + fc + 1]
                w2 = self.vecs[:, cwo + 2 * NFC + fc: cwo + 2 * NFC + fc + 1]
                ACT.op(nc.scalar.activation, [pab, self.vecs_b], [c1b[par]], out=c1[par][:, :], in_=pa[:, :], func=AF.Identity,
                       scale=w2, bias=self.vecs[:, cbo + fc:cbo + fc + 1])
                ACT.op(nc.scalar.activation, [pvb], [vsbb[par]], out=vsb[par][:, :], in_=pv[:, :], func=AF.Copy)
                DVE.op(nc.vector.scalar_tensor_tensor, [pab, self.vecs_b, c1b[par]], [c2b[par]], out=c2[par][:, 1:T], in0=pa[:, 0:T - 1],
                       scalar=w1, in1=c1[par][:, 1:T], op0=ALU.mult, op1=ALU.add)
                DVE.op(nc.vector.scalar_tensor_tensor, [hb, self.vecs_b, c1b[par]], [c2b[par]], out=c2[par][:, 0:1], in0=hl[:, fc, 1:2],
                       scalar=w1, in1=c1[par][:, 0:1], op0=ALU.mult, op1=ALU.add)
                DVE.op(nc.vector.scalar_tensor_tensor, [pab, self.vecs_b, c2b[par]], [c1b[par]], out=c1[par][:, 2:T], in0=pa[:, 0:T - 2],
                       scalar=w0, in1=c2[par][:, 2:T], op0=ALU.mult, op1=ALU.add)
                DVE.op(nc.vector.scalar_tensor_tensor, [hb, self.vecs_b, c2b[par]], [c1b[par]], out=c1[par][:, 0:2], in0=hl[:, fc, 0:2],
                       scalar=w0, in1=c2[par][:, 0:2], op0=ALU.mult, op1=ALU.add)
                DVE.op(nc.vector.tensor_copy, [pab], [hb], out=hl[:, fc, :], in_=pa[:, T - 2:T])
                ACT.op(nc.scalar.activation, [c1b[par]], [c2b[par]], out=c2[par][:, :], in_=c1[par][:, :], func=AF.Silu)
                DVE.op(nc.vector.tensor_tensor, [c2b[par], vsbb[par]], [gbs[fc]], out=gT[:, fc, :], in0=c2[par][:, :], in1=vsb[par][:, :], op=ALU.mult)
        for r in range(2):
            b0 = 4 if r == 0 else 0
            for j in range(3):
                slot, sbuf_ = self.w_next(f"wdn{l}_{r}_{j}")
                nf = 8 if j < 2 else 6
                for fl in range(nf):
                    fc = 8 * j + fl
                    for mm in range(4):
                        PE.op(nc.tensor.matmul, [sbuf_, gbs[fc]], [self.ps_b[b0 + mm]], self.ps[b0 + mm][:, :],
                              lhsT=slot[:, fl * 512 + mm * 128: fl * 512 + mm * 128 + 128], rhs=gT[:, fc, :],
                              start=(fc == 0), stop=(fc == NFC - 1))
            for mm in range(4):
                m = r * 4 + mm
                self.residual(x, xb, m, self.ps[b0 + mm], self.ps_b[b0 + mm], self.M(l, 5, m))
        self._psi = 4
        self.emit_ms()
        self.stage_end(es)

    def kv_stage(self, s, x, xb):
        nc = self.nc
        PE, ACT, DVE, SP = self.PE, self.ACT, self.DVE, self.SP
        es = self.stage_begin()
        kst = [self.sb(es, f"kst{i}", [128, T], BF16) for i in range(8)]
        kstb = [self.newbuf(f"kst{i}") for i in range(8)]
        vst = [self.sb(es, f"vst{i}", [128, 512], BF16) for i in range(12)]
        vstb = [self.newbuf(f"vst{i}") for i in range(12)]
        kvb = self.kv_bufs[s]
        self.norm(x, xb, lambda c: self.der[:, 64 + c:65 + c], lambda c: self.modT[:, 192 + c:193 + c], keep=s % 2)
        h16 = self.sb(es, "h16", [128, KC, T], BF16)
        h16b = self.newbuf("h16")
        for kc in range(KC):
            DVE.op(nc.vector.tensor_copy, [self.h_bs[kc]], [h16b], out=h16[:, kc, :].rearrange("p (r j) -> p r j", r=16),
                   in_=self.hT[:, kc, :].rearrange("p (j r) -> p r j", r=16))
        n = 0
        for i in (4, 5, 0, 1, 2, 3):
            slot, sbuf_ = self.w_next(f"wk_{i}")
            g = i // 2
            d = DIL[g]
            for mm in range(4):
                hp = (i % 2) * 4 + mm
                pi = self.psrr()
                pt, pbuf = self.ps[pi], self.ps_b[pi]
                for kc in range(KC):
                    PE.op(nc.tensor.matmul, [sbuf_, self.h_bs[kc]], [pbuf], pt[:, :], lhsT=slot[:, kc * 512 + mm * 128: kc * 512 + mm * 128 + 128],
                          rhs=self.hT[:, kc, :], start=(kc == 0), stop=(kc == KC - 1))
                kt, ktb = kst[n % 8], kstb[n % 8]
                n += 1
                nj = T // d
                if d == 1:
                    ACT.op(nc.scalar.activation, [pbuf], [ktb], out=kt[:, :], in_=pt[:, :], func=AF.Copy)
                    SP.dma(self.kTd[g, hp, :, s * T:(s + 1) * T], kt[:, :], [ktb], [kvb])
                else:
                    ACT.op(nc.scalar.activation, [pbuf], [ktb], out=kt[:, :].rearrange("p (r j) -> p r j", r=d),
                           in_=pt[:, :].rearrange("p (j r) -> p r j", r=d), func=AF.Copy)
                    if d == 16:
                        SP.dma(self.kT2d[hp, :, s, :], kt[:, :], [ktb], [kvb])
                    else:
                        dst = self.kTd[g, hp, :, :].rearrange("p (r j) -> p r j", r=d)[:, :, s * nj:(s + 1) * nj]
                        SP.dma(dst, kt[:, :].rearrange("p (r j) -> p r j", r=d), [ktb], [kvb])
        n = 0
        for i in range(6):
            slot, sbuf_ = self.w_next(f"wv_{i}")
            g = i // 2
            d = DIL[g]
            half = i % 2
            for ch in range(4):
                if d == 1:
                    cols = lambda kc: self.hT[:, kc, ch * 128:(ch + 1) * 128]
                elif d == 4:
                    cols = lambda kc: self.hT[:, kc, :].rearrange("p (j r) -> p r j", r=4)[:, ch, :]
                else:
                    cols = lambda kc: h16[:, kc, ch * 128:(ch + 1) * 128]
                pi = self.psrr()
                pt, pbuf = self.ps[pi], self.ps_b[pi]
                for kc in range(KC):
                    PE.op(nc.tensor.matmul, [sbuf_, self.h_bs[kc], h16b], [pbuf], pt[:, :], lhsT=cols(kc), rhs=slot[:, kc * 512:(kc + 1) * 512],
                          start=(kc == 0), stop=(kc == KC - 1))
                vt, vtb = vst[n % 12], vstb[n % 12]
                n += 1
                DVE.op(nc.vector.tensor_copy, [pbuf], [vtb], out=vt[:, 0:512], in_=pt[:, :])
                if d == 1:
                    r0 = s * T + ch * 128
                    SP.dma(self.vd[g, r0:r0 + 128, half * 512:(half + 1) * 512], vt[:, 0:512], [vtb], [kvb])
                elif d == 4:
                    r0 = ch * 1024 + s * 128
                    SP.dma(self.vd[g, r0:r0 + 128, half * 512:(half + 1) * 512], vt[:, 0:512], [vtb], [kvb])
                else:
                    dst = self.vd[g, :, half * 512:(half + 1) * 512].rearrange("(r j) f -> r j f", r=16)[ch * 4:(ch + 1) * 4, s * 32:(s + 1) * 32, :]
                    SP.dma(dst, vt[:, 0:512], [vtb], [kvb])
        self.stage_end(es)

    def attn_mixer(self, l, s, x, xb):
        nc = self.nc
        PE, ACT, DVE, SP = self.PE, self.ACT, self.DVE, self.SP
        es = self.stage_begin()
        oT = self.sb(es, "oT", [128, KC, T], BF16)
        oTb = self.newbuf("oT")
        qT = [self.sb(es, f"qT{i}", [128, 3, T], BF16) for i in range(2)]
        qTb = [self.newbuf(f"qT{i}") for i in range(2)]
        KW = 640 + 1024 + 2560
        kt = [self.sb(es, f"ktile{i}", [128, KW], BF16) for i in range(2)]
        ktb = [self.newbuf(f"ktile{i}") for i in range(2)]
        kraw = [self.sb(es, f"kraw{i}", [128, 5, T], BF16) for i in range(2)]
        krawb = [self.newbuf(f"kraw{i}") for i in range(2)]
        NVC = 5 + 8 + 32
        vt = [self.sb(es, f"vtile{i}", [128, NVC, 128], BF16) for i in range(2)]
        vtb = [self.newbuf(f"vtile{i}") for i in range(2)]
        NB = 3
        ET = [self.sb(es, f"E{i}", [128, T], BF16) for i in range(NB)]
        ETb = [self.newbuf(f"E{i}") for i in range(NB)]
        PT = [self.sb(es, f"P{i}", [128, T], BF16) for i in range(NB)]
        PTb = [self.newbuf(f"P{i}") for i in range(NB)]
        rD = self.sb(es, "rD", [128, T], F32)
        rDb = self.newbuf("rD")
        self.norm(x, xb, lambda c: self.A(l, 0, c), lambda c: self.M(l, 0, c), reuse=(s % 2 if l == 2 else None))
        kv_reads = [self.kv_bufs[t] for t in range(max(0, s - 4), s + 1)]
        na = min(128, 32 * s)

        ntl = na // 32 + 1

        def load_raw(hp):
            SP.dma(kraw[hp % 2][:, 0:ntl, :], self.kT2d[hp, :, s - ntl + 1:s + 1, :], kv_reads, [krawb[hp % 2]])

        load_raw(0)
        load_raw(1)

        def prep(hp):
            par = hp % 2
            K_, Kb = kt[par], ktb[par]
            V_, Vb = vt[par], vtb[par]
            lo0 = 128 if s == 0 else 0
            SP.dma(K_[:, lo0:640], self.kTd[0, hp, :, s * T - 128 + lo0: s * T + 512], kv_reads, [Kb])
            src_ = self.vd[0, s * T - 128 + lo0: s * T + 512, hp * 128:(hp + 1) * 128].rearrange("(c p) f -> p c f", p=128)
            SP.dma(V_[:, lo0 // 128:5, :], src_, kv_reads, [Vb])
            srck = self.kTd[1, hp, :, :].rearrange("p (r j) -> p r j", r=4)[:, :, 128 * (s - 1) + lo0: 128 * (s + 1)]
            SP.dma(K_[:, 640:640 + 1024].rearrange("p (r j) -> p r j", r=4)[:, :, lo0:256], srck, kv_reads, [Kb])
            c0 = lo0 // 128
            v1 = self.vd[1, :, hp * 128:(hp + 1) * 128].rearrange("(r c p) f -> p r c f", r=4, p=128)
            for c2 in range(c0, 2):
                SP.dma(V_[:, 5:13, :].rearrange("p (r c) f -> p r c f", r=4)[:, :, c2, :], v1[:, :, s - 1 + c2, :], kv_reads, [Vb])
            k2 = K_[:, 640 + 1024:].rearrange("p (r j) -> p r j", r=16)
            v2 = self.vd[2, :, hp * 128:(hp + 1) * 128].rearrange("(r j) f -> j r f", r=16)
            if na > 0:
                SP.dma(V_[0:na, 13:29, :], v2[32 * s - na:32 * s, :, :], kv_reads, [Vb])
            SP.dma(V_[0:32, 29:45, :], v2[32 * s:32 * s + 32, :, :], kv_reads, [Vb])
            slot, sbuf_ = self.w_next(f"wq{l}_{hp}")
            Q_, Qb = qT[par], qTb[par]
            for g in range(3):
                pt, pbuf = self.ps[3], self.ps_b[3]
                for kc in range(KC):
                    base = (g * 8 + kc) * 128
                    PE.op(nc.tensor.matmul, [sbuf_, self.h_bs[kc]], [pbuf], pt[:, :], lhsT=slot[:, base:base + 128], rhs=self.hT[:, kc, :],
                          start=(kc == 0), stop=(kc == KC - 1))
                d = DIL[g]
                if d == 1:
                    ACT.op(nc.scalar.activation, [pbuf], [Qb], out=Q_[:, g, :], in_=pt[:, :], func=AF.Copy)
                else:
                    ACT.op(nc.scalar.activation, [pbuf], [Qb], out=Q_[:, g, :].rearrange("p (r j) -> p r j", r=d),
                           in_=pt[:, :].rearrange("p (j r) -> p r j", r=d), func=AF.Copy)
            R_, Rb = kraw[par], krawb[par]
            DVE.op(nc.vector.tensor_copy, [Rb], [Kb], out=k2[:, :, 128 - na:160].rearrange("p r (t j) -> p r t j", j=32),
                   in_=R_[:, 0:ntl, :].rearrange("p t (r j) -> p r t j", r=16))
            if hp + 2 < 8:
                load_raw(hp + 2)

        units_all = []
        for hp in range(8):
            ulist = []
            for g in range(3):
                batches = []
                if g < 2:
                    for kind in (0, 1):
                        us = []
                        for u in range(4):
                            if g == 0:
                                if kind == 0 and s == 0 and u == 0:
                                    continue
                                us.append((u, (u + kind) * 128, u + kind))
                            else:
                                if kind == 0 and s == 0:
                                    continue
                                us.append((u, 640 + u * 256 + kind * 128, 5 + u * 2 + kind))
                        if us:
                            batches.append((kind, 128, 128, us))
                else:
                    if na > 0:
                        batches.append((0, 32, na, [(r, 640 + 1024 + r * 160 + 128 - na, 13 + r) for r in range(16)]))
                    batches.append((1, 32, 32, [(r, 640 + 1024 + r * 160 + 128, 29 + r) for r in range(16)]))
                for (kind, nq, nk, us) in batches:
                    for hh in range(2):
                        ulist.append(dict(hp=hp, g=g, kind=kind, nq=nq, nk=nk, us=us, hh=hh))
            ulist[0]["first_of_hp"] = True
            ulist[-1]["last_of_hp"] = True
            seen = set()
            for ud in ulist:
                if ud["hh"] not in seen:
                    ud["first_pv"] = True
                    seen.add(ud["hh"])
            units_all += ulist

        def s_phase(i, ud):
            hp, g, kind, nq, nk, us, hh = ud["hp"], ud["g"], ud["kind"], ud["nq"], ud["nk"], ud["us"], ud["hh"]
            par = hp % 2
            K_, Kb = kt[par], ktb[par]
            Q_, Qb = qT[par], qTb[par]
            h = hp * 2 + hh
            r0, r1 = hh * 64, hh * 64 + 64
            si = i % NB
            Sp, Spb = self.ps[si], self.ps_b[si]
            for (u, kcol, vch) in us:
                PE.op(nc.tensor.matmul, [Kb, Qb], [Spb], Sp[0:nk, u * nq:(u + 1) * nq], lhsT=K_[r0:r1, kcol:kcol + nk],
                      rhs=Q_[r0:r1, g, u * nq:(u + 1) * nq], start=True, stop=True)
            u0 = us[0][0]
            u1 = us[-1][0] + 1
            nu = u1 - u0
            E_, Eb = ET[si], ETb[si]
            P_, Pb_ = PT[si], PTb[si]
            ACT.op(nc.scalar.activation, [Spb], [Eb], out=E_[0:nk, u0 * nq:u1 * nq], in_=Sp[0:nk, u0 * nq:u1 * nq],
                   func=AF.Exp, scale=HD ** -0.5)
            if g == 2 and kind == 0 and nk < 128:
                mk = self.masks2[0:nk, nk // 32 - 1, h, 0:32]
            else:
                mk = self.masks[0:nk, g * NH + h, kind * 128: kind * 128 + nq]
            DVE.op(nc.vector.tensor_tensor, [Eb, self.masks_b], [Pb_],
                   out=P_[0:nk, u0 * nq:u1 * nq].rearrange("p (u q) -> p u q", u=nu),
                   in0=E_[0:nk, u0 * nq:u1 * nq].rearrange("p (u q) -> p u q", u=nu),
                   in1=mk.unsqueeze(1).to_broadcast([nk, nu, nq]), op=ALU.mult)

        def pv_phase(i, ud):
            hp, g, kind, nq, nk, us, hh = ud["hp"], ud["g"], ud["kind"], ud["nq"], ud["nk"], ud["us"], ud["hh"]
            par = hp % 2
            V_, Vb = vt[par], vtb[par]
            r0, r1 = hh * 64, hh * 64 + 64
            si = i % NB
            P_, Pb_ = PT[si], PTb[si]
            Np, Npb = self.ps[4 + par], self.ps_b[4 + par]
            Dp, Dpb = self.ps[6 + par], self.ps_b[6 + par]
            d = DIL[g]
            first = ud.get("first_pv", False)
            for (u, kcol, vch) in us:
                if d == 1:
                    No = Np[r0:r1, u * 128:(u + 1) * 128]
                    Do = Dp[r0:r1, u * 128:(u + 1) * 128]
                else:
                    No = Np[r0:r1, :].rearrange("p (j r) -> p r j", r=d)[:, u, :]
                    Do = Dp[r0:r1, :].rearrange("p (j r) -> p r j", r=d)[:, u, :]
                PE.op(nc.tensor.matmul, [Vb, Pb_], [Npb], No, lhsT=V_[0:nk, vch, r0:r1], rhs=P_[0:nk, u * nq:(u + 1) * nq],
                      start=first, stop=False, skip_group_check=True)
                PE.op(nc.tensor.matmul, [self.ones_b, Pb_], [Dpb], Do, lhsT=self.ones[0:nk, 0:64], rhs=P_[0:nk, u * nq:(u + 1) * nq],
                      start=first, stop=False, skip_group_check=True)
                first = False
            if ud.get("last_of_hp"):
                DVE.op(nc.vector.reciprocal, [Dpb], [rDb], out=rD[:, :], in_=Dp[:, :])
                DVE.op(nc.vector.tensor_tensor, [Npb, rDb], [oTb], out=oT[:, hp, :], in0=Np[:, :], in1=rD[:, :], op=ALU.mult)

        LOOK = 2
        n = len(units_all)
        prep(0)
        prep(1)
        for i in range(n + LOOK):
            if i < n:
                s_phase(i, units_all[i])
            if i - LOOK >= 0:
                ud = units_all[i - LOOK]
                pv_phase(i - LOOK, ud)
                if ud.get("last_of_hp") and ud["hp"] + 2 < 8:
                    prep(ud["hp"] + 2)
        self._psi = 0
        for half in range(2):
            slot, sbuf_ = self.w_next(f"wo{l}_{half}")
            for mm in range(4):
                m = half * 4 + mm
                pi = self.psrr()
                pt, pbuf = self.ps[pi], self.ps_b[pi]
                for kc in range(KC):
                    PE.op(nc.tensor.matmul, [sbuf_, oTb], [pbuf], pt[:, :], lhsT=slot[:, kc * 512 + mm * 128: kc * 512 + mm * 128 + 128],
                          rhs=oT[:, kc, :], start=(kc == 0), stop=(kc == KC - 1))
                self.residual(x, xb, m, pt, pbuf, self.M(l, 2, m))
        self.emit_ms()
        self.stage_end(es)


_CACHE = {}


def get_prog(n_tiles=NT, nstages=len(STAGES)):
    key = (n_tiles, nstages)
    if key not in _CACHE:
        p = Prog(n_tiles, nstages)
        p.build()
        _CACHE[key] = p
    return _CACHE[key]


def kernel(**inputs):
    inp = {k: np.asarray(v) for k, v in inputs.items()}
    W, A, per_core = host_prepare(inp)
    p = get_prog()
    in_maps = [{"xT": pc["xT"], "wts": W, "adaw": A, "vecs": pc["vecs"]} for pc in per_core]
    res = run_bass_kernel_spmd(p.nc, in_maps, core_ids=list(range(8)))
    out = np.stack([np.ascontiguousarray(r["outT"].T) for r in res.results], axis=0)
    return out.astype(np.float32)
```
